# Optimizing a Trainium2 kernel written in Bass

```python
import math
import jax
import jax.numpy as jnp
from jax import lax

D_MODEL = 1024
BATCH = 16
SEQ = 256
DEPTH = 4
DEC_BATCH = 8
DEC_SEQ = 4096
PAST_LEN = 256

GRID_W = 64
MIX_WIDTH = D_MODEL
CONV_CH = D_MODEL // 4
CONV_K = 31
GDN_HEADS = 4
GDN_DK = D_MODEL // 8
GDN_DV = D_MODEL // 8
SHORT_K = 3
CHUNK = 64
NA_HEADS = 4
NA_DIM = D_MODEL // 16
WIN_H = 8
WIN_W = 16
D_FF = 11 * D_MODEL // 4
FFN_K = 3
EPS = 1e-6
PROJ_SIZES = (CONV_CH, CONV_CH,
              GDN_HEADS * GDN_DK, GDN_HEADS * GDN_DK, GDN_HEADS * GDN_DV, GDN_HEADS * GDN_DV,
              2 * GDN_HEADS, 2 * GDN_HEADS,
              NA_HEADS * NA_DIM, NA_HEADS * NA_DIM, NA_HEADS * NA_DIM)
PROJ_DIM = sum(PROJ_SIZES)

kernel_name = 'hybrid_dit_conformer_gdn_natten_step'


def rmsnorm(x, g):
    xf = x.astype(jnp.float32)
    y = xf * lax.rsqrt(jnp.mean(xf * xf, axis=-1, keepdims=True) + EPS)
    return (y * g.astype(jnp.float32)).astype(x.dtype)


def layernorm(x, g, b):
    xf = x.astype(jnp.float32)
    mu = jnp.mean(xf, axis=-1, keepdims=True)
    var = jnp.mean(jnp.square(xf - mu), axis=-1, keepdims=True)
    y = (xf - mu) * lax.rsqrt(var + EPS)
    return (y * g.astype(jnp.float32) + b.astype(jnp.float32)).astype(x.dtype)


def l2norm(x):
    return x * lax.rsqrt(jnp.sum(x * x, axis=-1, keepdims=True) + EPS)


def dwconv(x, w):
    k = w.shape[0]
    return lax.conv_general_dilated(
        x, w[:, None, :].astype(x.dtype), window_strides=(1,), padding=[(k // 2, k // 2)],
        dimension_numbers=('NWC', 'WIO', 'NWC'), feature_group_count=x.shape[-1])


def split_projection(p):
    offsets, acc = [], 0
    for s in PROJ_SIZES[:-1]:
        acc += s
        offsets.append(acc)
    return jnp.split(p, offsets, axis=-1)


def conformer_conv(a_val, a_gate, w_dw, b_dw, ln_g, ln_b):
    u = a_val * jax.nn.sigmoid(a_gate)
    u = dwconv(u, w_dw) + b_dw
    return jax.nn.silu(layernorm(u, ln_g, ln_b))


def chunk_gated_delta(q, k, v, g, beta, s0):
    b, h, seq, dk = q.shape
    dv = v.shape[-1]
    n = seq // CHUNK
    q = q.reshape(b, h, n, CHUNK, dk)
    k = k.reshape(b, h, n, CHUNK, dk)
    v = v.reshape(b, h, n, CHUNK, dv)
    g = g.reshape(b, h, n, CHUNK)
    beta = beta.reshape(b, h, n, CHUNK)
    gc = jnp.cumsum(g, axis=-1)
    idx = jnp.arange(CHUNK)
    tril = idx[:, None] >= idx[None, :]
    strict = idx[:, None] > idx[None, :]
    decay = jnp.exp(jnp.where(tril, gc[..., :, None] - gc[..., None, :], -jnp.inf))
    kb = k * beta[..., None]
    lmat = jnp.where(strict, jnp.einsum('bhnid,bhnjd->bhnij', kb, k) * decay, 0.0)
    rhs = jnp.concatenate([v * beta[..., None], kb * jnp.exp(gc)[..., None]], axis=-1)
    sol = lax.linalg.triangular_solve(jnp.eye(CHUNK, dtype=q.dtype) + lmat, rhs,
                                      left_side=True, lower=True, unit_diagonal=True)
    u, w = sol[..., :dv], sol[..., dv:]
    intra = jnp.where(tril, jnp.einsum('bhnid,bhnjd->bhnij', q, k) * decay, 0.0)
    q_dec = q * jnp.exp(gc)[..., None]
    k_dec = k * jnp.exp(gc[..., -1:] - gc)[..., None]
    g_last = jnp.exp(gc[..., -1])

    def step(state, xs):
        qd, kd, ui, wi, it, gl = xs
        v_new = ui - jnp.einsum('bhcd,bhde->bhce', wi, state)
        o = jnp.einsum('bhcd,bhde->bhce', qd, state) + jnp.einsum('bhij,bhje->bhie', it, v_new)
        state = state * gl[..., None, None] + jnp.einsum('bhcd,bhce->bhde', kd, v_new)
        return state, o

    xs = tuple(jnp.moveaxis(t, 2, 0) for t in (q_dec, k_dec, u, w, intra, g_last))
    s_final, o = lax.scan(step, s0, xs)
    o = jnp.moveaxis(o, 0, 2).reshape(b, h, seq, dv)
    return o, s_final


def gated_deltanet(q, k, v, z, beta_raw, decay_raw, w_sc, a_log, dt_bias, norm_w, s0_f, s0_b):
    b, seq, _ = q.shape
    f32 = jnp.float32
    qkv = jax.nn.silu(dwconv(jnp.concatenate([q, k, v], axis=-1), w_sc))
    q, k, v = jnp.split(qkv, [GDN_HEADS * GDN_DK, 2 * GDN_HEADS * GDN_DK], axis=-1)

    def heads(t, d):
        return t.reshape(b, seq, GDN_HEADS, d).transpose(0, 2, 1, 3).astype(f32)

    qh = l2norm(heads(q, GDN_DK)) * (GDN_DK ** -0.5)
    kh = l2norm(heads(k, GDN_DK))
    vh = heads(v, GDN_DV)
    beta = jax.nn.sigmoid(beta_raw.astype(f32)).reshape(b, seq, 2, GDN_HEADS).transpose(2, 0, 3, 1)
    a_in = decay_raw.astype(f32).reshape(b, seq, 2, GDN_HEADS).transpose(2, 0, 3, 1)
    g = -jnp.exp(a_log.astype(f32))[:, None, :, None] * jax.nn.softplus(a_in + dt_bias.astype(f32)[:, None, :, None])
    o_f, s_f = chunk_gated_delta(qh, kh, vh, g[0], beta[0], s0_f)

    def flip(t):
        return jnp.flip(t, axis=2)

    o_b, s_b = chunk_gated_delta(flip(qh), flip(kh), flip(vh), flip(g[1]), flip(beta[1]), s0_b)
    o = (o_f + flip(o_b)).transpose(0, 2, 1, 3)
    o = rmsnorm(o, norm_w) * jax.nn.silu(z.reshape(b, seq, GDN_HEADS, GDN_DV).astype(f32))
    return o.reshape(b, seq, GDN_HEADS * GDN_DV).astype(z.dtype), s_f, s_b


def context_attention(q, k, v):
    b, seq, _ = q.shape
    qh, kh, vh = (t.reshape(b, seq, NA_HEADS, NA_DIM).transpose(0, 2, 1, 3) for t in (q, k, v))
    s = jnp.einsum('bhqd,bhkd->bhqk', qh, kh).astype(jnp.float32) * (NA_DIM ** -0.5)
    p = jax.nn.softmax(s, axis=-1).astype(v.dtype)
    o = jnp.einsum('bhqk,bhkd->bqhd', p, vh).reshape(b, seq, NA_HEADS * NA_DIM)
    return o, kh, vh


def neighbourhood_attention(q, k, v, k_ctx, v_ctx, rpb):
    b, n, _ = q.shape
    rows = n // GRID_W
    wr = min(WIN_H, rows)
    qg, kg, vg = (t.reshape(b, rows, GRID_W, NA_HEADS, NA_DIM) for t in (q, k, v))
    r = jnp.arange(rows)
    row_start = jnp.clip(r - wr // 2, 0, rows - wr)
    row_idx = row_start[:, None] + jnp.arange(wr)[None, :]
    k_win = kg[:, row_idx]
    v_win = vg[:, row_idx]
    col = jnp.arange(GRID_W)
    col_start = jnp.clip(col - WIN_W // 2, 0, GRID_W - WIN_W)
    col_ok = (col[None, :] >= col_start[:, None]) & (col[None, :] < col_start[:, None] + WIN_W)
    dr = row_idx - r[:, None] + (WIN_H - 1)
    dc = jnp.clip(col[None, :] - col[:, None] + (WIN_W - 1), 0, 2 * WIN_W - 2)
    bias = rpb[:, dr[:, None, :, None], dc[None, :, None, :]]
    scale = NA_DIM ** -0.5
    s_loc = jnp.einsum('brqhd,brikhd->bhrqik', qg, k_win).astype(jnp.float32) * scale + bias.astype(jnp.float32)
    s_loc = jnp.where(col_ok[:, None, :], s_loc, -jnp.inf).reshape(b, NA_HEADS, rows, GRID_W, wr * GRID_W)
    s_ctx = jnp.einsum('brqhd,bhkd->bhrqk', qg, k_ctx).astype(jnp.float32) * scale
    p = jax.nn.softmax(jnp.concatenate([s_loc, s_ctx], axis=-1), axis=-1).astype(v.dtype)
    p_loc = p[..., :wr * GRID_W].reshape(b, NA_HEADS, rows, GRID_W, wr, GRID_W)
    p_ctx = p[..., wr * GRID_W:]
    o = jnp.einsum('bhrqik,brikhd->brqhd', p_loc, v_win) + jnp.einsum('bhrqk,bhkd->brqhd', p_ctx, v_ctx)
    return o.reshape(b, n, NA_HEADS * NA_DIM)


def mixing(h, lp, ctx):
    (a_val, a_gate, q_b, k_b, v_b, z_b, beta_raw, decay_raw,
     q_c, k_c, v_c) = split_projection(h @ lp['w_in'])
    o_a = conformer_conv(a_val, a_gate, lp['conv_w'], lp['conv_b'], lp['conv_ln_g'], lp['conv_ln_b'])
    if ctx is None:
        s0 = jnp.zeros((h.shape[0], GDN_HEADS, GDN_DK, GDN_DV), jnp.float32)
        o_b, s_f, s_b = gated_deltanet(q_b, k_b, v_b, z_b, beta_raw, decay_raw, lp['gdn_conv_w'],
                                       lp['gdn_a_log'], lp['gdn_dt_bias'], lp['gdn_norm_w'], s0, s0)
        o_c, k_h, v_h = context_attention(q_c, k_c, v_c)
        extras = (k_h, v_h, s_f, s_b)
    else:
        k_ctx, v_ctx, s0_f, s0_b = ctx
        o_b, _, _ = gated_deltanet(q_b, k_b, v_b, z_b, beta_raw, decay_raw, lp['gdn_conv_w'],
                                   lp['gdn_a_log'], lp['gdn_dt_bias'], lp['gdn_norm_w'], s0_f, s0_b)
        o_c = neighbourhood_attention(q_c, k_c, v_c, k_ctx, v_ctx, lp['na_rpb'])
        extras = None
    out = jnp.concatenate([o_a, o_b, o_c], axis=-1) @ lp['w_out']
    return out, extras


def conv_ffn(h, w_up, w_conv, b_conv, w_down):
    u = dwconv(h @ w_up, w_conv) + b_conv
    gate, val = jnp.split(u, 2, axis=-1)
    return (jax.nn.silu(gate) * val) @ w_down


def trunk_layer(x, cvec, lp, ctx):
    sh1, sc1, g1, sh2, sc2, g2 = jnp.split(jax.nn.silu(cvec) @ lp['w_ada'] + lp['b_ada'], 6, axis=-1)
    h = rmsnorm(x, lp['g_pre_mix']) * (1.0 + sc1) + sh1
    m, extras = mixing(h, lp, ctx)
    x = x + g1 * rmsnorm(m, lp['g_post_mix'])
    h = rmsnorm(x, lp['g_pre_ffn']) * (1.0 + sc2) + sh2
    f = conv_ffn(h, lp['w_up'], lp['ffn_conv_w'], lp['ffn_conv_b'], lp['w_down'])
    x = x + g2 * rmsnorm(f, lp['g_post_ffn'])
    return x, extras


def setup_inputs(seed: int = 0) -> dict:
    key = jax.random.key(seed)
    ks = jax.random.split(key, 28)
    f32 = jnp.float32

    def nrm(k, shape, s):
        return s * jax.random.normal(k, shape, f32)

    def gain(k, shape):
        return 1.0 + 0.05 * jax.random.normal(k, shape, f32)

    dt = jnp.exp(jax.random.uniform(ks[20], (DEPTH, 2, GDN_HEADS), f32, math.log(0.001), math.log(0.1)))
    dt_bias = dt + jnp.log(-jnp.expm1(-dt))
    qkv_ch = 2 * GDN_HEADS * GDN_DK + GDN_HEADS * GDN_DV
    return {
        'x_prompt': nrm(ks[0], (BATCH, SEQ, D_MODEL), 1.0),
        'x_sample': nrm(ks[1], (DEC_BATCH, DEC_SEQ, D_MODEL), 1.0),
        'cache_attn_kv': nrm(ks[2], (DEC_BATCH, DEPTH, 2, NA_HEADS, PAST_LEN, NA_DIM), 1.0),
        'state_delta': nrm(ks[3], (DEC_BATCH, DEPTH, 2, GDN_HEADS, GDN_DK, GDN_DV), 0.1),
        'c': nrm(ks[4], (DEC_BATCH, D_MODEL), 1.0),
        'c_ctx': nrm(ks[5], (D_MODEL,), 1.0),
        'w_ada': nrm(ks[6], (DEPTH, D_MODEL, 6 * D_MODEL), 0.5 * D_MODEL ** -0.5),
        'b_ada': nrm(ks[7], (DEPTH, 6 * D_MODEL), 0.01),
        'g_pre_mix': gain(ks[8], (DEPTH, D_MODEL)),
        'g_post_mix': gain(ks[9], (DEPTH, D_MODEL)),
        'g_pre_ffn': gain(ks[10], (DEPTH, D_MODEL)),
        'g_post_ffn': gain(ks[11], (DEPTH, D_MODEL)),
        'w_in': nrm(ks[12], (DEPTH, D_MODEL, PROJ_DIM), D_MODEL ** -0.5),
        'w_out': nrm(ks[13], (DEPTH, MIX_WIDTH, D_MODEL), MIX_WIDTH ** -0.5),
        'conv_w': nrm(ks[14], (DEPTH, CONV_K, CONV_CH), CONV_K ** -0.5),
        'conv_b': nrm(ks[15], (DEPTH, CONV_CH), 0.01),
        'conv_ln_g': gain(ks[16], (DEPTH, CONV_CH)),
        'conv_ln_b': nrm(ks[17], (DEPTH, CONV_CH), 0.01),
        'gdn_conv_w': nrm(ks[18], (DEPTH, SHORT_K, qkv_ch), SHORT_K ** -0.5),
        'gdn_a_log': jnp.log(jax.random.uniform(ks[19], (DEPTH, 2, GDN_HEADS), f32, 1.0, 16.0)),
        'gdn_dt_bias': dt_bias,
        'gdn_norm_w': gain(ks[21], (DEPTH, GDN_DV)),
        'na_rpb': nrm(ks[22], (DEPTH, NA_HEADS, 2 * WIN_H - 1, 2 * WIN_W - 1), 0.1),
        'w_up': nrm(ks[23], (DEPTH, D_MODEL, 2 * D_FF), D_MODEL ** -0.5),
        'ffn_conv_w': nrm(ks[24], (DEPTH, FFN_K, 2 * D_FF), FFN_K ** -0.5),
        'ffn_conv_b': nrm(ks[25], (DEPTH, 2 * D_FF), 0.01),
        'w_down': nrm(ks[26], (DEPTH, D_FF, D_MODEL), D_FF ** -0.5),
    }


def reference(x_prompt, x_sample, cache_attn_kv, state_delta, c, c_ctx,
              w_ada, b_ada, g_pre_mix, g_post_mix, g_pre_ffn, g_post_ffn,
              w_in, w_out, conv_w, conv_b, conv_ln_g, conv_ln_b,
              gdn_conv_w, gdn_a_log, gdn_dt_bias, gdn_norm_w, na_rpb,
              w_up, ffn_conv_w, ffn_conv_b, w_down):
    cvec_ctx = c_ctx[None, None, :]
    cvec_lat = c[:, None, :]
    xp, xs = x_prompt, x_sample
    kv_new, st_new = [], []
    for l in range(DEPTH):
        lp = {
            'w_ada': w_ada[l], 'b_ada': b_ada[l],
            'g_pre_mix': g_pre_mix[l], 'g_post_mix': g_post_mix[l],
            'g_pre_ffn': g_pre_ffn[l], 'g_post_ffn': g_post_ffn[l],
            'w_in': w_in[l], 'w_out': w_out[l],
            'conv_w': conv_w[l], 'conv_b': conv_b[l], 'conv_ln_g': conv_ln_g[l], 'conv_ln_b': conv_ln_b[l],
            'gdn_conv_w': gdn_conv_w[l], 'gdn_a_log': gdn_a_log[l], 'gdn_dt_bias': gdn_dt_bias[l],
            'gdn_norm_w': gdn_norm_w[l], 'na_rpb': na_rpb[l],
            'w_up': w_up[l], 'ffn_conv_w': ffn_conv_w[l], 'ffn_conv_b': ffn_conv_b[l], 'w_down': w_down[l],
        }
        xp, (k_h, v_h, s_f, s_b) = trunk_layer(xp, cvec_ctx, lp, None)
        kv_new.append(jnp.stack([k_h, v_h], axis=1))
        st_new.append(jnp.stack([s_f, s_b], axis=1).astype(x_prompt.dtype))
        ctx = (cache_attn_kv[:, l, 0], cache_attn_kv[:, l, 1],
               state_delta[:, l, 0].astype(jnp.float32), state_delta[:, l, 1].astype(jnp.float32))
        xs, _ = trunk_layer(xs, cvec_lat, lp, ctx)
    new_cache_attn_kv = jnp.stack(kv_new, axis=1)
    new_state_delta = jnp.stack(st_new, axis=1)
    return (xp, xs, new_cache_attn_kv, new_state_delta)
```

```python
import contextlib
import numpy as np
import concourse.bass as bass
import concourse.mybir as mybir
from concourse.bass_utils import run_bass_kernel_spmd

F32 = mybir.dt.float32
BF16 = mybir.dt.bfloat16
AF = mybir.ActivationFunctionType
ALU = mybir.AluOpType
AX = mybir.AxisListType

PE, ACT, DVE, POOL, SP = "pe", "act", "dve", "pool", "sp"
COMPUTE = (PE, ACT, DVE, POOL)
ALLENG = (PE, ACT, DVE, POOL, SP)
DMAQ = (SP, ACT, POOL)
ND = 4
SAME_ENGINE_SYNC = True

D = 1024
DEPTH = 4
TL = 4096
TC = 256
TT = TL + 2 * TC
NCH = 27
PROJ = 3344
DFF = 2816
EPS = 1e-6
NEG = -30000.0
NEGBIG = -1.0e5
HW_ = TT + 6
SEQS = ((0, TL, 1, 1), (TL, TC, 4099, 0), (TL + TC, TC, 4357, 0))


class Buf:
    __slots__ = ("name", "t", "lw", "rd")

    def __init__(self, name, t):
        self.name = name
        self.t = t
        self.lw = None
        self.rd = []

    def __getitem__(self, k):
        return self.t[k]


class Op:
    __slots__ = ("eng", "fn", "deps", "marked", "is_dma", "sem", "cnt", "epoch", "kind")

    def __init__(self, eng, fn, is_dma, epoch, kind="op"):
        self.eng = eng
        self.fn = fn
        self.deps = []
        self.marked = False
        self.is_dma = is_dma
        self.sem = None
        self.cnt = 0
        self.epoch = epoch
        self.kind = kind


class Sched:
    def __init__(self, nc):
        self.nc = nc
        self.ops = []
        self.epoch = 0
        self.eng = {PE: nc.tensor, ACT: nc.scalar, DVE: nc.vector, POOL: nc.gpsimd, SP: nc.sync}

    def op(self, eng, fn, reads=(), writes=(), dma=False):
        o = Op(eng, fn, dma, self.epoch)
        deps = {}
        for b in reads:
            if b.lw is not None:
                deps[id(b.lw)] = (b.lw, "raw")
        for b in writes:
            if b.lw is not None:
                deps[id(b.lw)] = (b.lw, "waw")
            for r in b.rd:
                if id(r) not in deps:
                    deps[id(r)] = (r, "war")
        for d, kind in deps.values():
            if d.epoch != o.epoch or d is o:
                continue
            if (not d.is_dma) and d.eng == eng and not dma:
                if eng == PE or kind == "war" or not SAME_ENGINE_SYNC:
                    continue
            o.deps.append(d)
        for b in reads:
            b.rd.append(o)
        for b in writes:
            b.lw = o
            b.rd = []
        self.ops.append(o)
        return o

    def barrier(self, new_epoch=False):
        self.ops.append(Op(None, None, False, self.epoch, kind="barrier_ep" if new_epoch else "barrier"))
        if new_epoch:
            self.epoch += 1

    def emit(self, es):
        nc = self.nc
        nep = self.epoch + 1
        for o in self.ops:
            for d in o.deps:
                d.marked = True
        sems = []
        for ep in range(nep):
            d = {}
            for e in COMPUTE:
                d[e] = es.enter_context(nc.semaphore(f"s{ep}_{e}"))
            for q in DMAQ:
                for k in range(ND):
                    d[(q, k)] = es.enter_context(nc.semaphore(f"d{ep}_{q}{k}"))
            sems.append(d)
        cur = [dict((k, 0) for k in sems[ep]) for ep in range(nep)]
        rr = dict((q, 0) for q in DMAQ)
        seen = {}
        n_wait = 0
        for o in self.ops:
            ep = o.epoch
            if o.kind != "op":
                for e in ALLENG:
                    for key, val in cur[ep].items():
                        if val > 0 and seen.get((e, key), 0) < val:
                            self.eng[e].wait_ge(sems[ep][key], val)
                            seen[(e, key)] = val
                            n_wait += 1
                if o.kind == "barrier_ep":
                    seen = {}
                continue
            e = self.eng[o.eng]
            need = {}
            for d in o.deps:
                if need.get(d.sem, 0) < d.cnt:
                    need[d.sem] = d.cnt
            for key, val in need.items():
                if seen.get((o.eng, key), 0) >= val:
                    continue
                e.wait_ge(sems[ep][key], val)
                n_wait += 1
                seen[(o.eng, key)] = val
            inst = o.fn(e)
            if o.is_dma:
                key = (o.eng, rr[o.eng] % ND)
                rr[o.eng] += 1
                cur[ep][key] += 16
                o.sem, o.cnt = key, cur[ep][key]
                inst.then_inc(sems[ep][key], 16)
            elif o.marked:
                cur[ep][o.eng] += 1
                o.sem, o.cnt = o.eng, cur[ep][o.eng]
                inst.then_inc(sems[ep][o.eng], 1)
        ep = nep - 1
        for e in ALLENG:
            for key, val in cur[ep].items():
                if val > 0:
                    self.eng[e].wait_ge(sems[ep][key], val)
        return dict(n_ops=len(self.ops), n_wait=n_wait, counts=cur)


C_IDENT, C_ONES, C_JREV, C_TRIF, C_TRIB, C_BLK, C_HALFA, C_HALFB = range(8)
C_UINC, C_LSTR, C_LINC, C_USTR = 8, 9, 10, 11
C_CMH = 12
C_EPS = 13
NCST = 14


def _consts():
    p = np.arange(128)[:, None]
    f = np.arange(128)[None, :]
    same = (p // 64) == (f // 64)
    c = np.zeros((128, NCST, 128), np.float32)
    c[:, C_IDENT] = (p == f)
    c[:, C_ONES] = 1.0
    c[:, C_JREV] = same & ((p % 64) == (63 - f % 64))
    c[:, C_TRIF] = same & (p <= f)
    c[:, C_TRIB] = same & (p >= f)
    c[:, C_BLK] = same
    c[:, C_HALFA] = (p < 64) & (f >= 0)
    c[:, C_HALFB] = (p >= 64) & (f >= 0)
    c[:, C_UINC] = np.where(same & (f >= p), 0.0, NEGBIG)
    c[:, C_LSTR] = np.where(same & (p > f), 0.0, NEGBIG)
    c[:, C_LINC] = np.where(same & (f <= p), 0.0, NEGBIG)
    c[:, C_USTR] = np.where(same & (p < f), 0.0, NEGBIG)
    qc = 63 - (np.arange(128) % 64)
    cs = np.clip(qc - 8, 0, 48)
    kc = np.arange(64)[None, :]
    ok = (kc >= cs[:, None]) & (kc < cs[:, None] + 16)
    c[:, C_CMH, 0:64] = np.where(ok, 0.0, NEG)
    c[:, C_EPS, 0] = EPS
    c[:, C_EPS, 1] = 1.0
    return c.reshape(128, NCST * 128)


PV_GPM, PV_GPOM, PV_GPF, PV_GPOF = 0, 8, 16, 24
PV_BADA = 32
PV_CW = 80
PV_CB, PV_CLG, PV_CLB = 142, 144, 146
PV_GCW = 148
PV_GNW = 184
PV_FCW = 185
PV_FCB = 317
PV_N = 361


def _fm(v, nchunk):
    return np.ascontiguousarray(v.reshape(nchunk, 128).T)


def _pack_pv(inp):
    pv = np.zeros((128, DEPTH, PV_N), np.float32)
    for l in range(DEPTH):
        pv[:, l, PV_GPM:PV_GPM + 8] = _fm(inp["g_pre_mix"][l], 8)
        pv[:, l, PV_GPOM:PV_GPOM + 8] = _fm(inp["g_post_mix"][l], 8)
        pv[:, l, PV_GPF:PV_GPF + 8] = _fm(inp["g_pre_ffn"][l], 8)
        pv[:, l, PV_GPOF:PV_GPOF + 8] = _fm(inp["g_post_ffn"][l], 8)
        pv[:, l, PV_BADA:PV_BADA + 48] = _fm(inp["b_ada"][l], 48)
        cw = inp["conv_w"][l]
        pv[:, l, PV_CW:PV_CW + 62] = cw.T.reshape(2, 128, 31).transpose(1, 0, 2).reshape(128, 62)
        pv[:, l, PV_CB:PV_CB + 2] = _fm(inp["conv_b"][l], 2)
        pv[:, l, PV_CLG:PV_CLG + 2] = _fm(inp["conv_ln_g"][l], 2)
        pv[:, l, PV_CLB:PV_CLB + 2] = _fm(inp["conv_ln_b"][l], 2)
        gw = inp["gdn_conv_w"][l]
        pv[:, l, PV_GCW:PV_GCW + 36] = gw.T.reshape(12, 128, 3).transpose(1, 0, 2).reshape(128, 36)
        pv[:, l, PV_GNW] = inp["gdn_norm_w"][l]
        fw = inp["ffn_conv_w"][l]
        pv[:, l, PV_FCW:PV_FCW + 132] = fw.T.reshape(44, 128, 3).transpose(1, 0, 2).reshape(128, 132)
        pv[:, l, PV_FCB:PV_FCB + 44] = _fm(inp["ffn_conv_b"][l], 44)
    return pv.reshape(128, DEPTH * PV_N)


_WIN_PERM = np.concatenate([np.arange(0, 2560), np.arange(2576, 3344), np.arange(2560, 2576)])


class KB:
    def __init__(self, nlayers=DEPTH, dbg=None):
        self.nl = nlayers
        self.dbg = dbg or {}
        nc = self.nc = bass.Bass("TRN2", target_bir_lowering=False)
        self.S = Sched(nc)
        self.es = contextlib.ExitStack()
        self.uid = 0

    def mm(self, out, lhsT, rhs, start=True, stop=True, r=(), w=()):
        return self.S.op(PE, lambda e: e.matmul(out, lhsT, rhs, start=start, stop=stop), r, w)

    def tr(self, out, in_, ident, r=(), w=()):
        return self.S.op(PE, lambda e: e.transpose(out, in_, ident), r, w)

    def act(self, out, in_, func, r=(), w=(), eng=ACT, **kw):
        return self.S.op(eng, lambda e: e.activation(out=out, in_=in_, func=func, **kw), r, w)

    def tt(self, eng, out, in0, in1, op, r=(), w=()):
        return self.S.op(eng, lambda e: e.tensor_tensor(out=out, in0=in0, in1=in1, op=op), r, w)

    def ts(self, eng, out, in0, s1, s2, op0, op1=None, r=(), w=()):
        if op1 is None:
            return self.S.op(eng, lambda e: e.tensor_scalar(out=out, in0=in0, scalar1=s1, scalar2=None, op0=op0), r, w)
        return self.S.op(eng, lambda e: e.tensor_scalar(out=out, in0=in0, scalar1=s1, scalar2=s2, op0=op0, op1=op1), r, w)

    def stt(self, out, in0, scalar, in1, op0, op1, r=(), w=()):
        return self.S.op(DVE, lambda e: e.scalar_tensor_tensor(out=out, in0=in0, scalar=scalar, in1=in1, op0=op0, op1=op1), r, w)

    def cp(self, eng, out, in_, r=(), w=()):
        if eng == ACT:
            return self.S.op(ACT, lambda e: e.copy(out=out, in_=in_), r, w)
        return self.S.op(eng, lambda e: e.tensor_copy(out=out, in_=in_), r, w)

    def memset(self, eng, ap, val, w=()):
        return self.S.op(eng, lambda e: e.memset(ap, val), (), w)

    def recip(self, out, in_, r=(), w=()):
        return self.S.op(DVE, lambda e: e.reciprocal(out=out, in_=in_), r, w)

    def dma(self, q, out, in_, r=(), w=()):
        return self.S.op(q, lambda e: e.dma_start(out=out, in_=in_), r, w, dma=True)

    def sb(self, es, name, shape, dt=F32):
        self.uid += 1
        return Buf(name, es.enter_context(self.nc.sbuf_tensor(f"{name}_{self.uid}", shape, dt)))

    def dram(self, name, shape, dt, kind="Internal"):
        if name in self.dbg.get("out", ()):
            kind = "ExternalOutput"
        if name in self.dbg.get("in", ()):
            kind = "ExternalInput"
        return self.nc.dram_tensor(name, shape, dt, kind=kind).ap()

    def rsqrt_ps(self, out, ps_ap, scale, r, w):
        self.act(out, ps_ap, AF.Sqrt, r=list(r) + [self.cst], w=w, bias=self.c_eps, scale=scale)
        self.recip(out, out, r=w, w=w)

    def build(self):
        nc, S, es = self.nc, self.S, self.es
        nl = self.nl
        I = {}
        def ext(name, shape, dt=F32):
            I[name] = nc.dram_tensor(name, shape, dt, kind="ExternalInput").ap()
            return I[name]
        self.xin = ext("xin", [D, TT])
        self.cvec = ext("cvec", [128, 16])
        self.cst_d = ext("cst", [128, NCST * 128])
        self.pv_d = ext("pv", [128, DEPTH * PV_N])
        self.w_ada = ext("w_ada", [nl, D, 6 * D])
        self.w_in = ext("w_in", [nl, D, PROJ])
        self.w_out = ext("w_out", [nl, D, D])
        self.w_up = ext("w_up", [nl, D, 2 * DFF])
        self.w_down = ext("w_down", [nl, DFF, D])
        self.kcache = ext("kcache", [nl, 2, 128, TC])
        self.vcache = ext("vcache", [nl, 2, 128, 256])
        self.s0 = ext("s0", [nl, 128, 8, 128])
        self.gsc = ext("gsc", [nl, 16])
        self.rpb = ext("rpb", [nl, 60, 31])
        self.xout = nc.dram_tensor("xout", [D, TT], F32, kind="ExternalOutput").ap()
        self.kvout = nc.dram_tensor("kvout", [nl, 2, 2, 256, TC], F32, kind="ExternalOutput").ap()
        self.stout = nc.dram_tensor("stout", [2, nl, 2, 4, 128, 128], F32, kind="ExternalOutput").ap()
        self.P = self.dram("P", [NCH * 128, TT], F32)
        self.PB = self.dram("PB", [6 * 128, TT], BF16)
        self.CAT = self.dram("CAT", [D, TT], BF16)
        self.XA = self.dram("XA", [D, TT], F32)
        self.A = self.dram("A", [DFF, TT], BF16)
        self.OF = self.dram("OF", [512, TT], F32)
        self.OFB = self.dram("OFB", [512, TT], F32)
        self.FP = self.dram("FP", [nl, 60, 127], F32)

        with es:
            pst = es.enter_context(nc.psum_tensor("ps", [128, 4096], F32))
            self.bank = [Buf(f"bank{i}", pst[:, i * 512:(i + 1) * 512]) for i in range(8)]
            self.pst = pst
            self.cst = self.sb(es, "cst", [128, NCST, 128])
            self.cstb = self.sb(es, "cstb", [128, 3, 128], BF16)
            self.pv = self.sb(es, "pv", [128, DEPTH, PV_N])
            self.mod = self.sb(es, "mod", [128, 6, 8, 2])
            self.scv = self.sb(es, "scv", [128, 8, 2])
            self.c_eps = self.cst[:, C_EPS, 0:1]
            self.dma(SP, self.cst[:].rearrange("p a b -> p (a b)"), self.cst_d[:, :], w=[self.cst])
            self.dma(SP, self.pv[:].rearrange("p a b -> p (a b)"), self.pv_d[:, :], w=[self.pv])
            self.dma(SP, self.scv[:].rearrange("p a b -> p (a b)"), self.cvec[:, :], w=[self.scv])
            for i, ci in enumerate((C_IDENT, C_ONES, C_JREV)):
                self.cp(DVE, self.cstb[:, i, :], self.cst[:, ci, :], r=[self.cst], w=[self.cstb])
            self.ident = self.cst[:, C_IDENT, :]
            self.ones = self.cst[:, C_ONES, :]
            self.identb = self.cstb[:, 0, :]
            self.onesb = self.cstb[:, 1, :]
            self.jrevb = self.cstb[:, 2, :]
            self.act(self.scv[:], self.scv[:], AF.Silu, r=[self.scv], w=[self.scv])
            for l in range(nl):
                self.layer(l)
                if l + 1 < nl:
                    S.barrier(new_epoch=True)
            info = S.emit(es)
        return info

    def layer(self, l):
        S = self.S
        stages = self.dbg.get("stages", ("ada", "s1", "conf", "gdn", "attn", "s3", "s4a", "s4b"))
        xsrc = self.xin if l == 0 else self.xout
        if "ada" in stages:
            self.stage_ada(l)
            S.barrier()
        if "s1" in stages:
            self.stage_s1(l, xsrc)
            S.barrier()
        if "conf" in stages:
            self.stage_conf(l)
            S.barrier()
        if "attn" in stages:
            self.stage_attn(l)
            S.barrier()
        if "gdn" in stages:
            self.stage_gdn(l)
            S.barrier()
        if "s3" in stages:
            with contextlib.ExitStack() as es:
                self.Hb = self.sb(es, "H", [128, 8, HW_], BF16)
                self.stage_s3(l, xsrc, es)
                S.barrier()
                if "s4a" in stages:
                    self.stage_s4a(l)
                    S.barrier()
        if "s4b" in stages:
            self.stage_s4b(l)
            S.barrier()

    def stage_ada(self, l):
        with contextlib.ExitStack() as es:
            wb = [self.sb(es, f"adaw{i}", [128, 8, 512]) for i in range(6)]
            raw = self.sb(es, "adaraw", [128, 48, 2])
            ps = self.bank[0]
            for g in range(12):
                w = wb[g % 6]
                self.dma(SP, w[:],
                         self.w_ada[l, :, g * 512:(g + 1) * 512].rearrange("(kc p) n -> p kc n", p=128), w=[w])
                for j in range(4):
                    n = g * 4 + j
                    for kc in range(8):
                        self.mm(ps[:, 2 * n:2 * n + 2], w[:, kc, j * 128:(j + 1) * 128], self.scv[:, kc, :],
                                start=(kc == 0), stop=(kc == 7), r=[w, self.scv], w=[ps])
            pvl = self.pv[:, l, :]
            self.tt(DVE, raw[:], ps[:, 0:96].rearrange("p (n v) -> p n v", v=2),
                    pvl[:, PV_BADA:PV_BADA + 48].unsqueeze(2).to_broadcast([128, 48, 2]), ALU.add,
                    r=[ps, self.pv], w=[raw])
            def gain(off):
                return pvl[:, off:off + 8].unsqueeze(2).to_broadcast([128, 8, 2])
            m = self.mod
            self.ts(DVE, m[:, 0], raw[:, 8:16, :], 1.0, None, ALU.add, r=[raw], w=[m])
            self.tt(DVE, m[:, 0], m[:, 0], gain(PV_GPM), ALU.mult, r=[m, self.pv], w=[m])
            self.cp(DVE, m[:, 1], raw[:, 0:8, :], r=[raw], w=[m])
            self.tt(DVE, m[:, 2], raw[:, 16:24, :], gain(PV_GPOM), ALU.mult, r=[raw, self.pv], w=[m])
            self.ts(DVE, m[:, 3], raw[:, 32:40, :], 1.0, None, ALU.add, r=[raw], w=[m])
            self.tt(DVE, m[:, 3], m[:, 3], gain(PV_GPF), ALU.mult, r=[m, self.pv], w=[m])
            self.cp(DVE, m[:, 4], raw[:, 24:32, :], r=[raw], w=[m])
            self.tt(DVE, m[:, 5], raw[:, 40:48, :], gain(PV_GPOF), ALU.mult, r=[raw, self.pv], w=[m])

    def sumsq_rstd(self, src, sq, ps, rstd, n, nchunk=8, scale=1.0 / D, src_bufs=()):
        for c in range(nchunk):
            self.act(sq[:, c, 0:n], src[:, c, 0:n], AF.Square, r=list(src_bufs), w=[sq], eng=ACT)
        for c in range(nchunk):
            self.mm(ps[:, 0:n], self.onesb, sq[:, c, 0:n], start=(c == 0), stop=(c == nchunk - 1),
                    r=[sq, self.cstb], w=[ps])
        self.rsqrt_ps(rstd[:, 0:n], ps[:, 0:n], scale, r=[ps], w=[rstd])

    def s1_tiles(self):
        tl = [(j * 512, 512, 1, [(1 + j * 512, 512)]) for j in range(8)]
        tl.append((TL, 512, 0, [(4099, 256), (4357, 256)]))
        return tl

    def stage_s1(self, l, xsrc):
        with contextlib.ExitStack() as es:
            H = self.sb(es, "H", [128, 8, HW_], BF16)
            xt = [self.sb(es, f"xt{i}", [128, 8, 512]) for i in range(2)]
            sq = self.sb(es, "sq", [128, 8, 512], BF16)
            rstd = self.sb(es, "rstd", [128, 512])
            wck = [self.sb(es, f"wck{i}", [128, 8, 128], BF16) for i in range(3)]
            ev = [self.sb(es, f"ev{i}", [128, 512]) for i in range(4)]
            evb = [self.sb(es, f"evb{i}", [128, 512], BF16) for i in range(2)]
            tiles = self.s1_tiles()
            m = self.mod
            for ti, (t0, n, vec, hc) in enumerate(tiles):
                x = xt[ti % 2]
                self.dma(SP, x[:], xsrc[:, t0:t0 + n].rearrange("(c p) t -> p c t", p=128), w=[x])
                ps = self.bank[ti % 2]
                self.sumsq_rstd(x, sq, ps, rstd, n, src_bufs=[x])
                self.tt(DVE, x[:], x[:], rstd[:].unsqueeze(1).to_broadcast([128, 8, 512]), ALU.mult, r=[x, rstd], w=[x])
                for c in range(8):
                    off = 0
                    for (h0, nn) in hc:
                        eng = ACT if c % 2 == 0 else POOL
                        if eng == ACT:
                            self.act(H[:, c, h0:h0 + nn], x[:, c, off:off + nn], AF.Identity, r=[x, m], w=[H],
                                     scale=m[:, 0, c, vec:vec + 1], bias=m[:, 1, c, vec:vec + 1])
                        else:
                            self.ts(POOL, H[:, c, h0:h0 + nn], x[:, c, off:off + nn], m[:, 0, c, vec:vec + 1],
                                    m[:, 1, c, vec:vec + 1], ALU.mult, ALU.add, r=[x, m], w=[H])
                        off += nn
            k = 0
            for nci in range(NCH):
                ncols = 128 if nci < 26 else 16
                w = wck[nci % 3]
                self.dma(POOL, w[:, :, 0:ncols],
                         self.w_in[l, :, nci * 128:nci * 128 + ncols].rearrange("(kc p) n -> p kc n", p=128), w=[w])
                for ti, (t0, n, vec, hc) in enumerate(tiles):
                    ps = self.bank[2 + (k % 4)]
                    off = 0
                    for (h0, nn) in hc:
                        for kc in range(8):
                            self.mm(ps[0:ncols, off:off + nn], w[:, kc, 0:ncols], H[:, kc, h0:h0 + nn],
                                    start=(kc == 0), stop=(kc == 7), r=[w, H], w=[ps])
                        off += nn
                    e = ev[k % 4]
                    if k % 2 == 0:
                        self.cp(ACT, e[0:ncols, :], ps[0:ncols, :], r=[ps], w=[e])
                    else:
                        self.cp(DVE, e[0:ncols, :], ps[0:ncols, :], r=[ps], w=[e])
                    self.dma(SP, self.P[nci * 128:nci * 128 + ncols, t0:t0 + n], e[0:ncols, :], r=[e])
                    if 20 <= nci < 26:
                        eb = evb[k % 2]
                        self.cp(POOL, eb[:], e[:], r=[e], w=[eb])
                        self.dma(SP, self.PB[(nci - 20) * 128:(nci - 19) * 128, t0:t0 + n], eb[:], r=[eb])
                    if ti == 8 and 22 <= nci < 26:
                        kv = (nci - 22) // 2
                        f0 = ((nci - 22) % 2) * 128
                        for s in range(2):
                            self.dma(SP, self.kvout[l, s, kv, f0:f0 + 128, :], e[:, s * 256:(s + 1) * 256], r=[e])
                    k += 1

    def stage_s3(self, l, xsrc, es0):
        H = self.Hb
        with contextlib.ExitStack() as es:
            wo = self.sb(es, "wo", [128, 8, D], BF16)
            cat = [self.sb(es, f"cat{i}", [128, 8, 512], BF16) for i in range(2)]
            xt = [self.sb(es, f"xt{i}", [128, 8, 512]) for i in range(2)]
            msb = self.sb(es, "msb", [128, 8, 512])
            sq = self.sb(es, "sq", [128, 8, 512], BF16)
            rstd = self.sb(es, "rstd", [128, 512])
            zc = self.sb(es, "zc", [128, 8, 2], BF16)
            parts = self.dbg.get("s3parts", "wpx")
            if "w" in parts:
                self.dma(POOL, wo[:], self.w_out[l].rearrange("(kc p) n -> p kc n", p=128), w=[wo])
            if "p" in parts:
                self.memset(POOL, zc[:], 0.0, w=[zc])
                for c0, n0 in ((0, 1), (4097, 2), (4355, 2), (4613, 1)):
                    self.cp(POOL, H[:, :, c0:c0 + n0], zc[:, :, 0:n0], r=[zc], w=[H])
            if "x" not in parts:
                return
            m = self.mod
            lvl = int(self.dbg.get("s3n", 9))
            for ti, (t0, n, vec, hc) in enumerate(self.s1_tiles()):
                ct = cat[ti % 2]
                x = xt[ti % 2]
                self.dma(SP, ct[:], self.CAT[:, t0:t0 + n].rearrange("(c p) t -> p c t", p=128), w=[ct])
                self.dma(SP, x[:], xsrc[:, t0:t0 + n].rearrange("(c p) t -> p c t", p=128), w=[x])
                if lvl < 2:
                    continue
                for nn in range(8):
                    ps = self.bank[nn % 4]
                    for kc in range(8):
                        self.mm(ps[:, 0:n], wo[:, kc, nn * 128:(nn + 1) * 128], ct[:, kc, :],
                                start=(kc == 0), stop=(kc == 7), r=[wo, ct], w=[ps])
                    if "c" in self.dbg.get("s3l2", "cs"):
                        self.cp(DVE, msb[:, nn, :], ps[:, 0:n], r=[ps], w=[msb])
                    if "s" in self.dbg.get("s3l2", "cs"):
                        self.act(sq[:, nn, :], msb[:, nn, :], AF.Square, r=[msb], w=[sq])
                if lvl < 3:
                    continue
                pss = self.bank[4 + ti % 2]
                for c in range(8):
                    self.mm(pss[:, 0:n], self.onesb, sq[:, c, :], start=(c == 0), stop=(c == 7), r=[sq, self.cstb], w=[pss])
                self.rsqrt_ps(rstd[:, 0:n], pss[:, 0:n], 1.0 / D, r=[pss], w=[rstd])
                self.tt(DVE, msb[:], msb[:], rstd[:].unsqueeze(1).to_broadcast([128, 8, 512]), ALU.mult, r=[msb, rstd], w=[msb])
                if lvl < 4:
                    continue
                for c in range(8):
                    self.stt(x[:, c, :], msb[:, c, :], m[:, 2, c, vec:vec + 1], x[:, c, :], ALU.mult, ALU.add,
                             r=[msb, m, x], w=[x])
                self.dma(SP, self.XA[:, t0:t0 + n].rearrange("(c p) t -> p c t", p=128), x[:], r=[x])
                if lvl < 5:
                    continue
                pss2 = self.bank[6 + ti % 2]
                self.sumsq_rstd(x, sq, pss2, rstd, n, src_bufs=[x])
                self.tt(DVE, msb[:], x[:], rstd[:].unsqueeze(1).to_broadcast([128, 8, 512]), ALU.mult, r=[x, rstd], w=[msb])
                if lvl < 6:
                    continue
                for c in range(8):
                    off = 0
                    for (h0, nn2) in hc:
                        if c % 2 == 0:
                            self.act(H[:, c, h0:h0 + nn2], msb[:, c, off:off + nn2], AF.Identity, r=[msb, m], w=[H],
                                     scale=m[:, 3, c, vec:vec + 1], bias=m[:, 4, c, vec:vec + 1])
                        else:
                            self.ts(POOL, H[:, c, h0:h0 + nn2], msb[:, c, off:off + nn2], m[:, 3, c, vec:vec + 1],
                                    m[:, 4, c, vec:vec + 1], ALU.mult, ALU.add, r=[msb, m], w=[H])
                        off += nn2

    def ffn_tiles(self):
        tl = []
        t = 0
        while t < TL:
            n = min(456, TL - t)
            tl.append((t, n, t))
            t += n
        tl.append((TL, TC, 4098))
        tl.append((TL + TC, TC, 4356))
        return tl

    def stage_s4a(self, l):
        H = self.Hb
        with contextlib.ExitStack() as es:
            wg = [self.sb(es, f"wg{i}", [128, 8, 128], BF16) for i in range(2)]
            wv = [self.sb(es, f"wv{i}", [128, 8, 128], BF16) for i in range(2)]
            tg = [self.sb(es, f"tg{i}", [128, 512]) for i in range(2)]
            tv = [self.sb(es, f"tv{i}", [128, 512]) for i in range(2)]
            ao = [self.sb(es, f"ao{i}", [128, 512], BF16) for i in range(3)]
            pvl = self.pv[:, l, :]
            k = 0
            for j in range(22):
                g, v = wg[j % 2], wv[j % 2]
                self.dma(POOL, g[:], self.w_up[l, :, j * 128:(j + 1) * 128].rearrange("(kc p) n -> p kc n", p=128), w=[g])
                self.dma(POOL, v[:], self.w_up[l, :, (22 + j) * 128:(23 + j) * 128].rearrange("(kc p) n -> p kc n", p=128), w=[v])
                for (t0, n, h0) in self.ffn_tiles():
                    pg = self.bank[(2 * k) % 8]
                    pv_ = self.bank[(2 * k + 1) % 8]
                    for kc in range(8):
                        self.mm(pg[:, 0:n + 2], g[:, kc, :], H[:, kc, h0:h0 + n + 2], start=(kc == 0), stop=(kc == 7), r=[g, H], w=[pg])
                    for kc in range(8):
                        self.mm(pv_[:, 0:n + 2], v[:, kc, :], H[:, kc, h0:h0 + n + 2], start=(kc == 0), stop=(kc == 7), r=[v, H], w=[pv_])
                    a, b_ = tg[k % 2], tv[k % 2]
                    for (ps, t, ch) in ((pg, a, j), (pv_, b_, 22 + j)):
                        cw = PV_FCW + 3 * ch
                        self.act(t[:, 0:n], ps[:, 0:n], AF.Identity, r=[ps, self.pv], w=[t],
                                 scale=pvl[:, cw:cw + 1], bias=pvl[:, PV_FCB + ch:PV_FCB + ch + 1])
                        self.stt(t[:, 0:n], ps[:, 1:n + 1], pvl[:, cw + 1:cw + 2], t[:, 0:n], ALU.mult, ALU.add, r=[ps, self.pv, t], w=[t])
                        self.stt(t[:, 0:n], ps[:, 2:n + 2], pvl[:, cw + 2:cw + 3], t[:, 0:n], ALU.mult, ALU.add, r=[ps, self.pv, t], w=[t])
                    self.act(a[:, 0:n], a[:, 0:n], AF.Silu, r=[a], w=[a])
                    o = ao[k % 3]
                    self.tt(POOL, o[:, 0:n], a[:, 0:n], b_[:, 0:n], ALU.mult, r=[a, b_], w=[o])
                    self.dma(SP, self.A[j * 128:(j + 1) * 128, t0:t0 + n], o[:, 0:n], r=[o])
                    k += 1

    def stage_s4b(self, l):
        with contextlib.ExitStack() as es:
            wd = self.sb(es, "wd", [128, 22, D], BF16)
            at = [self.sb(es, f"at{i}", [128, 22, 512], BF16) for i in range(2)]
            xa = [self.sb(es, f"xa{i}", [128, 8, 512]) for i in range(2)]
            fsb = self.sb(es, "fsb", [128, 8, 512])
            sq = self.sb(es, "sq", [128, 8, 512], BF16)
            rstd = self.sb(es, "rstd", [128, 512])
            for h in range(2):
                self.dma(POOL, wd[:, h * 11:(h + 1) * 11, :],
                         self.w_down[l, h * 11 * 128:(h + 1) * 11 * 128, :].rearrange("(j p) n -> p j n", p=128), w=[wd])
            m = self.mod
            for ti, (t0, n, vec, hc) in enumerate(self.s1_tiles()):
                a = at[ti % 2]
                x = xa[ti % 2]
                self.dma(SP, a[:], self.A[:, t0:t0 + n].rearrange("(j p) t -> p j t", p=128), w=[a])
                self.dma(SP, x[:], self.XA[:, t0:t0 + n].rearrange("(c p) t -> p c t", p=128), w=[x])
                for nn in range(8):
                    ps = self.bank[nn % 4]
                    for j in range(22):
                        self.mm(ps[:, 0:n], wd[:, j, nn * 128:(nn + 1) * 128], a[:, j, :], start=(j == 0), stop=(j == 21), r=[wd, a], w=[ps])
                    self.cp(DVE, fsb[:, nn, :], ps[:, 0:n], r=[ps], w=[fsb])
                    self.act(sq[:, nn, :], fsb[:, nn, :], AF.Square, r=[fsb], w=[sq])
                pss = self.bank[4 + ti % 2]
                for c in range(8):
                    self.mm(pss[:, 0:n], self.onesb, sq[:, c, :], start=(c == 0), stop=(c == 7), r=[sq, self.cstb], w=[pss])
                self.rsqrt_ps(rstd[:, 0:n], pss[:, 0:n], 1.0 / D, r=[pss], w=[rstd])
                self.tt(DVE, fsb[:], fsb[:], rstd[:].unsqueeze(1).to_broadcast([128, 8, 512]), ALU.mult, r=[fsb, rstd], w=[fsb])
                for c in range(8):
                    self.stt(x[:, c, :], fsb[:, c, :], m[:, 5, c, vec:vec + 1], x[:, c, :], ALU.mult, ALU.add, r=[fsb, m, x], w=[x])
                self.dma(SP, self.xout[:, t0:t0 + n].rearrange("(c p) t -> p c t", p=128), x[:], r=[x])

    def stage_conf(self, l):
        pvl = self.pv[:, l, :]
        with contextlib.ExitStack() as es:
            av = [self.sb(es, f"cav{c}", [128, TL]) for c in range(2)]
            ag = [self.sb(es, f"cag{c}", [128, TL]) for c in range(2)]
            U = [self.sb(es, f"cU{c}", [128, TL + 30]) for c in range(2)]
            sq = [self.sb(es, f"csq{c}", [128, 512]) for c in range(2)]
            st = self.sb(es, "cst", [128, 3, 512])
            yb = [self.sb(es, f"cy{c}", [128, 512]) for c in range(2)]
            ob = [self.sb(es, f"cob{i}", [128, 512], BF16) for i in range(4)]
            k = 0
            for (tok0, T, hcol0, vec) in SEQS:
                for c in range(2):
                    self.dma(SP, av[c][:, 0:T], self.P[c * 128:(c + 1) * 128, tok0:tok0 + T], w=[av[c]])
                    self.dma(SP, ag[c][:, 0:T], self.P[(2 + c) * 128:(3 + c) * 128, tok0:tok0 + T], w=[ag[c]])
                    self.memset(POOL, U[c][:, 0:15], 0.0, w=[U[c]])
                    self.memset(POOL, U[c][:, 15 + T:30 + T], 0.0, w=[U[c]])
                    self.act(ag[c][:, 0:T], ag[c][:, 0:T], AF.Sigmoid, r=[ag[c]], w=[ag[c]])
                    self.tt(POOL, U[c][:, 15:15 + T], av[c][:, 0:T], ag[c][:, 0:T], ALU.mult, r=[av[c], ag[c]], w=[U[c]])
                    acc = av[c]
                    cw = PV_CW + 31 * c
                    self.act(acc[:, 0:T], U[c][:, 0:T], AF.Identity, r=[U[c], self.pv], w=[acc],
                             scale=pvl[:, cw:cw + 1], bias=pvl[:, PV_CB + c:PV_CB + c + 1])
                    for kk in range(1, 31):
                        self.stt(acc[:, 0:T], U[c][:, kk:kk + T], pvl[:, cw + kk:cw + kk + 1], acc[:, 0:T], ALU.mult, ALU.add,
                                 r=[U[c], self.pv, acc], w=[acc])
                for t in range(0, T, 512):
                    n = min(512, T - t)
                    p1 = self.bank[(2 * k) % 8]
                    p2 = self.bank[(2 * k + 1) % 8]
                    for c in range(2):
                        self.act(sq[c][:, 0:n], av[c][:, t:t + n], AF.Square, r=[av[c]], w=[sq[c]])
                    for c in range(2):
                        self.mm(p1[:, 0:n], self.ones, av[c][:, t:t + n], start=(c == 0), stop=(c == 1), r=[self.cst, av[c]], w=[p1])
                    for c in range(2):
                        self.mm(p2[:, 0:n], self.ones, sq[c][:, 0:n], start=(c == 0), stop=(c == 1), r=[self.cst, sq[c]], w=[p2])
                    self.ts(DVE, st[:, 0, 0:n], p1[:, 0:n], 1.0 / 256, None, ALU.mult, r=[p1], w=[st])
                    self.tt(DVE, st[:, 1, 0:n], st[:, 0, 0:n], st[:, 0, 0:n], ALU.mult, r=[st], w=[st])
                    self.stt(st[:, 1, 0:n], p2[:, 0:n], 1.0 / 256, st[:, 1, 0:n], ALU.mult, ALU.subtract, r=[p2, st], w=[st])
                    self.ts(DVE, st[:, 1, 0:n], st[:, 1, 0:n], 0.0, None, ALU.max, r=[st], w=[st])
                    self.act(st[:, 2, 0:n], st[:, 1, 0:n], AF.Sqrt, r=[st, self.cst], w=[st], bias=self.c_eps, scale=1.0)
                    self.recip(st[:, 2, 0:n], st[:, 2, 0:n], r=[st], w=[st])
                    for c in range(2):
                        self.tt(DVE, yb[c][:, 0:n], av[c][:, t:t + n], st[:, 0, 0:n], ALU.subtract, r=[av[c], st], w=[yb[c]])
                        self.tt(POOL, yb[c][:, 0:n], yb[c][:, 0:n], st[:, 2, 0:n], ALU.mult, r=[yb[c], st], w=[yb[c]])
                        o = ob[(2 * k + c) % 4]
                        self.act(o[:, 0:n], yb[c][:, 0:n], AF.Silu, r=[yb[c], self.pv], w=[o],
                                 scale=pvl[:, PV_CLG + c:PV_CLG + c + 1], bias=pvl[:, PV_CLB + c:PV_CLB + c + 1])
                        self.dma(SP, self.CAT[c * 128:(c + 1) * 128, tok0 + t:tok0 + t + n], o[:, 0:n], r=[o])
                    k += 1

    def attn_unit(self, slot, qf, hp, b0, t0, pieces, vts, otm, h, W):
        nk = sum(p[3] for p in pieces)
        nkt = nk // 128
        base = slot * 1024
        ps = self.pst[:, base:base + nk]
        pb = [self.bank[2 * slot], self.bank[2 * slot + 1]]
        lhsT = qf[b0:b0 + 64, hp, t0:t0 + 128]
        for (col, rhs, bm, n) in pieces:
            self.mm(self.pst[:, base + col:base + col + n], lhsT, rhs, start=True, stop=(bm is None), r=W["qk"], w=pb)
            if bm is not None:
                self.mm(self.pst[:, base + col:base + col + n], self.jrevb, bm, start=False, stop=True, r=W["bm"], w=pb)
        sm = W["sm"][slot * 2 + h % 2]
        pexp = W["pexp"][slot]
        pts = W["pts"][slot]
        yield
        self.S.op(DVE, lambda e: e.reduce_max(out=sm[:, 0:1], in_=ps, axis=AX.X), pb, [sm])
        yield
        self.ts(DVE, sm[:, 1:2], sm[:, 0:1], -0.125, None, ALU.mult, r=[sm], w=[sm])
        self.memset(DVE, sm[:, 2:3], 0.0, w=[sm])
        yield
        self.act(pexp[:, 0:nk], ps, AF.Exp, r=pb + [sm], w=[pexp, sm], scale=0.125, bias=sm[:, 1:2], accum_out=sm[:, 2:3])
        yield
        ptbank = self.bank[4 + slot]
        ptb = ptbank.t.bitcast(BF16)
        for c in range(nkt):
            self.tr(ptb[:, c * 128:(c + 1) * 128], pexp[:, c * 128:(c + 1) * 128], self.identb, r=[pexp, self.cstb], w=[ptbank])
        yield
        self.cp(ACT if slot == 0 else DVE, pts[:, 0:nkt, :].rearrange("p a b -> p (a b)"), ptb[:, 0:nk], r=[ptbank], w=[pts])
        self.recip(sm[:, 3:4], sm[:, 2:3], r=[sm], w=[sm])
        yield
        po = self.bank[6]
        for c in range(nkt):
            self.mm(po[:, 0:64], pts[:, c, :], vts[c], start=(c == 0), stop=(c == nkt - 1), r=[pts] + W["v"], w=[po])
        self.ts(DVE, otm[:, h * 64:(h + 1) * 64], po[:, 0:64], sm[:, 3:4], None, ALU.mult, r=[po, sm], w=[otm])
        yield

    def stage_attn(self, l):
        with contextlib.ExitStack() as es:
            qf = self.sb(es, "aqf", [128, 2, TT], BF16)
            kf = self.sb(es, "akf", [128, 2, TT], BF16)
            vf = self.sb(es, "avf", [128, 2, TT], BF16)
            VT = self.sb(es, "aVT", [128, 36, 256], BF16)
            kctx = self.sb(es, "akctx", [128, 2, TC], BF16)
            vctx = self.sb(es, "avctx", [128, 2, 256], BF16)
            HK = self.sb(es, "aHK", [128, 4, 15, 64])
            BMc = [self.sb(es, f"aBM{c}", [128, 4, 640], BF16) for c in range(5)]
            FT = self.sb(es, "aFT", [64, 127])
            r31 = self.sb(es, "ar31", [64, 31])
            ocr = self.sb(es, "aocr", [128, 2, TT], BF16)
            otms = [self.sb(es, f"aotm{i}", [128, 256], BF16) for i in range(2)]
            W = dict(sm=[self.sb(es, f"asm{i}", [128, 4]) for i in range(4)],
                     pexp=[self.sb(es, f"apexp{i}", [128, 896], BF16) for i in range(2)],
                     pts=[self.sb(es, f"apts{i}", [128, 7, 128], BF16) for i in range(2)])
            fpb = Buf("FPb", None)
            for hp in range(2):
                self.dma(SP, qf[:, hp, :], self.PB[hp * 128:(hp + 1) * 128, :], w=[qf])
                self.dma(SP, kf[:, hp, :], self.PB[(2 + hp) * 128:(3 + hp) * 128, :], w=[kf])
                self.dma(SP, vf[:, hp, :], self.PB[(4 + hp) * 128:(5 + hp) * 128, :], w=[vf])
            self.dma(POOL, kctx[:], self.kcache[l].rearrange("a p t -> p a t"), w=[kctx])
            self.dma(POOL, vctx[:], self.vcache[l].rearrange("a p f -> p a f"), w=[vctx])
            for g in range(9):
                bk = self.bank[g % 2]
                bkb = bk.t.bitcast(BF16)
                for jj in range(4):
                    tile = g * 4 + jj
                    for hp in range(2):
                        self.tr(bkb[:, (jj * 2 + hp) * 128:(jj * 2 + hp + 1) * 128], vf[:, hp, tile * 128:(tile + 1) * 128],
                                self.identb, r=[vf, self.cstb], w=[bk])
                self.cp(DVE if g % 2 == 0 else ACT, VT[:, g * 4:(g + 1) * 4, :].rearrange("p a f -> p (a f)"), bkb[:, 0:1024], r=[bk], w=[VT])
            self.memset(POOL, FT[0:60, :], NEG, w=[FT])
            self.dma(SP, r31[0:60, :], self.rpb[l], w=[r31])
            self.act(FT[0:60, 48:79], r31[0:60, :], AF.Copy, r=[r31, FT], w=[FT], scale=8.0)
            self.dma(SP, self.FP[l], FT[0:60, :], r=[FT], w=[fpb])
            for half in range(2):
                src = bass.AP(tensor=self.FP.tensor, offset=l * 60 * 127, ap=[[1, 64], [127, 60], [1, 64]])
                self.dma(SP, HK[64 * half:64 * half + 64].rearrange("p h d k -> p (h d) k"), src, r=[fpb], w=[HK])
            self.tt(DVE, HK[:].rearrange("p h d k -> p (h d) k"), HK[:].rearrange("p h d k -> p (h d) k"),
                    self.cst[:, C_CMH, 0:64].unsqueeze(1).to_broadcast([128, 60, 64]), ALU.add, r=[HK, self.cst], w=[HK])
            engs = (POOL, DVE, ACT, POOL, DVE)
            for case, i in ((0, 0), (1, 1), (2, 2), (3, 30), (4, 31)):
                bm = BMc[case]
                eng = engs[case]
                self.memset(eng if eng != ACT else POOL, bm[:], NEG, w=[bm])
                start = min(max(2 * i - 4, 0), 54)
                for qr in range(2):
                    r_ = 2 * i + qr
                    rs = min(max(r_ - 4, 0), 56)
                    for kr in range(10):
                        krow = start + kr
                        if rs <= krow < rs + 8:
                            dr = krow - r_ + 7
                            self.cp(eng, bm[64 * qr:64 * qr + 64, :, kr * 64:(kr + 1) * 64], HK[64 * qr:64 * qr + 64, :, dr, :], r=[HK], w=[bm])
            W["qk"] = [qf, kf, kctx]
            W["v"] = [VT, vctx]
            ob7 = self.bank[7].t.bitcast(BF16)

            def qtile(t0, units):
                def gen(slot):
                    otm = otms[slot]
                    for (hp, b0, pieces, vts, h, bmb) in units:
                        W["bm"] = [bmb, self.cstb] if bmb is not None else [self.cstb]
                        yield from self.attn_unit(slot, qf, hp, b0, t0, pieces, vts, otm, h, W)
                    ob = self.bank[7]
                    for hp in range(2):
                        self.tr(ob7[:, hp * 128:(hp + 1) * 128], otm[:, hp * 128:(hp + 1) * 128], self.identb, r=[otm, self.cstb], w=[ob])
                    self.cp(ACT if slot == 0 else DVE, ocr[:, :, t0:t0 + 128], ob7[:, 0:256].rearrange("p (a b) -> p a b", a=2), r=[ob], w=[ocr])
                    yield
                return gen

            jobs = []
            for i in range(32):
                t0 = 128 * i
                start = min(max(2 * i - 4, 0), 54)
                k0 = 64 * start
                case = {0: 0, 1: 1, 30: 3, 31: 4}.get(i, 2)
                units = []
                for h in range(4):
                    hp, b0 = h // 2, 64 * (h % 2)
                    bm = BMc[case]
                    pieces = [(0, kf[b0:b0 + 64, hp, k0:k0 + 512], bm[:, h, 0:512], 512),
                              (512, kf[b0:b0 + 64, hp, k0 + 512:k0 + 640], bm[:, h, 512:640], 128),
                              (640, kctx[b0:b0 + 64, hp, :], None, 256)]
                    vts = [VT[:, start // 2 + c, h * 64:(h + 1) * 64] for c in range(5)] + [vctx[:, c, h * 64:(h + 1) * 64] for c in range(2)]
                    units.append((hp, b0, pieces, vts, h, bm))
                jobs.append(qtile(t0, units))
            for s_ in range(2):
                tok0 = TL + s_ * TC
                for qi in range(2):
                    t0 = tok0 + 128 * qi
                    units = []
                    for h in range(4):
                        hp, b0 = h // 2, 64 * (h % 2)
                        pieces = [(0, kf[b0:b0 + 64, hp, tok0:tok0 + 256], None, 256)]
                        vts = [VT[:, tok0 // 128 + c, h * 64:(h + 1) * 64] for c in range(2)]
                        units.append((hp, b0, pieces, vts, h, None))
                    jobs.append(qtile(t0, units))
            self.run_pool(jobs, 2)
            for hp in range(2):
                self.dma(SP, self.CAT[768 + hp * 128:768 + (hp + 1) * 128, :], ocr[:, hp, :], r=[ocr])

    @staticmethod
    def run_pool(jobs, nslots):
        jobs = iter(jobs)
        active = {}
        free = list(range(nslots))
        while True:
            while free:
                jb = next(jobs, None)
                if jb is None:
                    break
                sl = free.pop(0)
                active[sl] = jb(sl)
            if not active:
                break
            for sl in list(active):
                try:
                    next(active[sl])
                except StopIteration:
                    del active[sl]
                    free.append(sl)

    def stage_gdn(self, l):
        pvl = self.pv[:, l, :]
        bank = self.bank
        S = self.S
        with contextlib.ExitStack() as es:
            sb = lambda name, shape, dt=F32: self.sb(es, "g" + name, shape, dt)
            qf, kf, vf = sb("qf", [128, 4, TL], BF16), sb("kf", [128, 4, TL], BF16), sb("vf", [128, 4, TL], BF16)
            NPM = TL // 128
            GT = sb("GT", [128, NPM, 16])
            beta, nbeta, g, GC, kds, negc = (sb(n, [128, NPM, 8]) for n in ("beta", "nbeta", "g", "GC", "kds", "negc"))
            t1, t2 = sb("t1", [128, NPM, 8]), sb("t2", [128, NPM, 8])
            egl = sb("egl", [128, NPM, 2, 8])
            gsb = sb("gsb", [128, 16])
            nega = sb("nega", [128, 8])
            ident, identb, ones = self.ident, self.identb, self.ones
            cst = self.cst
            one_ap = self.cst[:, C_EPS, 1:2]
            X, Y, Z, Wk, V, A = bank[0], bank[1], bank[2], bank[3], bank[4], bank[5]
            B2 = [bank[6], bank[7]]
            Bap = self.pst[:, 6 * 512:8 * 512].rearrange("p (h a k) -> p h a k", h=4, a=2)
            Xb = X.t.bitcast(BF16)
            def v3(b):
                return b.t.rearrange("p (h k) -> p h k", h=4)
            def bc_h(ap2):
                return ap2.unsqueeze(1).to_broadcast([128, 4, 128])
            def bc_i(ap2):
                return ap2.unsqueeze(2).to_broadcast([128, 4, 128])
            self.dma(SP, gsb[:], self.gsc[l:l + 1, :].partition_broadcast(128), w=[gsb])
            self.act(nega[:], gsb[:, 0:8], AF.Exp, r=[gsb], w=[nega])
            self.ts(DVE, nega[:], nega[:], -1.0, None, ALU.mult, r=[nega], w=[nega])
            for si, (tok0, T, hcol0, vec) in enumerate(SEQS):
                NP_ = T // 128
                with contextlib.ExitStack() as es2:
                    sb2 = lambda name, shape, dt=F32: self.sb(es2, "g" + name, shape, dt)
                    raw = sb2("raw", [128, 2050])
                    cv = sb2("cv", [128, 2048])
                    sq = sb2("sq", [128, 512], BF16)
                    rs = sb2("rs", [128, 512])
                    bd = sb2("bd", [16, 512])
                    for ti, (dst, pc0) in enumerate(((qf, 4), (kf, 8), (vf, 12))):
                        for h in range(4):
                            row0 = (pc0 + h) * 128
                            cw = PV_GCW + 3 * (ti * 4 + h)
                            for half in range(0, T, 2048):
                                n = min(2048, T - half)
                                a = max(half - 1, 0)
                                b_ = min(half + n + 1, T)
                                off = a - (half - 1)
                                if off > 0:
                                    self.memset(POOL, raw[:, 0:1], 0.0, w=[raw])
                                if b_ - (half - 1) < n + 2:
                                    self.memset(POOL, raw[:, n + 1:n + 2], 0.0, w=[raw])
                                self.dma(SP, raw[:, off:off + (b_ - a)], self.P[row0:row0 + 128, tok0 + a:tok0 + b_], w=[raw])
                                self.act(cv[:, 0:n], raw[:, 0:n], AF.Copy, r=[raw, self.pv], w=[cv], scale=pvl[:, cw:cw + 1])
                                self.stt(cv[:, 0:n], raw[:, 1:n + 1], pvl[:, cw + 1:cw + 2], cv[:, 0:n], ALU.mult, ALU.add, r=[raw, self.pv, cv], w=[cv])
                                self.stt(cv[:, 0:n], raw[:, 2:n + 2], pvl[:, cw + 2:cw + 3], cv[:, 0:n], ALU.mult, ALU.add, r=[raw, self.pv, cv], w=[cv])
                                self.act(cv[:, 0:n], cv[:, 0:n], AF.Silu, r=[cv], w=[cv])
                                if ti == 2:
                                    self.cp(POOL, dst[:, h, half:half + n], cv[:, 0:n], r=[cv], w=[dst])
                                else:
                                    for t in range(0, n, 512):
                                        m = min(512, n - t)
                                        ps = bank[(t // 512) % 4]
                                        self.act(sq[:, 0:m], cv[:, t:t + m], AF.Square, r=[cv], w=[sq])
                                        self.mm(ps[:, 0:m], self.onesb, sq[:, 0:m], r=[sq, self.cstb], w=[ps])
                                        self.rsqrt_ps(rs[:, 0:m], ps[:, 0:m], 1.0, r=[ps], w=[rs])
                                        if ti == 0:
                                            self.stt(dst[:, h, half + t:half + t + m], cv[:, t:t + m], 128.0 ** -0.5, rs[:, 0:m], ALU.mult, ALU.mult, r=[cv, rs], w=[dst])
                                        else:
                                            self.tt(DVE, dst[:, h, half + t:half + t + m], cv[:, t:t + m], rs[:, 0:m], ALU.mult, r=[cv, rs], w=[dst])
                    for jb in range(0, T, 512):
                        n = min(512, T - jb)
                        self.dma(SP, bd[:, 0:n], self.P[26 * 128:26 * 128 + 16, tok0 + jb:tok0 + jb + n], w=[bd])
                        for j in range(n // 128):
                            self.tr(X[:, j * 16:(j + 1) * 16], bd[0:16, j * 128:(j + 1) * 128], ident[0:16, 0:16], r=[bd, cst], w=[X])
                        self.cp(DVE, GT[:, jb // 128:jb // 128 + n // 128, :].rearrange("p a b -> p (a b)"), X[:, 0:(n // 128) * 16], r=[X], w=[GT])
                    NB = NP_ * 8
                    def f2(t):
                        return t[:, 0:NP_, :].rearrange("p a b -> p (a b)")
                    dtb = gsb[:, 8:16].unsqueeze(1).to_broadcast([128, NP_, 8])
                    self.act(beta[:, 0:NP_, :], GT[:, 0:NP_, 0:8], AF.Sigmoid, r=[GT], w=[beta])
                    self.ts(DVE, nbeta[:, 0:NP_, :], beta[:, 0:NP_, :], -1.0, None, ALU.mult, r=[beta], w=[nbeta])
                    self.tt(DVE, t1[:, 0:NP_, :], GT[:, 0:NP_, 8:16], dtb, ALU.add, r=[GT, gsb], w=[t1])
                    self.act(t2[:, 0:NP_, :], t1[:, 0:NP_, :], AF.Abs, r=[t1], w=[t2])
                    self.act(t2[:, 0:NP_, :], t2[:, 0:NP_, :], AF.Exp, r=[t2], w=[t2], scale=-1.0)
                    self.act(t2[:, 0:NP_, :], t2[:, 0:NP_, :], AF.Ln, r=[t2, cst], w=[t2], bias=one_ap, scale=1.0)
                    self.ts(DVE, t1[:, 0:NP_, :], t1[:, 0:NP_, :], 0.0, None, ALU.max, r=[t1], w=[t1])
                    self.tt(DVE, t1[:, 0:NP_, :], t1[:, 0:NP_, :], t2[:, 0:NP_, :], ALU.add, r=[t1, t2], w=[t1])
                    self.tt(DVE, g[:, 0:NP_, :], t1[:, 0:NP_, :], nega[:].unsqueeze(1).to_broadcast([128, NP_, 8]), ALU.mult, r=[t1, nega], w=[g])
                    g2 = f2(g)
                    self.mm(X[:, 0:NB], cst[:, C_TRIF, :], g2, r=[cst, g], w=[X])
                    self.mm(Y[:, 0:NB], cst[:, C_TRIB, :], g2, r=[cst, g], w=[Y])
                    self.mm(Z[:, 0:NB], cst[:, C_BLK, :], g2, r=[cst, g], w=[Z])
                    self.mm(Wk[:, 0:NB], cst[:, C_HALFA, :], g2, r=[cst, g], w=[Wk])
                    self.mm(V[:, 0:NB], cst[:, C_HALFB, :], g2, r=[cst, g], w=[V])
                    self.cp(DVE, GC[:, 0:NP_, 0:4], X[:, 0:NB].rearrange("p (a b) -> p a b", b=8)[:, :, 0:4], r=[X], w=[GC])
                    self.cp(DVE, GC[:, 0:NP_, 4:8], Y[:, 0:NB].rearrange("p (a b) -> p a b", b=8)[:, :, 4:8], r=[Y], w=[GC])
                    self.tt(DVE, f2(kds), Z[:, 0:NB], f2(GC), ALU.subtract, r=[Z, GC], w=[kds])
                    self.act(f2(kds), f2(kds), AF.Exp, r=[kds], w=[kds])
                    self.act(f2(t1), f2(GC), AF.Exp, r=[GC], w=[t1])
                    self.tt(DVE, f2(negc), f2(nbeta), f2(t1), ALU.mult, r=[nbeta, t1], w=[negc])
                    self.act(egl[:, 0:NP_, 0, :], Wk[:, 0:NB].rearrange("p (a b) -> p a b", b=8), AF.Exp, r=[Wk], w=[egl])
                    self.act(egl[:, 0:NP_, 1, :], V[:, 0:NB].rearrange("p (a b) -> p a b", b=8), AF.Exp, r=[V], w=[egl])
                S.barrier()
                with contextlib.ExitStack() as es3:
                    def mkset(d):
                        sb3 = lambda name, shape, dt=F32: self.sb(es3, f"g{d}" + name, shape, dt)
                        outs = [dict(Kd=sb3(f"Kd{i}", [128, 4, 128], BF16), Vb=sb3(f"Vb{i}", [128, 4, 128]), ITm=sb3(f"ITm{i}", [128, 4, 128], BF16),
                                     Yb=sb3(f"Yb{i}", [128, 4, 128], BF16), Qg=sb3(f"Qg{i}", [128, 4, 128], BF16)) for i in range(2)]
                        return dict(out=outs, GR=sb3("GR", [128, 4, 128]), D1=sb3("D1", [128, 4, 128]), D2=sb3("D2", [128, 4, 128]),
                                    Pm=sb3("Pm", [128, 4, 128]), PY=sb3("PY", [128, 4, 2, 128]),
                                    Rb=sb3("Rb", [128, 4, 128], BF16), VNb=sb3("VNb", [128, 4, 128], BF16), S=sb3("S", [128, 4, 128]),
                                    Sb=sb3("Sb", [128, 4, 128], BF16), OT=sb3("OT", [128, 4, 128]))
                    sets = [mkset(0), mkset(1)]
                    ofb = {}
                    done_pre = [0, 0]
                    done_scan = [0, 0]

                    def pre_chain(d):
                        def gen(slot):
                            W = sets[d]
                            GR, D1, D2, Pm, PY = (W[k] for k in ("GR", "D1", "D2", "Pm", "PY"))
                            DG = D1
                            m1c, m2c = (C_UINC, C_LSTR) if d == 0 else (C_LINC, C_USTR)
                            c0, c1, c2 = (bank[4 * d + q] for q in range(3))
                            X, Y, Z, Wk, V, A = c0, c1, c2, c0, c0, c2
                            B2 = [c0, c1]
                            Bap = self.pst[:, 4 * d * 512:(4 * d + 2) * 512].rearrange("p (h a k) -> p h a k", h=4, a=2)
                            Xb = X.t.bitcast(BF16)
                            order = list(range(NP_)) if d == 0 else list(range(NP_ - 1, -1, -1))
                            for idx, p in enumerate(order):
                                while idx - done_scan[d] >= 2:
                                    yield
                                O_ = W["out"][idx % 2]
                                Kd, Vb, ITm, Yb, Qg = (O_[k] for k in ("Kd", "Vb", "ITm", "Yb", "Qg"))
                                t0 = 128 * p
                                gcol = GC[:, p, 4 * d:4 * d + 4]
                                for h in range(4):
                                    self.tr(Xb[:, h * 128:(h + 1) * 128], kf[:, h, t0:t0 + 128], identb, r=[kf, self.cstb], w=[X])
                                    self.tr(Xb[:, (4 + h) * 128:(5 + h) * 128], vf[:, h, t0:t0 + 128], identb, r=[vf, self.cstb], w=[X])
                                self.tt(DVE, DG[:], bc_h(ident), bc_i(gcol), ALU.mult, r=[cst, GC], w=[DG])
                                for h in range(4):
                                    self.mm(Y[:, h * 128:(h + 1) * 128], kf[:, h, t0:t0 + 128], kf[:, h, t0:t0 + 128], r=[kf], w=[Y])
                                    self.mm(Z[:, h * 128:(h + 1) * 128], kf[:, h, t0:t0 + 128], qf[:, h, t0:t0 + 128], r=[kf, qf], w=[Z])
                                yield
                                self.tt(DVE, Kd[:], Xb[:, 0:512].rearrange("p (h k) -> p h k", h=4), bc_i(kds[:, p, 4 * d:4 * d + 4]), ALU.mult, r=[X, kds], w=[Kd])
                                self.tt(DVE, Vb[:], Xb[:, 512:1024].rearrange("p (h k) -> p h k", h=4), bc_i(beta[:, p, 4 * d:4 * d + 4]), ALU.mult, r=[X, beta], w=[Vb])
                                yield
                                self.mm(Wk[:, 0:512], ones, DG[:].rearrange("p h k -> p (h k)"), r=[cst, DG], w=[Wk])
                                yield
                                self.cp(ACT, GR[:].rearrange("p h k -> p (h k)"), Wk[:, 0:512], r=[Wk], w=[GR])
                                yield
                                self.tt(DVE, D1[:], GR[:], bc_i(gcol), ALU.subtract, r=[GR, GC], w=[D1])
                                self.tt(DVE, D2[:], bc_i(gcol), GR[:], ALU.subtract, r=[GR, GC], w=[D2])
                                yield
                                self.act(GR[:], GR[:], AF.Exp, r=[GR], w=[GR])
                                self.tt(POOL, D1[:], D1[:], bc_h(cst[:, m1c, :]), ALU.add, r=[D1, cst], w=[D1])
                                self.tt(POOL, D2[:], D2[:], bc_h(cst[:, m2c, :]), ALU.add, r=[D2, cst], w=[D2])
                                yield
                                self.tt(POOL, Qg[:], qf[:, :, t0:t0 + 128], GR[:], ALU.mult, r=[qf, GR], w=[Qg])
                                self.act(D2[:], D2[:], AF.Exp, r=[D2], w=[D2])
                                self.act(D1[:], D1[:], AF.Exp, r=[D1], w=[D1])
                                yield
                                for h in range(4):
                                    self.stt(Pm[:, h, :], Y[:, h * 128:(h + 1) * 128], nbeta[:, p, 4 * d + h:4 * d + h + 1], D2[:, h, :], ALU.mult, ALU.mult,
                                             r=[Y, nbeta, D2], w=[Pm])
                                self.tt(DVE, ITm[:], v3(Z), D1[:], ALU.mult, r=[Z, D1], w=[ITm])
                                yield
                                for h in range(4):
                                    self.tr(V[:, h * 128:(h + 1) * 128], Pm[:, h, :], ident, r=[Pm, cst], w=[V])
                                yield
                                self.cp(ACT, PY[:, :, 0, :], v3(V), r=[V], w=[PY])
                                yield
                                self.tt(POOL, PY[:, :, 1, :], PY[:, :, 0, :], bc_h(ident), ALU.add, r=[PY, cst], w=[PY])
                                for h in range(4):
                                    self.mm(A[:, h * 128:(h + 1) * 128], PY[:, h, 0, :], Pm[:, h, :], r=[PY, Pm], w=[A])
                                    self.mm(Bap[:, h, 0, :], Pm[:, h, :], PY[:, h, 0, :], r=[PY, Pm], w=B2)
                                yield
                                self.cp(ACT, Pm[:], v3(A), r=[A], w=[Pm])
                                self.cp(DVE, PY[:, :, 0, :], Bap[:, :, 0, :], r=B2, w=[PY])
                                yield
                                for k in range(1, 5):
                                    for h in range(4):
                                        self.mm(A[:, h * 128:(h + 1) * 128], PY[:, h, 0, :], Pm[:, h, :], r=[PY, Pm], w=[A])
                                        self.mm(Bap[:, h, :, :].rearrange("p a k -> p (a k)"), Pm[:, h, :], PY[:, h, :, :].rearrange("p a k -> p (a k)"), r=[PY, Pm], w=B2)
                                    yield
                                    self.cp(ACT, Pm[:], v3(A), r=[A], w=[Pm])
                                    self.cp(DVE, PY[:, :, 0, :], Bap[:, :, 0, :], r=B2, w=[PY])
                                    self.tt(DVE, PY[:, :, 1, :], PY[:, :, 1, :], Bap[:, :, 1, :], ALU.add, r=B2 + [PY], w=[PY])
                                    yield
                                for h in range(4):
                                    self.mm(A[:, h * 128:(h + 1) * 128], Pm[:, h, :], PY[:, h, 1, :], r=[PY, Pm], w=[A])
                                yield
                                self.tt(DVE, Yb[:], PY[:, :, 1, :], v3(A), ALU.add, r=[A, PY], w=[Yb])
                                done_pre[d] = idx + 1
                                yield
                        return gen

                    def scan_chain(d):
                        def gen(slot):
                            W = sets[d]
                            Rb, VNb, Sst, Sb, OT = (W[k] for k in ("Rb", "VNb", "S", "Sb", "OT"))
                            OFd = self.OF if d == 0 else self.OFB
                            C3 = bank[4 * d + 3]
                            self.memset(POOL, Rb[:], 0.0, w=[Rb])
                            self.memset(POOL, VNb[:], 0.0, w=[VNb])
                            if si == 0:
                                self.dma(SP, Sst[:], self.s0[l, :, 4 * d:4 * d + 4, :], w=[Sst])
                            else:
                                self.memset(POOL, Sst[:], 0.0, w=[Sst])
                            self.cp(POOL, Sb[:], Sst[:], r=[Sst], w=[Sb])
                            yield
                            order = list(range(NP_)) if d == 0 else list(range(NP_ - 1, -1, -1))
                            for idx, p in enumerate(order):
                                while done_pre[d] <= idx:
                                    yield
                                O_ = W["out"][idx % 2]
                                Kd, Vb, ITm, Yb, Qg = (O_[k] for k in ("Kd", "Vb", "ITm", "Yb", "Qg"))
                                t0 = 128 * p
                                for hf in ((0, 1) if d == 0 else (1, 0)):
                                    R = slice(64 * hf, 64 * hf + 64)
                                    cs = slice(64 * hf, 64 * hf + 64)
                                    for h in range(4):
                                        self.mm(C3[:, h * 128:(h + 1) * 128], kf[:, h, t0:t0 + 128], Sb[:, h, :], r=[kf, Sb], w=[C3])
                                    yield
                                    for h in range(4):
                                        self.stt(Rb[R, h, :], C3[R, h * 128:(h + 1) * 128], negc[R, p, 4 * d + h:4 * d + h + 1], Vb[R, h, :], ALU.mult, ALU.add,
                                                 r=[C3, negc, Vb], w=[Rb])
                                    yield
                                    for h in range(4):
                                        self.mm(C3[:, h * 128:(h + 1) * 128], Yb[:, h, :], Rb[:, h, :], r=[Yb, Rb], w=[C3])
                                    yield
                                    self.cp(ACT, VNb[R, :, :].rearrange("p h k -> p (h k)"), C3[R, 0:512], r=[C3], w=[VNb])
                                    yield
                                    for h in range(4):
                                        self.mm(C3[:, h * 128:(h + 1) * 128], Kd[R, h, :], VNb[R, h, :], r=[Kd, VNb], w=[C3])
                                    yield
                                    for h in range(4):
                                        self.stt(Sst[:, h, :], Sst[:, h, :], egl[:, p, hf, 4 * d + h:4 * d + h + 1], C3[:, h * 128:(h + 1) * 128], ALU.mult, ALU.add,
                                                 r=[Sst, egl, C3], w=[Sst])
                                    yield
                                    for h in range(4):
                                        self.mm(C3[:, h * 64:(h + 1) * 64], Sb[:, h, :], Qg[:, h, cs], start=True, stop=False, r=[Sb, Qg], w=[C3])
                                        self.mm(C3[:, h * 64:(h + 1) * 64], VNb[R, h, :], ITm[R, h, cs], start=False, stop=True, r=[VNb, ITm], w=[C3])
                                    self.cp(POOL, Sb[:], Sst[:], r=[Sst], w=[Sb])
                                    yield
                                    self.cp(ACT, OT[:, :, cs], C3[:, 0:256].rearrange("p (h k) -> p h k", h=4), r=[C3], w=[OT])
                                    yield
                                ofb[(d, p)] = Buf(f"of{d}_{p}", None)
                                self.dma(SP, OFd[:, tok0 + t0:tok0 + t0 + 128].rearrange("(h p) t -> p h t", p=128), OT[:], r=[OT], w=[ofb[(d, p)]])
                                done_scan[d] = idx + 1
                                yield
                            if si > 0:
                                self.dma(SP, self.stout[si - 1, l, d].rearrange("h k v -> k h v"), Sst[:], r=[Sst])
                        return gen

                    self.run_pool([pre_chain(0), pre_chain(1), scan_chain(0), scan_chain(1)], 4)
                    ep = [dict(ofl=self.sb(es3, f"gofl{i}", [128, 4, 128]), ofr=self.sb(es3, f"gofr{i}", [128, 4, 128]), zt=self.sb(es3, f"gzt{i}", [128, 4, 128]),
                               osq=self.sb(es3, f"gosq{i}", [128, 4, 128], BF16), ors=self.sb(es3, f"gors{i}", [128, 4, 128]),
                               ob=self.sb(es3, f"gob{i}", [128, 4, 128], BF16)) for i in range(2)]

                    def epilogue(p):
                        def gen(slot):
                            E = ep[slot]
                            ofl, ofr, zt, osq, ors, ob = (E[k] for k in ("ofl", "ofr", "zt", "osq", "ors", "ob"))
                            t0 = 128 * p
                            cols = slice(tok0 + t0, tok0 + t0 + 128)
                            self.dma(SP, ofl[:], self.OF[:, cols].rearrange("(h p) t -> p h t", p=128), r=[ofb[(0, p)]], w=[ofl])
                            self.dma(SP, ofr[:], self.OFB[:, cols].rearrange("(h p) t -> p h t", p=128), r=[ofb[(1, p)]], w=[ofr])
                            self.dma(SP, zt[:], self.P[16 * 128:20 * 128, cols].rearrange("(h p) t -> p h t", p=128), w=[zt])
                            yield
                            self.tt(POOL, ofl[:], ofl[:], ofr[:], ALU.add, r=[ofl, ofr], w=[ofl])
                            self.act(zt[:], zt[:], AF.Silu, r=[zt], w=[zt])
                            yield
                            self.act(osq[:], ofl[:], AF.Square, r=[ofl], w=[osq])
                            yield
                            pb_ = bank[2 + slot]
                            self.mm(pb_[:, 0:512], self.onesb, osq[:].rearrange("p h k -> p (h k)"), r=[osq, self.cstb], w=[pb_])
                            yield
                            self.act(ors[:].rearrange("p h k -> p (h k)"), pb_[:, 0:512], AF.Sqrt, r=[pb_, self.cst], w=[ors], bias=self.c_eps, scale=1.0 / 128)
                            yield
                            self.recip(ors[:], ors[:], r=[ors], w=[ors])
                            yield
                            self.tt(DVE, ofl[:], ofl[:], ors[:], ALU.mult, r=[ofl, ors], w=[ofl])
                            yield
                            self.stt(ob[:], ofl[:], pvl[:, PV_GNW:PV_GNW + 1], zt[:], ALU.mult, ALU.mult, r=[ofl, self.pv, zt], w=[ob])
                            yield
                            self.dma(SP, self.CAT[256:768, cols].rearrange("(h p) t -> p h t", p=128), ob[:], r=[ob])
                        return gen

                    self.run_pool([epilogue(p) for p in range(NP_)], 2)
                S.barrier()


def _prep_inputs(inp, nl=DEPTH):
    f = lambda a: np.ascontiguousarray(np.asarray(a, dtype=np.float32))
    inp = {k: f(v) for k, v in inp.items()}
    shared = {
        "cst": _consts(),
        "pv": _pack_pv(inp),
        "w_ada": inp["w_ada"][:nl],
        "w_in": np.ascontiguousarray(inp["w_in"][:nl][:, :, _WIN_PERM]),
        "w_out": inp["w_out"][:nl],
        "w_up": inp["w_up"][:nl],
        "w_down": inp["w_down"][:nl],
        "gsc": np.ascontiguousarray(np.concatenate([inp["gdn_a_log"].reshape(DEPTH, 8), inp["gdn_dt_bias"].reshape(DEPTH, 8)], axis=1)[:nl]),
        "rpb": np.ascontiguousarray(inp["na_rpb"].reshape(DEPTH, 60, 31)[:nl]),
    }
    maps = []
    for i in range(8):
        xin = np.concatenate([inp["x_sample"][i].T, inp["x_prompt"][2 * i].T, inp["x_prompt"][2 * i + 1].T], axis=1)
        cv = np.stack([inp["c_ctx"], inp["c"][i]], axis=1)
        cvec = cv.reshape(8, 128, 2).transpose(1, 0, 2).reshape(128, 16)
        kv = inp["cache_attn_kv"][i][:nl]
        kc = kv[:, 0].transpose(0, 1, 3, 2).reshape(nl, 2, 128, TC)
        vc = kv[:, 1].transpose(0, 2, 1, 3).reshape(nl, 2, 128, 256)
        s0 = inp["state_delta"][i][:nl].reshape(nl, 8, 128, 128).transpose(0, 2, 1, 3)
        m = dict(shared)
        m.update(xin=np.ascontiguousarray(xin), cvec=np.ascontiguousarray(cvec), kcache=np.ascontiguousarray(kc),
                 vcache=np.ascontiguousarray(vc), s0=np.ascontiguousarray(s0))
        maps.append(m)
    return maps


def _assemble(results, nl=DEPTH):
    y_prompt = np.zeros((16, TC, D), np.float32)
    y_sample = np.zeros((8, TL, D), np.float32)
    kvn = np.zeros((16, nl, 2, 4, TC, 64), np.float32)
    stn = np.zeros((16, nl, 2, 4, 128, 128), np.float32)
    for i, r in enumerate(results):
        xo = r["xout"]
        y_sample[i] = xo[:, 0:TL].T
        y_prompt[2 * i] = xo[:, TL:TL + TC].T
        y_prompt[2 * i + 1] = xo[:, TL + TC:TT].T
        kvo = r["kvout"]
        for s in range(2):
            kvn[2 * i + s] = kvo[:, s].reshape(nl, 2, 4, 64, TC).transpose(0, 1, 2, 4, 3)
            stn[2 * i + s] = r["stout"][s]
    return y_prompt, y_sample, kvn, stn


_CACHE = {}


def kernel(**inputs):
    if "kb" not in _CACHE:
        kb = KB()
        kb.build()
        _CACHE["kb"] = kb
    kb = _CACHE["kb"]
    maps = _prep_inputs(inputs)
    res = run_bass_kernel_spmd(kb.nc, maps, core_ids=list(range(8)))
    return _assemble(res.results)
```

```python
import contextlib
import numpy as np
import concourse.bass as bass
import concourse.mybir as mybir
from concourse.bass_utils import run_bass_kernel_spmd

F32 = mybir.dt.float32
BF16 = mybir.dt.bfloat16
AF = mybir.ActivationFunctionType
ALU = mybir.AluOpType
AX = mybir.AxisListType

PE, ACT, DVE, POOL, SP = "pe", "act", "dve", "pool", "sp"
COMPUTE = (PE, ACT, DVE, POOL)
ALLENG = (PE, ACT, DVE, POOL, SP)
DMAQ = (SP, ACT, POOL)
ND = 4
SAME_ENGINE_SYNC = True

D = 1024
DEPTH = 4
TL = 4096
TC = 256
TT = TL + 2 * TC
NCH = 27
PROJ = 3344
DFF = 2816
EPS = 1e-6
NEG = -30000.0
NEGBIG = -1.0e5
HW_ = TT + 6
SEQS = ((0, TL, 1, 1), (TL, TC, 4099, 0), (TL + TC, TC, 4357, 0))


class Buf:
    __slots__ = ("name", "t", "lw", "rd")

    def __init__(self, name, t):
        self.name = name
        self.t = t
        self.lw = None
        self.rd = []

    def __getitem__(self, k):
        return self.t[k]


class Op:
    __slots__ = ("eng", "fn", "deps", "marked", "is_dma", "sem", "cnt", "epoch", "kind")

    def __init__(self, eng, fn, is_dma, epoch, kind="op"):
        self.eng = eng
        self.fn = fn
        self.deps = []
        self.marked = False
        self.is_dma = is_dma
        self.sem = None
        self.cnt = 0
        self.epoch = epoch
        self.kind = kind


class Sched:
    def __init__(self, nc):
        self.nc = nc
        self.ops = []
        self.epoch = 0
        self.eng = {PE: nc.tensor, ACT: nc.scalar, DVE: nc.vector, POOL: nc.gpsimd, SP: nc.sync}

    def op(self, eng, fn, reads=(), writes=(), dma=False):
        o = Op(eng, fn, dma, self.epoch)
        deps = {}
        for b in reads:
            if b.lw is not None:
                deps[id(b.lw)] = (b.lw, "raw")
        for b in writes:
            if b.lw is not None:
                deps[id(b.lw)] = (b.lw, "waw")
            for r in b.rd:
                if id(r) not in deps:
                    deps[id(r)] = (r, "war")
        for d, kind in deps.values():
            if d.epoch != o.epoch or d is o:
                continue
            if (not d.is_dma) and d.eng == eng and not dma:
                if eng == PE or kind == "war" or not SAME_ENGINE_SYNC:
                    continue
            o.deps.append(d)
        for b in reads:
            b.rd.append(o)
        for b in writes:
            b.lw = o
            b.rd = []
        self.ops.append(o)
        return o

    def barrier(self, new_epoch=False):
        self.ops.append(Op(None, None, False, self.epoch, kind="barrier_ep" if new_epoch else "barrier"))
        if new_epoch:
            self.epoch += 1

    def emit(self, es):
        nc = self.nc
        nep = self.epoch + 1
        for o in self.ops:
            for d in o.deps:
                d.marked = True
        sems = []
        for ep in range(nep):
            d = {}
            for e in COMPUTE:
                d[e] = es.enter_context(nc.semaphore(f"s{ep}_{e}"))
            for q in DMAQ:
                for k in range(ND):
                    d[(q, k)] = es.enter_context(nc.semaphore(f"d{ep}_{q}{k}"))
            sems.append(d)
        cur = [dict((k, 0) for k in sems[ep]) for ep in range(nep)]
        rr = dict((q, 0) for q in DMAQ)
        seen = {}
        n_wait = 0
        for o in self.ops:
            ep = o.epoch
            if o.kind != "op":
                for e in ALLENG:
                    for key, val in cur[ep].items():
                        if val > 0 and seen.get((e, key), 0) < val:
                            self.eng[e].wait_ge(sems[ep][key], val)
                            seen[(e, key)] = val
                            n_wait += 1
                if o.kind == "barrier_ep":
                    seen = {}
                continue
            e = self.eng[o.eng]
            need = {}
            for d in o.deps:
                if need.get(d.sem, 0) < d.cnt:
                    need[d.sem] = d.cnt
            for key, val in need.items():
                if seen.get((o.eng, key), 0) >= val:
                    continue
                e.wait_ge(sems[ep][key], val)
                n_wait += 1
                seen[(o.eng, key)] = val
            inst = o.fn(e)
            if o.is_dma:
                key = (o.eng, rr[o.eng] % ND)
                rr[o.eng] += 1
                cur[ep][key] += 16
                o.sem, o.cnt = key, cur[ep][key]
                inst.then_inc(sems[ep][key], 16)
            elif o.marked:
                cur[ep][o.eng] += 1
                o.sem, o.cnt = o.eng, cur[ep][o.eng]
                inst.then_inc(sems[ep][o.eng], 1)
        ep = nep - 1
        for e in ALLENG:
            for key, val in cur[ep].items():
                if val > 0:
                    self.eng[e].wait_ge(sems[ep][key], val)
        return dict(n_ops=len(self.ops), n_wait=n_wait, counts=cur)


C_IDENT, C_ONES, C_JREV, C_TRIF, C_TRIB, C_BLK, C_HALFA, C_HALFB = range(8)
C_UINC, C_LSTR, C_LINC, C_USTR = 8, 9, 10, 11
C_CMH = 12
C_EPS = 13
NCST = 14


def _consts():
    p = np.arange(128)[:, None]
    f = np.arange(128)[None, :]
    same = (p // 64) == (f // 64)
    c = np.zeros((128, NCST, 128), np.float32)
    c[:, C_IDENT] = (p == f)
    c[:, C_ONES] = 1.0
    c[:, C_JREV] = same & ((p % 64) == (63 - f % 64))
    c[:, C_TRIF] = same & (p <= f)
    c[:, C_TRIB] = same & (p >= f)
    c[:, C_BLK] = same
    c[:, C_HALFA] = (p < 64) & (f >= 0)
    c[:, C_HALFB] = (p >= 64) & (f >= 0)
    c[:, C_UINC] = np.where(same & (f >= p), 0.0, NEGBIG)
    c[:, C_LSTR] = np.where(same & (p > f), 0.0, NEGBIG)
    c[:, C_LINC] = np.where(same & (f <= p), 0.0, NEGBIG)
    c[:, C_USTR] = np.where(same & (p < f), 0.0, NEGBIG)
    qc = 63 - (np.arange(128) % 64)
    cs = np.clip(qc - 8, 0, 48)
    kc = np.arange(64)[None, :]
    ok = (kc >= cs[:, None]) & (kc < cs[:, None] + 16)
    c[:, C_CMH, 0:64] = np.where(ok, 0.0, NEG)
    c[:, C_EPS, 0] = EPS
    c[:, C_EPS, 1] = 1.0
    return c.reshape(128, NCST * 128)


PV_GPM, PV_GPOM, PV_GPF, PV_GPOF = 0, 8, 16, 24
PV_BADA = 32
PV_CW = 80
PV_CB, PV_CLG, PV_CLB = 142, 144, 146
PV_GCW = 148
PV_GNW = 184
PV_FCW = 185
PV_FCB = 317
PV_N = 361


def _fm(v, nchunk):
    return np.ascontiguousarray(v.reshape(nchunk, 128).T)


def _pack_pv(inp):
    pv = np.zeros((128, DEPTH, PV_N), np.float32)
    for l in range(DEPTH):
        pv[:, l, PV_GPM:PV_GPM + 8] = _fm(inp["g_pre_mix"][l], 8)
        pv[:, l, PV_GPOM:PV_GPOM + 8] = _fm(inp["g_post_mix"][l], 8)
        pv[:, l, PV_GPF:PV_GPF + 8] = _fm(inp["g_pre_ffn"][l], 8)
        pv[:, l, PV_GPOF:PV_GPOF + 8] = _fm(inp["g_post_ffn"][l], 8)
        pv[:, l, PV_BADA:PV_BADA + 48] = _fm(inp["b_ada"][l], 48)
        cw = inp["conv_w"][l]
        pv[:, l, PV_CW:PV_CW + 62] = cw.T.reshape(2, 128, 31).transpose(1, 0, 2).reshape(128, 62)
        pv[:, l, PV_CB:PV_CB + 2] = _fm(inp["conv_b"][l], 2)
        pv[:, l, PV_CLG:PV_CLG + 2] = _fm(inp["conv_ln_g"][l], 2)
        pv[:, l, PV_CLB:PV_CLB + 2] = _fm(inp["conv_ln_b"][l], 2)
        gw = inp["gdn_conv_w"][l]
        pv[:, l, PV_GCW:PV_GCW + 36] = gw.T.reshape(12, 128, 3).transpose(1, 0, 2).reshape(128, 36)
        pv[:, l, PV_GNW] = inp["gdn_norm_w"][l]
        fw = inp["ffn_conv_w"][l]
        pv[:, l, PV_FCW:PV_FCW + 132] = fw.T.reshape(44, 128, 3).transpose(1, 0, 2).reshape(128, 132)
        pv[:, l, PV_FCB:PV_FCB + 44] = _fm(inp["ffn_conv_b"][l], 44)
    return pv.reshape(128, DEPTH * PV_N)


_WIN_PERM = np.concatenate([np.arange(0, 2560), np.arange(2576, 3344), np.arange(2560, 2576)])


class KB:
    def __init__(self, nlayers=DEPTH, dbg=None):
        self.nl = nlayers
        self.dbg = dbg or {}
        nc = self.nc = bass.Bass("TRN2", target_bir_lowering=False)
        self.S = Sched(nc)
        self.es = contextlib.ExitStack()
        self.uid = 0

    def mm(self, out, lhsT, rhs, start=True, stop=True, r=(), w=()):
        return self.S.op(PE, lambda e: e.matmul(out, lhsT, rhs, start=start, stop=stop), r, w)

    def tr(self, out, in_, ident, r=(), w=()):
        return self.S.op(PE, lambda e: e.transpose(out, in_, ident), r, w)

    def act(self, out, in_, func, r=(), w=(), eng=ACT, **kw):
        return self.S.op(eng, lambda e: e.activation(out=out, in_=in_, func=func, **kw), r, w)

    def tt(self, eng, out, in0, in1, op, r=(), w=()):
        return self.S.op(eng, lambda e: e.tensor_tensor(out=out, in0=in0, in1=in1, op=op), r, w)

    def ts(self, eng, out, in0, s1, s2, op0, op1=None, r=(), w=()):
        if op1 is None:
            return self.S.op(eng, lambda e: e.tensor_scalar(out=out, in0=in0, scalar1=s1, scalar2=None, op0=op0), r, w)
        return self.S.op(eng, lambda e: e.tensor_scalar(out=out, in0=in0, scalar1=s1, scalar2=s2, op0=op0, op1=op1), r, w)

    def stt(self, out, in0, scalar, in1, op0, op1, r=(), w=()):
        return self.S.op(DVE, lambda e: e.scalar_tensor_tensor(out=out, in0=in0, scalar=scalar, in1=in1, op0=op0, op1=op1), r, w)

    def cp(self, eng, out, in_, r=(), w=()):
        if eng == ACT:
            return self.S.op(ACT, lambda e: e.copy(out=out, in_=in_), r, w)
        return self.S.op(eng, lambda e: e.tensor_copy(out=out, in_=in_), r, w)

    def memset(self, eng, ap, val, w=()):
        return self.S.op(eng, lambda e: e.memset(ap, val), (), w)

    def recip(self, out, in_, r=(), w=()):
        return self.S.op(DVE, lambda e: e.reciprocal(out=out, in_=in_), r, w)

    def dma(self, q, out, in_, r=(), w=()):
        return self.S.op(q, lambda e: e.dma_start(out=out, in_=in_), r, w, dma=True)

    def sb(self, es, name, shape, dt=F32):
        self.uid += 1
        return Buf(name, es.enter_context(self.nc.sbuf_tensor(f"{name}_{self.uid}", shape, dt)))

    def dram(self, name, shape, dt, kind="Internal"):
        if name in self.dbg.get("out", ()):
            kind = "ExternalOutput"
        if name in self.dbg.get("in", ()):
            kind = "ExternalInput"
        return self.nc.dram_tensor(name, shape, dt, kind=kind).ap()

    def rsqrt_ps(self, out, ps_ap, scale, r, w):
        self.act(out, ps_ap, AF.Sqrt, r=list(r) + [self.cst], w=w, bias=self.c_eps, scale=scale)
        self.recip(out, out, r=w, w=w)

    def build(self):
        nc, S, es = self.nc, self.S, self.es
        nl = self.nl
        I = {}
        def ext(name, shape, dt=F32):
            I[name] = nc.dram_tensor(name, shape, dt, kind="ExternalInput").ap()
            return I[name]
        self.xin = ext("xin", [D, TT])
        self.cvec = ext("cvec", [128, 16])
        self.cst_d = ext("cst", [128, NCST * 128])
        self.pv_d = ext("pv", [128, DEPTH * PV_N])
        self.w_ada = ext("w_ada", [nl, D, 6 * D])
        self.w_in = ext("w_in", [nl, D, PROJ])
        self.w_out = ext("w_out", [nl, D, D])
        self.w_up = ext("w_up", [nl, D, 2 * DFF])
        self.w_down = ext("w_down", [nl, DFF, D])
        self.kcache = ext("kcache", [nl, 2, 128, TC])
        self.vcache = ext("vcache", [nl, 2, 128, 256])
        self.s0 = ext("s0", [nl, 128, 8, 128])
        self.gsc = ext("gsc", [nl, 16])
        self.rpb = ext("rpb", [nl, 60, 31])
        self.xout = nc.dram_tensor("xout", [D, TT], F32, kind="ExternalOutput").ap()
        self.kvout = nc.dram_tensor("kvout", [nl, 2, 2, 256, TC], F32, kind="ExternalOutput").ap()
        self.stout = nc.dram_tensor("stout", [2, nl, 2, 4, 128, 128], F32, kind="ExternalOutput").ap()
        self.P = self.dram("P", [NCH * 128, TT], F32)
        self.PB = self.dram("PB", [6 * 128, TT], BF16)
        self.CAT = self.dram("CAT", [D, TT], BF16)
        self.XA = self.dram("XA", [D, TT], F32)
        self.A = self.dram("A", [DFF, TT], BF16)
        self.OF = self.dram("OF", [512, TT], F32)
        self.OFB = self.dram("OFB", [512, TT], F32)
        self.FP = self.dram("FP", [nl, 60, 127], F32)

        with es:
            pst = es.enter_context(nc.psum_tensor("ps", [128, 4096], F32))
            self.bank = [Buf(f"bank{i}", pst[:, i * 512:(i + 1) * 512]) for i in range(8)]
            self.pst = pst
            self.cst = self.sb(es, "cst", [128, NCST, 128])
            self.cstb = self.sb(es, "cstb", [128, 3, 128], BF16)
            self.pv = self.sb(es, "pv", [128, DEPTH, PV_N])
            self.mod = self.sb(es, "mod", [128, 6, 8, 2])
            self.scv = self.sb(es, "scv", [128, 8, 2])
            self.c_eps = self.cst[:, C_EPS, 0:1]
            self.dma(SP, self.cst[:].rearrange("p a b -> p (a b)"), self.cst_d[:, :], w=[self.cst])
            self.dma(SP, self.pv[:].rearrange("p a b -> p (a b)"), self.pv_d[:, :], w=[self.pv])
            self.dma(SP, self.scv[:].rearrange("p a b -> p (a b)"), self.cvec[:, :], w=[self.scv])
            for i, ci in enumerate((C_IDENT, C_ONES, C_JREV)):
                self.cp(DVE, self.cstb[:, i, :], self.cst[:, ci, :], r=[self.cst], w=[self.cstb])
            self.ident = self.cst[:, C_IDENT, :]
            self.ones = self.cst[:, C_ONES, :]
            self.identb = self.cstb[:, 0, :]
            self.onesb = self.cstb[:, 1, :]
            self.jrevb = self.cstb[:, 2, :]
            self.act(self.scv[:], self.scv[:], AF.Silu, r=[self.scv], w=[self.scv])
            for l in range(nl):
                self.layer(l)
                if l + 1 < nl:
                    S.barrier(new_epoch=True)
            info = S.emit(es)
        return info

    def layer(self, l):
        S = self.S
        stages = self.dbg.get("stages", ("ada", "s1", "conf", "gdn", "attn", "s3", "s4a", "s4b"))
        xsrc = self.xin if l == 0 else self.xout
        if "ada" in stages:
            self.stage_ada(l)
            S.barrier()
        if "s1" in stages:
            self.stage_s1(l, xsrc)
            S.barrier()
        if "conf" in stages:
            self.stage_conf(l)
            S.barrier()
        if "attn" in stages:
            self.stage_attn(l)
            S.barrier()
        if "gdn" in stages:
            self.stage_gdn(l)
            S.barrier()
        if "s3" in stages:
            with contextlib.ExitStack() as es:
                self.Hb = self.sb(es, "H", [128, 8, HW_], BF16)
                self.stage_s3(l, xsrc, es)
                S.barrier()
                if "s4a" in stages:
                    self.stage_s4a(l)
                    S.barrier()
        if "s4b" in stages:
            self.stage_s4b(l)
            S.barrier()

    def stage_ada(self, l):
        with contextlib.ExitStack() as es:
            wb = [self.sb(es, f"adaw{i}", [128, 8, 512]) for i in range(6)]
            raw = self.sb(es, "adaraw", [128, 48, 2])
            ps = self.bank[0]
            for g in range(12):
                w = wb[g % 6]
                self.dma(SP, w[:],
                         self.w_ada[l, :, g * 512:(g + 1) * 512].rearrange("(kc p) n -> p kc n", p=128), w=[w])
                for j in range(4):
                    n = g * 4 + j
                    for kc in range(8):
                        self.mm(ps[:, 2 * n:2 * n + 2], w[:, kc, j * 128:(j + 1) * 128], self.scv[:, kc, :],
                                start=(kc == 0), stop=(kc == 7), r=[w, self.scv], w=[ps])
            pvl = self.pv[:, l, :]
            self.tt(DVE, raw[:], ps[:, 0:96].rearrange("p (n v) -> p n v", v=2),
                    pvl[:, PV_BADA:PV_BADA + 48].unsqueeze(2).to_broadcast([128, 48, 2]), ALU.add,
                    r=[ps, self.pv], w=[raw])
            def gain(off):
                return pvl[:, off:off + 8].unsqueeze(2).to_broadcast([128, 8, 2])
            m = self.mod
            self.ts(DVE, m[:, 0], raw[:, 8:16, :], 1.0, None, ALU.add, r=[raw], w=[m])
            self.tt(DVE, m[:, 0], m[:, 0], gain(PV_GPM), ALU.mult, r=[m, self.pv], w=[m])
            self.cp(DVE, m[:, 1], raw[:, 0:8, :], r=[raw], w=[m])
            self.tt(DVE, m[:, 2], raw[:, 16:24, :], gain(PV_GPOM), ALU.mult, r=[raw, self.pv], w=[m])
            self.ts(DVE, m[:, 3], raw[:, 32:40, :], 1.0, None, ALU.add, r=[raw], w=[m])
            self.tt(DVE, m[:, 3], m[:, 3], gain(PV_GPF), ALU.mult, r=[m, self.pv], w=[m])
            self.cp(DVE, m[:, 4], raw[:, 24:32, :], r=[raw], w=[m])
            self.tt(DVE, m[:, 5], raw[:, 40:48, :], gain(PV_GPOF), ALU.mult, r=[raw, self.pv], w=[m])

    def sumsq_rstd(self, src, sq, ps, rstd, n, nchunk=8, scale=1.0 / D, src_bufs=()):
        for c in range(nchunk):
            self.act(sq[:, c, 0:n], src[:, c, 0:n], AF.Square, r=list(src_bufs), w=[sq], eng=ACT)
        for c in range(nchunk):
            self.mm(ps[:, 0:n], self.onesb, sq[:, c, 0:n], start=(c == 0), stop=(c == nchunk - 1),
                    r=[sq, self.cstb], w=[ps])
        self.rsqrt_ps(rstd[:, 0:n], ps[:, 0:n], scale, r=[ps], w=[rstd])

    def s1_tiles(self):
        tl = [(j * 512, 512, 1, [(1 + j * 512, 512)]) for j in range(8)]
        tl.append((TL, 512, 0, [(4099, 256), (4357, 256)]))
        return tl

    def stage_s1(self, l, xsrc):
        with contextlib.ExitStack() as es:
            H = self.sb(es, "H", [128, 8, HW_], BF16)
            xt = [self.sb(es, f"xt{i}", [128, 8, 512]) for i in range(2)]
            sq = self.sb(es, "sq", [128, 8, 512], BF16)
            rstd = self.sb(es, "rstd", [128, 512])
            wck = [self.sb(es, f"wck{i}", [128, 8, 128], BF16) for i in range(3)]
            ev = [self.sb(es, f"ev{i}", [128, 512]) for i in range(4)]
            evb = [self.sb(es, f"evb{i}", [128, 512], BF16) for i in range(2)]
            tiles = self.s1_tiles()
            m = self.mod
            for ti, (t0, n, vec, hc) in enumerate(tiles):
                x = xt[ti % 2]
                self.dma(SP, x[:], xsrc[:, t0:t0 + n].rearrange("(c p) t -> p c t", p=128), w=[x])
                ps = self.bank[ti % 2]
                self.sumsq_rstd(x, sq, ps, rstd, n, src_bufs=[x])
                self.tt(DVE, x[:], x[:], rstd[:].unsqueeze(1).to_broadcast([128, 8, 512]), ALU.mult, r=[x, rstd], w=[x])
                for c in range(8):
                    off = 0
                    for (h0, nn) in hc:
                        eng = ACT if c % 2 == 0 else POOL
                        if eng == ACT:
                            self.act(H[:, c, h0:h0 + nn], x[:, c, off:off + nn], AF.Identity, r=[x, m], w=[H],
                                     scale=m[:, 0, c, vec:vec + 1], bias=m[:, 1, c, vec:vec + 1])
                        else:
                            self.ts(POOL, H[:, c, h0:h0 + nn], x[:, c, off:off + nn], m[:, 0, c, vec:vec + 1],
                                    m[:, 1, c, vec:vec + 1], ALU.mult, ALU.add, r=[x, m], w=[H])
                        off += nn
            k = 0
            for nci in range(NCH):
                ncols = 128 if nci < 26 else 16
                w = wck[nci % 3]
                self.dma(POOL, w[:, :, 0:ncols],
                         self.w_in[l, :, nci * 128:nci * 128 + ncols].rearrange("(kc p) n -> p kc n", p=128), w=[w])
                for ti, (t0, n, vec, hc) in enumerate(tiles):
                    ps = self.bank[2 + (k % 4)]
                    off = 0
                    for (h0, nn) in hc:
                        for kc in range(8):
                            self.mm(ps[0:ncols, off:off + nn], w[:, kc, 0:ncols], H[:, kc, h0:h0 + nn],
                                    start=(kc == 0), stop=(kc == 7), r=[w, H], w=[ps])
                        off += nn
                    e = ev[k % 4]
                    if k % 2 == 0:
                        self.cp(ACT, e[0:ncols, :], ps[0:ncols, :], r=[ps], w=[e])
                    else:
                        self.cp(DVE, e[0:ncols, :], ps[0:ncols, :], r=[ps], w=[e])
                    self.dma(SP, self.P[nci * 128:nci * 128 + ncols, t0:t0 + n], e[0:ncols, :], r=[e])
                    if 20 <= nci < 26:
                        eb = evb[k % 2]
                        self.cp(POOL, eb[:], e[:], r=[e], w=[eb])
                        self.dma(SP, self.PB[(nci - 20) * 128:(nci - 19) * 128, t0:t0 + n], eb[:], r=[eb])
                    if ti == 8 and 22 <= nci < 26:
                        kv = (nci - 22) // 2
                        f0 = ((nci - 22) % 2) * 128
                        for s in range(2):
                            self.dma(SP, self.kvout[l, s, kv, f0:f0 + 128, :], e[:, s * 256:(s + 1) * 256], r=[e])
                    k += 1

    def stage_s3(self, l, xsrc, es0):
        H = self.Hb
        with contextlib.ExitStack() as es:
            wo = self.sb(es, "wo", [128, 8, D], BF16)
            cat = [self.sb(es, f"cat{i}", [128, 8, 512], BF16) for i in range(2)]
            xt = [self.sb(es, f"xt{i}", [128, 8, 512]) for i in range(2)]
            msb = self.sb(es, "msb", [128, 8, 512])
            sq = self.sb(es, "sq", [128, 8, 512], BF16)
            rstd = self.sb(es, "rstd", [128, 512])
            zc = self.sb(es, "zc", [128, 8, 2], BF16)
            parts = self.dbg.get("s3parts", "wpx")
            if "w" in parts:
                self.dma(POOL, wo[:], self.w_out[l].rearrange("(kc p) n -> p kc n", p=128), w=[wo])
            if "p" in parts:
                self.memset(POOL, zc[:], 0.0, w=[zc])
                for c0, n0 in ((0, 1), (4097, 2), (4355, 2), (4613, 1)):
                    self.cp(POOL, H[:, :, c0:c0 + n0], zc[:, :, 0:n0], r=[zc], w=[H])
            if "x" not in parts:
                return
            m = self.mod
            lvl = int(self.dbg.get("s3n", 9))
            for ti, (t0, n, vec, hc) in enumerate(self.s1_tiles()):
                ct = cat[ti % 2]
                x = xt[ti % 2]
                self.dma(SP, ct[:], self.CAT[:, t0:t0 + n].rearrange("(c p) t -> p c t", p=128), w=[ct])
                self.dma(SP, x[:], xsrc[:, t0:t0 + n].rearrange("(c p) t -> p c t", p=128), w=[x])
                if lvl < 2:
                    continue
                for nn in range(8):
                    ps = self.bank[nn % 4]
                    for kc in range(8):
                        self.mm(ps[:, 0:n], wo[:, kc, nn * 128:(nn + 1) * 128], ct[:, kc, :],
                                start=(kc == 0), stop=(kc == 7), r=[wo, ct], w=[ps])
                    if "c" in self.dbg.get("s3l2", "cs"):
                        self.cp(DVE, msb[:, nn, :], ps[:, 0:n], r=[ps], w=[msb])
                    if "s" in self.dbg.get("s3l2", "cs"):
                        self.act(sq[:, nn, :], msb[:, nn, :], AF.Square, r=[msb], w=[sq])
                if lvl < 3:
                    continue
                pss = self.bank[4 + ti % 2]
                for c in range(8):
                    self.mm(pss[:, 0:n], self.onesb, sq[:, c, :], start=(c == 0), stop=(c == 7), r=[sq, self.cstb], w=[pss])
                self.rsqrt_ps(rstd[:, 0:n], pss[:, 0:n], 1.0 / D, r=[pss], w=[rstd])
                self.tt(DVE, msb[:], msb[:], rstd[:].unsqueeze(1).to_broadcast([128, 8, 512]), ALU.mult, r=[msb, rstd], w=[msb])
                if lvl < 4:
                    continue
                for c in range(8):
                    self.stt(x[:, c, :], msb[:, c, :], m[:, 2, c, vec:vec + 1], x[:, c, :], ALU.mult, ALU.add,
                             r=[msb, m, x], w=[x])
                self.dma(SP, self.XA[:, t0:t0 + n].rearrange("(c p) t -> p c t", p=128), x[:], r=[x])
                if lvl < 5:
                    continue
                pss2 = self.bank[6 + ti % 2]
                self.sumsq_rstd(x, sq, pss2, rstd, n, src_bufs=[x])
                self.tt(DVE, msb[:], x[:], rstd[:].unsqueeze(1).to_broadcast([128, 8, 512]), ALU.mult, r=[x, rstd], w=[msb])
                if lvl < 6:
                    continue
                for c in range(8):
                    off = 0
                    for (h0, nn2) in hc:
                        if c % 2 == 0:
                            self.act(H[:, c, h0:h0 + nn2], msb[:, c, off:off + nn2], AF.Identity, r=[msb, m], w=[H],
                                     scale=m[:, 3, c, vec:vec + 1], bias=m[:, 4, c, vec:vec + 1])
                        else:
                            self.ts(POOL, H[:, c, h0:h0 + nn2], msb[:, c, off:off + nn2], m[:, 3, c, vec:vec + 1],
                                    m[:, 4, c, vec:vec + 1], ALU.mult, ALU.add, r=[msb, m], w=[H])
                        off += nn2

    def ffn_tiles(self):
        tl = []
        t = 0
        while t < TL:
            n = min(456, TL - t)
            tl.append((t, n, t))
            t += n
        tl.append((TL, TC, 4098))
        tl.append((TL + TC, TC, 4356))
        return tl

    def stage_s4a(self, l):
        H = self.Hb
        with contextlib.ExitStack() as es:
            wg = [self.sb(es, f"wg{i}", [128, 8, 128], BF16) for i in range(2)]
            wv = [self.sb(es, f"wv{i}", [128, 8, 128], BF16) for i in range(2)]
            tg = [self.sb(es, f"tg{i}", [128, 512]) for i in range(2)]
            tv = [self.sb(es, f"tv{i}", [128, 512]) for i in range(2)]
            ao = [self.sb(es, f"ao{i}", [128, 512], BF16) for i in range(3)]
            pvl = self.pv[:, l, :]
            k = 0
            for j in range(22):
                g, v = wg[j % 2], wv[j % 2]
                self.dma(POOL, g[:], self.w_up[l, :, j * 128:(j + 1) * 128].rearrange("(kc p) n -> p kc n", p=128), w=[g])
                self.dma(POOL, v[:], self.w_up[l, :, (22 + j) * 128:(23 + j) * 128].rearrange("(kc p) n -> p kc n", p=128), w=[v])
                for (t0, n, h0) in self.ffn_tiles():
                    pg = self.bank[(2 * k) % 8]
                    pv_ = self.bank[(2 * k + 1) % 8]
                    for kc in range(8):
                        self.mm(pg[:, 0:n + 2], g[:, kc, :], H[:, kc, h0:h0 + n + 2], start=(kc == 0), stop=(kc == 7), r=[g, H], w=[pg])
                    for kc in range(8):
                        self.mm(pv_[:, 0:n + 2], v[:, kc, :], H[:, kc, h0:h0 + n + 2], start=(kc == 0), stop=(kc == 7), r=[v, H], w=[pv_])
                    a, b_ = tg[k % 2], tv[k % 2]
                    for (ps, t, ch) in ((pg, a, j), (pv_, b_, 22 + j)):
                        cw = PV_FCW + 3 * ch
                        self.act(t[:, 0:n], ps[:, 0:n], AF.Identity, r=[ps, self.pv], w=[t],
                                 scale=pvl[:, cw:cw + 1], bias=pvl[:, PV_FCB + ch:PV_FCB + ch + 1])
                        self.stt(t[:, 0:n], ps[:, 1:n + 1], pvl[:, cw + 1:cw + 2], t[:, 0:n], ALU.mult, ALU.add, r=[ps, self.pv, t], w=[t])
                        self.stt(t[:, 0:n], ps[:, 2:n + 2], pvl[:, cw + 2:cw + 3], t[:, 0:n], ALU.mult, ALU.add, r=[ps, self.pv, t], w=[t])
                    self.act(a[:, 0:n], a[:, 0:n], AF.Silu, r=[a], w=[a])
                    o = ao[k % 3]
                    self.tt(POOL, o[:, 0:n], a[:, 0:n], b_[:, 0:n], ALU.mult, r=[a, b_], w=[o])
                    self.dma(SP, self.A[j * 128:(j + 1) * 128, t0:t0 + n], o[:, 0:n], r=[o])
                    k += 1

    def stage_s4b(self, l):
        with contextlib.ExitStack() as es:
            wd = self.sb(es, "wd", [128, 22, D], BF16)
            at = [self.sb(es, f"at{i}", [128, 22, 512], BF16) for i in range(2)]
            xa = [self.sb(es, f"xa{i}", [128, 8, 512]) for i in range(2)]
            fsb = self.sb(es, "fsb", [128, 8, 512])
            sq = self.sb(es, "sq", [128, 8, 512], BF16)
            rstd = self.sb(es, "rstd", [128, 512])
            for h in range(2):
                self.dma(POOL, wd[:, h * 11:(h + 1) * 11, :],
                         self.w_down[l, h * 11 * 128:(h + 1) * 11 * 128, :].rearrange("(j p) n -> p j n", p=128), w=[wd])
            m = self.mod
            for ti, (t0, n, vec, hc) in enumerate(self.s1_tiles()):
                a = at[ti % 2]
                x = xa[ti % 2]
                self.dma(SP, a[:], self.A[:, t0:t0 + n].rearrange("(j p) t -> p j t", p=128), w=[a])
                self.dma(SP, x[:], self.XA[:, t0:t0 + n].rearrange("(c p) t -> p c t", p=128), w=[x])
                for nn in range(8):
                    ps = self.bank[nn % 4]
                    for j in range(22):
                        self.mm(ps[:, 0:n], wd[:, j, nn * 128:(nn + 1) * 128], a[:, j, :], start=(j == 0), stop=(j == 21), r=[wd, a], w=[ps])
                    self.cp(DVE, fsb[:, nn, :], ps[:, 0:n], r=[ps], w=[fsb])
                    self.act(sq[:, nn, :], fsb[:, nn, :], AF.Square, r=[fsb], w=[sq])
                pss = self.bank[4 + ti % 2]
                for c in range(8):
                    self.mm(pss[:, 0:n], self.onesb, sq[:, c, :], start=(c == 0), stop=(c == 7), r=[sq, self.cstb], w=[pss])
                self.rsqrt_ps(rstd[:, 0:n], pss[:, 0:n], 1.0 / D, r=[pss], w=[rstd])
                self.tt(DVE, fsb[:], fsb[:], rstd[:].unsqueeze(1).to_broadcast([128, 8, 512]), ALU.mult, r=[fsb, rstd], w=[fsb])
                for c in range(8):
                    self.stt(x[:, c, :], fsb[:, c, :], m[:, 5, c, vec:vec + 1], x[:, c, :], ALU.mult, ALU.add, r=[fsb, m, x], w=[x])
                self.dma(SP, self.xout[:, t0:t0 + n].rearrange("(c p) t -> p c t", p=128), x[:], r=[x])

    def stage_conf(self, l):
        pvl = self.pv[:, l, :]
        with contextlib.ExitStack() as es:
            av = [self.sb(es, f"cav{c}", [128, TL]) for c in range(2)]
            ag = [self.sb(es, f"cag{c}", [128, TL]) for c in range(2)]
            U = [self.sb(es, f"cU{c}", [128, TL + 30]) for c in range(2)]
            sq = [self.sb(es, f"csq{c}", [128, 512]) for c in range(2)]
            st = self.sb(es, "cst", [128, 3, 512])
            yb = [self.sb(es, f"cy{c}", [128, 512]) for c in range(2)]
            ob = [self.sb(es, f"cob{i}", [128, 512], BF16) for i in range(4)]
            k = 0
            for (tok0, T, hcol0, vec) in SEQS:
                for c in range(2):
                    self.dma(SP, av[c][:, 0:T], self.P[c * 128:(c + 1) * 128, tok0:tok0 + T], w=[av[c]])
                    self.dma(SP, ag[c][:, 0:T], self.P[(2 + c) * 128:(3 + c) * 128, tok0:tok0 + T], w=[ag[c]])
                    self.memset(POOL, U[c][:, 0:15], 0.0, w=[U[c]])
                    self.memset(POOL, U[c][:, 15 + T:30 + T], 0.0, w=[U[c]])
                    self.act(ag[c][:, 0:T], ag[c][:, 0:T], AF.Sigmoid, r=[ag[c]], w=[ag[c]])
                    self.tt(POOL, U[c][:, 15:15 + T], av[c][:, 0:T], ag[c][:, 0:T], ALU.mult, r=[av[c], ag[c]], w=[U[c]])
                    acc = av[c]
                    cw = PV_CW + 31 * c
                    self.act(acc[:, 0:T], U[c][:, 0:T], AF.Identity, r=[U[c], self.pv], w=[acc],
                             scale=pvl[:, cw:cw + 1], bias=pvl[:, PV_CB + c:PV_CB + c + 1])
                    for kk in range(1, 31):
                        self.stt(acc[:, 0:T], U[c][:, kk:kk + T], pvl[:, cw + kk:cw + kk + 1], acc[:, 0:T], ALU.mult, ALU.add,
                                 r=[U[c], self.pv, acc], w=[acc])
                for t in range(0, T, 512):
                    n = min(512, T - t)
                    p1 = self.bank[(2 * k) % 8]
                    p2 = self.bank[(2 * k + 1) % 8]
                    for c in range(2):
                        self.act(sq[c][:, 0:n], av[c][:, t:t + n], AF.Square, r=[av[c]], w=[sq[c]])
                    for c in range(2):
                        self.mm(p1[:, 0:n], self.ones, av[c][:, t:t + n], start=(c == 0), stop=(c == 1), r=[self.cst, av[c]], w=[p1])
                    for c in range(2):
                        self.mm(p2[:, 0:n], self.ones, sq[c][:, 0:n], start=(c == 0), stop=(c == 1), r=[self.cst, sq[c]], w=[p2])
                    self.ts(DVE, st[:, 0, 0:n], p1[:, 0:n], 1.0 / 256, None, ALU.mult, r=[p1], w=[st])
                    self.tt(DVE, st[:, 1, 0:n], st[:, 0, 0:n], st[:, 0, 0:n], ALU.mult, r=[st], w=[st])
                    self.stt(st[:, 1, 0:n], p2[:, 0:n], 1.0 / 256, st[:, 1, 0:n], ALU.mult, ALU.subtract, r=[p2, st], w=[st])
                    self.ts(DVE, st[:, 1, 0:n], st[:, 1, 0:n], 0.0, None, ALU.max, r=[st], w=[st])
                    self.act(st[:, 2, 0:n], st[:, 1, 0:n], AF.Sqrt, r=[st, self.cst], w=[st], bias=self.c_eps, scale=1.0)
                    self.recip(st[:, 2, 0:n], st[:, 2, 0:n], r=[st], w=[st])
                    for c in range(2):
                        self.tt(DVE, yb[c][:, 0:n], av[c][:, t:t + n], st[:, 0, 0:n], ALU.subtract, r=[av[c], st], w=[yb[c]])
                        self.tt(POOL, yb[c][:, 0:n], yb[c][:, 0:n], st[:, 2, 0:n], ALU.mult, r=[yb[c], st], w=[yb[c]])
                        o = ob[(2 * k + c) % 4]
                        self.act(o[:, 0:n], yb[c][:, 0:n], AF.Silu, r=[yb[c], self.pv], w=[o],
                                 scale=pvl[:, PV_CLG + c:PV_CLG + c + 1], bias=pvl[:, PV_CLB + c:PV_CLB + c + 1])
                        self.dma(SP, self.CAT[c * 128:(c + 1) * 128, tok0 + t:tok0 + t + n], o[:, 0:n], r=[o])
                    k += 1

    def attn_unit(self, slot, qf, hp, b0, t0, pieces, vts, otm, h, W):
        nk = sum(p[3] for p in pieces)
        nkt = nk // 128
        base = slot * 1024
        ps = self.pst[:, base:base + nk]
        pb = [self.bank[2 * slot], self.bank[2 * slot + 1]]
        lhsT = qf[b0:b0 + 64, hp, t0:t0 + 128]
        for (col, rhs, bm, n) in pieces:
            self.mm(self.pst[:, base + col:base + col + n], lhsT, rhs, start=True, stop=(bm is None), r=W["qk"], w=pb)
            if bm is not None:
                self.mm(self.pst[:, base + col:base + col + n], self.jrevb, bm, start=False, stop=True, r=W["bm"], w=pb)
        sm = W["sm"][slot * 2 + h % 2]
        pexp = W["pexp"][slot]
        pts = W["pts"][slot]
        yield
        self.S.op(DVE, lambda e: e.reduce_max(out=sm[:, 0:1], in_=ps, axis=AX.X), pb, [sm])
        yield
        self.ts(DVE, sm[:, 1:2], sm[:, 0:1], -0.125, None, ALU.mult, r=[sm], w=[sm])
        self.memset(DVE, sm[:, 2:3], 0.0, w=[sm])
        yield
        self.act(pexp[:, 0:nk], ps, AF.Exp, r=pb + [sm], w=[pexp, sm], scale=0.125, bias=sm[:, 1:2], accum_out=sm[:, 2:3])
        yield
        ptbank = self.bank[4 + slot]
        ptb = ptbank.t.bitcast(BF16)
        for c in range(nkt):
            self.tr(ptb[:, c * 128:(c + 1) * 128], pexp[:, c * 128:(c + 1) * 128], self.identb, r=[pexp, self.cstb], w=[ptbank])
        yield
        self.cp(ACT if slot == 0 else DVE, pts[:, 0:nkt, :].rearrange("p a b -> p (a b)"), ptb[:, 0:nk], r=[ptbank], w=[pts])
        self.recip(sm[:, 3:4], sm[:, 2:3], r=[sm], w=[sm])
        yield
        po = self.bank[6]
        for c in range(nkt):
            self.mm(po[:, 0:64], pts[:, c, :], vts[c], start=(c == 0), stop=(c == nkt - 1), r=[pts] + W["v"], w=[po])
        self.ts(DVE, otm[:, h * 64:(h + 1) * 64], po[:, 0:64], sm[:, 3:4], None, ALU.mult, r=[po, sm], w=[otm])
        yield

    def stage_attn(self, l):
        with contextlib.ExitStack() as es:
            qf = self.sb(es, "aqf", [128, 2, TT], BF16)
            kf = self.sb(es, "akf", [128, 2, TT], BF16)
            vf = self.sb(es, "avf", [128, 2, TT], BF16)
            VT = self.sb(es, "aVT", [128, 36, 256], BF16)
            kctx = self.sb(es, "akctx", [128, 2, TC], BF16)
            vctx = self.sb(es, "avctx", [128, 2, 256], BF16)
            HK = self.sb(es, "aHK", [128, 4, 15, 64])
            BMc = [self.sb(es, f"aBM{c}", [128, 4, 640], BF16) for c in range(5)]
            FT = self.sb(es, "aFT", [64, 127])
            r31 = self.sb(es, "ar31", [64, 31])
            ocr = self.sb(es, "aocr", [128, 2, TT], BF16)
            otms = [self.sb(es, f"aotm{i}", [128, 256], BF16) for i in range(2)]
            W = dict(sm=[self.sb(es, f"asm{i}", [128, 4]) for i in range(4)],
                     pexp=[self.sb(es, f"apexp{i}", [128, 896], BF16) for i in range(2)],
                     pts=[self.sb(es, f"apts{i}", [128, 7, 128], BF16) for i in range(2)])
            fpb = Buf("FPb", None)
            for hp in range(2):
                self.dma(SP, qf[:, hp, :], self.PB[hp * 128:(hp + 1) * 128, :], w=[qf])
                self.dma(SP, kf[:, hp, :], self.PB[(2 + hp) * 128:(3 + hp) * 128, :], w=[kf])
                self.dma(SP, vf[:, hp, :], self.PB[(4 + hp) * 128:(5 + hp) * 128, :], w=[vf])
            self.dma(POOL, kctx[:], self.kcache[l].rearrange("a p t -> p a t"), w=[kctx])
            self.dma(POOL, vctx[:], self.vcache[l].rearrange("a p f -> p a f"), w=[vctx])
            for g in range(9):
                bk = self.bank[g % 2]
                bkb = bk.t.bitcast(BF16)
                for jj in range(4):
                    tile = g * 4 + jj
                    for hp in range(2):
                        self.tr(bkb[:, (jj * 2 + hp) * 128:(jj * 2 + hp + 1) * 128], vf[:, hp, tile * 128:(tile + 1) * 128],
                                self.identb, r=[vf, self.cstb], w=[bk])
                self.cp(DVE if g % 2 == 0 else ACT, VT[:, g * 4:(g + 1) * 4, :].rearrange("p a f -> p (a f)"), bkb[:, 0:1024], r=[bk], w=[VT])
            self.memset(POOL, FT[0:60, :], NEG, w=[FT])
            self.dma(SP, r31[0:60, :], self.rpb[l], w=[r31])
            self.act(FT[0:60, 48:79], r31[0:60, :], AF.Copy, r=[r31, FT], w=[FT], scale=8.0)
            self.dma(SP, self.FP[l], FT[0:60, :], r=[FT], w=[fpb])
            for half in range(2):
                src = bass.AP(tensor=self.FP.tensor, offset=l * 60 * 127, ap=[[1, 64], [127, 60], [1, 64]])
                self.dma(SP, HK[64 * half:64 * half + 64].rearrange("p h d k -> p (h d) k"), src, r=[fpb], w=[HK])
            self.tt(DVE, HK[:].rearrange("p h d k -> p (h d) k"), HK[:].rearrange("p h d k -> p (h d) k"),
                    self.cst[:, C_CMH, 0:64].unsqueeze(1).to_broadcast([128, 60, 64]), ALU.add, r=[HK, self.cst], w=[HK])
            engs = (POOL, DVE, ACT, POOL, DVE)
            for case, i in ((0, 0), (1, 1), (2, 2), (3, 30), (4, 31)):
                bm = BMc[case]
                eng = engs[case]
                self.memset(eng if eng != ACT else POOL, bm[:], NEG, w=[bm])
                start = min(max(2 * i - 4, 0), 54)
                for qr in range(2):
                    r_ = 2 * i + qr
                    rs = min(max(r_ - 4, 0), 56)
                    for kr in range(10):
                        krow = start + kr
                        if rs <= krow < rs + 8:
                            dr = krow - r_ + 7
                            self.cp(eng, bm[64 * qr:64 * qr + 64, :, kr * 64:(kr + 1) * 64], HK[64 * qr:64 * qr + 64, :, dr, :], r=[HK], w=[bm])
            W["qk"] = [qf, kf, kctx]
            W["v"] = [VT, vctx]
            ob7 = self.bank[7].t.bitcast(BF16)

            def qtile(t0, units):
                def gen(slot):
                    otm = otms[slot]
                    for (hp, b0, pieces, vts, h, bmb) in units:
                        W["bm"] = [bmb, self.cstb] if bmb is not None else [self.cstb]
                        yield from self.attn_unit(slot, qf, hp, b0, t0, pieces, vts, otm, h, W)
                    ob = self.bank[7]
                    for hp in range(2):
                        self.tr(ob7[:, hp * 128:(hp + 1) * 128], otm[:, hp * 128:(hp + 1) * 128], self.identb, r=[otm, self.cstb], w=[ob])
                    self.cp(ACT if slot == 0 else DVE, ocr[:, :, t0:t0 + 128], ob7[:, 0:256].rearrange("p (a b) -> p a b", a=2), r=[ob], w=[ocr])
                    yield
                return gen

            jobs = []
            for i in range(32):
                t0 = 128 * i
                start = min(max(2 * i - 4, 0), 54)
                k0 = 64 * start
                case = {0: 0, 1: 1, 30: 3, 31: 4}.get(i, 2)
                units = []
                for h in range(4):
                    hp, b0 = h // 2, 64 * (h % 2)
                    bm = BMc[case]
                    pieces = [(0, kf[b0:b0 + 64, hp, k0:k0 + 512], bm[:, h, 0:512], 512),
                              (512, kf[b0:b0 + 64, hp, k0 + 512:k0 + 640], bm[:, h, 512:640], 128),
                              (640, kctx[b0:b0 + 64, hp, :], None, 256)]
                    vts = [VT[:, start // 2 + c, h * 64:(h + 1) * 64] for c in range(5)] + [vctx[:, c, h * 64:(h + 1) * 64] for c in range(2)]
                    units.append((hp, b0, pieces, vts, h, bm))
                jobs.append(qtile(t0, units))
            for s_ in range(2):
                tok0 = TL + s_ * TC
                for qi in range(2):
                    t0 = tok0 + 128 * qi
                    units = []
                    for h in range(4):
                        hp, b0 = h // 2, 64 * (h % 2)
                        pieces = [(0, kf[b0:b0 + 64, hp, tok0:tok0 + 256], None, 256)]
                        vts = [VT[:, tok0 // 128 + c, h * 64:(h + 1) * 64] for c in range(2)]
                        units.append((hp, b0, pieces, vts, h, None))
                    jobs.append(qtile(t0, units))
            self.run_pool(jobs, 2)
            for hp in range(2):
                self.dma(SP, self.CAT[768 + hp * 128:768 + (hp + 1) * 128, :], ocr[:, hp, :], r=[ocr])

    @staticmethod
    def run_pool(jobs, nslots):
        jobs = iter(jobs)
        active = {}
        free = list(range(nslots))
        while True:
            while free:
                jb = next(jobs, None)
                if jb is None:
                    break
                sl = free.pop(0)
                active[sl] = jb(sl)
            if not active:
                break
            for sl in list(active):
                try:
                    next(active[sl])
                except StopIteration:
                    del active[sl]
                    free.append(sl)

    def stage_gdn(self, l):
        pvl = self.pv[:, l, :]
        bank = self.bank
        S = self.S
        with contextlib.ExitStack() as es:
            sb = lambda name, shape, dt=F32: self.sb(es, "g" + name, shape, dt)
            qf, kf, vf = sb("qf", [128, 4, TL], BF16), sb("kf", [128, 4, TL], BF16), sb("vf", [128, 4, TL], BF16)
            NPM = TL // 128
            GT = sb("GT", [128, NPM, 16])
            beta, nbeta, g, GC, kds, negc = (sb(n, [128, NPM, 8]) for n in ("beta", "nbeta", "g", "GC", "kds", "negc"))
            t1, t2 = sb("t1", [128, NPM, 8]), sb("t2", [128, NPM, 8])
            egl = sb("egl", [128, NPM, 2, 8])
            gsb = sb("gsb", [128, 16])
            nega = sb("nega", [128, 8])
            ident, identb, ones = self.ident, self.identb, self.ones
            cst = self.cst
            one_ap = self.cst[:, C_EPS, 1:2]
            X, Y, Z, Wk, V, A = bank[0], bank[1], bank[2], bank[3], bank[4], bank[5]
            B2 = [bank[6], bank[7]]
            Bap = self.pst[:, 6 * 512:8 * 512].rearrange("p (h a k) -> p h a k", h=4, a=2)
            Xb = X.t.bitcast(BF16)
            def v3(b):
                return b.t.rearrange("p (h k) -> p h k", h=4)
            def bc_h(ap2):
                return ap2.unsqueeze(1).to_broadcast([128, 4, 128])
            def bc_i(ap2):
                return ap2.unsqueeze(2).to_broadcast([128, 4, 128])
            self.dma(SP, gsb[:], self.gsc[l:l + 1, :].partition_broadcast(128), w=[gsb])
            self.act(nega[:], gsb[:, 0:8], AF.Exp, r=[gsb], w=[nega])
            self.ts(DVE, nega[:], nega[:], -1.0, None, ALU.mult, r=[nega], w=[nega])
            for si, (tok0, T, hcol0, vec) in enumerate(SEQS):
                NP_ = T // 128
                with contextlib.ExitStack() as es2:
                    sb2 = lambda name, shape, dt=F32: self.sb(es2, "g" + name, shape, dt)
                    raws = [sb2(f"raw{i}", [128, 2050]) for i in range(2)]
                    cvs = [sb2(f"cv{i}", [128, 2048]) for i in range(2)]
                    sqs = [sb2(f"sq{i}", [128, 512], BF16) for i in range(2)]
                    rss = [sb2(f"rs{i}", [128, 512]) for i in range(2)]
                    bd = sb2("bd", [16, 512])
                    ui = 0
                    tj = 0
                    for ti, (dst, pc0) in enumerate(((qf, 4), (kf, 8), (vf, 12))):
                        for h in range(4):
                            row0 = (pc0 + h) * 128
                            cw = PV_GCW + 3 * (ti * 4 + h)
                            for half in range(0, T, 2048):
                                n = min(2048, T - half)
                                raw, cv = raws[ui % 2], cvs[ui % 2]
                                ui += 1
                                a = max(half - 1, 0)
                                b_ = min(half + n + 1, T)
                                off = a - (half - 1)
                                if off > 0:
                                    self.memset(POOL, raw[:, 0:1], 0.0, w=[raw])
                                if b_ - (half - 1) < n + 2:
                                    self.memset(POOL, raw[:, n + 1:n + 2], 0.0, w=[raw])
                                self.dma(SP, raw[:, off:off + (b_ - a)], self.P[row0:row0 + 128, tok0 + a:tok0 + b_], w=[raw])
                                self.act(cv[:, 0:n], raw[:, 0:n], AF.Copy, r=[raw, self.pv], w=[cv], scale=pvl[:, cw:cw + 1])
                                self.stt(cv[:, 0:n], raw[:, 1:n + 1], pvl[:, cw + 1:cw + 2], cv[:, 0:n], ALU.mult, ALU.add, r=[raw, self.pv, cv], w=[cv])
                                self.stt(cv[:, 0:n], raw[:, 2:n + 2], pvl[:, cw + 2:cw + 3], cv[:, 0:n], ALU.mult, ALU.add, r=[raw, self.pv, cv], w=[cv])
                                self.act(cv[:, 0:n], cv[:, 0:n], AF.Silu, r=[cv], w=[cv])
                                if ti == 2:
                                    self.cp(POOL, dst[:, h, half:half + n], cv[:, 0:n], r=[cv], w=[dst])
                                else:
                                    for t in range(0, n, 512):
                                        m = min(512, n - t)
                                        sq, rs = sqs[tj % 2], rss[tj % 2]
                                        tj += 1
                                        ps = bank[tj % 4]
                                        self.act(sq[:, 0:m], cv[:, t:t + m], AF.Square, r=[cv], w=[sq])
                                        self.mm(ps[:, 0:m], self.onesb, sq[:, 0:m], r=[sq, self.cstb], w=[ps])
                                        self.rsqrt_ps(rs[:, 0:m], ps[:, 0:m], 1.0, r=[ps], w=[rs])
                                        if ti == 0:
                                            self.stt(dst[:, h, half + t:half + t + m], cv[:, t:t + m], 128.0 ** -0.5, rs[:, 0:m], ALU.mult, ALU.mult, r=[cv, rs], w=[dst])
                                        else:
                                            self.tt(DVE, dst[:, h, half + t:half + t + m], cv[:, t:t + m], rs[:, 0:m], ALU.mult, r=[cv, rs], w=[dst])
                    for jb in range(0, T, 512):
                        n = min(512, T - jb)
                        self.dma(SP, bd[:, 0:n], self.P[26 * 128:26 * 128 + 16, tok0 + jb:tok0 + jb + n], w=[bd])
                        for j in range(n // 128):
                            self.tr(X[:, j * 16:(j + 1) * 16], bd[0:16, j * 128:(j + 1) * 128], ident[0:16, 0:16], r=[bd, cst], w=[X])
                        self.cp(DVE, GT[:, jb // 128:jb // 128 + n // 128, :].rearrange("p a b -> p (a b)"), X[:, 0:(n // 128) * 16], r=[X], w=[GT])
                    NB = NP_ * 8
                    def f2(t):
                        return t[:, 0:NP_, :].rearrange("p a b -> p (a b)")
                    dtb = gsb[:, 8:16].unsqueeze(1).to_broadcast([128, NP_, 8])
                    self.act(beta[:, 0:NP_, :], GT[:, 0:NP_, 0:8], AF.Sigmoid, r=[GT], w=[beta])
                    self.ts(DVE, nbeta[:, 0:NP_, :], beta[:, 0:NP_, :], -1.0, None, ALU.mult, r=[beta], w=[nbeta])
                    self.tt(DVE, t1[:, 0:NP_, :], GT[:, 0:NP_, 8:16], dtb, ALU.add, r=[GT, gsb], w=[t1])
                    self.act(t2[:, 0:NP_, :], t1[:, 0:NP_, :], AF.Abs, r=[t1], w=[t2])
                    self.act(t2[:, 0:NP_, :], t2[:, 0:NP_, :], AF.Exp, r=[t2], w=[t2], scale=-1.0)
                    self.act(t2[:, 0:NP_, :], t2[:, 0:NP_, :], AF.Ln, r=[t2, cst], w=[t2], bias=one_ap, scale=1.0)
                    self.ts(DVE, t1[:, 0:NP_, :], t1[:, 0:NP_, :], 0.0, None, ALU.max, r=[t1], w=[t1])
                    self.tt(DVE, t1[:, 0:NP_, :], t1[:, 0:NP_, :], t2[:, 0:NP_, :], ALU.add, r=[t1, t2], w=[t1])
                    self.tt(DVE, g[:, 0:NP_, :], t1[:, 0:NP_, :], nega[:].unsqueeze(1).to_broadcast([128, NP_, 8]), ALU.mult, r=[t1, nega], w=[g])
                    g2 = f2(g)
                    self.mm(X[:, 0:NB], cst[:, C_TRIF, :], g2, r=[cst, g], w=[X])
                    self.mm(Y[:, 0:NB], cst[:, C_TRIB, :], g2, r=[cst, g], w=[Y])
                    self.mm(Z[:, 0:NB], cst[:, C_BLK, :], g2, r=[cst, g], w=[Z])
                    self.mm(Wk[:, 0:NB], cst[:, C_HALFA, :], g2, r=[cst, g], w=[Wk])
                    self.mm(V[:, 0:NB], cst[:, C_HALFB, :], g2, r=[cst, g], w=[V])
                    self.cp(DVE, GC[:, 0:NP_, 0:4], X[:, 0:NB].rearrange("p (a b) -> p a b", b=8)[:, :, 0:4], r=[X], w=[GC])
                    self.cp(DVE, GC[:, 0:NP_, 4:8], Y[:, 0:NB].rearrange("p (a b) -> p a b", b=8)[:, :, 4:8], r=[Y], w=[GC])
                    self.tt(DVE, f2(kds), Z[:, 0:NB], f2(GC), ALU.subtract, r=[Z, GC], w=[kds])
                    self.act(f2(kds), f2(kds), AF.Exp, r=[kds], w=[kds])
                    self.act(f2(t1), f2(GC), AF.Exp, r=[GC], w=[t1])
                    self.tt(DVE, f2(negc), f2(nbeta), f2(t1), ALU.mult, r=[nbeta, t1], w=[negc])
                    self.act(egl[:, 0:NP_, 0, :], Wk[:, 0:NB].rearrange("p (a b) -> p a b", b=8), AF.Exp, r=[Wk], w=[egl])
                    self.act(egl[:, 0:NP_, 1, :], V[:, 0:NB].rearrange("p (a b) -> p a b", b=8), AF.Exp, r=[V], w=[egl])
                S.barrier()
                with contextlib.ExitStack() as es3:
                    def mkset(d):
                        sb3 = lambda name, shape, dt=F32: self.sb(es3, f"g{d}" + name, shape, dt)
                        outs = [dict(Kd=sb3(f"Kd{i}", [128, 4, 128], BF16), Vb=sb3(f"Vb{i}", [128, 4, 128]), ITm=sb3(f"ITm{i}", [128, 4, 128], BF16),
                                     Yb=sb3(f"Yb{i}", [128, 4, 128], BF16), Qg=sb3(f"Qg{i}", [128, 4, 128], BF16)) for i in range(2)]
                        return dict(out=outs, GR=sb3("GR", [128, 4, 128]), D1=sb3("D1", [128, 4, 128]), D2=sb3("D2", [128, 4, 128]),
                                    Pm=sb3("Pm", [128, 4, 128]), PY=sb3("PY", [128, 4, 2, 128]),
                                    Rb=sb3("Rb", [128, 4, 128], BF16), VNb=sb3("VNb", [128, 4, 128], BF16), S=sb3("S", [128, 4, 128]),
                                    Sb=sb3("Sb", [128, 4, 128], BF16), OT=sb3("OT", [128, 4, 128]))
                    sets = [mkset(0), mkset(1)]
                    ofb = {}
                    done_pre = [0, 0]
                    done_scan = [0, 0]

                    def pre_chain(d):
                        def gen(slot):
                            W = sets[d]
                            GR, D1, D2, Pm, PY = (W[k] for k in ("GR", "D1", "D2", "Pm", "PY"))
                            DG = D1
                            m1c, m2c = (C_UINC, C_LSTR) if d == 0 else (C_LINC, C_USTR)
                            c0, c1, c2 = (bank[4 * d + q] for q in range(3))
                            X, Y, Z, Wk, V, A = c0, c1, c2, c0, c0, c2
                            B2 = [c0, c1]
                            Bap = self.pst[:, 4 * d * 512:(4 * d + 2) * 512].rearrange("p (h a k) -> p h a k", h=4, a=2)
                            Xb = X.t.bitcast(BF16)
                            order = list(range(NP_)) if d == 0 else list(range(NP_ - 1, -1, -1))
                            for idx, p in enumerate(order):
                                while idx - done_scan[d] >= 2:
                                    yield
                                O_ = W["out"][idx % 2]
                                Kd, Vb, ITm, Yb, Qg = (O_[k] for k in ("Kd", "Vb", "ITm", "Yb", "Qg"))
                                t0 = 128 * p
                                gcol = GC[:, p, 4 * d:4 * d + 4]
                                for h in range(4):
                                    self.tr(Xb[:, h * 128:(h + 1) * 128], kf[:, h, t0:t0 + 128], identb, r=[kf, self.cstb], w=[X])
                                    self.tr(Xb[:, (4 + h) * 128:(5 + h) * 128], vf[:, h, t0:t0 + 128], identb, r=[vf, self.cstb], w=[X])
                                self.tt(DVE, DG[:], bc_h(ident), bc_i(gcol), ALU.mult, r=[cst, GC], w=[DG])
                                for h in range(4):
                                    self.mm(Y[:, h * 128:(h + 1) * 128], kf[:, h, t0:t0 + 128], kf[:, h, t0:t0 + 128], r=[kf], w=[Y])
                                    self.mm(Z[:, h * 128:(h + 1) * 128], kf[:, h, t0:t0 + 128], qf[:, h, t0:t0 + 128], r=[kf, qf], w=[Z])
                                yield
                                self.tt(DVE, Kd[:], Xb[:, 0:512].rearrange("p (h k) -> p h k", h=4), bc_i(kds[:, p, 4 * d:4 * d + 4]), ALU.mult, r=[X, kds], w=[Kd])
                                self.tt(DVE, Vb[:], Xb[:, 512:1024].rearrange("p (h k) -> p h k", h=4), bc_i(beta[:, p, 4 * d:4 * d + 4]), ALU.mult, r=[X, beta], w=[Vb])
                                yield
                                self.mm(Wk[:, 0:512], ones, DG[:].rearrange("p h k -> p (h k)"), r=[cst, DG], w=[Wk])
                                yield
                                self.cp(ACT, GR[:].rearrange("p h k -> p (h k)"), Wk[:, 0:512], r=[Wk], w=[GR])
                                yield
                                self.tt(DVE, D1[:], GR[:], bc_i(gcol), ALU.subtract, r=[GR, GC], w=[D1])
                                self.tt(DVE, D2[:], bc_i(gcol), GR[:], ALU.subtract, r=[GR, GC], w=[D2])
                                yield
                                self.act(GR[:], GR[:], AF.Exp, r=[GR], w=[GR])
                                self.tt(POOL, D1[:], D1[:], bc_h(cst[:, m1c, :]), ALU.add, r=[D1, cst], w=[D1])
                                self.tt(POOL, D2[:], D2[:], bc_h(cst[:, m2c, :]), ALU.add, r=[D2, cst], w=[D2])
                                yield
                                self.tt(POOL, Qg[:], qf[:, :, t0:t0 + 128], GR[:], ALU.mult, r=[qf, GR], w=[Qg])
                                self.act(D2[:], D2[:], AF.Exp, r=[D2], w=[D2])
                                self.act(D1[:], D1[:], AF.Exp, r=[D1], w=[D1])
                                yield
                                for h in range(4):
                                    self.stt(Pm[:, h, :], Y[:, h * 128:(h + 1) * 128], nbeta[:, p, 4 * d + h:4 * d + h + 1], D2[:, h, :], ALU.mult, ALU.mult,
                                             r=[Y, nbeta, D2], w=[Pm])
                                self.tt(DVE, ITm[:], v3(Z), D1[:], ALU.mult, r=[Z, D1], w=[ITm])
                                yield
                                for h in range(4):
                                    self.tr(V[:, h * 128:(h + 1) * 128], Pm[:, h, :], ident, r=[Pm, cst], w=[V])
                                yield
                                self.cp(ACT, PY[:, :, 0, :], v3(V), r=[V], w=[PY])
                                yield
                                self.tt(POOL, PY[:, :, 1, :], PY[:, :, 0, :], bc_h(ident), ALU.add, r=[PY, cst], w=[PY])
                                for h in range(4):
                                    self.mm(A[:, h * 128:(h + 1) * 128], PY[:, h, 0, :], Pm[:, h, :], r=[PY, Pm], w=[A])
                                    self.mm(Bap[:, h, 0, :], Pm[:, h, :], PY[:, h, 0, :], r=[PY, Pm], w=B2)
                                yield
                                self.cp(ACT, Pm[:], v3(A), r=[A], w=[Pm])
                                self.cp(DVE, PY[:, :, 0, :], Bap[:, :, 0, :], r=B2, w=[PY])
                                yield
                                for k in range(1, 5):
                                    for h in range(4):
                                        self.mm(A[:, h * 128:(h + 1) * 128], PY[:, h, 0, :], Pm[:, h, :], r=[PY, Pm], w=[A])
                                        self.mm(Bap[:, h, :, :].rearrange("p a k -> p (a k)"), Pm[:, h, :], PY[:, h, :, :].rearrange("p a k -> p (a k)"), r=[PY, Pm], w=B2)
                                    yield
                                    self.cp(ACT, Pm[:], v3(A), r=[A], w=[Pm])
                                    self.cp(DVE, PY[:, :, 0, :], Bap[:, :, 0, :], r=B2, w=[PY])
                                    self.tt(DVE, PY[:, :, 1, :], PY[:, :, 1, :], Bap[:, :, 1, :], ALU.add, r=B2 + [PY], w=[PY])
                                    yield
                                for h in range(4):
                                    self.mm(A[:, h * 128:(h + 1) * 128], Pm[:, h, :], PY[:, h, 1, :], r=[PY, Pm], w=[A])
                                yield
                                self.tt(DVE, Yb[:], PY[:, :, 1, :], v3(A), ALU.add, r=[A, PY], w=[Yb])
                                done_pre[d] = idx + 1
                                yield
                        return gen

                    def scan_chain(d):
                        def gen(slot):
                            W = sets[d]
                            Rb, VNb, Sst, Sb, OT = (W[k] for k in ("Rb", "VNb", "S", "Sb", "OT"))
                            OFd = self.OF if d == 0 else self.OFB
                            C3 = bank[4 * d + 3]
                            self.memset(POOL, Rb[:], 0.0, w=[Rb])
                            self.memset(POOL, VNb[:], 0.0, w=[VNb])
                            if si == 0:
                                self.dma(SP, Sst[:], self.s0[l, :, 4 * d:4 * d + 4, :], w=[Sst])
                            else:
                                self.memset(POOL, Sst[:], 0.0, w=[Sst])
                            self.cp(POOL, Sb[:], Sst[:], r=[Sst], w=[Sb])
                            yield
                            order = list(range(NP_)) if d == 0 else list(range(NP_ - 1, -1, -1))
                            for idx, p in enumerate(order):
                                while done_pre[d] <= idx:
                                    yield
                                O_ = W["out"][idx % 2]
                                Kd, Vb, ITm, Yb, Qg = (O_[k] for k in ("Kd", "Vb", "ITm", "Yb", "Qg"))
                                t0 = 128 * p
                                for hf in ((0, 1) if d == 0 else (1, 0)):
                                    R = slice(64 * hf, 64 * hf + 64)
                                    cs = slice(64 * hf, 64 * hf + 64)
                                    for h in range(4):
                                        self.mm(C3[:, h * 128:(h + 1) * 128], kf[:, h, t0:t0 + 128], Sb[:, h, :], r=[kf, Sb], w=[C3])
                                    yield
                                    for h in range(4):
                                        self.stt(Rb[R, h, :], C3[R, h * 128:(h + 1) * 128], negc[R, p, 4 * d + h:4 * d + h + 1], Vb[R, h, :], ALU.mult, ALU.add,
                                                 r=[C3, negc, Vb], w=[Rb])
                                    yield
                                    for h in range(4):
                                        self.mm(C3[:, h * 128:(h + 1) * 128], Yb[:, h, :], Rb[:, h, :], r=[Yb, Rb], w=[C3])
                                    yield
                                    self.cp(ACT, VNb[R, :, :].rearrange("p h k -> p (h k)"), C3[R, 0:512], r=[C3], w=[VNb])
                                    yield
                                    for h in range(4):
                                        self.mm(C3[:, h * 128:(h + 1) * 128], Kd[R, h, :], VNb[R, h, :], r=[Kd, VNb], w=[C3])
                                    yield
                                    for h in range(4):
                                        self.stt(Sst[:, h, :], Sst[:, h, :], egl[:, p, hf, 4 * d + h:4 * d + h + 1], C3[:, h * 128:(h + 1) * 128], ALU.mult, ALU.add,
                                                 r=[Sst, egl, C3], w=[Sst])
                                    yield
                                    for h in range(4):
                                        self.mm(C3[:, h * 64:(h + 1) * 64], Sb[:, h, :], Qg[:, h, cs], start=True, stop=False, r=[Sb, Qg], w=[C3])
                                        self.mm(C3[:, h * 64:(h + 1) * 64], VNb[R, h, :], ITm[R, h, cs], start=False, stop=True, r=[VNb, ITm], w=[C3])
                                    self.cp(POOL, Sb[:], Sst[:], r=[Sst], w=[Sb])
                                    yield
                                    self.cp(ACT, OT[:, :, cs], C3[:, 0:256].rearrange("p (h k) -> p h k", h=4), r=[C3], w=[OT])
                                    yield
                                ofb[(d, p)] = Buf(f"of{d}_{p}", None)
                                self.dma(SP, OFd[:, tok0 + t0:tok0 + t0 + 128].rearrange("(h p) t -> p h t", p=128), OT[:], r=[OT], w=[ofb[(d, p)]])
                                done_scan[d] = idx + 1
                                yield
                            if si > 0:
                                self.dma(SP, self.stout[si - 1, l, d].rearrange("h k v -> k h v"), Sst[:], r=[Sst])
                        return gen

                    self.run_pool([pre_chain(0), pre_chain(1), scan_chain(0), scan_chain(1)], 4)
                    ep = [dict(ofl=self.sb(es3, f"gofl{i}", [128, 4, 128]), ofr=self.sb(es3, f"gofr{i}", [128, 4, 128]), zt=self.sb(es3, f"gzt{i}", [128, 4, 128]),
                               osq=self.sb(es3, f"gosq{i}", [128, 4, 128], BF16), ors=self.sb(es3, f"gors{i}", [128, 4, 128]),
                               ob=self.sb(es3, f"gob{i}", [128, 4, 128], BF16)) for i in range(2)]

                    def epilogue(p):
                        def gen(slot):
                            E = ep[slot]
                            ofl, ofr, zt, osq, ors, ob = (E[k] for k in ("ofl", "ofr", "zt", "osq", "ors", "ob"))
                            t0 = 128 * p
                            cols = slice(tok0 + t0, tok0 + t0 + 128)
                            self.dma(SP, ofl[:], self.OF[:, cols].rearrange("(h p) t -> p h t", p=128), r=[ofb[(0, p)]], w=[ofl])
                            self.dma(SP, ofr[:], self.OFB[:, cols].rearrange("(h p) t -> p h t", p=128), r=[ofb[(1, p)]], w=[ofr])
                            self.dma(SP, zt[:], self.P[16 * 128:20 * 128, cols].rearrange("(h p) t -> p h t", p=128), w=[zt])
                            yield
                            self.tt(POOL, ofl[:], ofl[:], ofr[:], ALU.add, r=[ofl, ofr], w=[ofl])
                            self.act(zt[:], zt[:], AF.Silu, r=[zt], w=[zt])
                            yield
                            self.act(osq[:], ofl[:], AF.Square, r=[ofl], w=[osq])
                            yield
                            pb_ = bank[2 + slot]
                            self.mm(pb_[:, 0:512], self.onesb, osq[:].rearrange("p h k -> p (h k)"), r=[osq, self.cstb], w=[pb_])
                            yield
                            self.act(ors[:].rearrange("p h k -> p (h k)"), pb_[:, 0:512], AF.Sqrt, r=[pb_, self.cst], w=[ors], bias=self.c_eps, scale=1.0 / 128)
                            yield
                            self.recip(ors[:], ors[:], r=[ors], w=[ors])
                            yield
                            self.tt(DVE, ofl[:], ofl[:], ors[:], ALU.mult, r=[ofl, ors], w=[ofl])
                            yield
                            self.stt(ob[:], ofl[:], pvl[:, PV_GNW:PV_GNW + 1], zt[:], ALU.mult, ALU.mult, r=[ofl, self.pv, zt], w=[ob])
                            yield
                            self.dma(SP, self.CAT[256:768, cols].rearrange("(h p) t -> p h t", p=128), ob[:], r=[ob])
                        return gen

                    self.run_pool([epilogue(p) for p in range(NP_)], 2)
                S.barrier()


def _prep_inputs(inp, nl=DEPTH):
    f = lambda a: np.ascontiguousarray(np.asarray(a, dtype=np.float32))
    inp = {k: f(v) for k, v in inp.items()}
    shared = {
        "cst": _consts(),
        "pv": _pack_pv(inp),
        "w_ada": inp["w_ada"][:nl],
        "w_in": np.ascontiguousarray(inp["w_in"][:nl][:, :, _WIN_PERM]),
        "w_out": inp["w_out"][:nl],
        "w_up": inp["w_up"][:nl],
        "w_down": inp["w_down"][:nl],
        "gsc": np.ascontiguousarray(np.concatenate([inp["gdn_a_log"].reshape(DEPTH, 8), inp["gdn_dt_bias"].reshape(DEPTH, 8)], axis=1)[:nl]),
        "rpb": np.ascontiguousarray(inp["na_rpb"].reshape(DEPTH, 60, 31)[:nl]),
    }
    maps = []
    for i in range(8):
        xin = np.concatenate([inp["x_sample"][i].T, inp["x_prompt"][2 * i].T, inp["x_prompt"][2 * i + 1].T], axis=1)
        cv = np.stack([inp["c_ctx"], inp["c"][i]], axis=1)
        cvec = cv.reshape(8, 128, 2).transpose(1, 0, 2).reshape(128, 16)
        kv = inp["cache_attn_kv"][i][:nl]
        kc = kv[:, 0].transpose(0, 1, 3, 2).reshape(nl, 2, 128, TC)
        vc = kv[:, 1].transpose(0, 2, 1, 3).reshape(nl, 2, 128, 256)
        s0 = inp["state_delta"][i][:nl].reshape(nl, 8, 128, 128).transpose(0, 2, 1, 3)
        m = dict(shared)
        m.update(xin=np.ascontiguousarray(xin), cvec=np.ascontiguousarray(cvec), kcache=np.ascontiguousarray(kc),
                 vcache=np.ascontiguousarray(vc), s0=np.ascontiguousarray(s0))
        maps.append(m)
    return maps


def _assemble(results, nl=DEPTH):
    y_prompt = np.zeros((16, TC, D), np.float32)
    y_sample = np.zeros((8, TL, D), np.float32)
    kvn = np.zeros((16, nl, 2, 4, TC, 64), np.float32)
    stn = np.zeros((16, nl, 2, 4, 128, 128), np.float32)
    for i, r in enumerate(results):
        xo = r["xout"]
        y_sample[i] = xo[:, 0:TL].T
        y_prompt[2 * i] = xo[:, TL:TL + TC].T
        y_prompt[2 * i + 1] = xo[:, TL + TC:TT].T
        kvo = r["kvout"]
        for s in range(2):
            kvn[2 * i + s] = kvo[:, s].reshape(nl, 2, 4, 64, TC).transpose(0, 1, 2, 4, 3)
            stn[2 * i + s] = r["stout"][s]
    return y_prompt, y_sample, kvn, stn


_CACHE = {}


def kernel(**inputs):
    if "kb" not in _CACHE:
        kb = KB()
        kb.build()
        _CACHE["kb"] = kb
    kb = _CACHE["kb"]
    maps = _prep_inputs(inputs)
    res = run_bass_kernel_spmd(kb.nc, maps, core_ids=list(range(8)))
    return _assemble(res.results)
```

```python
import contextlib
import numpy as np
import concourse.bass as bass
import concourse.mybir as mybir
from concourse.bass_utils import run_bass_kernel_spmd

F32 = mybir.dt.float32
BF16 = mybir.dt.bfloat16
AF = mybir.ActivationFunctionType
ALU = mybir.AluOpType
AX = mybir.AxisListType

PE, ACT, DVE, POOL, SP = "pe", "act", "dve", "pool", "sp"
COMPUTE = (PE, ACT, DVE, POOL)
ALLENG = (PE, ACT, DVE, POOL, SP)
DMAQ = (SP, ACT, POOL)
ND = 4
SAME_ENGINE_SYNC = True

D = 1024
DEPTH = 4
TL = 4096
TC = 256
TT = TL + 2 * TC
NCH = 27
PROJ = 3344
DFF = 2816
EPS = 1e-6
NEG = -30000.0
NEGBIG = -1.0e5
HW_ = TT + 6
SEQS = ((0, TL, 1, 1), (TL, TC, 4099, 0), (TL + TC, TC, 4357, 0))


class Buf:
    __slots__ = ("name", "t", "lw", "rd")

    def __init__(self, name, t):
        self.name = name
        self.t = t
        self.lw = None
        self.rd = []

    def __getitem__(self, k):
        return self.t[k]


class Op:
    __slots__ = ("eng", "fn", "deps", "marked", "is_dma", "sem", "cnt", "epoch", "kind")

    def __init__(self, eng, fn, is_dma, epoch, kind="op"):
        self.eng = eng
        self.fn = fn
        self.deps = []
        self.marked = False
        self.is_dma = is_dma
        self.sem = None
        self.cnt = 0
        self.epoch = epoch
        self.kind = kind


class Sched:
    def __init__(self, nc):
        self.nc = nc
        self.ops = []
        self.epoch = 0
        self.eng = {PE: nc.tensor, ACT: nc.scalar, DVE: nc.vector, POOL: nc.gpsimd, SP: nc.sync}

    def op(self, eng, fn, reads=(), writes=(), dma=False):
        o = Op(eng, fn, dma, self.epoch)
        deps = {}
        for b in reads:
            if b.lw is not None:
                deps[id(b.lw)] = (b.lw, "raw")
        for b in writes:
            if b.lw is not None:
                deps[id(b.lw)] = (b.lw, "waw")
            for r in b.rd:
                if id(r) not in deps:
                    deps[id(r)] = (r, "war")
        for d, kind in deps.values():
            if d.epoch != o.epoch or d is o:
                continue
            if (not d.is_dma) and d.eng == eng and not dma:
                if eng == PE or kind == "war" or not SAME_ENGINE_SYNC:
                    continue
            o.deps.append(d)
        for b in reads:
            b.rd.append(o)
        for b in writes:
            b.lw = o
            b.rd = []
        self.ops.append(o)
        return o

    def barrier(self, new_epoch=False):
        self.ops.append(Op(None, None, False, self.epoch, kind="barrier_ep" if new_epoch else "barrier"))
        if new_epoch:
            self.epoch += 1

    def emit(self, es):
        nc = self.nc
        nep = self.epoch + 1
        for o in self.ops:
            for d in o.deps:
                d.marked = True
        sems = []
        for ep in range(nep):
            d = {}
            for e in COMPUTE:
                d[e] = es.enter_context(nc.semaphore(f"s{ep}_{e}"))
            for q in DMAQ:
                for k in range(ND):
                    d[(q, k)] = es.enter_context(nc.semaphore(f"d{ep}_{q}{k}"))
            sems.append(d)
        cur = [dict((k, 0) for k in sems[ep]) for ep in range(nep)]
        rr = dict((q, 0) for q in DMAQ)
        seen = {}
        n_wait = 0
        for o in self.ops:
            ep = o.epoch
            if o.kind != "op":
                for e in ALLENG:
                    for key, val in cur[ep].items():
                        if val > 0 and seen.get((e, key), 0) < val:
                            self.eng[e].wait_ge(sems[ep][key], val)
                            seen[(e, key)] = val
                            n_wait += 1
                if o.kind == "barrier_ep":
                    seen = {}
                continue
            e = self.eng[o.eng]
            need = {}
            for d in o.deps:
                if need.get(d.sem, 0) < d.cnt:
                    need[d.sem] = d.cnt
            for key, val in need.items():
                if seen.get((o.eng, key), 0) >= val:
                    continue
                e.wait_ge(sems[ep][key], val)
                n_wait += 1
                seen[(o.eng, key)] = val
            inst = o.fn(e)
            if o.is_dma:
                key = (o.eng, rr[o.eng] % ND)
                rr[o.eng] += 1
                cur[ep][key] += 16
                o.sem, o.cnt = key, cur[ep][key]
                inst.then_inc(sems[ep][key], 16)
            elif o.marked:
                cur[ep][o.eng] += 1
                o.sem, o.cnt = o.eng, cur[ep][o.eng]
                inst.then_inc(sems[ep][o.eng], 1)
        ep = nep - 1
        for e in ALLENG:
            for key, val in cur[ep].items():
                if val > 0:
                    self.eng[e].wait_ge(sems[ep][key], val)
        return dict(n_ops=len(self.ops), n_wait=n_wait, counts=cur)


C_IDENT, C_ONES, C_JREV, C_TRIF, C_TRIB, C_BLK, C_HALFA, C_HALFB = range(8)
C_UINC, C_LSTR, C_LINC, C_USTR = 8, 9, 10, 11
C_CMH = 12
C_EPS = 13
NCST = 14


def _consts():
    p = np.arange(128)[:, None]
    f = np.arange(128)[None, :]
    same = (p // 64) == (f // 64)
    c = np.zeros((128, NCST, 128), np.float32)
    c[:, C_IDENT] = (p == f)
    c[:, C_ONES] = 1.0
    c[:, C_JREV] = same & ((p % 64) == (63 - f % 64))
    c[:, C_TRIF] = same & (p <= f)
    c[:, C_TRIB] = same & (p >= f)
    c[:, C_BLK] = same
    c[:, C_HALFA] = (p < 64) & (f >= 0)
    c[:, C_HALFB] = (p >= 64) & (f >= 0)
    c[:, C_UINC] = np.where(same & (f >= p), 0.0, NEGBIG)
    c[:, C_LSTR] = np.where(same & (p > f), 0.0, NEGBIG)
    c[:, C_LINC] = np.where(same & (f <= p), 0.0, NEGBIG)
    c[:, C_USTR] = np.where(same & (p < f), 0.0, NEGBIG)
    qc = 63 - (np.arange(128) % 64)
    cs = np.clip(qc - 8, 0, 48)
    kc = np.arange(64)[None, :]
    ok = (kc >= cs[:, None]) & (kc < cs[:, None] + 16)
    c[:, C_CMH, 0:64] = np.where(ok, 0.0, NEG)
    c[:, C_EPS, 0] = EPS
    c[:, C_EPS, 1] = 1.0
    return c.reshape(128, NCST * 128)


PV_GPM, PV_GPOM, PV_GPF, PV_GPOF = 0, 8, 16, 24
PV_BADA = 32
PV_CW = 80
PV_CB, PV_CLG, PV_CLB = 142, 144, 146
PV_GCW = 148
PV_GNW = 184
PV_FCW = 185
PV_FCB = 317
PV_N = 361


def _fm(v, nchunk):
    return np.ascontiguousarray(v.reshape(nchunk, 128).T)


def _pack_pv(inp):
    pv = np.zeros((128, DEPTH, PV_N), np.float32)
    for l in range(DEPTH):
        pv[:, l, PV_GPM:PV_GPM + 8] = _fm(inp["g_pre_mix"][l], 8)
        pv[:, l, PV_GPOM:PV_GPOM + 8] = _fm(inp["g_post_mix"][l], 8)
        pv[:, l, PV_GPF:PV_GPF + 8] = _fm(inp["g_pre_ffn"][l], 8)
        pv[:, l, PV_GPOF:PV_GPOF + 8] = _fm(inp["g_post_ffn"][l], 8)
        pv[:, l, PV_BADA:PV_BADA + 48] = _fm(inp["b_ada"][l], 48)
        cw = inp["conv_w"][l]
        pv[:, l, PV_CW:PV_CW + 62] = cw.T.reshape(2, 128, 31).transpose(1, 0, 2).reshape(128, 62)
        pv[:, l, PV_CB:PV_CB + 2] = _fm(inp["conv_b"][l], 2)
        pv[:, l, PV_CLG:PV_CLG + 2] = _fm(inp["conv_ln_g"][l], 2)
        pv[:, l, PV_CLB:PV_CLB + 2] = _fm(inp["conv_ln_b"][l], 2)
        gw = inp["gdn_conv_w"][l]
        pv[:, l, PV_GCW:PV_GCW + 36] = gw.T.reshape(12, 128, 3).transpose(1, 0, 2).reshape(128, 36)
        pv[:, l, PV_GNW] = inp["gdn_norm_w"][l]
        fw = inp["ffn_conv_w"][l]
        pv[:, l, PV_FCW:PV_FCW + 132] = fw.T.reshape(44, 128, 3).transpose(1, 0, 2).reshape(128, 132)
        pv[:, l, PV_FCB:PV_FCB + 44] = _fm(inp["ffn_conv_b"][l], 44)
    return pv.reshape(128, DEPTH * PV_N)


_WIN_PERM = np.concatenate([np.arange(0, 2560), np.arange(2576, 3344), np.arange(2560, 2576)])


class KB:
    def __init__(self, nlayers=DEPTH, dbg=None):
        self.nl = nlayers
        self.dbg = dbg or {}
        nc = self.nc = bass.Bass("TRN2", target_bir_lowering=False)
        self.S = Sched(nc)
        self.es = contextlib.ExitStack()
        self.uid = 0

    def mm(self, out, lhsT, rhs, start=True, stop=True, r=(), w=()):
        return self.S.op(PE, lambda e: e.matmul(out, lhsT, rhs, start=start, stop=stop), r, w)

    def tr(self, out, in_, ident, r=(), w=()):
        return self.S.op(PE, lambda e: e.transpose(out, in_, ident), r, w)

    def act(self, out, in_, func, r=(), w=(), eng=ACT, **kw):
        return self.S.op(eng, lambda e: e.activation(out=out, in_=in_, func=func, **kw), r, w)

    def tt(self, eng, out, in0, in1, op, r=(), w=()):
        return self.S.op(eng, lambda e: e.tensor_tensor(out=out, in0=in0, in1=in1, op=op), r, w)

    def ts(self, eng, out, in0, s1, s2, op0, op1=None, r=(), w=()):
        if op1 is None:
            return self.S.op(eng, lambda e: e.tensor_scalar(out=out, in0=in0, scalar1=s1, scalar2=None, op0=op0), r, w)
        return self.S.op(eng, lambda e: e.tensor_scalar(out=out, in0=in0, scalar1=s1, scalar2=s2, op0=op0, op1=op1), r, w)

    def stt(self, out, in0, scalar, in1, op0, op1, r=(), w=()):
        return self.S.op(DVE, lambda e: e.scalar_tensor_tensor(out=out, in0=in0, scalar=scalar, in1=in1, op0=op0, op1=op1), r, w)

    def cp(self, eng, out, in_, r=(), w=()):
        if eng == ACT:
            return self.S.op(ACT, lambda e: e.copy(out=out, in_=in_), r, w)
        return self.S.op(eng, lambda e: e.tensor_copy(out=out, in_=in_), r, w)

    def memset(self, eng, ap, val, w=()):
        return self.S.op(eng, lambda e: e.memset(ap, val), (), w)

    def recip(self, out, in_, r=(), w=()):
        return self.S.op(DVE, lambda e: e.reciprocal(out=out, in_=in_), r, w)

    def dma(self, q, out, in_, r=(), w=()):
        return self.S.op(q, lambda e: e.dma_start(out=out, in_=in_), r, w, dma=True)

    def sb(self, es, name, shape, dt=F32):
        self.uid += 1
        return Buf(name, es.enter_context(self.nc.sbuf_tensor(f"{name}_{self.uid}", shape, dt)))

    def dram(self, name, shape, dt, kind="Internal"):
        if name in self.dbg.get("out", ()):
            kind = "ExternalOutput"
        if name in self.dbg.get("in", ()):
            kind = "ExternalInput"
        return self.nc.dram_tensor(name, shape, dt, kind=kind).ap()

    def rsqrt_ps(self, out, ps_ap, scale, r, w):
        self.act(out, ps_ap, AF.Sqrt, r=list(r) + [self.cst], w=w, bias=self.c_eps, scale=scale)
        self.recip(out, out, r=w, w=w)

    def build(self):
        nc, S, es = self.nc, self.S, self.es
        nl = self.nl
        I = {}
        def ext(name, shape, dt=F32):
            I[name] = nc.dram_tensor(name, shape, dt, kind="ExternalInput").ap()
            return I[name]
        self.xin = ext("xin", [D, TT])
        self.cvec = ext("cvec", [128, 16])
        self.cst_d = ext("cst", [128, NCST * 128])
        self.pv_d = ext("pv", [128, DEPTH * PV_N])
        self.w_ada = ext("w_ada", [nl, D, 6 * D])
        self.w_in = ext("w_in", [nl, D, PROJ])
        self.w_out = ext("w_out", [nl, D, D])
        self.w_up = ext("w_up", [nl, D, 2 * DFF])
        self.w_down = ext("w_down", [nl, DFF, D])
        self.kcache = ext("kcache", [nl, 2, 128, TC])
        self.vcache = ext("vcache", [nl, 2, 128, 256])
        self.s0 = ext("s0", [nl, 128, 8, 128])
        self.gsc = ext("gsc", [nl, 16])
        self.rpb = ext("rpb", [nl, 60, 31])
        self.xout = nc.dram_tensor("xout", [D, TT], F32, kind="ExternalOutput").ap()
        self.kvout = nc.dram_tensor("kvout", [nl, 2, 2, 256, TC], F32, kind="ExternalOutput").ap()
        self.stout = nc.dram_tensor("stout", [2, nl, 2, 4, 128, 128], F32, kind="ExternalOutput").ap()
        self.P = self.dram("P", [NCH * 128, TT], F32)
        self.PB = self.dram("PB", [6 * 128, TT], BF16)
        self.CAT = self.dram("CAT", [D, TT], BF16)
        self.XA = self.dram("XA", [D, TT], F32)
        self.A = self.dram("A", [DFF, TT], BF16)
        self.OF = self.dram("OF", [512, TT], F32)
        self.OFB = self.dram("OFB", [512, TT], F32)
        self.FP = self.dram("FP", [nl, 60, 127], F32)

        with es:
            pst = es.enter_context(nc.psum_tensor("ps", [128, 4096], F32))
            self.bank = [Buf(f"bank{i}", pst[:, i * 512:(i + 1) * 512]) for i in range(8)]
            self.pst = pst
            self.cst = self.sb(es, "cst", [128, NCST, 128])
            self.cstb = self.sb(es, "cstb", [128, 3, 128], BF16)
            self.pv = self.sb(es, "pv", [128, DEPTH, PV_N])
            self.mod = self.sb(es, "mod", [128, 6, 8, 2])
            self.scv = self.sb(es, "scv", [128, 8, 2])
            self.c_eps = self.cst[:, C_EPS, 0:1]
            self.dma(SP, self.cst[:].rearrange("p a b -> p (a b)"), self.cst_d[:, :], w=[self.cst])
            self.dma(SP, self.pv[:].rearrange("p a b -> p (a b)"), self.pv_d[:, :], w=[self.pv])
            self.dma(SP, self.scv[:].rearrange("p a b -> p (a b)"), self.cvec[:, :], w=[self.scv])
            for i, ci in enumerate((C_IDENT, C_ONES, C_JREV)):
                self.cp(DVE, self.cstb[:, i, :], self.cst[:, ci, :], r=[self.cst], w=[self.cstb])
            self.ident = self.cst[:, C_IDENT, :]
            self.ones = self.cst[:, C_ONES, :]
            self.identb = self.cstb[:, 0, :]
            self.onesb = self.cstb[:, 1, :]
            self.jrevb = self.cstb[:, 2, :]
            self.act(self.scv[:], self.scv[:], AF.Silu, r=[self.scv], w=[self.scv])
            for l in range(nl):
                self.layer(l)
                if l + 1 < nl:
                    S.barrier(new_epoch=True)
            info = S.emit(es)
        return info

    def layer(self, l):
        S = self.S
        stages = self.dbg.get("stages", ("ada", "s1", "conf", "gdn", "attn", "s3", "s4a", "s4b"))
        xsrc = self.xin if l == 0 else self.xout
        if "ada" in stages:
            self.stage_ada(l)
            S.barrier()
        if "s1" in stages:
            self.stage_s1(l, xsrc)
            S.barrier()
        if "conf" in stages:
            self.stage_conf(l)
            S.barrier()
        if "attn" in stages:
            self.stage_attn(l)
            S.barrier()
        if "gdn" in stages:
            self.stage_gdn(l)
            S.barrier()
        if "s3" in stages:
            with contextlib.ExitStack() as es:
                self.Hb = self.sb(es, "H", [128, 8, HW_], BF16)
                self.stage_s3(l, xsrc, es)
                S.barrier()
                if "s4a" in stages:
                    self.stage_s4a(l)
                    S.barrier()
        if "s4b" in stages:
            self.stage_s4b(l)
            S.barrier()

    def stage_ada(self, l):
        with contextlib.ExitStack() as es:
            wb = [self.sb(es, f"adaw{i}", [128, 8, 512]) for i in range(6)]
            raw = self.sb(es, "adaraw", [128, 48, 2])
            ps = self.bank[0]
            for g in range(12):
                w = wb[g % 6]
                self.dma(SP, w[:],
                         self.w_ada[l, :, g * 512:(g + 1) * 512].rearrange("(kc p) n -> p kc n", p=128), w=[w])
                for j in range(4):
                    n = g * 4 + j
                    for kc in range(8):
                        self.mm(ps[:, 2 * n:2 * n + 2], w[:, kc, j * 128:(j + 1) * 128], self.scv[:, kc, :],
                                start=(kc == 0), stop=(kc == 7), r=[w, self.scv], w=[ps])
            pvl = self.pv[:, l, :]
            self.tt(DVE, raw[:], ps[:, 0:96].rearrange("p (n v) -> p n v", v=2),
                    pvl[:, PV_BADA:PV_BADA + 48].unsqueeze(2).to_broadcast([128, 48, 2]), ALU.add,
                    r=[ps, self.pv], w=[raw])
            def gain(off):
                return pvl[:, off:off + 8].unsqueeze(2).to_broadcast([128, 8, 2])
            m = self.mod
            self.ts(DVE, m[:, 0], raw[:, 8:16, :], 1.0, None, ALU.add, r=[raw], w=[m])
            self.tt(DVE, m[:, 0], m[:, 0], gain(PV_GPM), ALU.mult, r=[m, self.pv], w=[m])
            self.cp(DVE, m[:, 1], raw[:, 0:8, :], r=[raw], w=[m])
            self.tt(DVE, m[:, 2], raw[:, 16:24, :], gain(PV_GPOM), ALU.mult, r=[raw, self.pv], w=[m])
            self.ts(DVE, m[:, 3], raw[:, 32:40, :], 1.0, None, ALU.add, r=[raw], w=[m])
            self.tt(DVE, m[:, 3], m[:, 3], gain(PV_GPF), ALU.mult, r=[m, self.pv], w=[m])
            self.cp(DVE, m[:, 4], raw[:, 24:32, :], r=[raw], w=[m])
            self.tt(DVE, m[:, 5], raw[:, 40:48, :], gain(PV_GPOF), ALU.mult, r=[raw, self.pv], w=[m])

    def sumsq_rstd(self, src, sq, ps, rstd, n, nchunk=8, scale=1.0 / D, src_bufs=()):
        for c in range(nchunk):
            self.act(sq[:, c, 0:n], src[:, c, 0:n], AF.Square, r=list(src_bufs), w=[sq], eng=ACT)
        for c in range(nchunk):
            self.mm(ps[:, 0:n], self.onesb, sq[:, c, 0:n], start=(c == 0), stop=(c == nchunk - 1),
                    r=[sq, self.cstb], w=[ps])
        self.rsqrt_ps(rstd[:, 0:n], ps[:, 0:n], scale, r=[ps], w=[rstd])

    def s1_tiles(self):
        tl = [(j * 512, 512, 1, [(1 + j * 512, 512)]) for j in range(8)]
        tl.append((TL, 512, 0, [(4099, 256), (4357, 256)]))
        return tl

    def stage_s1(self, l, xsrc):
        with contextlib.ExitStack() as es:
            H = self.sb(es, "H", [128, 8, HW_], BF16)
            xt = [self.sb(es, f"xt{i}", [128, 8, 512]) for i in range(2)]
            sq = self.sb(es, "sq", [128, 8, 512], BF16)
            rstd = self.sb(es, "rstd", [128, 512])
            wck = [self.sb(es, f"wck{i}", [128, 8, 128], BF16) for i in range(3)]
            ev = [self.sb(es, f"ev{i}", [128, 512]) for i in range(4)]
            evb = [self.sb(es, f"evb{i}", [128, 512], BF16) for i in range(2)]
            tiles = self.s1_tiles()
            m = self.mod
            for ti, (t0, n, vec, hc) in enumerate(tiles):
                x = xt[ti % 2]
                self.dma(SP, x[:], xsrc[:, t0:t0 + n].rearrange("(c p) t -> p c t", p=128), w=[x])
                ps = self.bank[ti % 2]
                self.sumsq_rstd(x, sq, ps, rstd, n, src_bufs=[x])
                self.tt(DVE, x[:], x[:], rstd[:].unsqueeze(1).to_broadcast([128, 8, 512]), ALU.mult, r=[x, rstd], w=[x])
                for c in range(8):
                    off = 0
                    for (h0, nn) in hc:
                        eng = ACT if c % 2 == 0 else POOL
                        if eng == ACT:
                            self.act(H[:, c, h0:h0 + nn], x[:, c, off:off + nn], AF.Identity, r=[x, m], w=[H],
                                     scale=m[:, 0, c, vec:vec + 1], bias=m[:, 1, c, vec:vec + 1])
                        else:
                            self.ts(POOL, H[:, c, h0:h0 + nn], x[:, c, off:off + nn], m[:, 0, c, vec:vec + 1],
                                    m[:, 1, c, vec:vec + 1], ALU.mult, ALU.add, r=[x, m], w=[H])
                        off += nn
            k = 0
            for nci in range(NCH):
                ncols = 128 if nci < 26 else 16
                w = wck[nci % 3]
                self.dma(POOL, w[:, :, 0:ncols],
                         self.w_in[l, :, nci * 128:nci * 128 + ncols].rearrange("(kc p) n -> p kc n", p=128), w=[w])
                for ti, (t0, n, vec, hc) in enumerate(tiles):
                    ps = self.bank[2 + (k % 4)]
                    off = 0
                    for (h0, nn) in hc:
                        for kc in range(8):
                            self.mm(ps[0:ncols, off:off + nn], w[:, kc, 0:ncols], H[:, kc, h0:h0 + nn],
                                    start=(kc == 0), stop=(kc == 7), r=[w, H], w=[ps])
                        off += nn
                    e = ev[k % 4]
                    if k % 2 == 0:
                        self.cp(ACT, e[0:ncols, :], ps[0:ncols, :], r=[ps], w=[e])
                    else:
                        self.cp(DVE, e[0:ncols, :], ps[0:ncols, :], r=[ps], w=[e])
                    self.dma(SP, self.P[nci * 128:nci * 128 + ncols, t0:t0 + n], e[0:ncols, :], r=[e])
                    if 20 <= nci < 26:
                        eb = evb[k % 2]
                        self.cp(POOL, eb[:], e[:], r=[e], w=[eb])
                        self.dma(SP, self.PB[(nci - 20) * 128:(nci - 19) * 128, t0:t0 + n], eb[:], r=[eb])
                    if ti == 8 and 22 <= nci < 26:
                        kv = (nci - 22) // 2
                        f0 = ((nci - 22) % 2) * 128
                        for s in range(2):
                            self.dma(SP, self.kvout[l, s, kv, f0:f0 + 128, :], e[:, s * 256:(s + 1) * 256], r=[e])
                    k += 1

    def stage_s3(self, l, xsrc, es0):
        H = self.Hb
        with contextlib.ExitStack() as es:
            wo = self.sb(es, "wo", [128, 8, D], BF16)
            cat = [self.sb(es, f"cat{i}", [128, 8, 512], BF16) for i in range(2)]
            xt = [self.sb(es, f"xt{i}", [128, 8, 512]) for i in range(2)]
            msb = self.sb(es, "msb", [128, 8, 512])
            sq = self.sb(es, "sq", [128, 8, 512], BF16)
            rstd = self.sb(es, "rstd", [128, 512])
            zc = self.sb(es, "zc", [128, 8, 2], BF16)
            parts = self.dbg.get("s3parts", "wpx")
            if "w" in parts:
                self.dma(POOL, wo[:], self.w_out[l].rearrange("(kc p) n -> p kc n", p=128), w=[wo])
            if "p" in parts:
                self.memset(POOL, zc[:], 0.0, w=[zc])
                for c0, n0 in ((0, 1), (4097, 2), (4355, 2), (4613, 1)):
                    self.cp(POOL, H[:, :, c0:c0 + n0], zc[:, :, 0:n0], r=[zc], w=[H])
            if "x" not in parts:
                return
            m = self.mod
            lvl = int(self.dbg.get("s3n", 9))
            for ti, (t0, n, vec, hc) in enumerate(self.s1_tiles()):
                ct = cat[ti % 2]
                x = xt[ti % 2]
                self.dma(SP, ct[:], self.CAT[:, t0:t0 + n].rearrange("(c p) t -> p c t", p=128), w=[ct])
                self.dma(SP, x[:], xsrc[:, t0:t0 + n].rearrange("(c p) t -> p c t", p=128), w=[x])
                if lvl < 2:
                    continue
                for nn in range(8):
                    ps = self.bank[nn % 4]
                    for kc in range(8):
                        self.mm(ps[:, 0:n], wo[:, kc, nn * 128:(nn + 1) * 128], ct[:, kc, :],
                                start=(kc == 0), stop=(kc == 7), r=[wo, ct], w=[ps])
                    if "c" in self.dbg.get("s3l2", "cs"):
                        self.cp(DVE, msb[:, nn, :], ps[:, 0:n], r=[ps], w=[msb])
                    if "s" in self.dbg.get("s3l2", "cs"):
                        self.act(sq[:, nn, :], msb[:, nn, :], AF.Square, r=[msb], w=[sq])
                if lvl < 3:
                    continue
                pss = self.bank[4 + ti % 2]
                for c in range(8):
                    self.mm(pss[:, 0:n], self.onesb, sq[:, c, :], start=(c == 0), stop=(c == 7), r=[sq, self.cstb], w=[pss])
                self.rsqrt_ps(rstd[:, 0:n], pss[:, 0:n], 1.0 / D, r=[pss], w=[rstd])
                self.tt(DVE, msb[:], msb[:], rstd[:].unsqueeze(1).to_broadcast([128, 8, 512]), ALU.mult, r=[msb, rstd], w=[msb])
                if lvl < 4:
                    continue
                for c in range(8):
                    self.stt(x[:, c, :], msb[:, c, :], m[:, 2, c, vec:vec + 1], x[:, c, :], ALU.mult, ALU.add,
                             r=[msb, m, x], w=[x])
                self.dma(SP, self.XA[:, t0:t0 + n].rearrange("(c p) t -> p c t", p=128), x[:], r=[x])
                if lvl < 5:
                    continue
                pss2 = self.bank[6 + ti % 2]
                self.sumsq_rstd(x, sq, pss2, rstd, n, src_bufs=[x])
                self.tt(DVE, msb[:], x[:], rstd[:].unsqueeze(1).to_broadcast([128, 8, 512]), ALU.mult, r=[x, rstd], w=[msb])
                if lvl < 6:
                    continue
                for c in range(8):
                    off = 0
                    for (h0, nn2) in hc:
                        if c % 2 == 0:
                            self.act(H[:, c, h0:h0 + nn2], msb[:, c, off:off + nn2], AF.Identity, r=[msb, m], w=[H],
                                     scale=m[:, 3, c, vec:vec + 1], bias=m[:, 4, c, vec:vec + 1])
                        else:
                            self.ts(POOL, H[:, c, h0:h0 + nn2], msb[:, c, off:off + nn2], m[:, 3, c, vec:vec + 1],
                                    m[:, 4, c, vec:vec + 1], ALU.mult, ALU.add, r=[msb, m], w=[H])
                        off += nn2

    def ffn_tiles(self):
        tl = []
        t = 0
        while t < TL:
            n = min(456, TL - t)
            tl.append((t, n, t))
            t += n
        tl.append((TL, TC, 4098))
        tl.append((TL + TC, TC, 4356))
        return tl

    def stage_s4a(self, l):
        H = self.Hb
        with contextlib.ExitStack() as es:
            wg = [self.sb(es, f"wg{i}", [128, 8, 128], BF16) for i in range(2)]
            wv = [self.sb(es, f"wv{i}", [128, 8, 128], BF16) for i in range(2)]
            tg = [self.sb(es, f"tg{i}", [128, 512]) for i in range(2)]
            tv = [self.sb(es, f"tv{i}", [128, 512]) for i in range(2)]
            ao = [self.sb(es, f"ao{i}", [128, 512], BF16) for i in range(3)]
            pvl = self.pv[:, l, :]
            k = 0
            for j in range(22):
                g, v = wg[j % 2], wv[j % 2]
                self.dma(POOL, g[:], self.w_up[l, :, j * 128:(j + 1) * 128].rearrange("(kc p) n -> p kc n", p=128), w=[g])
                self.dma(POOL, v[:], self.w_up[l, :, (22 + j) * 128:(23 + j) * 128].rearrange("(kc p) n -> p kc n", p=128), w=[v])
                for (t0, n, h0) in self.ffn_tiles():
                    pg = self.bank[(2 * k) % 8]
                    pv_ = self.bank[(2 * k + 1) % 8]
                    for kc in range(8):
                        self.mm(pg[:, 0:n + 2], g[:, kc, :], H[:, kc, h0:h0 + n + 2], start=(kc == 0), stop=(kc == 7), r=[g, H], w=[pg])
                    for kc in range(8):
                        self.mm(pv_[:, 0:n + 2], v[:, kc, :], H[:, kc, h0:h0 + n + 2], start=(kc == 0), stop=(kc == 7), r=[v, H], w=[pv_])
                    a, b_ = tg[k % 2], tv[k % 2]
                    for (ps, t, ch) in ((pg, a, j), (pv_, b_, 22 + j)):
                        cw = PV_FCW + 3 * ch
                        self.act(t[:, 0:n], ps[:, 0:n], AF.Identity, r=[ps, self.pv], w=[t],
                                 scale=pvl[:, cw:cw + 1], bias=pvl[:, PV_FCB + ch:PV_FCB + ch + 1])
                        self.stt(t[:, 0:n], ps[:, 1:n + 1], pvl[:, cw + 1:cw + 2], t[:, 0:n], ALU.mult, ALU.add, r=[ps, self.pv, t], w=[t])
                        self.stt(t[:, 0:n], ps[:, 2:n + 2], pvl[:, cw + 2:cw + 3], t[:, 0:n], ALU.mult, ALU.add, r=[ps, self.pv, t], w=[t])
                    self.act(a[:, 0:n], a[:, 0:n], AF.Silu, r=[a], w=[a])
                    o = ao[k % 3]
                    self.tt(POOL, o[:, 0:n], a[:, 0:n], b_[:, 0:n], ALU.mult, r=[a, b_], w=[o])
                    self.dma(SP, self.A[j * 128:(j + 1) * 128, t0:t0 + n], o[:, 0:n], r=[o])
                    k += 1

    def stage_s4b(self, l):
        with contextlib.ExitStack() as es:
            wd = self.sb(es, "wd", [128, 22, D], BF16)
            at = [self.sb(es, f"at{i}", [128, 22, 512], BF16) for i in range(2)]
            xa = [self.sb(es, f"xa{i}", [128, 8, 512]) for i in range(2)]
            fsb = self.sb(es, "fsb", [128, 8, 512])
            sq = self.sb(es, "sq", [128, 8, 512], BF16)
            rstd = self.sb(es, "rstd", [128, 512])
            for h in range(2):
                self.dma(POOL, wd[:, h * 11:(h + 1) * 11, :],
                         self.w_down[l, h * 11 * 128:(h + 1) * 11 * 128, :].rearrange("(j p) n -> p j n", p=128), w=[wd])
            m = self.mod
            for ti, (t0, n, vec, hc) in enumerate(self.s1_tiles()):
                a = at[ti % 2]
                x = xa[ti % 2]
                self.dma(SP, a[:], self.A[:, t0:t0 + n].rearrange("(j p) t -> p j t", p=128), w=[a])
                self.dma(SP, x[:], self.XA[:, t0:t0 + n].rearrange("(c p) t -> p c t", p=128), w=[x])
                for nn in range(8):
                    ps = self.bank[nn % 4]
                    for j in range(22):
                        self.mm(ps[:, 0:n], wd[:, j, nn * 128:(nn + 1) * 128], a[:, j, :], start=(j == 0), stop=(j == 21), r=[wd, a], w=[ps])
                    self.cp(DVE, fsb[:, nn, :], ps[:, 0:n], r=[ps], w=[fsb])
                    self.act(sq[:, nn, :], fsb[:, nn, :], AF.Square, r=[fsb], w=[sq])
                pss = self.bank[4 + ti % 2]
                for c in range(8):
                    self.mm(pss[:, 0:n], self.onesb, sq[:, c, :], start=(c == 0), stop=(c == 7), r=[sq, self.cstb], w=[pss])
                self.rsqrt_ps(rstd[:, 0:n], pss[:, 0:n], 1.0 / D, r=[pss], w=[rstd])
                self.tt(DVE, fsb[:], fsb[:], rstd[:].unsqueeze(1).to_broadcast([128, 8, 512]), ALU.mult, r=[fsb, rstd], w=[fsb])
                for c in range(8):
                    self.stt(x[:, c, :], fsb[:, c, :], m[:, 5, c, vec:vec + 1], x[:, c, :], ALU.mult, ALU.add, r=[fsb, m, x], w=[x])
                self.dma(SP, self.xout[:, t0:t0 + n].rearrange("(c p) t -> p c t", p=128), x[:], r=[x])

    def stage_conf(self, l):
        pvl = self.pv[:, l, :]
        with contextlib.ExitStack() as es:
            av = [self.sb(es, f"cav{c}", [128, TL]) for c in range(2)]
            ag = [self.sb(es, f"cag{c}", [128, TL]) for c in range(2)]
            U = [self.sb(es, f"cU{c}", [128, TL + 30], BF16) for c in range(2)]
            DGc = [self.sb(es, f"cDG{c}", [128, 31, 128], BF16) for c in range(2)]
            for c in range(2):
                for kk in range(31):
                    col = PV_CW + 31 * c + kk
                    if c == 0:
                        self.ts(POOL, DGc[c][:, kk, :], self.ident, pvl[:, col:col + 1], None, ALU.mult, r=[self.cst, self.pv], w=[DGc[c]])
                    else:
                        self.act(DGc[c][:, kk, :], self.ident, AF.Copy, r=[self.cst, self.pv], w=[DGc[c]], scale=pvl[:, col:col + 1])
            kconv = 0
            sq = [self.sb(es, f"csq{c}", [128, 512]) for c in range(2)]
            st = self.sb(es, "cst", [128, 3, 512])
            yb = [self.sb(es, f"cy{c}", [128, 512]) for c in range(2)]
            ob = [self.sb(es, f"cob{i}", [128, 512], BF16) for i in range(4)]
            k = 0
            for (tok0, T, hcol0, vec) in SEQS:
                for c in range(2):
                    self.dma(SP, av[c][:, 0:T], self.P[c * 128:(c + 1) * 128, tok0:tok0 + T], w=[av[c]])
                    self.dma(SP, ag[c][:, 0:T], self.P[(2 + c) * 128:(3 + c) * 128, tok0:tok0 + T], w=[ag[c]])
                    self.memset(POOL, U[c][:, 0:15], 0.0, w=[U[c]])
                    self.memset(POOL, U[c][:, 15 + T:30 + T], 0.0, w=[U[c]])
                    self.act(ag[c][:, 0:T], ag[c][:, 0:T], AF.Sigmoid, r=[ag[c]], w=[ag[c]])
                    self.tt(POOL, U[c][:, 15:15 + T], av[c][:, 0:T], ag[c][:, 0:T], ALU.mult, r=[av[c], ag[c]], w=[U[c]])
                    acc = av[c]
                    for t in range(0, T, 512):
                        n = min(512, T - t)
                        psb = self.bank[6 + kconv % 2]
                        kconv += 1
                        for kk in range(31):
                            self.mm(psb[:, 0:n], DGc[c][:, kk, :], U[c][:, t + kk:t + kk + n], start=(kk == 0), stop=(kk == 30),
                                    r=[DGc[c], U[c]], w=[psb])
                        self.act(acc[:, t:t + n], psb[:, 0:n], AF.Identity, r=[psb, self.pv], w=[acc],
                                 bias=pvl[:, PV_CB + c:PV_CB + c + 1], scale=1.0)
                for t in range(0, T, 512):
                    n = min(512, T - t)
                    p1 = self.bank[(2 * k) % 6]
                    p2 = self.bank[(2 * k + 1) % 6]
                    for c in range(2):
                        self.act(sq[c][:, 0:n], av[c][:, t:t + n], AF.Square, r=[av[c]], w=[sq[c]])
                    for c in range(2):
                        self.mm(p1[:, 0:n], self.ones, av[c][:, t:t + n], start=(c == 0), stop=(c == 1), r=[self.cst, av[c]], w=[p1])
                    for c in range(2):
                        self.mm(p2[:, 0:n], self.ones, sq[c][:, 0:n], start=(c == 0), stop=(c == 1), r=[self.cst, sq[c]], w=[p2])
                    self.ts(DVE, st[:, 0, 0:n], p1[:, 0:n], 1.0 / 256, None, ALU.mult, r=[p1], w=[st])
                    self.tt(DVE, st[:, 1, 0:n], st[:, 0, 0:n], st[:, 0, 0:n], ALU.mult, r=[st], w=[st])
                    self.stt(st[:, 1, 0:n], p2[:, 0:n], 1.0 / 256, st[:, 1, 0:n], ALU.mult, ALU.subtract, r=[p2, st], w=[st])
                    self.ts(DVE, st[:, 1, 0:n], st[:, 1, 0:n], 0.0, None, ALU.max, r=[st], w=[st])
                    self.act(st[:, 2, 0:n], st[:, 1, 0:n], AF.Sqrt, r=[st, self.cst], w=[st], bias=self.c_eps, scale=1.0)
                    self.recip(st[:, 2, 0:n], st[:, 2, 0:n], r=[st], w=[st])
                    for c in range(2):
                        self.tt(DVE, yb[c][:, 0:n], av[c][:, t:t + n], st[:, 0, 0:n], ALU.subtract, r=[av[c], st], w=[yb[c]])
                        self.tt(POOL, yb[c][:, 0:n], yb[c][:, 0:n], st[:, 2, 0:n], ALU.mult, r=[yb[c], st], w=[yb[c]])
                        o = ob[(2 * k + c) % 4]
                        self.act(o[:, 0:n], yb[c][:, 0:n], AF.Silu, r=[yb[c], self.pv], w=[o],
                                 scale=pvl[:, PV_CLG + c:PV_CLG + c + 1], bias=pvl[:, PV_CLB + c:PV_CLB + c + 1])
                        self.dma(SP, self.CAT[c * 128:(c + 1) * 128, tok0 + t:tok0 + t + n], o[:, 0:n], r=[o])
                    k += 1

    def attn_unit(self, slot, qf, hp, b0, t0, pieces, vts, otm, h, W):
        nk = sum(p[3] for p in pieces)
        nkt = nk // 128
        base = slot * 1024
        ps = self.pst[:, base:base + nk]
        pb = [self.bank[2 * slot], self.bank[2 * slot + 1]]
        lhsT = qf[b0:b0 + 64, hp, t0:t0 + 128]
        for (col, rhs, bm, n) in pieces:
            self.mm(self.pst[:, base + col:base + col + n], lhsT, rhs, start=True, stop=(bm is None), r=W["qk"], w=pb)
            if bm is not None:
                self.mm(self.pst[:, base + col:base + col + n], self.jrevb, bm, start=False, stop=True, r=W["bm"], w=pb)
        sm = W["sm"][slot * 2 + h % 2]
        pexp = W["pexp"][slot]
        pts = W["pts"][slot]
        yield
        self.S.op(DVE, lambda e: e.reduce_max(out=sm[:, 0:1], in_=ps, axis=AX.X), pb, [sm])
        yield
        self.ts(DVE, sm[:, 1:2], sm[:, 0:1], -0.125, None, ALU.mult, r=[sm], w=[sm])
        self.memset(DVE, sm[:, 2:3], 0.0, w=[sm])
        yield
        self.act(pexp[:, 0:nk], ps, AF.Exp, r=pb + [sm], w=[pexp, sm], scale=0.125, bias=sm[:, 1:2], accum_out=sm[:, 2:3])
        yield
        ptbank = self.bank[4 + slot]
        ptb = ptbank.t.bitcast(BF16)
        for c in range(nkt):
            self.tr(ptb[:, c * 128:(c + 1) * 128], pexp[:, c * 128:(c + 1) * 128], self.identb, r=[pexp, self.cstb], w=[ptbank])
        yield
        self.cp(ACT if slot == 0 else DVE, pts[:, 0:nkt, :].rearrange("p a b -> p (a b)"), ptb[:, 0:nk], r=[ptbank], w=[pts])
        self.recip(sm[:, 3:4], sm[:, 2:3], r=[sm], w=[sm])
        yield
        po = self.bank[6]
        for c in range(nkt):
            self.mm(po[:, 0:64], pts[:, c, :], vts[c], start=(c == 0), stop=(c == nkt - 1), r=[pts] + W["v"], w=[po])
        self.ts(DVE, otm[:, h * 64:(h + 1) * 64], po[:, 0:64], sm[:, 3:4], None, ALU.mult, r=[po, sm], w=[otm])
        yield

    def stage_attn(self, l):
        with contextlib.ExitStack() as es:
            qf = self.sb(es, "aqf", [128, 2, TT], BF16)
            kf = self.sb(es, "akf", [128, 2, TT], BF16)
            vf = self.sb(es, "avf", [128, 2, TT], BF16)
            VT = self.sb(es, "aVT", [128, 36, 256], BF16)
            kctx = self.sb(es, "akctx", [128, 2, TC], BF16)
            vctx = self.sb(es, "avctx", [128, 2, 256], BF16)
            HK = self.sb(es, "aHK", [128, 4, 15, 64])
            BMc = [self.sb(es, f"aBM{c}", [128, 4, 640], BF16) for c in range(5)]
            FT = self.sb(es, "aFT", [64, 127])
            r31 = self.sb(es, "ar31", [64, 31])
            ocr = self.sb(es, "aocr", [128, 2, TT], BF16)
            otms = [self.sb(es, f"aotm{i}", [128, 256], BF16) for i in range(2)]
            W = dict(sm=[self.sb(es, f"asm{i}", [128, 4]) for i in range(4)],
                     pexp=[self.sb(es, f"apexp{i}", [128, 896], BF16) for i in range(2)],
                     pts=[self.sb(es, f"apts{i}", [128, 7, 128], BF16) for i in range(2)])
            fpb = Buf("FPb", None)
            for hp in range(2):
                self.dma(SP, qf[:, hp, :], self.PB[hp * 128:(hp + 1) * 128, :], w=[qf])
                self.dma(SP, kf[:, hp, :], self.PB[(2 + hp) * 128:(3 + hp) * 128, :], w=[kf])
                self.dma(SP, vf[:, hp, :], self.PB[(4 + hp) * 128:(5 + hp) * 128, :], w=[vf])
            self.dma(POOL, kctx[:], self.kcache[l].rearrange("a p t -> p a t"), w=[kctx])
            self.dma(POOL, vctx[:], self.vcache[l].rearrange("a p f -> p a f"), w=[vctx])
            for g in range(9):
                bk = self.bank[g % 2]
                bkb = bk.t.bitcast(BF16)
                for jj in range(4):
                    tile = g * 4 + jj
                    for hp in range(2):
                        self.tr(bkb[:, (jj * 2 + hp) * 128:(jj * 2 + hp + 1) * 128], vf[:, hp, tile * 128:(tile + 1) * 128],
                                self.identb, r=[vf, self.cstb], w=[bk])
                self.cp(DVE if g % 2 == 0 else ACT, VT[:, g * 4:(g + 1) * 4, :].rearrange("p a f -> p (a f)"), bkb[:, 0:1024], r=[bk], w=[VT])
            self.memset(POOL, FT[0:60, :], NEG, w=[FT])
            self.dma(SP, r31[0:60, :], self.rpb[l], w=[r31])
            self.act(FT[0:60, 48:79], r31[0:60, :], AF.Copy, r=[r31, FT], w=[FT], scale=8.0)
            self.dma(SP, self.FP[l], FT[0:60, :], r=[FT], w=[fpb])
            for half in range(2):
                src = bass.AP(tensor=self.FP.tensor, offset=l * 60 * 127, ap=[[1, 64], [127, 60], [1, 64]])
                self.dma(SP, HK[64 * half:64 * half + 64].rearrange("p h d k -> p (h d) k"), src, r=[fpb], w=[HK])
            self.tt(DVE, HK[:].rearrange("p h d k -> p (h d) k"), HK[:].rearrange("p h d k -> p (h d) k"),
                    self.cst[:, C_CMH, 0:64].unsqueeze(1).to_broadcast([128, 60, 64]), ALU.add, r=[HK, self.cst], w=[HK])
            engs = (POOL, DVE, ACT, POOL, DVE)
            for case, i in ((0, 0), (1, 1), (2, 2), (3, 30), (4, 31)):
                bm = BMc[case]
                eng = engs[case]
                self.memset(eng if eng != ACT else POOL, bm[:], NEG, w=[bm])
                start = min(max(2 * i - 4, 0), 54)
                for qr in range(2):
                    r_ = 2 * i + qr
                    rs = min(max(r_ - 4, 0), 56)
                    for kr in range(10):
                        krow = start + kr
                        if rs <= krow < rs + 8:
                            dr = krow - r_ + 7
                            self.cp(eng, bm[64 * qr:64 * qr + 64, :, kr * 64:(kr + 1) * 64], HK[64 * qr:64 * qr + 64, :, dr, :], r=[HK], w=[bm])
            W["qk"] = [qf, kf, kctx]
            W["v"] = [VT, vctx]
            ob7 = self.bank[7].t.bitcast(BF16)

            def qtile(t0, units):
                def gen(slot):
                    otm = otms[slot]
                    for (hp, b0, pieces, vts, h, bmb) in units:
                        W["bm"] = [bmb, self.cstb] if bmb is not None else [self.cstb]
                        yield from self.attn_unit(slot, qf, hp, b0, t0, pieces, vts, otm, h, W)
                    ob = self.bank[7]
                    for hp in range(2):
                        self.tr(ob7[:, hp * 128:(hp + 1) * 128], otm[:, hp * 128:(hp + 1) * 128], self.identb, r=[otm, self.cstb], w=[ob])
                    self.cp(ACT if slot == 0 else DVE, ocr[:, :, t0:t0 + 128], ob7[:, 0:256].rearrange("p (a b) -> p a b", a=2), r=[ob], w=[ocr])
                    yield
                return gen

            jobs = []
            for i in range(32):
                t0 = 128 * i
                start = min(max(2 * i - 4, 0), 54)
                k0 = 64 * start
                case = {0: 0, 1: 1, 30: 3, 31: 4}.get(i, 2)
                units = []
                for h in range(4):
                    hp, b0 = h // 2, 64 * (h % 2)
                    bm = BMc[case]
                    pieces = [(0, kf[b0:b0 + 64, hp, k0:k0 + 512], bm[:, h, 0:512], 512),
                              (512, kf[b0:b0 + 64, hp, k0 + 512:k0 + 640], bm[:, h, 512:640], 128),
                              (640, kctx[b0:b0 + 64, hp, :], None, 256)]
                    vts = [VT[:, start // 2 + c, h * 64:(h + 1) * 64] for c in range(5)] + [vctx[:, c, h * 64:(h + 1) * 64] for c in range(2)]
                    units.append((hp, b0, pieces, vts, h, bm))
                jobs.append(qtile(t0, units))
            for s_ in range(2):
                tok0 = TL + s_ * TC
                for qi in range(2):
                    t0 = tok0 + 128 * qi
                    units = []
                    for h in range(4):
                        hp, b0 = h // 2, 64 * (h % 2)
                        pieces = [(0, kf[b0:b0 + 64, hp, tok0:tok0 + 256], None, 256)]
                        vts = [VT[:, tok0 // 128 + c, h * 64:(h + 1) * 64] for c in range(2)]
                        units.append((hp, b0, pieces, vts, h, None))
                    jobs.append(qtile(t0, units))
            self.run_pool(jobs, 2)
            for hp in range(2):
                self.dma(SP, self.CAT[768 + hp * 128:768 + (hp + 1) * 128, :], ocr[:, hp, :], r=[ocr])

    @staticmethod
    def run_pool(jobs, nslots):
        jobs = iter(jobs)
        active = {}
        free = list(range(nslots))
        while True:
            while free:
                jb = next(jobs, None)
                if jb is None:
                    break
                sl = free.pop(0)
                active[sl] = jb(sl)
            if not active:
                break
            for sl in list(active):
                try:
                    next(active[sl])
                except StopIteration:
                    del active[sl]
                    free.append(sl)

    def stage_gdn(self, l):
        pvl = self.pv[:, l, :]
        bank = self.bank
        S = self.S
        with contextlib.ExitStack() as es:
            sb = lambda name, shape, dt=F32: self.sb(es, "g" + name, shape, dt)
            qf, kf, vf = sb("qf", [128, 4, TL], BF16), sb("kf", [128, 4, TL], BF16), sb("vf", [128, 4, TL], BF16)
            NPM = TL // 128
            GT = sb("GT", [128, NPM, 16])
            beta, nbeta, g, GC, kds, negc = (sb(n, [128, NPM, 8]) for n in ("beta", "nbeta", "g", "GC", "kds", "negc"))
            t1, t2 = sb("t1", [128, NPM, 8]), sb("t2", [128, NPM, 8])
            egl = sb("egl", [128, NPM, 2, 8])
            gsb = sb("gsb", [128, 16])
            nega = sb("nega", [128, 8])
            ident, identb, ones = self.ident, self.identb, self.ones
            cst = self.cst
            one_ap = self.cst[:, C_EPS, 1:2]
            X, Y, Z, Wk, V, A = bank[0], bank[1], bank[2], bank[3], bank[4], bank[5]
            B2 = [bank[6], bank[7]]
            Bap = self.pst[:, 6 * 512:8 * 512].rearrange("p (h a k) -> p h a k", h=4, a=2)
            Xb = X.t.bitcast(BF16)
            def v3(b):
                return b.t.rearrange("p (h k) -> p h k", h=4)
            def bc_h(ap2):
                return ap2.unsqueeze(1).to_broadcast([128, 4, 128])
            def bc_i(ap2):
                return ap2.unsqueeze(2).to_broadcast([128, 4, 128])
            self.dma(SP, gsb[:], self.gsc[l:l + 1, :].partition_broadcast(128), w=[gsb])
            self.act(nega[:], gsb[:, 0:8], AF.Exp, r=[gsb], w=[nega])
            self.ts(DVE, nega[:], nega[:], -1.0, None, ALU.mult, r=[nega], w=[nega])
            for si, (tok0, T, hcol0, vec) in enumerate(SEQS):
                NP_ = T // 128
                with contextlib.ExitStack() as es2:
                    sb2 = lambda name, shape, dt=F32: self.sb(es2, "g" + name, shape, dt)
                    raws = [sb2(f"raw{i}", [128, 2050]) for i in range(2)]
                    cvs = [sb2(f"cv{i}", [128, 2048]) for i in range(2)]
                    sqs = [sb2(f"sq{i}", [128, 512], BF16) for i in range(2)]
                    rss = [sb2(f"rs{i}", [128, 512]) for i in range(2)]
                    bd = sb2("bd", [16, 512])
                    ui = 0
                    tj = 0
                    for ti, (dst, pc0) in enumerate(((qf, 4), (kf, 8), (vf, 12))):
                        for h in range(4):
                            row0 = (pc0 + h) * 128
                            cw = PV_GCW + 3 * (ti * 4 + h)
                            for half in range(0, T, 2048):
                                n = min(2048, T - half)
                                raw, cv = raws[ui % 2], cvs[ui % 2]
                                ui += 1
                                a = max(half - 1, 0)
                                b_ = min(half + n + 1, T)
                                off = a - (half - 1)
                                if off > 0:
                                    self.memset(POOL, raw[:, 0:1], 0.0, w=[raw])
                                if b_ - (half - 1) < n + 2:
                                    self.memset(POOL, raw[:, n + 1:n + 2], 0.0, w=[raw])
                                self.dma(SP, raw[:, off:off + (b_ - a)], self.P[row0:row0 + 128, tok0 + a:tok0 + b_], w=[raw])
                                self.act(cv[:, 0:n], raw[:, 0:n], AF.Copy, r=[raw, self.pv], w=[cv], scale=pvl[:, cw:cw + 1])
                                self.stt(cv[:, 0:n], raw[:, 1:n + 1], pvl[:, cw + 1:cw + 2], cv[:, 0:n], ALU.mult, ALU.add, r=[raw, self.pv, cv], w=[cv])
                                self.stt(cv[:, 0:n], raw[:, 2:n + 2], pvl[:, cw + 2:cw + 3], cv[:, 0:n], ALU.mult, ALU.add, r=[raw, self.pv, cv], w=[cv])
                                self.act(cv[:, 0:n], cv[:, 0:n], AF.Silu, r=[cv], w=[cv])
                                if ti == 2:
                                    self.cp(POOL, dst[:, h, half:half + n], cv[:, 0:n], r=[cv], w=[dst])
                                else:
                                    for t in range(0, n, 512):
                                        m = min(512, n - t)
                                        sq, rs = sqs[tj % 2], rss[tj % 2]
                                        tj += 1
                                        ps = bank[tj % 4]
                                        self.act(sq[:, 0:m], cv[:, t:t + m], AF.Square, r=[cv], w=[sq])
                                        self.mm(ps[:, 0:m], self.onesb, sq[:, 0:m], r=[sq, self.cstb], w=[ps])
                                        self.rsqrt_ps(rs[:, 0:m], ps[:, 0:m], 1.0, r=[ps], w=[rs])
                                        if ti == 0:
                                            self.stt(dst[:, h, half + t:half + t + m], cv[:, t:t + m], 128.0 ** -0.5, rs[:, 0:m], ALU.mult, ALU.mult, r=[cv, rs], w=[dst])
                                        else:
                                            self.tt(DVE, dst[:, h, half + t:half + t + m], cv[:, t:t + m], rs[:, 0:m], ALU.mult, r=[cv, rs], w=[dst])
                    for jb in range(0, T, 512):
                        n = min(512, T - jb)
                        self.dma(SP, bd[:, 0:n], self.P[26 * 128:26 * 128 + 16, tok0 + jb:tok0 + jb + n], w=[bd])
                        for j in range(n // 128):
                            self.tr(X[:, j * 16:(j + 1) * 16], bd[0:16, j * 128:(j + 1) * 128], ident[0:16, 0:16], r=[bd, cst], w=[X])
                        self.cp(DVE, GT[:, jb // 128:jb // 128 + n // 128, :].rearrange("p a b -> p (a b)"), X[:, 0:(n // 128) * 16], r=[X], w=[GT])
                    NB = NP_ * 8
                    def f2(t):
                        return t[:, 0:NP_, :].rearrange("p a b -> p (a b)")
                    dtb = gsb[:, 8:16].unsqueeze(1).to_broadcast([128, NP_, 8])
                    self.act(beta[:, 0:NP_, :], GT[:, 0:NP_, 0:8], AF.Sigmoid, r=[GT], w=[beta])
                    self.ts(DVE, nbeta[:, 0:NP_, :], beta[:, 0:NP_, :], -1.0, None, ALU.mult, r=[beta], w=[nbeta])
                    self.tt(DVE, t1[:, 0:NP_, :], GT[:, 0:NP_, 8:16], dtb, ALU.add, r=[GT, gsb], w=[t1])
                    self.act(t2[:, 0:NP_, :], t1[:, 0:NP_, :], AF.Abs, r=[t1], w=[t2])
                    self.act(t2[:, 0:NP_, :], t2[:, 0:NP_, :], AF.Exp, r=[t2], w=[t2], scale=-1.0)
                    self.act(t2[:, 0:NP_, :], t2[:, 0:NP_, :], AF.Ln, r=[t2, cst], w=[t2], bias=one_ap, scale=1.0)
                    self.ts(DVE, t1[:, 0:NP_, :], t1[:, 0:NP_, :], 0.0, None, ALU.max, r=[t1], w=[t1])
                    self.tt(DVE, t1[:, 0:NP_, :], t1[:, 0:NP_, :], t2[:, 0:NP_, :], ALU.add, r=[t1, t2], w=[t1])
                    self.tt(DVE, g[:, 0:NP_, :], t1[:, 0:NP_, :], nega[:].unsqueeze(1).to_broadcast([128, NP_, 8]), ALU.mult, r=[t1, nega], w=[g])
                    g2 = f2(g)
                    self.mm(X[:, 0:NB], cst[:, C_TRIF, :], g2, r=[cst, g], w=[X])
                    self.mm(Y[:, 0:NB], cst[:, C_TRIB, :], g2, r=[cst, g], w=[Y])
                    self.mm(Z[:, 0:NB], cst[:, C_BLK, :], g2, r=[cst, g], w=[Z])
                    self.mm(Wk[:, 0:NB], cst[:, C_HALFA, :], g2, r=[cst, g], w=[Wk])
                    self.mm(V[:, 0:NB], cst[:, C_HALFB, :], g2, r=[cst, g], w=[V])
                    self.cp(DVE, GC[:, 0:NP_, 0:4], X[:, 0:NB].rearrange("p (a b) -> p a b", b=8)[:, :, 0:4], r=[X], w=[GC])
                    self.cp(DVE, GC[:, 0:NP_, 4:8], Y[:, 0:NB].rearrange("p (a b) -> p a b", b=8)[:, :, 4:8], r=[Y], w=[GC])
                    self.tt(DVE, f2(kds), Z[:, 0:NB], f2(GC), ALU.subtract, r=[Z, GC], w=[kds])
                    self.act(f2(kds), f2(kds), AF.Exp, r=[kds], w=[kds])
                    self.act(f2(t1), f2(GC), AF.Exp, r=[GC], w=[t1])
                    self.tt(DVE, f2(negc), f2(nbeta), f2(t1), ALU.mult, r=[nbeta, t1], w=[negc])
                    self.act(egl[:, 0:NP_, 0, :], Wk[:, 0:NB].rearrange("p (a b) -> p a b", b=8), AF.Exp, r=[Wk], w=[egl])
                    self.act(egl[:, 0:NP_, 1, :], V[:, 0:NB].rearrange("p (a b) -> p a b", b=8), AF.Exp, r=[V], w=[egl])
                S.barrier()
                with contextlib.ExitStack() as es3:
                    def mkset(d):
                        sb3 = lambda name, shape, dt=F32: self.sb(es3, f"g{d}" + name, shape, dt)
                        outs = [dict(Kd=sb3(f"Kd{i}", [128, 4, 128], BF16), Vb=sb3(f"Vb{i}", [128, 4, 128]), ITm=sb3(f"ITm{i}", [128, 4, 128], BF16),
                                     Yb=sb3(f"Yb{i}", [128, 4, 128], BF16), Qg=sb3(f"Qg{i}", [128, 4, 128], BF16)) for i in range(2)]
                        return dict(out=outs, GR=sb3("GR", [128, 4, 128]), D1=sb3("D1", [128, 4, 128]), D2=sb3("D2", [128, 4, 128]),
                                    Pm=sb3("Pm", [128, 4, 128]), PY=sb3("PY", [128, 4, 2, 128]),
                                    Rb=sb3("Rb", [128, 4, 128], BF16), VNb=sb3("VNb", [128, 4, 128], BF16), S=sb3("S", [128, 4, 128]),
                                    Sb=sb3("Sb", [128, 4, 128], BF16), OT=sb3("OT", [128, 4, 128]))
                    sets = [mkset(0), mkset(1)]
                    ofb = {}
                    done_pre = [0, 0]
                    done_scan = [0, 0]

                    def pre_chain(d):
                        def gen(slot):
                            W = sets[d]
                            GR, D1, D2, Pm, PY = (W[k] for k in ("GR", "D1", "D2", "Pm", "PY"))
                            DG = D1
                            m1c, m2c = (C_UINC, C_LSTR) if d == 0 else (C_LINC, C_USTR)
                            c0, c1, c2 = (bank[4 * d + q] for q in range(3))
                            X, Y, Z, Wk, V, A = c0, c1, c2, c0, c0, c2
                            B2 = [c0, c1]
                            Bap = self.pst[:, 4 * d * 512:(4 * d + 2) * 512].rearrange("p (h a k) -> p h a k", h=4, a=2)
                            Xb = X.t.bitcast(BF16)
                            order = list(range(NP_)) if d == 0 else list(range(NP_ - 1, -1, -1))
                            for idx, p in enumerate(order):
                                while idx - done_scan[d] >= 2:
                                    yield
                                O_ = W["out"][idx % 2]
                                Kd, Vb, ITm, Yb, Qg = (O_[k] for k in ("Kd", "Vb", "ITm", "Yb", "Qg"))
                                t0 = 128 * p
                                gcol = GC[:, p, 4 * d:4 * d + 4]
                                for h in range(4):
                                    self.tr(Xb[:, h * 128:(h + 1) * 128], kf[:, h, t0:t0 + 128], identb, r=[kf, self.cstb], w=[X])
                                    self.tr(Xb[:, (4 + h) * 128:(5 + h) * 128], vf[:, h, t0:t0 + 128], identb, r=[vf, self.cstb], w=[X])
                                self.tt(DVE, DG[:], bc_h(ident), bc_i(gcol), ALU.mult, r=[cst, GC], w=[DG])
                                for h in range(4):
                                    self.mm(Y[:, h * 128:(h + 1) * 128], kf[:, h, t0:t0 + 128], kf[:, h, t0:t0 + 128], r=[kf], w=[Y])
                                    self.mm(Z[:, h * 128:(h + 1) * 128], kf[:, h, t0:t0 + 128], qf[:, h, t0:t0 + 128], r=[kf, qf], w=[Z])
                                yield
                                self.tt(DVE, Kd[:], Xb[:, 0:512].rearrange("p (h k) -> p h k", h=4), bc_i(kds[:, p, 4 * d:4 * d + 4]), ALU.mult, r=[X, kds], w=[Kd])
                                self.tt(DVE, Vb[:], Xb[:, 512:1024].rearrange("p (h k) -> p h k", h=4), bc_i(beta[:, p, 4 * d:4 * d + 4]), ALU.mult, r=[X, beta], w=[Vb])
                                yield
                                self.mm(Wk[:, 0:512], ones, DG[:].rearrange("p h k -> p (h k)"), r=[cst, DG], w=[Wk])
                                yield
                                self.cp(ACT, GR[:].rearrange("p h k -> p (h k)"), Wk[:, 0:512], r=[Wk], w=[GR])
                                yield
                                self.tt(DVE, D1[:], GR[:], bc_i(gcol), ALU.subtract, r=[GR, GC], w=[D1])
                                self.tt(DVE, D2[:], bc_i(gcol), GR[:], ALU.subtract, r=[GR, GC], w=[D2])
                                yield
                                self.act(GR[:], GR[:], AF.Exp, r=[GR], w=[GR])
                                self.tt(POOL, D1[:], D1[:], bc_h(cst[:, m1c, :]), ALU.add, r=[D1, cst], w=[D1])
                                self.tt(POOL, D2[:], D2[:], bc_h(cst[:, m2c, :]), ALU.add, r=[D2, cst], w=[D2])
                                yield
                                self.tt(POOL, Qg[:], qf[:, :, t0:t0 + 128], GR[:], ALU.mult, r=[qf, GR], w=[Qg])
                                self.act(D2[:], D2[:], AF.Exp, r=[D2], w=[D2])
                                self.act(D1[:], D1[:], AF.Exp, r=[D1], w=[D1])
                                yield
                                for h in range(4):
                                    self.stt(Pm[:, h, :], Y[:, h * 128:(h + 1) * 128], nbeta[:, p, 4 * d + h:4 * d + h + 1], D2[:, h, :], ALU.mult, ALU.mult,
                                             r=[Y, nbeta, D2], w=[Pm])
                                self.tt(DVE, ITm[:], v3(Z), D1[:], ALU.mult, r=[Z, D1], w=[ITm])
                                yield
                                for h in range(4):
                                    self.tr(V[:, h * 128:(h + 1) * 128], Pm[:, h, :], ident, r=[Pm, cst], w=[V])
                                yield
                                self.cp(ACT, PY[:, :, 0, :], v3(V), r=[V], w=[PY])
                                yield
                                self.tt(POOL, PY[:, :, 1, :], PY[:, :, 0, :], bc_h(ident), ALU.add, r=[PY, cst], w=[PY])
                                for h in range(4):
                                    self.mm(A[:, h * 128:(h + 1) * 128], PY[:, h, 0, :], Pm[:, h, :], r=[PY, Pm], w=[A])
                                    self.mm(Bap[:, h, 0, :], Pm[:, h, :], PY[:, h, 0, :], r=[PY, Pm], w=B2)
                                yield
                                self.cp(ACT, Pm[:], v3(A), r=[A], w=[Pm])
                                self.cp(DVE, PY[:, :, 0, :], Bap[:, :, 0, :], r=B2, w=[PY])
                                yield
                                for k in range(1, 5):
                                    for h in range(4):
                                        self.mm(A[:, h * 128:(h + 1) * 128], PY[:, h, 0, :], Pm[:, h, :], r=[PY, Pm], w=[A])
                                        self.mm(Bap[:, h, :, :].rearrange("p a k -> p (a k)"), Pm[:, h, :], PY[:, h, :, :].rearrange("p a k -> p (a k)"), r=[PY, Pm], w=B2)
                                    yield
                                    self.cp(ACT, Pm[:], v3(A), r=[A], w=[Pm])
                                    self.cp(DVE, PY[:, :, 0, :], Bap[:, :, 0, :], r=B2, w=[PY])
                                    self.tt(DVE, PY[:, :, 1, :], PY[:, :, 1, :], Bap[:, :, 1, :], ALU.add, r=B2 + [PY], w=[PY])
                                    yield
                                for h in range(4):
                                    self.mm(A[:, h * 128:(h + 1) * 128], Pm[:, h, :], PY[:, h, 1, :], r=[PY, Pm], w=[A])
                                yield
                                self.tt(DVE, Yb[:], PY[:, :, 1, :], v3(A), ALU.add, r=[A, PY], w=[Yb])
                                done_pre[d] = idx + 1
                                yield
                        return gen

                    def scan_chain(d):
                        def gen(slot):
                            W = sets[d]
                            Rb, VNb, Sst, Sb, OT = (W[k] for k in ("Rb", "VNb", "S", "Sb", "OT"))
                            OFd = self.OF if d == 0 else self.OFB
                            C3 = bank[4 * d + 3]
                            self.memset(POOL, Rb[:], 0.0, w=[Rb])
                            self.memset(POOL, VNb[:], 0.0, w=[VNb])
                            if si == 0:
                                self.dma(SP, Sst[:], self.s0[l, :, 4 * d:4 * d + 4, :], w=[Sst])
                            else:
                                self.memset(POOL, Sst[:], 0.0, w=[Sst])
                            self.cp(POOL, Sb[:], Sst[:], r=[Sst], w=[Sb])
                            yield
                            order = list(range(NP_)) if d == 0 else list(range(NP_ - 1, -1, -1))
                            for idx, p in enumerate(order):
                                while done_pre[d] <= idx:
                                    yield
                                O_ = W["out"][idx % 2]
                                Kd, Vb, ITm, Yb, Qg = (O_[k] for k in ("Kd", "Vb", "ITm", "Yb", "Qg"))
                                t0 = 128 * p
                                for hf in ((0, 1) if d == 0 else (1, 0)):
                                    R = slice(64 * hf, 64 * hf + 64)
                                    cs = slice(64 * hf, 64 * hf + 64)
                                    for h in range(4):
                                        self.mm(C3[:, h * 128:(h + 1) * 128], kf[:, h, t0:t0 + 128], Sb[:, h, :], r=[kf, Sb], w=[C3])
                                    yield
                                    for h in range(4):
                                        self.stt(Rb[R, h, :], C3[R, h * 128:(h + 1) * 128], negc[R, p, 4 * d + h:4 * d + h + 1], Vb[R, h, :], ALU.mult, ALU.add,
                                                 r=[C3, negc, Vb], w=[Rb])
                                    yield
                                    for h in range(4):
                                        self.mm(C3[:, h * 128:(h + 1) * 128], Yb[:, h, :], Rb[:, h, :], r=[Yb, Rb], w=[C3])
                                    yield
                                    self.cp(ACT, VNb[R, :, :].rearrange("p h k -> p (h k)"), C3[R, 0:512], r=[C3], w=[VNb])
                                    yield
                                    for h in range(4):
                                        self.mm(C3[:, h * 128:(h + 1) * 128], Kd[R, h, :], VNb[R, h, :], r=[Kd, VNb], w=[C3])
                                    yield
                                    for h in range(4):
                                        self.stt(Sst[:, h, :], Sst[:, h, :], egl[:, p, hf, 4 * d + h:4 * d + h + 1], C3[:, h * 128:(h + 1) * 128], ALU.mult, ALU.add,
                                                 r=[Sst, egl, C3], w=[Sst])
                                    yield
                                    for h in range(4):
                                        self.mm(C3[:, h * 64:(h + 1) * 64], Sb[:, h, :], Qg[:, h, cs], start=True, stop=False, r=[Sb, Qg], w=[C3])
                                        self.mm(C3[:, h * 64:(h + 1) * 64], VNb[R, h, :], ITm[R, h, cs], start=False, stop=True, r=[VNb, ITm], w=[C3])
                                    self.cp(POOL, Sb[:], Sst[:], r=[Sst], w=[Sb])
                                    yield
                                    self.cp(ACT, OT[:, :, cs], C3[:, 0:256].rearrange("p (h k) -> p h k", h=4), r=[C3], w=[OT])
                                    yield
                                ofb[(d, p)] = Buf(f"of{d}_{p}", None)
                                self.dma(SP, OFd[:, tok0 + t0:tok0 + t0 + 128].rearrange("(h p) t -> p h t", p=128), OT[:], r=[OT], w=[ofb[(d, p)]])
                                done_scan[d] = idx + 1
                                yield
                            if si > 0:
                                self.dma(SP, self.stout[si - 1, l, d].rearrange("h k v -> k h v"), Sst[:], r=[Sst])
                        return gen

                    self.run_pool([pre_chain(0), pre_chain(1), scan_chain(0), scan_chain(1)], 4)
                    ep = [dict(ofl=self.sb(es3, f"gofl{i}", [128, 4, 128]), ofr=self.sb(es3, f"gofr{i}", [128, 4, 128]), zt=self.sb(es3, f"gzt{i}", [128, 4, 128]),
                               osq=self.sb(es3, f"gosq{i}", [128, 4, 128], BF16), ors=self.sb(es3, f"gors{i}", [128, 4, 128]),
                               ob=self.sb(es3, f"gob{i}", [128, 4, 128], BF16)) for i in range(2)]

                    def epilogue(p):
                        def gen(slot):
                            E = ep[slot]
                            ofl, ofr, zt, osq, ors, ob = (E[k] for k in ("ofl", "ofr", "zt", "osq", "ors", "ob"))
                            t0 = 128 * p
                            cols = slice(tok0 + t0, tok0 + t0 + 128)
                            self.dma(SP, ofl[:], self.OF[:, cols].rearrange("(h p) t -> p h t", p=128), r=[ofb[(0, p)]], w=[ofl])
                            self.dma(SP, ofr[:], self.OFB[:, cols].rearrange("(h p) t -> p h t", p=128), r=[ofb[(1, p)]], w=[ofr])
                            self.dma(SP, zt[:], self.P[16 * 128:20 * 128, cols].rearrange("(h p) t -> p h t", p=128), w=[zt])
                            yield
                            self.tt(POOL, ofl[:], ofl[:], ofr[:], ALU.add, r=[ofl, ofr], w=[ofl])
                            self.act(zt[:], zt[:], AF.Silu, r=[zt], w=[zt])
                            yield
                            self.act(osq[:], ofl[:], AF.Square, r=[ofl], w=[osq])
                            yield
                            pb_ = bank[2 + slot]
                            self.mm(pb_[:, 0:512], self.onesb, osq[:].rearrange("p h k -> p (h k)"), r=[osq, self.cstb], w=[pb_])
                            yield
                            self.act(ors[:].rearrange("p h k -> p (h k)"), pb_[:, 0:512], AF.Sqrt, r=[pb_, self.cst], w=[ors], bias=self.c_eps, scale=1.0 / 128)
                            yield
                            self.recip(ors[:], ors[:], r=[ors], w=[ors])
                            yield
                            self.tt(DVE, ofl[:], ofl[:], ors[:], ALU.mult, r=[ofl, ors], w=[ofl])
                            yield
                            self.stt(ob[:], ofl[:], pvl[:, PV_GNW:PV_GNW + 1], zt[:], ALU.mult, ALU.mult, r=[ofl, self.pv, zt], w=[ob])
                            yield
                            self.dma(SP, self.CAT[256:768, cols].rearrange("(h p) t -> p h t", p=128), ob[:], r=[ob])
                        return gen

                    self.run_pool([epilogue(p) for p in range(NP_)], 2)
                S.barrier()


def _prep_inputs(inp, nl=DEPTH):
    f = lambda a: np.ascontiguousarray(np.asarray(a, dtype=np.float32))
    inp = {k: f(v) for k, v in inp.items()}
    shared = {
        "cst": _consts(),
        "pv": _pack_pv(inp),
        "w_ada": inp["w_ada"][:nl],
        "w_in": np.ascontiguousarray(inp["w_in"][:nl][:, :, _WIN_PERM]),
        "w_out": inp["w_out"][:nl],
        "w_up": inp["w_up"][:nl],
        "w_down": inp["w_down"][:nl],
        "gsc": np.ascontiguousarray(np.concatenate([inp["gdn_a_log"].reshape(DEPTH, 8), inp["gdn_dt_bias"].reshape(DEPTH, 8)], axis=1)[:nl]),
        "rpb": np.ascontiguousarray(inp["na_rpb"].reshape(DEPTH, 60, 31)[:nl]),
    }
    maps = []
    for i in range(8):
        xin = np.concatenate([inp["x_sample"][i].T, inp["x_prompt"][2 * i].T, inp["x_prompt"][2 * i + 1].T], axis=1)
        cv = np.stack([inp["c_ctx"], inp["c"][i]], axis=1)
        cvec = cv.reshape(8, 128, 2).transpose(1, 0, 2).reshape(128, 16)
        kv = inp["cache_attn_kv"][i][:nl]
        kc = kv[:, 0].transpose(0, 1, 3, 2).reshape(nl, 2, 128, TC)
        vc = kv[:, 1].transpose(0, 2, 1, 3).reshape(nl, 2, 128, 256)
        s0 = inp["state_delta"][i][:nl].reshape(nl, 8, 128, 128).transpose(0, 2, 1, 3)
        m = dict(shared)
        m.update(xin=np.ascontiguousarray(xin), cvec=np.ascontiguousarray(cvec), kcache=np.ascontiguousarray(kc),
                 vcache=np.ascontiguousarray(vc), s0=np.ascontiguousarray(s0))
        maps.append(m)
    return maps


def _assemble(results, nl=DEPTH):
    y_prompt = np.zeros((16, TC, D), np.float32)
    y_sample = np.zeros((8, TL, D), np.float32)
    kvn = np.zeros((16, nl, 2, 4, TC, 64), np.float32)
    stn = np.zeros((16, nl, 2, 4, 128, 128), np.float32)
    for i, r in enumerate(results):
        xo = r["xout"]
        y_sample[i] = xo[:, 0:TL].T
        y_prompt[2 * i] = xo[:, TL:TL + TC].T
        y_prompt[2 * i + 1] = xo[:, TL + TC:TT].T
        kvo = r["kvout"]
        for s in range(2):
            kvn[2 * i + s] = kvo[:, s].reshape(nl, 2, 4, 64, TC).transpose(0, 1, 2, 4, 3)
            stn[2 * i + s] = r["stout"][s]
    return y_prompt, y_sample, kvn, stn


_CACHE = {}


def kernel(**inputs):
    if "kb" not in _CACHE:
        kb = KB()
        kb.build()
        _CACHE["kb"] = kb
    kb = _CACHE["kb"]
    maps = _prep_inputs(inputs)
    res = run_bass_kernel_spmd(kb.nc, maps, core_ids=list(range(8)))
    return _assemble(res.results)
```

```python
import contextlib
import numpy as np
import concourse.bass as bass
import concourse.mybir as mybir
from concourse.bass_utils import run_bass_kernel_spmd

F32 = mybir.dt.float32
BF16 = mybir.dt.bfloat16
AF = mybir.ActivationFunctionType
ALU = mybir.AluOpType
AX = mybir.AxisListType

PE, ACT, DVE, POOL, SP = "pe", "act", "dve", "pool", "sp"
COMPUTE = (PE, ACT, DVE, POOL)
ALLENG = (PE, ACT, DVE, POOL, SP)
DMAQ = (SP, ACT, POOL)
ND = 4
SAME_ENGINE_SYNC = True

D = 1024
DEPTH = 4
TL = 4096
TC = 256
TT = TL + 2 * TC
NCH = 27
PROJ = 3344
DFF = 2816
EPS = 1e-6
NEG = -30000.0
NEGBIG = -1.0e5
HW_ = TT + 6
SEQS = ((0, TL, 1, 1), (TL, TC, 4099, 0), (TL + TC, TC, 4357, 0))


class Buf:
    __slots__ = ("name", "t", "lw", "rd")

    def __init__(self, name, t):
        self.name = name
        self.t = t
        self.lw = None
        self.rd = []

    def __getitem__(self, k):
        return self.t[k]


class Op:
    __slots__ = ("eng", "fn", "deps", "marked", "is_dma", "sem", "cnt", "epoch", "kind")

    def __init__(self, eng, fn, is_dma, epoch, kind="op"):
        self.eng = eng
        self.fn = fn
        self.deps = []
        self.marked = False
        self.is_dma = is_dma
        self.sem = None
        self.cnt = 0
        self.epoch = epoch
        self.kind = kind


class Sched:
    def __init__(self, nc):
        self.nc = nc
        self.ops = []
        self.epoch = 0
        self.eng = {PE: nc.tensor, ACT: nc.scalar, DVE: nc.vector, POOL: nc.gpsimd, SP: nc.sync}

    def op(self, eng, fn, reads=(), writes=(), dma=False):
        o = Op(eng, fn, dma, self.epoch)
        deps = {}
        for b in reads:
            if b.lw is not None:
                deps[id(b.lw)] = (b.lw, "raw")
        for b in writes:
            if b.lw is not None:
                deps[id(b.lw)] = (b.lw, "waw")
            for r in b.rd:
                if id(r) not in deps:
                    deps[id(r)] = (r, "war")
        for d, kind in deps.values():
            if d.epoch != o.epoch or d is o:
                continue
            if (not d.is_dma) and d.eng == eng and not dma:
                if eng == PE or kind == "war" or not SAME_ENGINE_SYNC:
                    continue
            o.deps.append(d)
        for b in reads:
            b.rd.append(o)
        for b in writes:
            b.lw = o
            b.rd = []
        self.ops.append(o)
        return o

    def barrier(self, new_epoch=False):
        self.ops.append(Op(None, None, False, self.epoch, kind="barrier_ep" if new_epoch else "barrier"))
        if new_epoch:
            self.epoch += 1

    def emit(self, es):
        nc = self.nc
        nep = self.epoch + 1
        for o in self.ops:
            for d in o.deps:
                d.marked = True
        sems = []
        for ep in range(nep):
            d = {}
            for e in COMPUTE:
                d[e] = es.enter_context(nc.semaphore(f"s{ep}_{e}"))
            for q in DMAQ:
                for k in range(ND):
                    d[(q, k)] = es.enter_context(nc.semaphore(f"d{ep}_{q}{k}"))
            sems.append(d)
        cur = [dict((k, 0) for k in sems[ep]) for ep in range(nep)]
        rr = dict((q, 0) for q in DMAQ)
        seen = {}
        n_wait = 0
        for o in self.ops:
            ep = o.epoch
            if o.kind != "op":
                for e in ALLENG:
                    for key, val in cur[ep].items():
                        if val > 0 and seen.get((e, key), 0) < val:
                            self.eng[e].wait_ge(sems[ep][key], val)
                            seen[(e, key)] = val
                            n_wait += 1
                if o.kind == "barrier_ep":
                    seen = {}
                continue
            e = self.eng[o.eng]
            need = {}
            for d in o.deps:
                if need.get(d.sem, 0) < d.cnt:
                    need[d.sem] = d.cnt
            for key, val in need.items():
                if seen.get((o.eng, key), 0) >= val:
                    continue
                e.wait_ge(sems[ep][key], val)
                n_wait += 1
                seen[(o.eng, key)] = val
            inst = o.fn(e)
            if o.is_dma:
                key = (o.eng, rr[o.eng] % ND)
                rr[o.eng] += 1
                cur[ep][key] += 16
                o.sem, o.cnt = key, cur[ep][key]
                inst.then_inc(sems[ep][key], 16)
            elif o.marked:
                cur[ep][o.eng] += 1
                o.sem, o.cnt = o.eng, cur[ep][o.eng]
                inst.then_inc(sems[ep][o.eng], 1)
        ep = nep - 1
        for e in ALLENG:
            for key, val in cur[ep].items():
                if val > 0:
                    self.eng[e].wait_ge(sems[ep][key], val)
        return dict(n_ops=len(self.ops), n_wait=n_wait, counts=cur)


C_IDENT, C_ONES, C_JREV, C_TRIF, C_TRIB, C_BLK, C_HALFA, C_HALFB = range(8)
C_UINC, C_LSTR, C_LINC, C_USTR = 8, 9, 10, 11
C_CMH = 12
C_EPS = 13
NCST = 14


def _consts():
    p = np.arange(128)[:, None]
    f = np.arange(128)[None, :]
    same = (p // 64) == (f // 64)
    c = np.zeros((128, NCST, 128), np.float32)
    c[:, C_IDENT] = (p == f)
    c[:, C_ONES] = 1.0
    c[:, C_JREV] = same & ((p % 64) == (63 - f % 64))
    c[:, C_TRIF] = same & (p <= f)
    c[:, C_TRIB] = same & (p >= f)
    c[:, C_BLK] = same
    c[:, C_HALFA] = (p < 64) & (f >= 0)
    c[:, C_HALFB] = (p >= 64) & (f >= 0)
    c[:, C_UINC] = np.where(same & (f >= p), 0.0, NEGBIG)
    c[:, C_LSTR] = np.where(same & (p > f), 0.0, NEGBIG)
    c[:, C_LINC] = np.where(same & (f <= p), 0.0, NEGBIG)
    c[:, C_USTR] = np.where(same & (p < f), 0.0, NEGBIG)
    qc = 63 - (np.arange(128) % 64)
    cs = np.clip(qc - 8, 0, 48)
    kc = np.arange(64)[None, :]
    ok = (kc >= cs[:, None]) & (kc < cs[:, None] + 16)
    c[:, C_CMH, 0:64] = np.where(ok, 0.0, NEG)
    c[:, C_EPS, 0] = EPS
    c[:, C_EPS, 1] = 1.0
    return c.reshape(128, NCST * 128)


PV_GPM, PV_GPOM, PV_GPF, PV_GPOF = 0, 8, 16, 24
PV_BADA = 32
PV_CW = 80
PV_CB, PV_CLG, PV_CLB = 142, 144, 146
PV_GCW = 148
PV_GNW = 184
PV_FCW = 185
PV_FCB = 317
PV_N = 361


def _fm(v, nchunk):
    return np.ascontiguousarray(v.reshape(nchunk, 128).T)


def _pack_pv(inp):
    pv = np.zeros((128, DEPTH, PV_N), np.float32)
    for l in range(DEPTH):
        pv[:, l, PV_GPM:PV_GPM + 8] = _fm(inp["g_pre_mix"][l], 8)
        pv[:, l, PV_GPOM:PV_GPOM + 8] = _fm(inp["g_post_mix"][l], 8)
        pv[:, l, PV_GPF:PV_GPF + 8] = _fm(inp["g_pre_ffn"][l], 8)
        pv[:, l, PV_GPOF:PV_GPOF + 8] = _fm(inp["g_post_ffn"][l], 8)
        pv[:, l, PV_BADA:PV_BADA + 48] = _fm(inp["b_ada"][l], 48)
        cw = inp["conv_w"][l]
        pv[:, l, PV_CW:PV_CW + 62] = cw.T.reshape(2, 128, 31).transpose(1, 0, 2).reshape(128, 62)
        pv[:, l, PV_CB:PV_CB + 2] = _fm(inp["conv_b"][l], 2)
        pv[:, l, PV_CLG:PV_CLG + 2] = _fm(inp["conv_ln_g"][l], 2)
        pv[:, l, PV_CLB:PV_CLB + 2] = _fm(inp["conv_ln_b"][l], 2)
        gw = inp["gdn_conv_w"][l]
        pv[:, l, PV_GCW:PV_GCW + 36] = gw.T.reshape(12, 128, 3).transpose(1, 0, 2).reshape(128, 36)
        pv[:, l, PV_GNW] = inp["gdn_norm_w"][l]
        fw = inp["ffn_conv_w"][l]
        pv[:, l, PV_FCW:PV_FCW + 132] = fw.T.reshape(44, 128, 3).transpose(1, 0, 2).reshape(128, 132)
        pv[:, l, PV_FCB:PV_FCB + 44] = _fm(inp["ffn_conv_b"][l], 44)
    return pv.reshape(128, DEPTH * PV_N)


_WIN_PERM = np.concatenate([np.arange(0, 2560), np.arange(2576, 3344), np.arange(2560, 2576)])


class KB:
    def __init__(self, nlayers=DEPTH, dbg=None):
        self.nl = nlayers
        self.dbg = dbg or {}
        nc = self.nc = bass.Bass("TRN2", target_bir_lowering=False)
        self.S = Sched(nc)
        self.es = contextlib.ExitStack()
        self.uid = 0

    def mm(self, out, lhsT, rhs, start=True, stop=True, r=(), w=()):
        return self.S.op(PE, lambda e: e.matmul(out, lhsT, rhs, start=start, stop=stop), r, w)

    def tr(self, out, in_, ident, r=(), w=()):
        return self.S.op(PE, lambda e: e.transpose(out, in_, ident), r, w)

    def act(self, out, in_, func, r=(), w=(), eng=ACT, **kw):
        return self.S.op(eng, lambda e: e.activation(out=out, in_=in_, func=func, **kw), r, w)

    def tt(self, eng, out, in0, in1, op, r=(), w=()):
        return self.S.op(eng, lambda e: e.tensor_tensor(out=out, in0=in0, in1=in1, op=op), r, w)

    def ts(self, eng, out, in0, s1, s2, op0, op1=None, r=(), w=()):
        if op1 is None:
            return self.S.op(eng, lambda e: e.tensor_scalar(out=out, in0=in0, scalar1=s1, scalar2=None, op0=op0), r, w)
        return self.S.op(eng, lambda e: e.tensor_scalar(out=out, in0=in0, scalar1=s1, scalar2=s2, op0=op0, op1=op1), r, w)

    def stt(self, out, in0, scalar, in1, op0, op1, r=(), w=()):
        return self.S.op(DVE, lambda e: e.scalar_tensor_tensor(out=out, in0=in0, scalar=scalar, in1=in1, op0=op0, op1=op1), r, w)

    def cp(self, eng, out, in_, r=(), w=()):
        if eng == ACT:
            return self.S.op(ACT, lambda e: e.copy(out=out, in_=in_), r, w)
        return self.S.op(eng, lambda e: e.tensor_copy(out=out, in_=in_), r, w)

    def memset(self, eng, ap, val, w=()):
        return self.S.op(eng, lambda e: e.memset(ap, val), (), w)

    def recip(self, out, in_, r=(), w=()):
        return self.S.op(DVE, lambda e: e.reciprocal(out=out, in_=in_), r, w)

    def dma(self, q, out, in_, r=(), w=()):
        return self.S.op(q, lambda e: e.dma_start(out=out, in_=in_), r, w, dma=True)

    def sb(self, es, name, shape, dt=F32):
        self.uid += 1
        return Buf(name, es.enter_context(self.nc.sbuf_tensor(f"{name}_{self.uid}", shape, dt)))

    def dram(self, name, shape, dt, kind="Internal"):
        if name in self.dbg.get("out", ()):
            kind = "ExternalOutput"
        if name in self.dbg.get("in", ()):
            kind = "ExternalInput"
        return self.nc.dram_tensor(name, shape, dt, kind=kind).ap()

    def rsqrt_ps(self, out, ps_ap, scale, r, w):
        self.act(out, ps_ap, AF.Sqrt, r=list(r) + [self.cst], w=w, bias=self.c_eps, scale=scale)
        self.recip(out, out, r=w, w=w)

    def build(self):
        nc, S, es = self.nc, self.S, self.es
        nl = self.nl
        I = {}
        def ext(name, shape, dt=F32):
            I[name] = nc.dram_tensor(name, shape, dt, kind="ExternalInput").ap()
            return I[name]
        self.xin = ext("xin", [D, TT])
        self.cvec = ext("cvec", [128, 16])
        self.cst_d = ext("cst", [128, NCST * 128])
        self.pv_d = ext("pv", [128, DEPTH * PV_N])
        self.w_ada = ext("w_ada", [nl, D, 6 * D])
        self.w_in = ext("w_in", [nl, D, PROJ])
        self.w_out = ext("w_out", [nl, D, D])
        self.w_up = ext("w_up", [nl, D, 2 * DFF])
        self.w_down = ext("w_down", [nl, DFF, D])
        self.kcache = ext("kcache", [nl, 2, 128, TC])
        self.vcache = ext("vcache", [nl, 2, 128, 256])
        self.s0 = ext("s0", [nl, 128, 8, 128])
        self.gsc = ext("gsc", [nl, 16])
        self.rpb = ext("rpb", [nl, 60, 31])
        self.xout = nc.dram_tensor("xout", [D, TT], F32, kind="ExternalOutput").ap()
        self.kvout = nc.dram_tensor("kvout", [nl, 2, 2, 256, TC], F32, kind="ExternalOutput").ap()
        self.stout = nc.dram_tensor("stout", [2, nl, 2, 4, 128, 128], F32, kind="ExternalOutput").ap()
        self.P = self.dram("P", [NCH * 128, TT], F32)
        self.PB = self.dram("PB", [6 * 128, TT], BF16)
        self.CAT = self.dram("CAT", [D, TT], BF16)
        self.XA = self.dram("XA", [D, TT], F32)
        self.A = self.dram("A", [DFF, TT], BF16)
        self.OF = self.dram("OF", [512, TT], F32)
        self.OFB = self.dram("OFB", [512, TT], F32)
        self.FP = self.dram("FP", [nl, 60, 127], F32)

        with es:
            pst = es.enter_context(nc.psum_tensor("ps", [128, 4096], F32))
            self.bank = [Buf(f"bank{i}", pst[:, i * 512:(i + 1) * 512]) for i in range(8)]
            self.pst = pst
            self.cst = self.sb(es, "cst", [128, NCST, 128])
            self.cstb = self.sb(es, "cstb", [128, 3, 128], BF16)
            self.pv = self.sb(es, "pv", [128, DEPTH, PV_N])
            self.mod = self.sb(es, "mod", [128, 6, 8, 2])
            self.scv = self.sb(es, "scv", [128, 8, 2])
            self.c_eps = self.cst[:, C_EPS, 0:1]
            self.dma(SP, self.cst[:].rearrange("p a b -> p (a b)"), self.cst_d[:, :], w=[self.cst])
            self.dma(SP, self.pv[:].rearrange("p a b -> p (a b)"), self.pv_d[:, :], w=[self.pv])
            self.dma(SP, self.scv[:].rearrange("p a b -> p (a b)"), self.cvec[:, :], w=[self.scv])
            for i, ci in enumerate((C_IDENT, C_ONES, C_JREV)):
                self.cp(DVE, self.cstb[:, i, :], self.cst[:, ci, :], r=[self.cst], w=[self.cstb])
            self.ident = self.cst[:, C_IDENT, :]
            self.ones = self.cst[:, C_ONES, :]
            self.identb = self.cstb[:, 0, :]
            self.onesb = self.cstb[:, 1, :]
            self.jrevb = self.cstb[:, 2, :]
            self.act(self.scv[:], self.scv[:], AF.Silu, r=[self.scv], w=[self.scv])
            for l in range(nl):
                self.layer(l)
                if l + 1 < nl:
                    S.barrier(new_epoch=True)
            info = S.emit(es)
        return info

    def layer(self, l):
        S = self.S
        stages = self.dbg.get("stages", ("ada", "s1", "conf", "gdn", "attn", "s3", "s4a", "s4b"))
        xsrc = self.xin if l == 0 else self.xout
        if "ada" in stages:
            self.stage_ada(l)
            S.barrier()
        if "s1" in stages:
            self.stage_s1(l, xsrc)
            S.barrier()
        if "conf" in stages:
            self.stage_conf(l)
            S.barrier()
        if "attn" in stages:
            self.stage_attn(l)
            S.barrier()
        if "gdn" in stages:
            self.stage_gdn(l)
            S.barrier()
        if "s3" in stages:
            with contextlib.ExitStack() as es:
                self.Hb = self.sb(es, "H", [128, 8, HW_], BF16)
                self.stage_s3(l, xsrc, es)
                S.barrier()
                if "s4a" in stages:
                    self.stage_s4a(l)
                    S.barrier()
        if "s4b" in stages:
            self.stage_s4b(l)
            S.barrier()

    def stage_ada(self, l):
        with contextlib.ExitStack() as es:
            wb = [self.sb(es, f"adaw{i}", [128, 8, 512]) for i in range(6)]
            raw = self.sb(es, "adaraw", [128, 48, 2])
            ps = self.bank[0]
            for g in range(12):
                w = wb[g % 6]
                self.dma(SP, w[:],
                         self.w_ada[l, :, g * 512:(g + 1) * 512].rearrange("(kc p) n -> p kc n", p=128), w=[w])
                for j in range(4):
                    n = g * 4 + j
                    for kc in range(8):
                        self.mm(ps[:, 2 * n:2 * n + 2], w[:, kc, j * 128:(j + 1) * 128], self.scv[:, kc, :],
                                start=(kc == 0), stop=(kc == 7), r=[w, self.scv], w=[ps])
            pvl = self.pv[:, l, :]
            self.tt(DVE, raw[:], ps[:, 0:96].rearrange("p (n v) -> p n v", v=2),
                    pvl[:, PV_BADA:PV_BADA + 48].unsqueeze(2).to_broadcast([128, 48, 2]), ALU.add,
                    r=[ps, self.pv], w=[raw])
            def gain(off):
                return pvl[:, off:off + 8].unsqueeze(2).to_broadcast([128, 8, 2])
            m = self.mod
            self.ts(DVE, m[:, 0], raw[:, 8:16, :], 1.0, None, ALU.add, r=[raw], w=[m])
            self.tt(DVE, m[:, 0], m[:, 0], gain(PV_GPM), ALU.mult, r=[m, self.pv], w=[m])
            self.cp(DVE, m[:, 1], raw[:, 0:8, :], r=[raw], w=[m])
            self.tt(DVE, m[:, 2], raw[:, 16:24, :], gain(PV_GPOM), ALU.mult, r=[raw, self.pv], w=[m])
            self.ts(DVE, m[:, 3], raw[:, 32:40, :], 1.0, None, ALU.add, r=[raw], w=[m])
            self.tt(DVE, m[:, 3], m[:, 3], gain(PV_GPF), ALU.mult, r=[m, self.pv], w=[m])
            self.cp(DVE, m[:, 4], raw[:, 24:32, :], r=[raw], w=[m])
            self.tt(DVE, m[:, 5], raw[:, 40:48, :], gain(PV_GPOF), ALU.mult, r=[raw, self.pv], w=[m])

    def sumsq_rstd(self, src, sq, ps, rstd, n, nchunk=8, scale=1.0 / D, src_bufs=()):
        for c in range(nchunk):
            self.act(sq[:, c, 0:n], src[:, c, 0:n], AF.Square, r=list(src_bufs), w=[sq], eng=ACT)
        for c in range(nchunk):
            self.mm(ps[:, 0:n], self.onesb, sq[:, c, 0:n], start=(c == 0), stop=(c == nchunk - 1),
                    r=[sq, self.cstb], w=[ps])
        self.rsqrt_ps(rstd[:, 0:n], ps[:, 0:n], scale, r=[ps], w=[rstd])

    def s1_tiles(self):
        tl = [(j * 512, 512, 1, [(1 + j * 512, 512)]) for j in range(8)]
        tl.append((TL, 512, 0, [(4099, 256), (4357, 256)]))
        return tl

    def stage_s1(self, l, xsrc):
        with contextlib.ExitStack() as es:
            H = self.sb(es, "H", [128, 8, HW_], BF16)
            xt = [self.sb(es, f"xt{i}", [128, 8, 512]) for i in range(2)]
            sq = self.sb(es, "sq", [128, 8, 512], BF16)
            rstd = self.sb(es, "rstd", [128, 512])
            wck = [self.sb(es, f"wck{i}", [128, 8, 128], BF16) for i in range(3)]
            ev = [self.sb(es, f"ev{i}", [128, 512]) for i in range(4)]
            evb = [self.sb(es, f"evb{i}", [128, 512], BF16) for i in range(2)]
            tiles = self.s1_tiles()
            m = self.mod
            for ti, (t0, n, vec, hc) in enumerate(tiles):
                x = xt[ti % 2]
                self.dma(SP, x[:], xsrc[:, t0:t0 + n].rearrange("(c p) t -> p c t", p=128), w=[x])
                ps = self.bank[ti % 2]
                self.sumsq_rstd(x, sq, ps, rstd, n, src_bufs=[x])
                self.tt(DVE, x[:], x[:], rstd[:].unsqueeze(1).to_broadcast([128, 8, 512]), ALU.mult, r=[x, rstd], w=[x])
                for c in range(8):
                    off = 0
                    for (h0, nn) in hc:
                        eng = ACT if c % 2 == 0 else POOL
                        if eng == ACT:
                            self.act(H[:, c, h0:h0 + nn], x[:, c, off:off + nn], AF.Identity, r=[x, m], w=[H],
                                     scale=m[:, 0, c, vec:vec + 1], bias=m[:, 1, c, vec:vec + 1])
                        else:
                            self.ts(POOL, H[:, c, h0:h0 + nn], x[:, c, off:off + nn], m[:, 0, c, vec:vec + 1],
                                    m[:, 1, c, vec:vec + 1], ALU.mult, ALU.add, r=[x, m], w=[H])
                        off += nn
            k = 0
            for nci in range(NCH):
                ncols = 128 if nci < 26 else 16
                w = wck[nci % 3]
                self.dma(POOL, w[:, :, 0:ncols],
                         self.w_in[l, :, nci * 128:nci * 128 + ncols].rearrange("(kc p) n -> p kc n", p=128), w=[w])
                for ti, (t0, n, vec, hc) in enumerate(tiles):
                    ps = self.bank[2 + (k % 4)]
                    off = 0
                    for (h0, nn) in hc:
                        for kc in range(8):
                            self.mm(ps[0:ncols, off:off + nn], w[:, kc, 0:ncols], H[:, kc, h0:h0 + nn],
                                    start=(kc == 0), stop=(kc == 7), r=[w, H], w=[ps])
                        off += nn
                    e = ev[k % 4]
                    if k % 2 == 0:
                        self.cp(ACT, e[0:ncols, :], ps[0:ncols, :], r=[ps], w=[e])
                    else:
                        self.cp(DVE, e[0:ncols, :], ps[0:ncols, :], r=[ps], w=[e])
                    self.dma(SP, self.P[nci * 128:nci * 128 + ncols, t0:t0 + n], e[0:ncols, :], r=[e])
                    if 20 <= nci < 26:
                        eb = evb[k % 2]
                        self.cp(POOL, eb[:], e[:], r=[e], w=[eb])
                        self.dma(SP, self.PB[(nci - 20) * 128:(nci - 19) * 128, t0:t0 + n], eb[:], r=[eb])
                    if ti == 8 and 22 <= nci < 26:
                        kv = (nci - 22) // 2
                        f0 = ((nci - 22) % 2) * 128
                        for s in range(2):
                            self.dma(SP, self.kvout[l, s, kv, f0:f0 + 128, :], e[:, s * 256:(s + 1) * 256], r=[e])
                    k += 1

    def stage_s3(self, l, xsrc, es0):
        H = self.Hb
        with contextlib.ExitStack() as es:
            wo = self.sb(es, "wo", [128, 8, D], BF16)
            cat = [self.sb(es, f"cat{i}", [128, 8, 512], BF16) for i in range(2)]
            xt = [self.sb(es, f"xt{i}", [128, 8, 512]) for i in range(2)]
            msb = self.sb(es, "msb", [128, 8, 512])
            sq = self.sb(es, "sq", [128, 8, 512], BF16)
            rstd = self.sb(es, "rstd", [128, 512])
            zc = self.sb(es, "zc", [128, 8, 2], BF16)
            parts = self.dbg.get("s3parts", "wpx")
            if "w" in parts:
                self.dma(POOL, wo[:], self.w_out[l].rearrange("(kc p) n -> p kc n", p=128), w=[wo])
            if "p" in parts:
                self.memset(POOL, zc[:], 0.0, w=[zc])
                for c0, n0 in ((0, 1), (4097, 2), (4355, 2), (4613, 1)):
                    self.cp(POOL, H[:, :, c0:c0 + n0], zc[:, :, 0:n0], r=[zc], w=[H])
            if "x" not in parts:
                return
            m = self.mod
            lvl = int(self.dbg.get("s3n", 9))
            for ti, (t0, n, vec, hc) in enumerate(self.s1_tiles()):
                ct = cat[ti % 2]
                x = xt[ti % 2]
                self.dma(SP, ct[:], self.CAT[:, t0:t0 + n].rearrange("(c p) t -> p c t", p=128), w=[ct])
                self.dma(SP, x[:], xsrc[:, t0:t0 + n].rearrange("(c p) t -> p c t", p=128), w=[x])
                if lvl < 2:
                    continue
                for nn in range(8):
                    ps = self.bank[nn % 4]
                    for kc in range(8):
                        self.mm(ps[:, 0:n], wo[:, kc, nn * 128:(nn + 1) * 128], ct[:, kc, :],
                                start=(kc == 0), stop=(kc == 7), r=[wo, ct], w=[ps])
                    if "c" in self.dbg.get("s3l2", "cs"):
                        self.cp(DVE, msb[:, nn, :], ps[:, 0:n], r=[ps], w=[msb])
                    if "s" in self.dbg.get("s3l2", "cs"):
                        self.act(sq[:, nn, :], msb[:, nn, :], AF.Square, r=[msb], w=[sq])
                if lvl < 3:
                    continue
                pss = self.bank[4 + ti % 2]
                for c in range(8):
                    self.mm(pss[:, 0:n], self.onesb, sq[:, c, :], start=(c == 0), stop=(c == 7), r=[sq, self.cstb], w=[pss])
                self.rsqrt_ps(rstd[:, 0:n], pss[:, 0:n], 1.0 / D, r=[pss], w=[rstd])
                self.tt(DVE, msb[:], msb[:], rstd[:].unsqueeze(1).to_broadcast([128, 8, 512]), ALU.mult, r=[msb, rstd], w=[msb])
                if lvl < 4:
                    continue
                for c in range(8):
                    self.stt(x[:, c, :], msb[:, c, :], m[:, 2, c, vec:vec + 1], x[:, c, :], ALU.mult, ALU.add,
                             r=[msb, m, x], w=[x])
                self.dma(SP, self.XA[:, t0:t0 + n].rearrange("(c p) t -> p c t", p=128), x[:], r=[x])
                if lvl < 5:
                    continue
                pss2 = self.bank[6 + ti % 2]
                self.sumsq_rstd(x, sq, pss2, rstd, n, src_bufs=[x])
                self.tt(DVE, msb[:], x[:], rstd[:].unsqueeze(1).to_broadcast([128, 8, 512]), ALU.mult, r=[x, rstd], w=[msb])
                if lvl < 6:
                    continue
                for c in range(8):
                    off = 0
                    for (h0, nn2) in hc:
                        if c % 2 == 0:
                            self.act(H[:, c, h0:h0 + nn2], msb[:, c, off:off + nn2], AF.Identity, r=[msb, m], w=[H],
                                     scale=m[:, 3, c, vec:vec + 1], bias=m[:, 4, c, vec:vec + 1])
                        else:
                            self.ts(POOL, H[:, c, h0:h0 + nn2], msb[:, c, off:off + nn2], m[:, 3, c, vec:vec + 1],
                                    m[:, 4, c, vec:vec + 1], ALU.mult, ALU.add, r=[msb, m], w=[H])
                        off += nn2

    def ffn_tiles(self):
        tl = []
        t = 0
        while t < TL:
            n = min(456, TL - t)
            tl.append((t, n, t))
            t += n
        tl.append((TL, TC, 4098))
        tl.append((TL + TC, TC, 4356))
        return tl

    def stage_s4a(self, l):
        H = self.Hb
        with contextlib.ExitStack() as es:
            wg = [self.sb(es, f"wg{i}", [128, 8, 128], BF16) for i in range(2)]
            wv = [self.sb(es, f"wv{i}", [128, 8, 128], BF16) for i in range(2)]
            tg = [self.sb(es, f"tg{i}", [128, 512]) for i in range(2)]
            tv = [self.sb(es, f"tv{i}", [128, 512]) for i in range(2)]
            ao = [self.sb(es, f"ao{i}", [128, 512], BF16) for i in range(3)]
            pvl = self.pv[:, l, :]
            k = 0
            for j in range(22):
                g, v = wg[j % 2], wv[j % 2]
                self.dma(POOL, g[:], self.w_up[l, :, j * 128:(j + 1) * 128].rearrange("(kc p) n -> p kc n", p=128), w=[g])
                self.dma(POOL, v[:], self.w_up[l, :, (22 + j) * 128:(23 + j) * 128].rearrange("(kc p) n -> p kc n", p=128), w=[v])
                for (t0, n, h0) in self.ffn_tiles():
                    pg = self.bank[(2 * k) % 8]
                    pv_ = self.bank[(2 * k + 1) % 8]
                    for kc in range(8):
                        self.mm(pg[:, 0:n + 2], g[:, kc, :], H[:, kc, h0:h0 + n + 2], start=(kc == 0), stop=(kc == 7), r=[g, H], w=[pg])
                    for kc in range(8):
                        self.mm(pv_[:, 0:n + 2], v[:, kc, :], H[:, kc, h0:h0 + n + 2], start=(kc == 0), stop=(kc == 7), r=[v, H], w=[pv_])
                    a, b_ = tg[k % 2], tv[k % 2]
                    for (ps, t, ch) in ((pg, a, j), (pv_, b_, 22 + j)):
                        cw = PV_FCW + 3 * ch
                        self.act(t[:, 0:n], ps[:, 0:n], AF.Identity, r=[ps, self.pv], w=[t],
                                 scale=pvl[:, cw:cw + 1], bias=pvl[:, PV_FCB + ch:PV_FCB + ch + 1])
                        self.stt(t[:, 0:n], ps[:, 1:n + 1], pvl[:, cw + 1:cw + 2], t[:, 0:n], ALU.mult, ALU.add, r=[ps, self.pv, t], w=[t])
                        self.stt(t[:, 0:n], ps[:, 2:n + 2], pvl[:, cw + 2:cw + 3], t[:, 0:n], ALU.mult, ALU.add, r=[ps, self.pv, t], w=[t])
                    self.act(a[:, 0:n], a[:, 0:n], AF.Silu, r=[a], w=[a])
                    o = ao[k % 3]
                    self.tt(POOL, o[:, 0:n], a[:, 0:n], b_[:, 0:n], ALU.mult, r=[a, b_], w=[o])
                    self.dma(SP, self.A[j * 128:(j + 1) * 128, t0:t0 + n], o[:, 0:n], r=[o])
                    k += 1

    def stage_s4b(self, l):
        with contextlib.ExitStack() as es:
            wd = self.sb(es, "wd", [128, 22, D], BF16)
            at = [self.sb(es, f"at{i}", [128, 22, 512], BF16) for i in range(2)]
            xa = [self.sb(es, f"xa{i}", [128, 8, 512]) for i in range(2)]
            fsbs = [self.sb(es, f"fsb{i}", [128, 8, 512]) for i in range(2)]
            sqs = [self.sb(es, f"sq{i}", [128, 8, 512], BF16) for i in range(2)]
            rstds = [self.sb(es, f"rstd{i}", [128, 512]) for i in range(2)]
            for h in range(2):
                self.dma(POOL, wd[:, h * 11:(h + 1) * 11, :],
                         self.w_down[l, h * 11 * 128:(h + 1) * 11 * 128, :].rearrange("(j p) n -> p j n", p=128), w=[wd])
            m = self.mod
            for ti, (t0, n, vec, hc) in enumerate(self.s1_tiles()):
                a = at[ti % 2]
                x = xa[ti % 2]
                fsb, sq, rstd = fsbs[ti % 2], sqs[ti % 2], rstds[ti % 2]
                self.dma(SP, a[:], self.A[:, t0:t0 + n].rearrange("(j p) t -> p j t", p=128), w=[a])
                self.dma(SP, x[:], self.XA[:, t0:t0 + n].rearrange("(c p) t -> p c t", p=128), w=[x])
                for nn in range(8):
                    ps = self.bank[nn % 4]
                    for j in range(22):
                        self.mm(ps[:, 0:n], wd[:, j, nn * 128:(nn + 1) * 128], a[:, j, :], start=(j == 0), stop=(j == 21), r=[wd, a], w=[ps])
                    self.cp(DVE, fsb[:, nn, :], ps[:, 0:n], r=[ps], w=[fsb])
                    self.act(sq[:, nn, :], fsb[:, nn, :], AF.Square, r=[fsb], w=[sq])
                pss = self.bank[4 + ti % 2]
                for c in range(8):
                    self.mm(pss[:, 0:n], self.onesb, sq[:, c, :], start=(c == 0), stop=(c == 7), r=[sq, self.cstb], w=[pss])
                self.rsqrt_ps(rstd[:, 0:n], pss[:, 0:n], 1.0 / D, r=[pss], w=[rstd])
                self.tt(DVE, fsb[:], fsb[:], rstd[:].unsqueeze(1).to_broadcast([128, 8, 512]), ALU.mult, r=[fsb, rstd], w=[fsb])
                for c in range(8):
                    self.stt(x[:, c, :], fsb[:, c, :], m[:, 5, c, vec:vec + 1], x[:, c, :], ALU.mult, ALU.add, r=[fsb, m, x], w=[x])
                self.dma(SP, self.xout[:, t0:t0 + n].rearrange("(c p) t -> p c t", p=128), x[:], r=[x])

    def stage_conf(self, l):
        pvl = self.pv[:, l, :]
        with contextlib.ExitStack() as es:
            av = [self.sb(es, f"cav{c}", [128, TL]) for c in range(2)]
            ag = [self.sb(es, f"cag{c}", [128, TL]) for c in range(2)]
            U = [self.sb(es, f"cU{c}", [128, TL + 30], BF16) for c in range(2)]
            DGc = [self.sb(es, f"cDG{c}", [128, 31, 128], BF16) for c in range(2)]
            for c in range(2):
                for kk in range(31):
                    col = PV_CW + 31 * c + kk
                    if c == 0:
                        self.ts(POOL, DGc[c][:, kk, :], self.ident, pvl[:, col:col + 1], None, ALU.mult, r=[self.cst, self.pv], w=[DGc[c]])
                    else:
                        self.act(DGc[c][:, kk, :], self.ident, AF.Copy, r=[self.cst, self.pv], w=[DGc[c]], scale=pvl[:, col:col + 1])
            kconv = 0
            sq = [self.sb(es, f"csq{c}", [128, 512]) for c in range(2)]
            st = self.sb(es, "cst", [128, 3, 512])
            yb = [self.sb(es, f"cy{c}", [128, 512]) for c in range(2)]
            ob = [self.sb(es, f"cob{i}", [128, 512], BF16) for i in range(4)]
            k = 0
            for (tok0, T, hcol0, vec) in SEQS:
                for c in range(2):
                    self.dma(SP, av[c][:, 0:T], self.P[c * 128:(c + 1) * 128, tok0:tok0 + T], w=[av[c]])
                    self.dma(SP, ag[c][:, 0:T], self.P[(2 + c) * 128:(3 + c) * 128, tok0:tok0 + T], w=[ag[c]])
                    self.memset(POOL, U[c][:, 0:15], 0.0, w=[U[c]])
                    self.memset(POOL, U[c][:, 15 + T:30 + T], 0.0, w=[U[c]])
                    self.act(ag[c][:, 0:T], ag[c][:, 0:T], AF.Sigmoid, r=[ag[c]], w=[ag[c]])
                    self.tt(POOL, U[c][:, 15:15 + T], av[c][:, 0:T], ag[c][:, 0:T], ALU.mult, r=[av[c], ag[c]], w=[U[c]])
                    acc = av[c]
                    for t in range(0, T, 512):
                        n = min(512, T - t)
                        psb = self.bank[6 + kconv % 2]
                        kconv += 1
                        for kk in range(31):
                            self.mm(psb[:, 0:n], DGc[c][:, kk, :], U[c][:, t + kk:t + kk + n], start=(kk == 0), stop=(kk == 30),
                                    r=[DGc[c], U[c]], w=[psb])
                        self.act(acc[:, t:t + n], psb[:, 0:n], AF.Identity, r=[psb, self.pv], w=[acc],
                                 bias=pvl[:, PV_CB + c:PV_CB + c + 1], scale=1.0)
                for t in range(0, T, 512):
                    n = min(512, T - t)
                    p1 = self.bank[(2 * k) % 6]
                    p2 = self.bank[(2 * k + 1) % 6]
                    for c in range(2):
                        self.act(sq[c][:, 0:n], av[c][:, t:t + n], AF.Square, r=[av[c]], w=[sq[c]])
                    for c in range(2):
                        self.mm(p1[:, 0:n], self.ones, av[c][:, t:t + n], start=(c == 0), stop=(c == 1), r=[self.cst, av[c]], w=[p1])
                    for c in range(2):
                        self.mm(p2[:, 0:n], self.ones, sq[c][:, 0:n], start=(c == 0), stop=(c == 1), r=[self.cst, sq[c]], w=[p2])
                    self.ts(DVE, st[:, 0, 0:n], p1[:, 0:n], 1.0 / 256, None, ALU.mult, r=[p1], w=[st])
                    self.tt(DVE, st[:, 1, 0:n], st[:, 0, 0:n], st[:, 0, 0:n], ALU.mult, r=[st], w=[st])
                    self.stt(st[:, 1, 0:n], p2[:, 0:n], 1.0 / 256, st[:, 1, 0:n], ALU.mult, ALU.subtract, r=[p2, st], w=[st])
                    self.ts(DVE, st[:, 1, 0:n], st[:, 1, 0:n], 0.0, None, ALU.max, r=[st], w=[st])
                    self.act(st[:, 2, 0:n], st[:, 1, 0:n], AF.Sqrt, r=[st, self.cst], w=[st], bias=self.c_eps, scale=1.0)
                    self.recip(st[:, 2, 0:n], st[:, 2, 0:n], r=[st], w=[st])
                    for c in range(2):
                        self.tt(DVE, yb[c][:, 0:n], av[c][:, t:t + n], st[:, 0, 0:n], ALU.subtract, r=[av[c], st], w=[yb[c]])
                        self.tt(POOL, yb[c][:, 0:n], yb[c][:, 0:n], st[:, 2, 0:n], ALU.mult, r=[yb[c], st], w=[yb[c]])
                        o = ob[(2 * k + c) % 4]
                        self.act(o[:, 0:n], yb[c][:, 0:n], AF.Silu, r=[yb[c], self.pv], w=[o],
                                 scale=pvl[:, PV_CLG + c:PV_CLG + c + 1], bias=pvl[:, PV_CLB + c:PV_CLB + c + 1])
                        self.dma(SP, self.CAT[c * 128:(c + 1) * 128, tok0 + t:tok0 + t + n], o[:, 0:n], r=[o])
                    k += 1

    def attn_unit(self, slot, qf, hp, b0, t0, pieces, vts, otm, h, W):
        nk = sum(p[3] for p in pieces)
        nkt = nk // 128
        base = slot * 1024
        ps = self.pst[:, base:base + nk]
        pb = [self.bank[2 * slot], self.bank[2 * slot + 1]]
        lhsT = qf[b0:b0 + 64, hp, t0:t0 + 128]
        for (col, rhs, bm, n) in pieces:
            self.mm(self.pst[:, base + col:base + col + n], lhsT, rhs, start=True, stop=(bm is None), r=W["qk"], w=pb)
            if bm is not None:
                self.mm(self.pst[:, base + col:base + col + n], self.jrevb, bm, start=False, stop=True, r=W["bm"], w=pb)
        sm = W["sm"][slot * 2 + h % 2]
        pexp = W["pexp"][slot]
        pts = W["pts"][slot]
        yield
        self.S.op(DVE, lambda e: e.reduce_max(out=sm[:, 0:1], in_=ps, axis=AX.X), pb, [sm])
        yield
        self.ts(DVE, sm[:, 1:2], sm[:, 0:1], -0.125, None, ALU.mult, r=[sm], w=[sm])
        self.memset(DVE, sm[:, 2:3], 0.0, w=[sm])
        yield
        self.act(pexp[:, 0:nk], ps, AF.Exp, r=pb + [sm], w=[pexp, sm], scale=0.125, bias=sm[:, 1:2], accum_out=sm[:, 2:3])
        yield
        ptbank = self.bank[4 + slot]
        ptb = ptbank.t.bitcast(BF16)
        for c in range(nkt):
            self.tr(ptb[:, c * 128:(c + 1) * 128], pexp[:, c * 128:(c + 1) * 128], self.identb, r=[pexp, self.cstb], w=[ptbank])
        yield
        self.cp(ACT if slot == 0 else DVE, pts[:, 0:nkt, :].rearrange("p a b -> p (a b)"), ptb[:, 0:nk], r=[ptbank], w=[pts])
        self.recip(sm[:, 3:4], sm[:, 2:3], r=[sm], w=[sm])
        yield
        po = self.bank[6]
        for c in range(nkt):
            self.mm(po[:, 0:64], pts[:, c, :], vts[c], start=(c == 0), stop=(c == nkt - 1), r=[pts] + W["v"], w=[po])
        self.ts(DVE, otm[:, h * 64:(h + 1) * 64], po[:, 0:64], sm[:, 3:4], None, ALU.mult, r=[po, sm], w=[otm])
        yield

    def stage_attn(self, l):
        with contextlib.ExitStack() as es:
            qf = self.sb(es, "aqf", [128, 2, TT], BF16)
            kf = self.sb(es, "akf", [128, 2, TT], BF16)
            vf = self.sb(es, "avf", [128, 2, TT], BF16)
            VT = self.sb(es, "aVT", [128, 36, 256], BF16)
            kctx = self.sb(es, "akctx", [128, 2, TC], BF16)
            vctx = self.sb(es, "avctx", [128, 2, 256], BF16)
            HK = self.sb(es, "aHK", [128, 4, 15, 64])
            BMc = [self.sb(es, f"aBM{c}", [128, 4, 640], BF16) for c in range(5)]
            FT = self.sb(es, "aFT", [64, 127])
            r31 = self.sb(es, "ar31", [64, 31])
            ocr = self.sb(es, "aocr", [128, 2, TT], BF16)
            otms = [self.sb(es, f"aotm{i}", [128, 256], BF16) for i in range(2)]
            W = dict(sm=[self.sb(es, f"asm{i}", [128, 4]) for i in range(4)],
                     pexp=[self.sb(es, f"apexp{i}", [128, 896], BF16) for i in range(2)],
                     pts=[self.sb(es, f"apts{i}", [128, 7, 128], BF16) for i in range(2)])
            fpb = Buf("FPb", None)
            for hp in range(2):
                self.dma(SP, qf[:, hp, :], self.PB[hp * 128:(hp + 1) * 128, :], w=[qf])
                self.dma(SP, kf[:, hp, :], self.PB[(2 + hp) * 128:(3 + hp) * 128, :], w=[kf])
                self.dma(SP, vf[:, hp, :], self.PB[(4 + hp) * 128:(5 + hp) * 128, :], w=[vf])
            self.dma(POOL, kctx[:], self.kcache[l].rearrange("a p t -> p a t"), w=[kctx])
            self.dma(POOL, vctx[:], self.vcache[l].rearrange("a p f -> p a f"), w=[vctx])
            for g in range(9):
                bk = self.bank[g % 2]
                bkb = bk.t.bitcast(BF16)
                for jj in range(4):
                    tile = g * 4 + jj
                    for hp in range(2):
                        self.tr(bkb[:, (jj * 2 + hp) * 128:(jj * 2 + hp + 1) * 128], vf[:, hp, tile * 128:(tile + 1) * 128],
                                self.identb, r=[vf, self.cstb], w=[bk])
                self.cp(DVE if g % 2 == 0 else ACT, VT[:, g * 4:(g + 1) * 4, :].rearrange("p a f -> p (a f)"), bkb[:, 0:1024], r=[bk], w=[VT])
            self.memset(POOL, FT[0:60, :], NEG, w=[FT])
            self.dma(SP, r31[0:60, :], self.rpb[l], w=[r31])
            self.act(FT[0:60, 48:79], r31[0:60, :], AF.Copy, r=[r31, FT], w=[FT], scale=8.0)
            self.dma(SP, self.FP[l], FT[0:60, :], r=[FT], w=[fpb])
            for half in range(2):
                src = bass.AP(tensor=self.FP.tensor, offset=l * 60 * 127, ap=[[1, 64], [127, 60], [1, 64]])
                self.dma(SP, HK[64 * half:64 * half + 64].rearrange("p h d k -> p (h d) k"), src, r=[fpb], w=[HK])
            self.tt(DVE, HK[:].rearrange("p h d k -> p (h d) k"), HK[:].rearrange("p h d k -> p (h d) k"),
                    self.cst[:, C_CMH, 0:64].unsqueeze(1).to_broadcast([128, 60, 64]), ALU.add, r=[HK, self.cst], w=[HK])
            engs = (POOL, DVE, ACT, POOL, DVE)
            for case, i in ((0, 0), (1, 1), (2, 2), (3, 30), (4, 31)):
                bm = BMc[case]
                eng = engs[case]
                self.memset(eng if eng != ACT else POOL, bm[:], NEG, w=[bm])
                start = min(max(2 * i - 4, 0), 54)
                for qr in range(2):
                    r_ = 2 * i + qr
                    rs = min(max(r_ - 4, 0), 56)
                    for kr in range(10):
                        krow = start + kr
                        if rs <= krow < rs + 8:
                            dr = krow - r_ + 7
                            self.cp(eng, bm[64 * qr:64 * qr + 64, :, kr * 64:(kr + 1) * 64], HK[64 * qr:64 * qr + 64, :, dr, :], r=[HK], w=[bm])
            W["qk"] = [qf, kf, kctx]
            W["v"] = [VT, vctx]
            ob7 = self.bank[7].t.bitcast(BF16)

            def qtile(t0, units):
                def gen(slot):
                    otm = otms[slot]
                    for (hp, b0, pieces, vts, h, bmb) in units:
                        W["bm"] = [bmb, self.cstb] if bmb is not None else [self.cstb]
                        yield from self.attn_unit(slot, qf, hp, b0, t0, pieces, vts, otm, h, W)
                    ob = self.bank[7]
                    for hp in range(2):
                        self.tr(ob7[:, hp * 128:(hp + 1) * 128], otm[:, hp * 128:(hp + 1) * 128], self.identb, r=[otm, self.cstb], w=[ob])
                    self.cp(ACT if slot == 0 else DVE, ocr[:, :, t0:t0 + 128], ob7[:, 0:256].rearrange("p (a b) -> p a b", a=2), r=[ob], w=[ocr])
                    yield
                return gen

            jobs = []
            for i in range(32):
                t0 = 128 * i
                start = min(max(2 * i - 4, 0), 54)
                k0 = 64 * start
                case = {0: 0, 1: 1, 30: 3, 31: 4}.get(i, 2)
                units = []
                for h in range(4):
                    hp, b0 = h // 2, 64 * (h % 2)
                    bm = BMc[case]
                    pieces = [(0, kf[b0:b0 + 64, hp, k0:k0 + 512], bm[:, h, 0:512], 512),
                              (512, kf[b0:b0 + 64, hp, k0 + 512:k0 + 640], bm[:, h, 512:640], 128),
                              (640, kctx[b0:b0 + 64, hp, :], None, 256)]
                    vts = [VT[:, start // 2 + c, h * 64:(h + 1) * 64] for c in range(5)] + [vctx[:, c, h * 64:(h + 1) * 64] for c in range(2)]
                    units.append((hp, b0, pieces, vts, h, bm))
                jobs.append(qtile(t0, units))
            for s_ in range(2):
                tok0 = TL + s_ * TC
                for qi in range(2):
                    t0 = tok0 + 128 * qi
                    units = []
                    for h in range(4):
                        hp, b0 = h // 2, 64 * (h % 2)
                        pieces = [(0, kf[b0:b0 + 64, hp, tok0:tok0 + 256], None, 256)]
                        vts = [VT[:, tok0 // 128 + c, h * 64:(h + 1) * 64] for c in range(2)]
                        units.append((hp, b0, pieces, vts, h, None))
                    jobs.append(qtile(t0, units))
            self.run_pool(jobs, 2)
            for hp in range(2):
                self.dma(SP, self.CAT[768 + hp * 128:768 + (hp + 1) * 128, :], ocr[:, hp, :], r=[ocr])

    @staticmethod
    def run_pool(jobs, nslots):
        jobs = iter(jobs)
        active = {}
        free = list(range(nslots))
        while True:
            while free:
                jb = next(jobs, None)
                if jb is None:
                    break
                sl = free.pop(0)
                active[sl] = jb(sl)
            if not active:
                break
            for sl in list(active):
                try:
                    next(active[sl])
                except StopIteration:
                    del active[sl]
                    free.append(sl)

    def stage_gdn(self, l):
        pvl = self.pv[:, l, :]
        bank = self.bank
        S = self.S
        with contextlib.ExitStack() as es:
            sb = lambda name, shape, dt=F32: self.sb(es, "g" + name, shape, dt)
            qf, kf, vf = sb("qf", [128, 4, TL], BF16), sb("kf", [128, 4, TL], BF16), sb("vf", [128, 4, TL], BF16)
            NPM = TL // 128
            GT = sb("GT", [128, NPM, 16])
            beta, nbeta, g, GC, kds, negc = (sb(n, [128, NPM, 8]) for n in ("beta", "nbeta", "g", "GC", "kds", "negc"))
            t1, t2 = sb("t1", [128, NPM, 8]), sb("t2", [128, NPM, 8])
            egl = sb("egl", [128, NPM, 2, 8])
            gsb = sb("gsb", [128, 16])
            nega = sb("nega", [128, 8])
            ident, identb, ones = self.ident, self.identb, self.ones
            cst = self.cst
            one_ap = self.cst[:, C_EPS, 1:2]
            X, Y, Z, Wk, V, A = bank[0], bank[1], bank[2], bank[3], bank[4], bank[5]
            B2 = [bank[6], bank[7]]
            Bap = self.pst[:, 6 * 512:8 * 512].rearrange("p (h a k) -> p h a k", h=4, a=2)
            Xb = X.t.bitcast(BF16)
            def v3(b):
                return b.t.rearrange("p (h k) -> p h k", h=4)
            def bc_h(ap2):
                return ap2.unsqueeze(1).to_broadcast([128, 4, 128])
            def bc_i(ap2):
                return ap2.unsqueeze(2).to_broadcast([128, 4, 128])
            self.dma(SP, gsb[:], self.gsc[l:l + 1, :].partition_broadcast(128), w=[gsb])
            self.act(nega[:], gsb[:, 0:8], AF.Exp, r=[gsb], w=[nega])
            self.ts(DVE, nega[:], nega[:], -1.0, None, ALU.mult, r=[nega], w=[nega])
            for si, (tok0, T, hcol0, vec) in enumerate(SEQS):
                NP_ = T // 128
                with contextlib.ExitStack() as es2:
                    sb2 = lambda name, shape, dt=F32: self.sb(es2, "g" + name, shape, dt)
                    raws = [sb2(f"raw{i}", [128, 2050]) for i in range(2)]
                    cvs = [sb2(f"cv{i}", [128, 2048]) for i in range(2)]
                    sqs = [sb2(f"sq{i}", [128, 512], BF16) for i in range(2)]
                    rss = [sb2(f"rs{i}", [128, 512]) for i in range(2)]
                    bd = sb2("bd", [16, 512])
                    ui = 0
                    tj = 0
                    for ti, (dst, pc0) in enumerate(((qf, 4), (kf, 8), (vf, 12))):
                        for h in range(4):
                            row0 = (pc0 + h) * 128
                            cw = PV_GCW + 3 * (ti * 4 + h)
                            for half in range(0, T, 2048):
                                n = min(2048, T - half)
                                raw, cv = raws[ui % 2], cvs[ui % 2]
                                ui += 1
                                a = max(half - 1, 0)
                                b_ = min(half + n + 1, T)
                                off = a - (half - 1)
                                if off > 0:
                                    self.memset(POOL, raw[:, 0:1], 0.0, w=[raw])
                                if b_ - (half - 1) < n + 2:
                                    self.memset(POOL, raw[:, n + 1:n + 2], 0.0, w=[raw])
                                self.dma(SP, raw[:, off:off + (b_ - a)], self.P[row0:row0 + 128, tok0 + a:tok0 + b_], w=[raw])
                                self.act(cv[:, 0:n], raw[:, 0:n], AF.Copy, r=[raw, self.pv], w=[cv], scale=pvl[:, cw:cw + 1])
                                self.stt(cv[:, 0:n], raw[:, 1:n + 1], pvl[:, cw + 1:cw + 2], cv[:, 0:n], ALU.mult, ALU.add, r=[raw, self.pv, cv], w=[cv])
                                self.stt(cv[:, 0:n], raw[:, 2:n + 2], pvl[:, cw + 2:cw + 3], cv[:, 0:n], ALU.mult, ALU.add, r=[raw, self.pv, cv], w=[cv])
                                self.act(cv[:, 0:n], cv[:, 0:n], AF.Silu, r=[cv], w=[cv])
                                if ti == 2:
                                    self.cp(POOL, dst[:, h, half:half + n], cv[:, 0:n], r=[cv], w=[dst])
                                else:
                                    for t in range(0, n, 512):
                                        m = min(512, n - t)
                                        sq, rs = sqs[tj % 2], rss[tj % 2]
                                        tj += 1
                                        ps = bank[tj % 4]
                                        self.act(sq[:, 0:m], cv[:, t:t + m], AF.Square, r=[cv], w=[sq])
                                        self.mm(ps[:, 0:m], self.onesb, sq[:, 0:m], r=[sq, self.cstb], w=[ps])
                                        self.rsqrt_ps(rs[:, 0:m], ps[:, 0:m], 1.0, r=[ps], w=[rs])
                                        if ti == 0:
                                            self.stt(dst[:, h, half + t:half + t + m], cv[:, t:t + m], 128.0 ** -0.5, rs[:, 0:m], ALU.mult, ALU.mult, r=[cv, rs], w=[dst])
                                        else:
                                            self.tt(DVE, dst[:, h, half + t:half + t + m], cv[:, t:t + m], rs[:, 0:m], ALU.mult, r=[cv, rs], w=[dst])
                    for jb in range(0, T, 512):
                        n = min(512, T - jb)
                        self.dma(SP, bd[:, 0:n], self.P[26 * 128:26 * 128 + 16, tok0 + jb:tok0 + jb + n], w=[bd])
                        for j in range(n // 128):
                            self.tr(X[:, j * 16:(j + 1) * 16], bd[0:16, j * 128:(j + 1) * 128], ident[0:16, 0:16], r=[bd, cst], w=[X])
                        self.cp(DVE, GT[:, jb // 128:jb // 128 + n // 128, :].rearrange("p a b -> p (a b)"), X[:, 0:(n // 128) * 16], r=[X], w=[GT])
                    NB = NP_ * 8
                    def f2(t):
                        return t[:, 0:NP_, :].rearrange("p a b -> p (a b)")
                    dtb = gsb[:, 8:16].unsqueeze(1).to_broadcast([128, NP_, 8])
                    self.act(beta[:, 0:NP_, :], GT[:, 0:NP_, 0:8], AF.Sigmoid, r=[GT], w=[beta])
                    self.ts(DVE, nbeta[:, 0:NP_, :], beta[:, 0:NP_, :], -1.0, None, ALU.mult, r=[beta], w=[nbeta])
                    self.tt(DVE, t1[:, 0:NP_, :], GT[:, 0:NP_, 8:16], dtb, ALU.add, r=[GT, gsb], w=[t1])
                    self.act(t2[:, 0:NP_, :], t1[:, 0:NP_, :], AF.Abs, r=[t1], w=[t2])
                    self.act(t2[:, 0:NP_, :], t2[:, 0:NP_, :], AF.Exp, r=[t2], w=[t2], scale=-1.0)
                    self.act(t2[:, 0:NP_, :], t2[:, 0:NP_, :], AF.Ln, r=[t2, cst], w=[t2], bias=one_ap, scale=1.0)
                    self.ts(DVE, t1[:, 0:NP_, :], t1[:, 0:NP_, :], 0.0, None, ALU.max, r=[t1], w=[t1])
                    self.tt(DVE, t1[:, 0:NP_, :], t1[:, 0:NP_, :], t2[:, 0:NP_, :], ALU.add, r=[t1, t2], w=[t1])
                    self.tt(DVE, g[:, 0:NP_, :], t1[:, 0:NP_, :], nega[:].unsqueeze(1).to_broadcast([128, NP_, 8]), ALU.mult, r=[t1, nega], w=[g])
                    g2 = f2(g)
                    self.mm(X[:, 0:NB], cst[:, C_TRIF, :], g2, r=[cst, g], w=[X])
                    self.mm(Y[:, 0:NB], cst[:, C_TRIB, :], g2, r=[cst, g], w=[Y])
                    self.mm(Z[:, 0:NB], cst[:, C_BLK, :], g2, r=[cst, g], w=[Z])
                    self.mm(Wk[:, 0:NB], cst[:, C_HALFA, :], g2, r=[cst, g], w=[Wk])
                    self.mm(V[:, 0:NB], cst[:, C_HALFB, :], g2, r=[cst, g], w=[V])
                    self.cp(DVE, GC[:, 0:NP_, 0:4], X[:, 0:NB].rearrange("p (a b) -> p a b", b=8)[:, :, 0:4], r=[X], w=[GC])
                    self.cp(DVE, GC[:, 0:NP_, 4:8], Y[:, 0:NB].rearrange("p (a b) -> p a b", b=8)[:, :, 4:8], r=[Y], w=[GC])
                    self.tt(DVE, f2(kds), Z[:, 0:NB], f2(GC), ALU.subtract, r=[Z, GC], w=[kds])
                    self.act(f2(kds), f2(kds), AF.Exp, r=[kds], w=[kds])
                    self.act(f2(t1), f2(GC), AF.Exp, r=[GC], w=[t1])
                    self.tt(DVE, f2(negc), f2(nbeta), f2(t1), ALU.mult, r=[nbeta, t1], w=[negc])
                    self.act(egl[:, 0:NP_, 0, :], Wk[:, 0:NB].rearrange("p (a b) -> p a b", b=8), AF.Exp, r=[Wk], w=[egl])
                    self.act(egl[:, 0:NP_, 1, :], V[:, 0:NB].rearrange("p (a b) -> p a b", b=8), AF.Exp, r=[V], w=[egl])
                S.barrier()
                with contextlib.ExitStack() as es3:
                    def mkset(d):
                        sb3 = lambda name, shape, dt=F32: self.sb(es3, f"g{d}" + name, shape, dt)
                        outs = [dict(Kd=sb3(f"Kd{i}", [128, 4, 128], BF16), Vb=sb3(f"Vb{i}", [128, 4, 128]), ITm=sb3(f"ITm{i}", [128, 4, 128], BF16),
                                     Yb=sb3(f"Yb{i}", [128, 4, 128], BF16), Qg=sb3(f"Qg{i}", [128, 4, 128], BF16)) for i in range(2)]
                        return dict(out=outs, GR=sb3("GR", [128, 4, 128]), D1=sb3("D1", [128, 4, 128]), D2=sb3("D2", [128, 4, 128]),
                                    Pm=sb3("Pm", [128, 4, 128]), PY=sb3("PY", [128, 4, 2, 128]),
                                    Rb=sb3("Rb", [128, 4, 128], BF16), VNb=sb3("VNb", [128, 4, 128], BF16), S=sb3("S", [128, 4, 128]),
                                    Sb=sb3("Sb", [128, 4, 128], BF16), OT=sb3("OT", [128, 4, 128]))
                    sets = [mkset(0), mkset(1)]
                    ofb = {}
                    done_pre = [0, 0]
                    done_scan = [0, 0]

                    def pre_chain(d):
                        def gen(slot):
                            W = sets[d]
                            GR, D1, D2, Pm, PY = (W[k] for k in ("GR", "D1", "D2", "Pm", "PY"))
                            DG = D1
                            m1c, m2c = (C_UINC, C_LSTR) if d == 0 else (C_LINC, C_USTR)
                            c0, c1, c2 = (bank[4 * d + q] for q in range(3))
                            X, Y, Z, Wk, V, A = c0, c1, c2, c0, c0, c2
                            B2 = [c0, c1]
                            Bap = self.pst[:, 4 * d * 512:(4 * d + 2) * 512].rearrange("p (h a k) -> p h a k", h=4, a=2)
                            Xb = X.t.bitcast(BF16)
                            order = list(range(NP_)) if d == 0 else list(range(NP_ - 1, -1, -1))
                            for idx, p in enumerate(order):
                                while idx - done_scan[d] >= 2:
                                    yield
                                O_ = W["out"][idx % 2]
                                Kd, Vb, ITm, Yb, Qg = (O_[k] for k in ("Kd", "Vb", "ITm", "Yb", "Qg"))
                                t0 = 128 * p
                                gcol = GC[:, p, 4 * d:4 * d + 4]
                                for h in range(4):
                                    self.tr(Xb[:, h * 128:(h + 1) * 128], kf[:, h, t0:t0 + 128], identb, r=[kf, self.cstb], w=[X])
                                    self.tr(Xb[:, (4 + h) * 128:(5 + h) * 128], vf[:, h, t0:t0 + 128], identb, r=[vf, self.cstb], w=[X])
                                self.tt(DVE, DG[:], bc_h(ident), bc_i(gcol), ALU.mult, r=[cst, GC], w=[DG])
                                for h in range(4):
                                    self.mm(Y[:, h * 128:(h + 1) * 128], kf[:, h, t0:t0 + 128], kf[:, h, t0:t0 + 128], r=[kf], w=[Y])
                                    self.mm(Z[:, h * 128:(h + 1) * 128], kf[:, h, t0:t0 + 128], qf[:, h, t0:t0 + 128], r=[kf, qf], w=[Z])
                                yield
                                self.tt(DVE, Kd[:], Xb[:, 0:512].rearrange("p (h k) -> p h k", h=4), bc_i(kds[:, p, 4 * d:4 * d + 4]), ALU.mult, r=[X, kds], w=[Kd])
                                self.tt(DVE, Vb[:], Xb[:, 512:1024].rearrange("p (h k) -> p h k", h=4), bc_i(beta[:, p, 4 * d:4 * d + 4]), ALU.mult, r=[X, beta], w=[Vb])
                                yield
                                self.mm(Wk[:, 0:512], ones, DG[:].rearrange("p h k -> p (h k)"), r=[cst, DG], w=[Wk])
                                yield
                                self.cp(ACT, GR[:].rearrange("p h k -> p (h k)"), Wk[:, 0:512], r=[Wk], w=[GR])
                                yield
                                self.tt(DVE, D1[:], GR[:], bc_i(gcol), ALU.subtract, r=[GR, GC], w=[D1])
                                self.tt(DVE, D2[:], bc_i(gcol), GR[:], ALU.subtract, r=[GR, GC], w=[D2])
                                yield
                                self.act(GR[:], GR[:], AF.Exp, r=[GR], w=[GR])
                                self.tt(POOL, D1[:], D1[:], bc_h(cst[:, m1c, :]), ALU.add, r=[D1, cst], w=[D1])
                                self.tt(POOL, D2[:], D2[:], bc_h(cst[:, m2c, :]), ALU.add, r=[D2, cst], w=[D2])
                                yield
                                self.tt(POOL, Qg[:], qf[:, :, t0:t0 + 128], GR[:], ALU.mult, r=[qf, GR], w=[Qg])
                                self.act(D2[:], D2[:], AF.Exp, r=[D2], w=[D2])
                                self.act(D1[:], D1[:], AF.Exp, r=[D1], w=[D1])
                                yield
                                for h in range(4):
                                    self.stt(Pm[:, h, :], Y[:, h * 128:(h + 1) * 128], nbeta[:, p, 4 * d + h:4 * d + h + 1], D2[:, h, :], ALU.mult, ALU.mult,
                                             r=[Y, nbeta, D2], w=[Pm])
                                self.tt(DVE, ITm[:], v3(Z), D1[:], ALU.mult, r=[Z, D1], w=[ITm])
                                yield
                                for h in range(4):
                                    self.tr(V[:, h * 128:(h + 1) * 128], Pm[:, h, :], ident, r=[Pm, cst], w=[V])
                                yield
                                self.cp(ACT, PY[:, :, 0, :], v3(V), r=[V], w=[PY])
                                yield
                                self.tt(POOL, PY[:, :, 1, :], PY[:, :, 0, :], bc_h(ident), ALU.add, r=[PY, cst], w=[PY])
                                for h in range(4):
                                    self.mm(A[:, h * 128:(h + 1) * 128], PY[:, h, 0, :], Pm[:, h, :], r=[PY, Pm], w=[A])
                                    self.mm(Bap[:, h, 0, :], Pm[:, h, :], PY[:, h, 0, :], r=[PY, Pm], w=B2)
                                yield
                                self.cp(ACT, Pm[:], v3(A), r=[A], w=[Pm])
                                self.cp(DVE, PY[:, :, 0, :], Bap[:, :, 0, :], r=B2, w=[PY])
                                yield
                                for k in range(1, 5):
                                    for h in range(4):
                                        self.mm(A[:, h * 128:(h + 1) * 128], PY[:, h, 0, :], Pm[:, h, :], r=[PY, Pm], w=[A])
                                        self.mm(Bap[:, h, :, :].rearrange("p a k -> p (a k)"), Pm[:, h, :], PY[:, h, :, :].rearrange("p a k -> p (a k)"), r=[PY, Pm], w=B2)
                                    yield
                                    self.cp(ACT, Pm[:], v3(A), r=[A], w=[Pm])
                                    self.cp(DVE, PY[:, :, 0, :], Bap[:, :, 0, :], r=B2, w=[PY])
                                    self.tt(DVE, PY[:, :, 1, :], PY[:, :, 1, :], Bap[:, :, 1, :], ALU.add, r=B2 + [PY], w=[PY])
                                    yield
                                for h in range(4):
                                    self.mm(A[:, h * 128:(h + 1) * 128], Pm[:, h, :], PY[:, h, 1, :], r=[PY, Pm], w=[A])
                                yield
                                self.tt(DVE, Yb[:], PY[:, :, 1, :], v3(A), ALU.add, r=[A, PY], w=[Yb])
                                done_pre[d] = idx + 1
                                yield
                        return gen

                    def scan_chain(d):
                        def gen(slot):
                            W = sets[d]
                            Rb, VNb, Sst, Sb, OT = (W[k] for k in ("Rb", "VNb", "S", "Sb", "OT"))
                            OFd = self.OF if d == 0 else self.OFB
                            C3 = bank[4 * d + 3]
                            self.memset(POOL, Rb[:], 0.0, w=[Rb])
                            self.memset(POOL, VNb[:], 0.0, w=[VNb])
                            if si == 0:
                                self.dma(SP, Sst[:], self.s0[l, :, 4 * d:4 * d + 4, :], w=[Sst])
                            else:
                                self.memset(POOL, Sst[:], 0.0, w=[Sst])
                            self.cp(POOL, Sb[:], Sst[:], r=[Sst], w=[Sb])
                            yield
                            order = list(range(NP_)) if d == 0 else list(range(NP_ - 1, -1, -1))
                            for idx, p in enumerate(order):
                                while done_pre[d] <= idx:
                                    yield
                                O_ = W["out"][idx % 2]
                                Kd, Vb, ITm, Yb, Qg = (O_[k] for k in ("Kd", "Vb", "ITm", "Yb", "Qg"))
                                t0 = 128 * p
                                for hf in ((0, 1) if d == 0 else (1, 0)):
                                    R = slice(64 * hf, 64 * hf + 64)
                                    cs = slice(64 * hf, 64 * hf + 64)
                                    for h in range(4):
                                        self.mm(C3[:, h * 128:(h + 1) * 128], kf[:, h, t0:t0 + 128], Sb[:, h, :], r=[kf, Sb], w=[C3])
                                    yield
                                    for h in range(4):
                                        self.stt(Rb[R, h, :], C3[R, h * 128:(h + 1) * 128], negc[R, p, 4 * d + h:4 * d + h + 1], Vb[R, h, :], ALU.mult, ALU.add,
                                                 r=[C3, negc, Vb], w=[Rb])
                                    yield
                                    for h in range(4):
                                        self.mm(C3[:, h * 128:(h + 1) * 128], Yb[:, h, :], Rb[:, h, :], r=[Yb, Rb], w=[C3])
                                    yield
                                    self.cp(ACT, VNb[R, :, :].rearrange("p h k -> p (h k)"), C3[R, 0:512], r=[C3], w=[VNb])
                                    yield
                                    for h in range(4):
                                        self.mm(C3[:, h * 128:(h + 1) * 128], Kd[R, h, :], VNb[R, h, :], r=[Kd, VNb], w=[C3])
                                    yield
                                    for h in range(4):
                                        self.stt(Sst[:, h, :], Sst[:, h, :], egl[:, p, hf, 4 * d + h:4 * d + h + 1], C3[:, h * 128:(h + 1) * 128], ALU.mult, ALU.add,
                                                 r=[Sst, egl, C3], w=[Sst])
                                    yield
                                    for h in range(4):
                                        self.mm(C3[:, h * 64:(h + 1) * 64], Sb[:, h, :], Qg[:, h, cs], start=True, stop=False, r=[Sb, Qg], w=[C3])
                                        self.mm(C3[:, h * 64:(h + 1) * 64], VNb[R, h, :], ITm[R, h, cs], start=False, stop=True, r=[VNb, ITm], w=[C3])
                                    self.cp(POOL, Sb[:], Sst[:], r=[Sst], w=[Sb])
                                    yield
                                    self.cp(ACT, OT[:, :, cs], C3[:, 0:256].rearrange("p (h k) -> p h k", h=4), r=[C3], w=[OT])
                                    yield
                                ofb[(d, p)] = Buf(f"of{d}_{p}", None)
                                self.dma(SP, OFd[:, tok0 + t0:tok0 + t0 + 128].rearrange("(h p) t -> p h t", p=128), OT[:], r=[OT], w=[ofb[(d, p)]])
                                done_scan[d] = idx + 1
                                yield
                            if si > 0:
                                self.dma(SP, self.stout[si - 1, l, d].rearrange("h k v -> k h v"), Sst[:], r=[Sst])
                        return gen

                    self.run_pool([pre_chain(0), pre_chain(1), scan_chain(0), scan_chain(1)], 4)
                    ep = [dict(ofl=self.sb(es3, f"gofl{i}", [128, 4, 128]), ofr=self.sb(es3, f"gofr{i}", [128, 4, 128]), zt=self.sb(es3, f"gzt{i}", [128, 4, 128]),
                               osq=self.sb(es3, f"gosq{i}", [128, 4, 128], BF16), ors=self.sb(es3, f"gors{i}", [128, 4, 128]),
                               ob=self.sb(es3, f"gob{i}", [128, 4, 128], BF16)) for i in range(2)]

                    def epilogue(p):
                        def gen(slot):
                            E = ep[slot]
                            ofl, ofr, zt, osq, ors, ob = (E[k] for k in ("ofl", "ofr", "zt", "osq", "ors", "ob"))
                            t0 = 128 * p
                            cols = slice(tok0 + t0, tok0 + t0 + 128)
                            self.dma(SP, ofl[:], self.OF[:, cols].rearrange("(h p) t -> p h t", p=128), r=[ofb[(0, p)]], w=[ofl])
                            self.dma(SP, ofr[:], self.OFB[:, cols].rearrange("(h p) t -> p h t", p=128), r=[ofb[(1, p)]], w=[ofr])
                            self.dma(SP, zt[:], self.P[16 * 128:20 * 128, cols].rearrange("(h p) t -> p h t", p=128), w=[zt])
                            yield
                            self.tt(POOL, ofl[:], ofl[:], ofr[:], ALU.add, r=[ofl, ofr], w=[ofl])
                            self.act(zt[:], zt[:], AF.Silu, r=[zt], w=[zt])
                            yield
                            self.act(osq[:], ofl[:], AF.Square, r=[ofl], w=[osq])
                            yield
                            pb_ = bank[2 + slot]
                            self.mm(pb_[:, 0:512], self.onesb, osq[:].rearrange("p h k -> p (h k)"), r=[osq, self.cstb], w=[pb_])
                            yield
                            self.act(ors[:].rearrange("p h k -> p (h k)"), pb_[:, 0:512], AF.Sqrt, r=[pb_, self.cst], w=[ors], bias=self.c_eps, scale=1.0 / 128)
                            yield
                            self.recip(ors[:], ors[:], r=[ors], w=[ors])
                            yield
                            self.tt(DVE, ofl[:], ofl[:], ors[:], ALU.mult, r=[ofl, ors], w=[ofl])
                            yield
                            self.stt(ob[:], ofl[:], pvl[:, PV_GNW:PV_GNW + 1], zt[:], ALU.mult, ALU.mult, r=[ofl, self.pv, zt], w=[ob])
                            yield
                            self.dma(SP, self.CAT[256:768, cols].rearrange("(h p) t -> p h t", p=128), ob[:], r=[ob])
                        return gen

                    self.run_pool([epilogue(p) for p in range(NP_)], 2)
                S.barrier()


def _prep_inputs(inp, nl=DEPTH):
    f = lambda a: np.ascontiguousarray(np.asarray(a, dtype=np.float32))
    inp = {k: f(v) for k, v in inp.items()}
    shared = {
        "cst": _consts(),
        "pv": _pack_pv(inp),
        "w_ada": inp["w_ada"][:nl],
        "w_in": np.ascontiguousarray(inp["w_in"][:nl][:, :, _WIN_PERM]),
        "w_out": inp["w_out"][:nl],
        "w_up": inp["w_up"][:nl],
        "w_down": inp["w_down"][:nl],
        "gsc": np.ascontiguousarray(np.concatenate([inp["gdn_a_log"].reshape(DEPTH, 8), inp["gdn_dt_bias"].reshape(DEPTH, 8)], axis=1)[:nl]),
        "rpb": np.ascontiguousarray(inp["na_rpb"].reshape(DEPTH, 60, 31)[:nl]),
    }
    maps = []
    for i in range(8):
        xin = np.concatenate([inp["x_sample"][i].T, inp["x_prompt"][2 * i].T, inp["x_prompt"][2 * i + 1].T], axis=1)
        cv = np.stack([inp["c_ctx"], inp["c"][i]], axis=1)
        cvec = cv.reshape(8, 128, 2).transpose(1, 0, 2).reshape(128, 16)
        kv = inp["cache_attn_kv"][i][:nl]
        kc = kv[:, 0].transpose(0, 1, 3, 2).reshape(nl, 2, 128, TC)
        vc = kv[:, 1].transpose(0, 2, 1, 3).reshape(nl, 2, 128, 256)
        s0 = inp["state_delta"][i][:nl].reshape(nl, 8, 128, 128).transpose(0, 2, 1, 3)
        m = dict(shared)
        m.update(xin=np.ascontiguousarray(xin), cvec=np.ascontiguousarray(cvec), kcache=np.ascontiguousarray(kc),
                 vcache=np.ascontiguousarray(vc), s0=np.ascontiguousarray(s0))
        maps.append(m)
    return maps


def _assemble(results, nl=DEPTH):
    y_prompt = np.zeros((16, TC, D), np.float32)
    y_sample = np.zeros((8, TL, D), np.float32)
    kvn = np.zeros((16, nl, 2, 4, TC, 64), np.float32)
    stn = np.zeros((16, nl, 2, 4, 128, 128), np.float32)
    for i, r in enumerate(results):
        xo = r["xout"]
        y_sample[i] = xo[:, 0:TL].T
        y_prompt[2 * i] = xo[:, TL:TL + TC].T
        y_prompt[2 * i + 1] = xo[:, TL + TC:TT].T
        kvo = r["kvout"]
        for s in range(2):
            kvn[2 * i + s] = kvo[:, s].reshape(nl, 2, 4, 64, TC).transpose(0, 1, 2, 4, 3)
            stn[2 * i + s] = r["stout"][s]
    return y_prompt, y_sample, kvn, stn


_CACHE = {}


def kernel(**inputs):
    if "kb" not in _CACHE:
        kb = KB()
        kb.build()
        _CACHE["kb"] = kb
    kb = _CACHE["kb"]
    maps = _prep_inputs(inputs)
    res = run_bass_kernel_spmd(kb.nc, maps, core_ids=list(range(8)))
    return _assemble(res.results)
```

```python
import contextlib
import numpy as np
import concourse.bass as bass
import concourse.mybir as mybir
from concourse.bass_utils import run_bass_kernel_spmd

F32 = mybir.dt.float32
BF16 = mybir.dt.bfloat16
F32R = mybir.dt.float32r
AF = mybir.ActivationFunctionType
ALU = mybir.AluOpType
AX = mybir.AxisListType

PE, ACT, DVE, POOL, SP = "pe", "act", "dve", "pool", "sp"
COMPUTE = (PE, ACT, DVE, POOL)
ALLENG = (PE, ACT, DVE, POOL, SP)
DMAQ = (SP, ACT, POOL)
ND = 4
SAME_ENGINE_SYNC = True

D = 1024
DEPTH = 4
TL = 4096
TC = 256
TT = TL + 2 * TC
NCH = 27
PROJ = 3344
DFF = 2816
EPS = 1e-6
NEG = -30000.0
NEGBIG = -1.0e5
HW_ = TT + 6
SEQS = ((0, TL, 1, 1), (TL, TC, 4099, 0), (TL + TC, TC, 4357, 0))


class Buf:
    __slots__ = ("name", "t", "lw", "rd")

    def __init__(self, name, t):
        self.name = name
        self.t = t
        self.lw = None
        self.rd = []

    def __getitem__(self, k):
        return self.t[k]


class Op:
    __slots__ = ("eng", "fn", "deps", "marked", "is_dma", "sem", "cnt", "epoch", "kind")

    def __init__(self, eng, fn, is_dma, epoch, kind="op"):
        self.eng = eng
        self.fn = fn
        self.deps = []
        self.marked = False
        self.is_dma = is_dma
        self.sem = None
        self.cnt = 0
        self.epoch = epoch
        self.kind = kind


class Sched:
    def __init__(self, nc):
        self.nc = nc
        self.ops = []
        self.epoch = 0
        self.eng = {PE: nc.tensor, ACT: nc.scalar, DVE: nc.vector, POOL: nc.gpsimd, SP: nc.sync}

    def op(self, eng, fn, reads=(), writes=(), dma=False):
        o = Op(eng, fn, dma, self.epoch)
        deps = {}
        for b in reads:
            if b.lw is not None:
                deps[id(b.lw)] = (b.lw, "raw")
        for b in writes:
            if b.lw is not None:
                deps[id(b.lw)] = (b.lw, "waw")
            for r in b.rd:
                if id(r) not in deps:
                    deps[id(r)] = (r, "war")
        for d, kind in deps.values():
            if d.epoch != o.epoch or d is o:
                continue
            if (not d.is_dma) and d.eng == eng and not dma:
                if eng == PE or kind == "war" or not SAME_ENGINE_SYNC:
                    continue
            o.deps.append(d)
        for b in reads:
            b.rd.append(o)
        for b in writes:
            b.lw = o
            b.rd = []
        self.ops.append(o)
        return o

    def barrier(self, new_epoch=False):
        self.ops.append(Op(None, None, False, self.epoch, kind="barrier_ep" if new_epoch else "barrier"))
        if new_epoch:
            self.epoch += 1

    def emit(self, es):
        nc = self.nc
        nep = self.epoch + 1
        for o in self.ops:
            for d in o.deps:
                d.marked = True
        sems = []
        for ep in range(nep):
            d = {}
            for e in COMPUTE:
                d[e] = es.enter_context(nc.semaphore(f"s{ep}_{e}"))
            for q in DMAQ:
                for k in range(ND):
                    d[(q, k)] = es.enter_context(nc.semaphore(f"d{ep}_{q}{k}"))
            sems.append(d)
        cur = [dict((k, 0) for k in sems[ep]) for ep in range(nep)]
        rr = dict((q, 0) for q in DMAQ)
        seen = {}
        n_wait = 0
        for o in self.ops:
            ep = o.epoch
            if o.kind != "op":
                for e in ALLENG:
                    for key, val in cur[ep].items():
                        if val > 0 and seen.get((e, key), 0) < val:
                            self.eng[e].wait_ge(sems[ep][key], val)
                            seen[(e, key)] = val
                            n_wait += 1
                if o.kind == "barrier_ep":
                    seen = {}
                continue
            e = self.eng[o.eng]
            need = {}
            for d in o.deps:
                if need.get(d.sem, 0) < d.cnt:
                    need[d.sem] = d.cnt
            for key, val in need.items():
                if seen.get((o.eng, key), 0) >= val:
                    continue
                e.wait_ge(sems[ep][key], val)
                n_wait += 1
                seen[(o.eng, key)] = val
            inst = o.fn(e)
            if o.is_dma:
                key = (o.eng, rr[o.eng] % ND)
                rr[o.eng] += 1
                cur[ep][key] += 16
                o.sem, o.cnt = key, cur[ep][key]
                inst.then_inc(sems[ep][key], 16)
            elif o.marked:
                cur[ep][o.eng] += 1
                o.sem, o.cnt = o.eng, cur[ep][o.eng]
                inst.then_inc(sems[ep][o.eng], 1)
        ep = nep - 1
        for e in ALLENG:
            for key, val in cur[ep].items():
                if val > 0:
                    self.eng[e].wait_ge(sems[ep][key], val)
        return dict(n_ops=len(self.ops), n_wait=n_wait, counts=cur)


C_IDENT, C_ONES, C_JREV, C_TRIF, C_TRIB, C_BLK, C_HALFA, C_HALFB = range(8)
C_UINC, C_LSTR, C_LINC, C_USTR = 8, 9, 10, 11
C_CMH = 12
C_EPS = 13
NCST = 14


def _consts():
    p = np.arange(128)[:, None]
    f = np.arange(128)[None, :]
    same = (p // 64) == (f // 64)
    c = np.zeros((128, NCST, 128), np.float32)
    c[:, C_IDENT] = (p == f)
    c[:, C_ONES] = 1.0
    c[:, C_JREV] = same & ((p % 64) == (63 - f % 64))
    c[:, C_TRIF] = same & (p <= f)
    c[:, C_TRIB] = same & (p >= f)
    c[:, C_BLK] = same
    c[:, C_HALFA] = (p < 64) & (f >= 0)
    c[:, C_HALFB] = (p >= 64) & (f >= 0)
    c[:, C_UINC] = np.where(same & (f >= p), 0.0, NEGBIG)
    c[:, C_LSTR] = np.where(same & (p > f), 0.0, NEGBIG)
    c[:, C_LINC] = np.where(same & (f <= p), 0.0, NEGBIG)
    c[:, C_USTR] = np.where(same & (p < f), 0.0, NEGBIG)
    qc = 63 - (np.arange(128) % 64)
    cs = np.clip(qc - 8, 0, 48)
    kc = np.arange(64)[None, :]
    ok = (kc >= cs[:, None]) & (kc < cs[:, None] + 16)
    c[:, C_CMH, 0:64] = np.where(ok, 0.0, NEG)
    c[:, C_EPS, 0] = EPS
    c[:, C_EPS, 1] = 1.0
    return c.reshape(128, NCST * 128)


PV_GPM, PV_GPOM, PV_GPF, PV_GPOF = 0, 8, 16, 24
PV_BADA = 32
PV_CW = 80
PV_CB, PV_CLG, PV_CLB = 142, 144, 146
PV_GCW = 148
PV_GNW = 184
PV_FCW = 185
PV_FCB = 317
PV_N = 361


def _fm(v, nchunk):
    return np.ascontiguousarray(v.reshape(nchunk, 128).T)


def _pack_pv(inp):
    pv = np.zeros((128, DEPTH, PV_N), np.float32)
    for l in range(DEPTH):
        pv[:, l, PV_GPM:PV_GPM + 8] = _fm(inp["g_pre_mix"][l], 8)
        pv[:, l, PV_GPOM:PV_GPOM + 8] = _fm(inp["g_post_mix"][l], 8)
        pv[:, l, PV_GPF:PV_GPF + 8] = _fm(inp["g_pre_ffn"][l], 8)
        pv[:, l, PV_GPOF:PV_GPOF + 8] = _fm(inp["g_post_ffn"][l], 8)
        pv[:, l, PV_BADA:PV_BADA + 48] = _fm(inp["b_ada"][l], 48)
        cw = inp["conv_w"][l]
        pv[:, l, PV_CW:PV_CW + 62] = cw.T.reshape(2, 128, 31).transpose(1, 0, 2).reshape(128, 62)
        pv[:, l, PV_CB:PV_CB + 2] = _fm(inp["conv_b"][l], 2)
        pv[:, l, PV_CLG:PV_CLG + 2] = _fm(inp["conv_ln_g"][l], 2)
        pv[:, l, PV_CLB:PV_CLB + 2] = _fm(inp["conv_ln_b"][l], 2)
        gw = inp["gdn_conv_w"][l]
        pv[:, l, PV_GCW:PV_GCW + 36] = gw.T.reshape(12, 128, 3).transpose(1, 0, 2).reshape(128, 36)
        pv[:, l, PV_GNW] = inp["gdn_norm_w"][l]
        fw = inp["ffn_conv_w"][l]
        pv[:, l, PV_FCW:PV_FCW + 132] = fw.T.reshape(44, 128, 3).transpose(1, 0, 2).reshape(128, 132)
        pv[:, l, PV_FCB:PV_FCB + 44] = _fm(inp["ffn_conv_b"][l], 44)
    return pv.reshape(128, DEPTH * PV_N)


_WIN_PERM = np.concatenate([np.arange(0, 2560), np.arange(2576, 3344), np.arange(2560, 2576)])


class KB:
    def __init__(self, nlayers=DEPTH, dbg=None):
        self.nl = nlayers
        self.dbg = dbg or {}
        nc = self.nc = bass.Bass("TRN2", target_bir_lowering=False)
        self.S = Sched(nc)
        self.es = contextlib.ExitStack()
        self.uid = 0

    def mm(self, out, lhsT, rhs, start=True, stop=True, r=(), w=()):
        return self.S.op(PE, lambda e: e.matmul(out, lhsT, rhs, start=start, stop=stop), r, w)

    def tr(self, out, in_, ident, r=(), w=()):
        return self.S.op(PE, lambda e: e.transpose(out, in_, ident), r, w)

    def act(self, out, in_, func, r=(), w=(), eng=ACT, **kw):
        return self.S.op(eng, lambda e: e.activation(out=out, in_=in_, func=func, **kw), r, w)

    def tt(self, eng, out, in0, in1, op, r=(), w=()):
        return self.S.op(eng, lambda e: e.tensor_tensor(out=out, in0=in0, in1=in1, op=op), r, w)

    def ts(self, eng, out, in0, s1, s2, op0, op1=None, r=(), w=()):
        if op1 is None:
            return self.S.op(eng, lambda e: e.tensor_scalar(out=out, in0=in0, scalar1=s1, scalar2=None, op0=op0), r, w)
        return self.S.op(eng, lambda e: e.tensor_scalar(out=out, in0=in0, scalar1=s1, scalar2=s2, op0=op0, op1=op1), r, w)

    def stt(self, out, in0, scalar, in1, op0, op1, r=(), w=()):
        return self.S.op(DVE, lambda e: e.scalar_tensor_tensor(out=out, in0=in0, scalar=scalar, in1=in1, op0=op0, op1=op1), r, w)

    def cp(self, eng, out, in_, r=(), w=()):
        if eng == ACT:
            return self.S.op(ACT, lambda e: e.copy(out=out, in_=in_), r, w)
        return self.S.op(eng, lambda e: e.tensor_copy(out=out, in_=in_), r, w)

    def memset(self, eng, ap, val, w=()):
        return self.S.op(eng, lambda e: e.memset(ap, val), (), w)

    def recip(self, out, in_, r=(), w=()):
        return self.S.op(DVE, lambda e: e.reciprocal(out=out, in_=in_), r, w)

    def dma(self, q, out, in_, r=(), w=()):
        return self.S.op(q, lambda e: e.dma_start(out=out, in_=in_), r, w, dma=True)

    def sb(self, es, name, shape, dt=F32):
        self.uid += 1
        return Buf(name, es.enter_context(self.nc.sbuf_tensor(f"{name}_{self.uid}", shape, dt)))

    def dram(self, name, shape, dt, kind="Internal"):
        if name in self.dbg.get("out", ()):
            kind = "ExternalOutput"
        if name in self.dbg.get("in", ()):
            kind = "ExternalInput"
        return self.nc.dram_tensor(name, shape, dt, kind=kind).ap()

    def rsqrt_ps(self, out, ps_ap, scale, r, w):
        self.act(out, ps_ap, AF.Sqrt, r=list(r) + [self.cst], w=w, bias=self.c_eps, scale=scale)
        self.recip(out, out, r=w, w=w)

    def build(self):
        nc, S, es = self.nc, self.S, self.es
        nl = self.nl
        I = {}
        def ext(name, shape, dt=F32):
            I[name] = nc.dram_tensor(name, shape, dt, kind="ExternalInput").ap()
            return I[name]
        self.xin = ext("xin", [D, TT])
        self.cvec = ext("cvec", [128, 16])
        self.cst_d = ext("cst", [128, NCST * 128])
        self.pv_d = ext("pv", [128, DEPTH * PV_N])
        self.w_ada = ext("w_ada", [nl, D, 6 * D])
        self.w_in = ext("w_in", [nl, D, PROJ])
        self.w_out = ext("w_out", [nl, D, D])
        self.w_up = ext("w_up", [nl, D, 2 * DFF])
        self.w_down = ext("w_down", [nl, DFF, D])
        self.kcache = ext("kcache", [nl, 2, 128, TC])
        self.vcache = ext("vcache", [nl, 2, 128, 256])
        self.s0 = ext("s0", [nl, 128, 8, 128])
        self.gsc = ext("gsc", [nl, 16])
        self.rpb = ext("rpb", [nl, 60, 31])
        self.xout = nc.dram_tensor("xout", [D, TT], F32, kind="ExternalOutput").ap()
        self.kvout = nc.dram_tensor("kvout", [nl, 2, 2, 256, TC], F32, kind="ExternalOutput").ap()
        self.stout = nc.dram_tensor("stout", [2, nl, 2, 4, 128, 128], F32, kind="ExternalOutput").ap()
        self.P = self.dram("P", [NCH * 128, TT], F32)
        self.PB = self.dram("PB", [6 * 128, TT], BF16)
        self.CAT = self.dram("CAT", [D, TT], BF16)
        self.XA = self.dram("XA", [D, TT], F32)
        self.A = self.dram("A", [DFF, TT], BF16)
        self.OF = self.dram("OF", [512, TT], F32)
        self.OFB = self.dram("OFB", [512, TT], F32)
        self.FP = self.dram("FP", [nl, 60, 127], F32)

        with es:
            pst = es.enter_context(nc.psum_tensor("ps", [128, 4096], F32))
            self.bank = [Buf(f"bank{i}", pst[:, i * 512:(i + 1) * 512]) for i in range(8)]
            self.pst = pst
            self.cst = self.sb(es, "cst", [128, NCST, 128])
            self.cstb = self.sb(es, "cstb", [128, 3, 128], BF16)
            self.pv = self.sb(es, "pv", [128, DEPTH, PV_N])
            self.mod = self.sb(es, "mod", [128, 6, 8, 2])
            self.scv = self.sb(es, "scv", [128, 8, 2])
            self.c_eps = self.cst[:, C_EPS, 0:1]
            self.dma(SP, self.cst[:].rearrange("p a b -> p (a b)"), self.cst_d[:, :], w=[self.cst])
            self.dma(SP, self.pv[:].rearrange("p a b -> p (a b)"), self.pv_d[:, :], w=[self.pv])
            self.dma(SP, self.scv[:].rearrange("p a b -> p (a b)"), self.cvec[:, :], w=[self.scv])
            for i, ci in enumerate((C_IDENT, C_ONES, C_JREV)):
                self.cp(DVE, self.cstb[:, i, :], self.cst[:, ci, :], r=[self.cst], w=[self.cstb])
            self.ident = self.cst[:, C_IDENT, :]
            self.ones = self.cst[:, C_ONES, :]
            self.identb = self.cstb[:, 0, :]
            self.onesb = self.cstb[:, 1, :]
            self.jrevb = self.cstb[:, 2, :]
            self.act(self.scv[:], self.scv[:], AF.Silu, r=[self.scv], w=[self.scv])
            for l in range(nl):
                self.layer(l)
                if l + 1 < nl:
                    S.barrier(new_epoch=True)
            info = S.emit(es)
        return info

    def layer(self, l):
        S = self.S
        stages = self.dbg.get("stages", ("ada", "s1", "conf", "gdn", "attn", "s3", "s4a", "s4b"))
        xsrc = self.xin if l == 0 else self.xout
        if "ada" in stages:
            self.stage_ada(l)
            S.barrier()
        if "s1" in stages:
            self.stage_s1(l, xsrc)
            S.barrier()
        if "conf" in stages:
            self.stage_conf(l)
            S.barrier()
        if "attn" in stages:
            self.stage_attn(l)
            S.barrier()
        if "gdn" in stages:
            self.stage_gdn(l)
            S.barrier()
        if "s3" in stages:
            with contextlib.ExitStack() as es:
                self.Hb = self.sb(es, "H", [128, 8, HW_], BF16)
                self.stage_s3(l, xsrc, es)
                S.barrier()
                if "s4a" in stages:
                    self.stage_s4a(l)
                    S.barrier()
        if "s4b" in stages:
            self.stage_s4b(l)
            S.barrier()

    def stage_ada(self, l):
        with contextlib.ExitStack() as es:
            wb = [self.sb(es, f"adaw{i}", [128, 8, 512]) for i in range(6)]
            raw = self.sb(es, "adaraw", [128, 48, 2])
            ps = self.bank[0]
            for g in range(12):
                w = wb[g % 6]
                self.dma(SP, w[:],
                         self.w_ada[l, :, g * 512:(g + 1) * 512].rearrange("(kc p) n -> p kc n", p=128), w=[w])
                for j in range(4):
                    n = g * 4 + j
                    for kc in range(8):
                        self.mm(ps[:, 2 * n:2 * n + 2], w[:, kc, j * 128:(j + 1) * 128], self.scv[:, kc, :],
                                start=(kc == 0), stop=(kc == 7), r=[w, self.scv], w=[ps])
            pvl = self.pv[:, l, :]
            self.tt(DVE, raw[:], ps[:, 0:96].rearrange("p (n v) -> p n v", v=2),
                    pvl[:, PV_BADA:PV_BADA + 48].unsqueeze(2).to_broadcast([128, 48, 2]), ALU.add,
                    r=[ps, self.pv], w=[raw])
            def gain(off):
                return pvl[:, off:off + 8].unsqueeze(2).to_broadcast([128, 8, 2])
            m = self.mod
            self.ts(DVE, m[:, 0], raw[:, 8:16, :], 1.0, None, ALU.add, r=[raw], w=[m])
            self.tt(DVE, m[:, 0], m[:, 0], gain(PV_GPM), ALU.mult, r=[m, self.pv], w=[m])
            self.cp(DVE, m[:, 1], raw[:, 0:8, :], r=[raw], w=[m])
            self.tt(DVE, m[:, 2], raw[:, 16:24, :], gain(PV_GPOM), ALU.mult, r=[raw, self.pv], w=[m])
            self.ts(DVE, m[:, 3], raw[:, 32:40, :], 1.0, None, ALU.add, r=[raw], w=[m])
            self.tt(DVE, m[:, 3], m[:, 3], gain(PV_GPF), ALU.mult, r=[m, self.pv], w=[m])
            self.cp(DVE, m[:, 4], raw[:, 24:32, :], r=[raw], w=[m])
            self.tt(DVE, m[:, 5], raw[:, 40:48, :], gain(PV_GPOF), ALU.mult, r=[raw, self.pv], w=[m])

    def sumsq_rstd(self, src, sq, ps, rstd, n, nchunk=8, scale=1.0 / D, src_bufs=()):
        for c in range(nchunk):
            self.act(sq[:, c, 0:n], src[:, c, 0:n], AF.Square, r=list(src_bufs), w=[sq], eng=ACT)
        for c in range(nchunk):
            self.mm(ps[:, 0:n], self.onesb, sq[:, c, 0:n], start=(c == 0), stop=(c == nchunk - 1),
                    r=[sq, self.cstb], w=[ps])
        self.rsqrt_ps(rstd[:, 0:n], ps[:, 0:n], scale, r=[ps], w=[rstd])

    def s1_tiles(self):
        tl = [(j * 512, 512, 1, [(1 + j * 512, 512)]) for j in range(8)]
        tl.append((TL, 512, 0, [(4099, 256), (4357, 256)]))
        return tl

    def stage_s1(self, l, xsrc):
        with contextlib.ExitStack() as es:
            H = self.sb(es, "H", [128, 8, HW_], BF16)
            xt = [self.sb(es, f"xt{i}", [128, 8, 512]) for i in range(2)]
            sq = self.sb(es, "sq", [128, 8, 512], BF16)
            rstd = self.sb(es, "rstd", [128, 512])
            wck = [self.sb(es, f"wck{i}", [128, 8, 128], BF16) for i in range(3)]
            ev = [self.sb(es, f"ev{i}", [128, 512]) for i in range(4)]
            evb = [self.sb(es, f"evb{i}", [128, 512], BF16) for i in range(2)]
            tiles = self.s1_tiles()
            m = self.mod
            for ti, (t0, n, vec, hc) in enumerate(tiles):
                x = xt[ti % 2]
                self.dma(SP, x[:], xsrc[:, t0:t0 + n].rearrange("(c p) t -> p c t", p=128), w=[x])
                ps = self.bank[ti % 2]
                self.sumsq_rstd(x, sq, ps, rstd, n, src_bufs=[x])
                self.tt(DVE, x[:], x[:], rstd[:].unsqueeze(1).to_broadcast([128, 8, 512]), ALU.mult, r=[x, rstd], w=[x])
                for c in range(8):
                    off = 0
                    for (h0, nn) in hc:
                        eng = ACT if c % 2 == 0 else POOL
                        if eng == ACT:
                            self.act(H[:, c, h0:h0 + nn], x[:, c, off:off + nn], AF.Identity, r=[x, m], w=[H],
                                     scale=m[:, 0, c, vec:vec + 1], bias=m[:, 1, c, vec:vec + 1])
                        else:
                            self.ts(POOL, H[:, c, h0:h0 + nn], x[:, c, off:off + nn], m[:, 0, c, vec:vec + 1],
                                    m[:, 1, c, vec:vec + 1], ALU.mult, ALU.add, r=[x, m], w=[H])
                        off += nn
            k = 0
            for nci in range(NCH):
                ncols = 128 if nci < 26 else 16
                w = wck[nci % 3]
                self.dma(POOL, w[:, :, 0:ncols],
                         self.w_in[l, :, nci * 128:nci * 128 + ncols].rearrange("(kc p) n -> p kc n", p=128), w=[w])
                for ti, (t0, n, vec, hc) in enumerate(tiles):
                    ps = self.bank[2 + (k % 4)]
                    off = 0
                    for (h0, nn) in hc:
                        for kc in range(8):
                            self.mm(ps[0:ncols, off:off + nn], w[:, kc, 0:ncols], H[:, kc, h0:h0 + nn],
                                    start=(kc == 0), stop=(kc == 7), r=[w, H], w=[ps])
                        off += nn
                    e = ev[k % 4]
                    if k % 2 == 0:
                        self.cp(ACT, e[0:ncols, :], ps[0:ncols, :], r=[ps], w=[e])
                    else:
                        self.cp(DVE, e[0:ncols, :], ps[0:ncols, :], r=[ps], w=[e])
                    self.dma(SP, self.P[nci * 128:nci * 128 + ncols, t0:t0 + n], e[0:ncols, :], r=[e])
                    if 20 <= nci < 26:
                        eb = evb[k % 2]
                        self.cp(POOL, eb[:], e[:], r=[e], w=[eb])
                        self.dma(SP, self.PB[(nci - 20) * 128:(nci - 19) * 128, t0:t0 + n], eb[:], r=[eb])
                    if ti == 8 and 22 <= nci < 26:
                        kv = (nci - 22) // 2
                        f0 = ((nci - 22) % 2) * 128
                        for s in range(2):
                            self.dma(SP, self.kvout[l, s, kv, f0:f0 + 128, :], e[:, s * 256:(s + 1) * 256], r=[e])
                    k += 1

    def stage_s3(self, l, xsrc, es0):
        H = self.Hb
        with contextlib.ExitStack() as es:
            wo = self.sb(es, "wo", [128, 8, D], BF16)
            cat = [self.sb(es, f"cat{i}", [128, 8, 512], BF16) for i in range(2)]
            xt = [self.sb(es, f"xt{i}", [128, 8, 512]) for i in range(2)]
            msb = self.sb(es, "msb", [128, 8, 512])
            sq = self.sb(es, "sq", [128, 8, 512], BF16)
            rstd = self.sb(es, "rstd", [128, 512])
            zc = self.sb(es, "zc", [128, 8, 2], BF16)
            parts = self.dbg.get("s3parts", "wpx")
            if "w" in parts:
                self.dma(POOL, wo[:], self.w_out[l].rearrange("(kc p) n -> p kc n", p=128), w=[wo])
            if "p" in parts:
                self.memset(POOL, zc[:], 0.0, w=[zc])
                for c0, n0 in ((0, 1), (4097, 2), (4355, 2), (4613, 1)):
                    self.cp(POOL, H[:, :, c0:c0 + n0], zc[:, :, 0:n0], r=[zc], w=[H])
            if "x" not in parts:
                return
            m = self.mod
            lvl = int(self.dbg.get("s3n", 9))
            for ti, (t0, n, vec, hc) in enumerate(self.s1_tiles()):
                ct = cat[ti % 2]
                x = xt[ti % 2]
                self.dma(SP, ct[:], self.CAT[:, t0:t0 + n].rearrange("(c p) t -> p c t", p=128), w=[ct])
                self.dma(SP, x[:], xsrc[:, t0:t0 + n].rearrange("(c p) t -> p c t", p=128), w=[x])
                if lvl < 2:
                    continue
                for nn in range(8):
                    ps = self.bank[nn % 4]
                    for kc in range(8):
                        self.mm(ps[:, 0:n], wo[:, kc, nn * 128:(nn + 1) * 128], ct[:, kc, :],
                                start=(kc == 0), stop=(kc == 7), r=[wo, ct], w=[ps])
                    if "c" in self.dbg.get("s3l2", "cs"):
                        self.cp(DVE, msb[:, nn, :], ps[:, 0:n], r=[ps], w=[msb])
                    if "s" in self.dbg.get("s3l2", "cs"):
                        self.act(sq[:, nn, :], msb[:, nn, :], AF.Square, r=[msb], w=[sq])
                if lvl < 3:
                    continue
                pss = self.bank[4 + ti % 2]
                for c in range(8):
                    self.mm(pss[:, 0:n], self.onesb, sq[:, c, :], start=(c == 0), stop=(c == 7), r=[sq, self.cstb], w=[pss])
                self.rsqrt_ps(rstd[:, 0:n], pss[:, 0:n], 1.0 / D, r=[pss], w=[rstd])
                self.tt(DVE, msb[:], msb[:], rstd[:].unsqueeze(1).to_broadcast([128, 8, 512]), ALU.mult, r=[msb, rstd], w=[msb])
                if lvl < 4:
                    continue
                for c in range(8):
                    self.stt(x[:, c, :], msb[:, c, :], m[:, 2, c, vec:vec + 1], x[:, c, :], ALU.mult, ALU.add,
                             r=[msb, m, x], w=[x])
                self.dma(SP, self.XA[:, t0:t0 + n].rearrange("(c p) t -> p c t", p=128), x[:], r=[x])
                if lvl < 5:
                    continue
                pss2 = self.bank[6 + ti % 2]
                self.sumsq_rstd(x, sq, pss2, rstd, n, src_bufs=[x])
                self.tt(DVE, msb[:], x[:], rstd[:].unsqueeze(1).to_broadcast([128, 8, 512]), ALU.mult, r=[x, rstd], w=[msb])
                if lvl < 6:
                    continue
                for c in range(8):
                    off = 0
                    for (h0, nn2) in hc:
                        if c % 2 == 0:
                            self.act(H[:, c, h0:h0 + nn2], msb[:, c, off:off + nn2], AF.Identity, r=[msb, m], w=[H],
                                     scale=m[:, 3, c, vec:vec + 1], bias=m[:, 4, c, vec:vec + 1])
                        else:
                            self.ts(POOL, H[:, c, h0:h0 + nn2], msb[:, c, off:off + nn2], m[:, 3, c, vec:vec + 1],
                                    m[:, 4, c, vec:vec + 1], ALU.mult, ALU.add, r=[msb, m], w=[H])
                        off += nn2

    def ffn_tiles(self):
        tl = []
        t = 0
        while t < TL:
            n = min(456, TL - t)
            tl.append((t, n, t))
            t += n
        tl.append((TL, TC, 4098))
        tl.append((TL + TC, TC, 4356))
        return tl

    def stage_s4a(self, l):
        H = self.Hb
        with contextlib.ExitStack() as es:
            wg = [self.sb(es, f"wg{i}", [128, 8, 128], BF16) for i in range(2)]
            wv = [self.sb(es, f"wv{i}", [128, 8, 128], BF16) for i in range(2)]
            tg = [self.sb(es, f"tg{i}", [128, 512]) for i in range(2)]
            tv = [self.sb(es, f"tv{i}", [128, 512]) for i in range(2)]
            ao = [self.sb(es, f"ao{i}", [128, 512], BF16) for i in range(3)]
            pvl = self.pv[:, l, :]
            k = 0
            for j in range(22):
                g, v = wg[j % 2], wv[j % 2]
                self.dma(POOL, g[:], self.w_up[l, :, j * 128:(j + 1) * 128].rearrange("(kc p) n -> p kc n", p=128), w=[g])
                self.dma(POOL, v[:], self.w_up[l, :, (22 + j) * 128:(23 + j) * 128].rearrange("(kc p) n -> p kc n", p=128), w=[v])
                for (t0, n, h0) in self.ffn_tiles():
                    pg = self.bank[(2 * k) % 8]
                    pv_ = self.bank[(2 * k + 1) % 8]
                    for kc in range(8):
                        self.mm(pg[:, 0:n + 2], g[:, kc, :], H[:, kc, h0:h0 + n + 2], start=(kc == 0), stop=(kc == 7), r=[g, H], w=[pg])
                    for kc in range(8):
                        self.mm(pv_[:, 0:n + 2], v[:, kc, :], H[:, kc, h0:h0 + n + 2], start=(kc == 0), stop=(kc == 7), r=[v, H], w=[pv_])
                    a, b_ = tg[k % 2], tv[k % 2]
                    for (ps, t, ch) in ((pg, a, j), (pv_, b_, 22 + j)):
                        cw = PV_FCW + 3 * ch
                        self.act(t[:, 0:n], ps[:, 0:n], AF.Identity, r=[ps, self.pv], w=[t],
                                 scale=pvl[:, cw:cw + 1], bias=pvl[:, PV_FCB + ch:PV_FCB + ch + 1])
                        self.stt(t[:, 0:n], ps[:, 1:n + 1], pvl[:, cw + 1:cw + 2], t[:, 0:n], ALU.mult, ALU.add, r=[ps, self.pv, t], w=[t])
                        self.stt(t[:, 0:n], ps[:, 2:n + 2], pvl[:, cw + 2:cw + 3], t[:, 0:n], ALU.mult, ALU.add, r=[ps, self.pv, t], w=[t])
                    self.act(a[:, 0:n], a[:, 0:n], AF.Silu, r=[a], w=[a])
                    o = ao[k % 3]
                    self.tt(POOL, o[:, 0:n], a[:, 0:n], b_[:, 0:n], ALU.mult, r=[a, b_], w=[o])
                    self.dma(SP, self.A[j * 128:(j + 1) * 128, t0:t0 + n], o[:, 0:n], r=[o])
                    k += 1

    def stage_s4b(self, l):
        with contextlib.ExitStack() as es:
            wd = self.sb(es, "wd", [128, 22, D], BF16)
            at = [self.sb(es, f"at{i}", [128, 22, 512], BF16) for i in range(2)]
            xa = [self.sb(es, f"xa{i}", [128, 8, 512]) for i in range(2)]
            fsb = self.sb(es, "fsb", [128, 8, 512])
            sq = self.sb(es, "sq", [128, 8, 512], BF16)
            rstd = self.sb(es, "rstd", [128, 512])
            for h in range(2):
                self.dma(POOL, wd[:, h * 11:(h + 1) * 11, :],
                         self.w_down[l, h * 11 * 128:(h + 1) * 11 * 128, :].rearrange("(j p) n -> p j n", p=128), w=[wd])
            m = self.mod
            for ti, (t0, n, vec, hc) in enumerate(self.s1_tiles()):
                a = at[ti % 2]
                x = xa[ti % 2]
                self.dma(SP, a[:], self.A[:, t0:t0 + n].rearrange("(j p) t -> p j t", p=128), w=[a])
                self.dma(SP, x[:], self.XA[:, t0:t0 + n].rearrange("(c p) t -> p c t", p=128), w=[x])
                for nn in range(8):
                    ps = self.bank[nn % 4]
                    for j in range(22):
                        self.mm(ps[:, 0:n], wd[:, j, nn * 128:(nn + 1) * 128], a[:, j, :], start=(j == 0), stop=(j == 21), r=[wd, a], w=[ps])
                    self.cp(DVE, fsb[:, nn, :], ps[:, 0:n], r=[ps], w=[fsb])
                    self.act(sq[:, nn, :], fsb[:, nn, :], AF.Square, r=[fsb], w=[sq])
                pss = self.bank[4 + ti % 2]
                for c in range(8):
                    self.mm(pss[:, 0:n], self.onesb, sq[:, c, :], start=(c == 0), stop=(c == 7), r=[sq, self.cstb], w=[pss])
                self.rsqrt_ps(rstd[:, 0:n], pss[:, 0:n], 1.0 / D, r=[pss], w=[rstd])
                self.tt(DVE, fsb[:], fsb[:], rstd[:].unsqueeze(1).to_broadcast([128, 8, 512]), ALU.mult, r=[fsb, rstd], w=[fsb])
                for c in range(8):
                    self.stt(x[:, c, :], fsb[:, c, :], m[:, 5, c, vec:vec + 1], x[:, c, :], ALU.mult, ALU.add, r=[fsb, m, x], w=[x])
                self.dma(SP, self.xout[:, t0:t0 + n].rearrange("(c p) t -> p c t", p=128), x[:], r=[x])

    def stage_conf(self, l):
        pvl = self.pv[:, l, :]
        with contextlib.ExitStack() as es:
            av = [self.sb(es, f"cav{c}", [128, TL]) for c in range(2)]
            ag = [self.sb(es, f"cag{c}", [128, TL]) for c in range(2)]
            U = [self.sb(es, f"cU{c}", [128, TL + 30], BF16) for c in range(2)]
            DGc = [self.sb(es, f"cDG{c}", [128, 31, 128], BF16) for c in range(2)]
            for c in range(2):
                for kk in range(31):
                    col = PV_CW + 31 * c + kk
                    if c == 0:
                        self.ts(POOL, DGc[c][:, kk, :], self.ident, pvl[:, col:col + 1], None, ALU.mult, r=[self.cst, self.pv], w=[DGc[c]])
                    else:
                        self.act(DGc[c][:, kk, :], self.ident, AF.Copy, r=[self.cst, self.pv], w=[DGc[c]], scale=pvl[:, col:col + 1])
            kconv = 0
            sq = [self.sb(es, f"csq{c}", [128, 512]) for c in range(2)]
            st = self.sb(es, "cst", [128, 3, 512])
            yb = [self.sb(es, f"cy{c}", [128, 512]) for c in range(2)]
            ob = [self.sb(es, f"cob{i}", [128, 512], BF16) for i in range(4)]
            k = 0
            for (tok0, T, hcol0, vec) in SEQS:
                for c in range(2):
                    self.dma(SP, av[c][:, 0:T], self.P[c * 128:(c + 1) * 128, tok0:tok0 + T], w=[av[c]])
                    self.dma(SP, ag[c][:, 0:T], self.P[(2 + c) * 128:(3 + c) * 128, tok0:tok0 + T], w=[ag[c]])
                    self.memset(POOL, U[c][:, 0:15], 0.0, w=[U[c]])
                    self.memset(POOL, U[c][:, 15 + T:30 + T], 0.0, w=[U[c]])
                    self.act(ag[c][:, 0:T], ag[c][:, 0:T], AF.Sigmoid, r=[ag[c]], w=[ag[c]])
                    self.tt(POOL, U[c][:, 15:15 + T], av[c][:, 0:T], ag[c][:, 0:T], ALU.mult, r=[av[c], ag[c]], w=[U[c]])
                    acc = av[c]
                    for t in range(0, T, 512):
                        n = min(512, T - t)
                        psb = self.bank[6 + kconv % 2]
                        kconv += 1
                        for kk in range(31):
                            self.mm(psb[:, 0:n], DGc[c][:, kk, :], U[c][:, t + kk:t + kk + n], start=(kk == 0), stop=(kk == 30),
                                    r=[DGc[c], U[c]], w=[psb])
                        self.act(acc[:, t:t + n], psb[:, 0:n], AF.Identity, r=[psb, self.pv], w=[acc],
                                 bias=pvl[:, PV_CB + c:PV_CB + c + 1], scale=1.0)
                for t in range(0, T, 512):
                    n = min(512, T - t)
                    p1 = self.bank[(2 * k) % 6]
                    p2 = self.bank[(2 * k + 1) % 6]
                    for c in range(2):
                        self.act(sq[c][:, 0:n], av[c][:, t:t + n], AF.Square, r=[av[c]], w=[sq[c]])
                    for c in range(2):
                        self.mm(p1[:, 0:n], self.ones, av[c][:, t:t + n], start=(c == 0), stop=(c == 1), r=[self.cst, av[c]], w=[p1])
                    for c in range(2):
                        self.mm(p2[:, 0:n], self.ones, sq[c][:, 0:n], start=(c == 0), stop=(c == 1), r=[self.cst, sq[c]], w=[p2])
                    self.ts(DVE, st[:, 0, 0:n], p1[:, 0:n], 1.0 / 256, None, ALU.mult, r=[p1], w=[st])
                    self.tt(DVE, st[:, 1, 0:n], st[:, 0, 0:n], st[:, 0, 0:n], ALU.mult, r=[st], w=[st])
                    self.stt(st[:, 1, 0:n], p2[:, 0:n], 1.0 / 256, st[:, 1, 0:n], ALU.mult, ALU.subtract, r=[p2, st], w=[st])
                    self.ts(DVE, st[:, 1, 0:n], st[:, 1, 0:n], 0.0, None, ALU.max, r=[st], w=[st])
                    self.act(st[:, 2, 0:n], st[:, 1, 0:n], AF.Sqrt, r=[st, self.cst], w=[st], bias=self.c_eps, scale=1.0)
                    self.recip(st[:, 2, 0:n], st[:, 2, 0:n], r=[st], w=[st])
                    for c in range(2):
                        self.tt(DVE, yb[c][:, 0:n], av[c][:, t:t + n], st[:, 0, 0:n], ALU.subtract, r=[av[c], st], w=[yb[c]])
                        self.tt(POOL, yb[c][:, 0:n], yb[c][:, 0:n], st[:, 2, 0:n], ALU.mult, r=[yb[c], st], w=[yb[c]])
                        o = ob[(2 * k + c) % 4]
                        self.act(o[:, 0:n], yb[c][:, 0:n], AF.Silu, r=[yb[c], self.pv], w=[o],
                                 scale=pvl[:, PV_CLG + c:PV_CLG + c + 1], bias=pvl[:, PV_CLB + c:PV_CLB + c + 1])
                        self.dma(SP, self.CAT[c * 128:(c + 1) * 128, tok0 + t:tok0 + t + n], o[:, 0:n], r=[o])
                    k += 1

    def attn_unit(self, slot, qf, hp, b0, t0, pieces, vts, otm, h, W):
        nk = sum(p[3] for p in pieces)
        nkt = nk // 128
        base = slot * 1024
        ps = self.pst[:, base:base + nk]
        pb = [self.bank[2 * slot], self.bank[2 * slot + 1]]
        lhsT = qf[b0:b0 + 64, hp, t0:t0 + 128]
        for (col, rhs, bm, n) in pieces:
            self.mm(self.pst[:, base + col:base + col + n], lhsT, rhs, start=True, stop=(bm is None), r=W["qk"], w=pb)
            if bm is not None:
                self.mm(self.pst[:, base + col:base + col + n], self.jrevb, bm, start=False, stop=True, r=W["bm"], w=pb)
        sm = W["sm"][slot * 2 + h % 2]
        pexp = W["pexp"][slot]
        pts = W["pts"][slot]
        yield
        self.S.op(DVE, lambda e: e.reduce_max(out=sm[:, 0:1], in_=ps, axis=AX.X), pb, [sm])
        yield
        self.ts(DVE, sm[:, 1:2], sm[:, 0:1], -0.125, None, ALU.mult, r=[sm], w=[sm])
        self.memset(DVE, sm[:, 2:3], 0.0, w=[sm])
        yield
        self.act(pexp[:, 0:nk], ps, AF.Exp, r=pb + [sm], w=[pexp, sm], scale=0.125, bias=sm[:, 1:2], accum_out=sm[:, 2:3])
        yield
        ptbank = self.bank[4 + slot]
        ptb = ptbank.t.bitcast(BF16)
        for c in range(nkt):
            self.tr(ptb[:, c * 128:(c + 1) * 128], pexp[:, c * 128:(c + 1) * 128], self.identb, r=[pexp, self.cstb], w=[ptbank])
        yield
        self.cp(ACT if slot == 0 else DVE, pts[:, 0:nkt, :].rearrange("p a b -> p (a b)"), ptb[:, 0:nk], r=[ptbank], w=[pts])
        self.recip(sm[:, 3:4], sm[:, 2:3], r=[sm], w=[sm])
        yield
        po = self.bank[6]
        for c in range(nkt):
            self.mm(po[:, 0:64], pts[:, c, :], vts[c], start=(c == 0), stop=(c == nkt - 1), r=[pts] + W["v"], w=[po])
        self.ts(DVE, otm[:, h * 64:(h + 1) * 64], po[:, 0:64], sm[:, 3:4], None, ALU.mult, r=[po, sm], w=[otm])
        yield

    def stage_attn(self, l):
        with contextlib.ExitStack() as es:
            qf = self.sb(es, "aqf", [128, 2, TT], BF16)
            kf = self.sb(es, "akf", [128, 2, TT], BF16)
            vf = self.sb(es, "avf", [128, 2, TT], BF16)
            VT = self.sb(es, "aVT", [128, 36, 256], BF16)
            kctx = self.sb(es, "akctx", [128, 2, TC], BF16)
            vctx = self.sb(es, "avctx", [128, 2, 256], BF16)
            HK = self.sb(es, "aHK", [128, 4, 15, 64])
            BMc = [self.sb(es, f"aBM{c}", [128, 4, 640], BF16) for c in range(5)]
            FT = self.sb(es, "aFT", [64, 127])
            r31 = self.sb(es, "ar31", [64, 31])
            ocr = self.sb(es, "aocr", [128, 2, TT], BF16)
            otms = [self.sb(es, f"aotm{i}", [128, 256], BF16) for i in range(2)]
            W = dict(sm=[self.sb(es, f"asm{i}", [128, 4]) for i in range(4)],
                     pexp=[self.sb(es, f"apexp{i}", [128, 896], BF16) for i in range(2)],
                     pts=[self.sb(es, f"apts{i}", [128, 7, 128], BF16) for i in range(2)])
            fpb = Buf("FPb", None)
            for hp in range(2):
                self.dma(SP, qf[:, hp, :], self.PB[hp * 128:(hp + 1) * 128, :], w=[qf])
                self.dma(SP, kf[:, hp, :], self.PB[(2 + hp) * 128:(3 + hp) * 128, :], w=[kf])
                self.dma(SP, vf[:, hp, :], self.PB[(4 + hp) * 128:(5 + hp) * 128, :], w=[vf])
            self.dma(POOL, kctx[:], self.kcache[l].rearrange("a p t -> p a t"), w=[kctx])
            self.dma(POOL, vctx[:], self.vcache[l].rearrange("a p f -> p a f"), w=[vctx])
            for g in range(9):
                bk = self.bank[g % 2]
                bkb = bk.t.bitcast(BF16)
                for jj in range(4):
                    tile = g * 4 + jj
                    for hp in range(2):
                        self.tr(bkb[:, (jj * 2 + hp) * 128:(jj * 2 + hp + 1) * 128], vf[:, hp, tile * 128:(tile + 1) * 128],
                                self.identb, r=[vf, self.cstb], w=[bk])
                self.cp(DVE if g % 2 == 0 else ACT, VT[:, g * 4:(g + 1) * 4, :].rearrange("p a f -> p (a f)"), bkb[:, 0:1024], r=[bk], w=[VT])
            self.memset(POOL, FT[0:60, :], NEG, w=[FT])
            self.dma(SP, r31[0:60, :], self.rpb[l], w=[r31])
            self.act(FT[0:60, 48:79], r31[0:60, :], AF.Copy, r=[r31, FT], w=[FT], scale=8.0)
            self.dma(SP, self.FP[l], FT[0:60, :], r=[FT], w=[fpb])
            for half in range(2):
                src = bass.AP(tensor=self.FP.tensor, offset=l * 60 * 127, ap=[[1, 64], [127, 60], [1, 64]])
                self.dma(SP, HK[64 * half:64 * half + 64].rearrange("p h d k -> p (h d) k"), src, r=[fpb], w=[HK])
            self.tt(DVE, HK[:].rearrange("p h d k -> p (h d) k"), HK[:].rearrange("p h d k -> p (h d) k"),
                    self.cst[:, C_CMH, 0:64].unsqueeze(1).to_broadcast([128, 60, 64]), ALU.add, r=[HK, self.cst], w=[HK])
            engs = (POOL, DVE, ACT, POOL, DVE)
            for case, i in ((0, 0), (1, 1), (2, 2), (3, 30), (4, 31)):
                bm = BMc[case]
                eng = engs[case]
                self.memset(eng if eng != ACT else POOL, bm[:], NEG, w=[bm])
                start = min(max(2 * i - 4, 0), 54)
                for qr in range(2):
                    r_ = 2 * i + qr
                    rs = min(max(r_ - 4, 0), 56)
                    for kr in range(10):
                        krow = start + kr
                        if rs <= krow < rs + 8:
                            dr = krow - r_ + 7
                            self.cp(eng, bm[64 * qr:64 * qr + 64, :, kr * 64:(kr + 1) * 64], HK[64 * qr:64 * qr + 64, :, dr, :], r=[HK], w=[bm])
            W["qk"] = [qf, kf, kctx]
            W["v"] = [VT, vctx]
            ob7 = self.bank[7].t.bitcast(BF16)

            def qtile(t0, units):
                def gen(slot):
                    otm = otms[slot]
                    for (hp, b0, pieces, vts, h, bmb) in units:
                        W["bm"] = [bmb, self.cstb] if bmb is not None else [self.cstb]
                        yield from self.attn_unit(slot, qf, hp, b0, t0, pieces, vts, otm, h, W)
                    ob = self.bank[7]
                    for hp in range(2):
                        self.tr(ob7[:, hp * 128:(hp + 1) * 128], otm[:, hp * 128:(hp + 1) * 128], self.identb, r=[otm, self.cstb], w=[ob])
                    self.cp(ACT if slot == 0 else DVE, ocr[:, :, t0:t0 + 128], ob7[:, 0:256].rearrange("p (a b) -> p a b", a=2), r=[ob], w=[ocr])
                    yield
                return gen

            jobs = []
            for i in range(32):
                t0 = 128 * i
                start = min(max(2 * i - 4, 0), 54)
                k0 = 64 * start
                case = {0: 0, 1: 1, 30: 3, 31: 4}.get(i, 2)
                units = []
                for h in range(4):
                    hp, b0 = h // 2, 64 * (h % 2)
                    bm = BMc[case]
                    pieces = [(0, kf[b0:b0 + 64, hp, k0:k0 + 512], bm[:, h, 0:512], 512),
                              (512, kf[b0:b0 + 64, hp, k0 + 512:k0 + 640], bm[:, h, 512:640], 128),
                              (640, kctx[b0:b0 + 64, hp, :], None, 256)]
                    vts = [VT[:, start // 2 + c, h * 64:(h + 1) * 64] for c in range(5)] + [vctx[:, c, h * 64:(h + 1) * 64] for c in range(2)]
                    units.append((hp, b0, pieces, vts, h, bm))
                jobs.append(qtile(t0, units))
            for s_ in range(2):
                tok0 = TL + s_ * TC
                for qi in range(2):
                    t0 = tok0 + 128 * qi
                    units = []
                    for h in range(4):
                        hp, b0 = h // 2, 64 * (h % 2)
                        pieces = [(0, kf[b0:b0 + 64, hp, tok0:tok0 + 256], None, 256)]
                        vts = [VT[:, tok0 // 128 + c, h * 64:(h + 1) * 64] for c in range(2)]
                        units.append((hp, b0, pieces, vts, h, None))
                    jobs.append(qtile(t0, units))
            self.run_pool(jobs, 2)
            for hp in range(2):
                self.dma(SP, self.CAT[768 + hp * 128:768 + (hp + 1) * 128, :], ocr[:, hp, :], r=[ocr])

    @staticmethod
    def run_pool(jobs, nslots):
        jobs = iter(jobs)
        active = {}
        free = list(range(nslots))
        while True:
            while free:
                jb = next(jobs, None)
                if jb is None:
                    break
                sl = free.pop(0)
                active[sl] = jb(sl)
            if not active:
                break
            for sl in list(active):
                try:
                    next(active[sl])
                except StopIteration:
                    del active[sl]
                    free.append(sl)

    def stage_gdn(self, l):
        pvl = self.pv[:, l, :]
        bank = self.bank
        S = self.S
        with contextlib.ExitStack() as es:
            sb = lambda name, shape, dt=F32: self.sb(es, "g" + name, shape, dt)
            qf, kf, vf = sb("qf", [128, 4, TL], BF16), sb("kf", [128, 4, TL], BF16), sb("vf", [128, 4, TL], BF16)
            NPM = TL // 128
            GT = sb("GT", [128, NPM, 16])
            beta, nbeta, g, GC, kds, negc = (sb(n, [128, NPM, 8]) for n in ("beta", "nbeta", "g", "GC", "kds", "negc"))
            t1, t2 = sb("t1", [128, NPM, 8]), sb("t2", [128, NPM, 8])
            egl = sb("egl", [128, NPM, 2, 8])
            gsb = sb("gsb", [128, 16])
            nega = sb("nega", [128, 8])
            ident, identb, ones = self.ident, self.identb, self.ones
            cst = self.cst
            one_ap = self.cst[:, C_EPS, 1:2]
            X, Y, Z, Wk, V, A = bank[0], bank[1], bank[2], bank[3], bank[4], bank[5]
            B2 = [bank[6], bank[7]]
            Bap = self.pst[:, 6 * 512:8 * 512].rearrange("p (h a k) -> p h a k", h=4, a=2)
            Xb = X.t.bitcast(BF16)
            def v3(b):
                return b.t.rearrange("p (h k) -> p h k", h=4)
            def bc_h(ap2):
                return ap2.unsqueeze(1).to_broadcast([128, 4, 128])
            def bc_i(ap2):
                return ap2.unsqueeze(2).to_broadcast([128, 4, 128])
            self.dma(SP, gsb[:], self.gsc[l:l + 1, :].partition_broadcast(128), w=[gsb])
            self.act(nega[:], gsb[:, 0:8], AF.Exp, r=[gsb], w=[nega])
            self.ts(DVE, nega[:], nega[:], -1.0, None, ALU.mult, r=[nega], w=[nega])
            for si, (tok0, T, hcol0, vec) in enumerate(SEQS):
                NP_ = T // 128
                with contextlib.ExitStack() as es2:
                    sb2 = lambda name, shape, dt=F32: self.sb(es2, "g" + name, shape, dt)
                    raws = [sb2(f"raw{i}", [128, 2050]) for i in range(2)]
                    cvs = [sb2(f"cv{i}", [128, 2048]) for i in range(2)]
                    sqs = [sb2(f"sq{i}", [128, 512], BF16) for i in range(2)]
                    rss = [sb2(f"rs{i}", [128, 512]) for i in range(2)]
                    bd = sb2("bd", [16, 512])
                    ui = 0
                    tj = 0
                    for ti, (dst, pc0) in enumerate(((qf, 4), (kf, 8), (vf, 12))):
                        for h in range(4):
                            row0 = (pc0 + h) * 128
                            cw = PV_GCW + 3 * (ti * 4 + h)
                            for half in range(0, T, 2048):
                                n = min(2048, T - half)
                                raw, cv = raws[ui % 2], cvs[ui % 2]
                                ui += 1
                                a = max(half - 1, 0)
                                b_ = min(half + n + 1, T)
                                off = a - (half - 1)
                                if off > 0:
                                    self.memset(POOL, raw[:, 0:1], 0.0, w=[raw])
                                if b_ - (half - 1) < n + 2:
                                    self.memset(POOL, raw[:, n + 1:n + 2], 0.0, w=[raw])
                                self.dma(SP, raw[:, off:off + (b_ - a)], self.P[row0:row0 + 128, tok0 + a:tok0 + b_], w=[raw])
                                self.act(cv[:, 0:n], raw[:, 0:n], AF.Copy, r=[raw, self.pv], w=[cv], scale=pvl[:, cw:cw + 1])
                                self.stt(cv[:, 0:n], raw[:, 1:n + 1], pvl[:, cw + 1:cw + 2], cv[:, 0:n], ALU.mult, ALU.add, r=[raw, self.pv, cv], w=[cv])
                                self.stt(cv[:, 0:n], raw[:, 2:n + 2], pvl[:, cw + 2:cw + 3], cv[:, 0:n], ALU.mult, ALU.add, r=[raw, self.pv, cv], w=[cv])
                                self.act(cv[:, 0:n], cv[:, 0:n], AF.Silu, r=[cv], w=[cv])
                                if ti == 2:
                                    self.cp(POOL, dst[:, h, half:half + n], cv[:, 0:n], r=[cv], w=[dst])
                                else:
                                    for t in range(0, n, 512):
                                        m = min(512, n - t)
                                        sq, rs = sqs[tj % 2], rss[tj % 2]
                                        tj += 1
                                        ps = bank[tj % 4]
                                        self.act(sq[:, 0:m], cv[:, t:t + m], AF.Square, r=[cv], w=[sq])
                                        self.mm(ps[:, 0:m], self.onesb, sq[:, 0:m], r=[sq, self.cstb], w=[ps])
                                        self.rsqrt_ps(rs[:, 0:m], ps[:, 0:m], 1.0, r=[ps], w=[rs])
                                        if ti == 0:
                                            self.stt(dst[:, h, half + t:half + t + m], cv[:, t:t + m], 128.0 ** -0.5, rs[:, 0:m], ALU.mult, ALU.mult, r=[cv, rs], w=[dst])
                                        else:
                                            self.tt(DVE, dst[:, h, half + t:half + t + m], cv[:, t:t + m], rs[:, 0:m], ALU.mult, r=[cv, rs], w=[dst])
                    for jb in range(0, T, 512):
                        n = min(512, T - jb)
                        self.dma(SP, bd[:, 0:n], self.P[26 * 128:26 * 128 + 16, tok0 + jb:tok0 + jb + n], w=[bd])
                        for j in range(n // 128):
                            self.tr(X[:, j * 16:(j + 1) * 16], bd[0:16, j * 128:(j + 1) * 128], ident[0:16, 0:16], r=[bd, cst], w=[X])
                        self.cp(DVE, GT[:, jb // 128:jb // 128 + n // 128, :].rearrange("p a b -> p (a b)"), X[:, 0:(n // 128) * 16], r=[X], w=[GT])
                    NB = NP_ * 8
                    def f2(t):
                        return t[:, 0:NP_, :].rearrange("p a b -> p (a b)")
                    dtb = gsb[:, 8:16].unsqueeze(1).to_broadcast([128, NP_, 8])
                    self.act(beta[:, 0:NP_, :], GT[:, 0:NP_, 0:8], AF.Sigmoid, r=[GT], w=[beta])
                    self.ts(DVE, nbeta[:, 0:NP_, :], beta[:, 0:NP_, :], -1.0, None, ALU.mult, r=[beta], w=[nbeta])
                    self.tt(DVE, t1[:, 0:NP_, :], GT[:, 0:NP_, 8:16], dtb, ALU.add, r=[GT, gsb], w=[t1])
                    self.act(t2[:, 0:NP_, :], t1[:, 0:NP_, :], AF.Abs, r=[t1], w=[t2])
                    self.act(t2[:, 0:NP_, :], t2[:, 0:NP_, :], AF.Exp, r=[t2], w=[t2], scale=-1.0)
                    self.act(t2[:, 0:NP_, :], t2[:, 0:NP_, :], AF.Ln, r=[t2, cst], w=[t2], bias=one_ap, scale=1.0)
                    self.ts(DVE, t1[:, 0:NP_, :], t1[:, 0:NP_, :], 0.0, None, ALU.max, r=[t1], w=[t1])
                    self.tt(DVE, t1[:, 0:NP_, :], t1[:, 0:NP_, :], t2[:, 0:NP_, :], ALU.add, r=[t1, t2], w=[t1])
                    self.tt(DVE, g[:, 0:NP_, :], t1[:, 0:NP_, :], nega[:].unsqueeze(1).to_broadcast([128, NP_, 8]), ALU.mult, r=[t1, nega], w=[g])
                    g2 = f2(g)
                    self.mm(X[:, 0:NB], cst[:, C_TRIF, :], g2, r=[cst, g], w=[X])
                    self.mm(Y[:, 0:NB], cst[:, C_TRIB, :], g2, r=[cst, g], w=[Y])
                    self.mm(Z[:, 0:NB], cst[:, C_BLK, :], g2, r=[cst, g], w=[Z])
                    self.mm(Wk[:, 0:NB], cst[:, C_HALFA, :], g2, r=[cst, g], w=[Wk])
                    self.mm(V[:, 0:NB], cst[:, C_HALFB, :], g2, r=[cst, g], w=[V])
                    self.cp(DVE, GC[:, 0:NP_, 0:4], X[:, 0:NB].rearrange("p (a b) -> p a b", b=8)[:, :, 0:4], r=[X], w=[GC])
                    self.cp(DVE, GC[:, 0:NP_, 4:8], Y[:, 0:NB].rearrange("p (a b) -> p a b", b=8)[:, :, 4:8], r=[Y], w=[GC])
                    self.tt(DVE, f2(kds), Z[:, 0:NB], f2(GC), ALU.subtract, r=[Z, GC], w=[kds])
                    self.act(f2(kds), f2(kds), AF.Exp, r=[kds], w=[kds])
                    self.act(f2(t1), f2(GC), AF.Exp, r=[GC], w=[t1])
                    self.tt(DVE, f2(negc), f2(nbeta), f2(t1), ALU.mult, r=[nbeta, t1], w=[negc])
                    self.act(egl[:, 0:NP_, 0, :], Wk[:, 0:NB].rearrange("p (a b) -> p a b", b=8), AF.Exp, r=[Wk], w=[egl])
                    self.act(egl[:, 0:NP_, 1, :], V[:, 0:NB].rearrange("p (a b) -> p a b", b=8), AF.Exp, r=[V], w=[egl])
                S.barrier()
                with contextlib.ExitStack() as es3:
                    def mkset(d):
                        sb3 = lambda name, shape, dt=F32: self.sb(es3, f"g{d}" + name, shape, dt)
                        outs = [dict(Kd=sb3(f"Kd{i}", [128, 4, 128], BF16), Vb=sb3(f"Vb{i}", [128, 4, 128]), ITm=sb3(f"ITm{i}", [128, 4, 128], BF16),
                                     Yb=sb3(f"Yb{i}", [128, 4, 128], BF16), Qg=sb3(f"Qg{i}", [128, 4, 128], BF16)) for i in range(2)]
                        return dict(out=outs, GR=sb3("GR", [128, 4, 128]), D1=sb3("D1", [128, 4, 128]), D2=sb3("D2", [128, 4, 128]),
                                    Pm=sb3("Pm", [128, 4, 128]), PY=sb3("PY", [128, 4, 2, 128]),
                                    PmR=sb3("PmR", [128, 4, 128], F32R), PYR=sb3("PYR", [128, 4, 2, 128], F32R),
                                    Rb=sb3("Rb", [128, 4, 128], BF16), VNb=sb3("VNb", [128, 4, 128], BF16), S=sb3("S", [128, 4, 128]),
                                    Sb=sb3("Sb", [128, 4, 128], BF16), OT=sb3("OT", [128, 4, 128]))
                    sets = [mkset(0), mkset(1)]
                    ofb = {}
                    done_pre = [0, 0]
                    done_scan = [0, 0]

                    def pre_chain(d):
                        def gen(slot):
                            W = sets[d]
                            GR, D1, D2, Pm, PY = (W[k] for k in ("GR", "D1", "D2", "Pm", "PY"))
                            PmR, PYR = W["PmR"], W["PYR"]
                            DG = D1
                            m1c, m2c = (C_UINC, C_LSTR) if d == 0 else (C_LINC, C_USTR)
                            c0, c1, c2 = (bank[4 * d + q] for q in range(3))
                            X, Y, Z, Wk, V, A = c0, c1, c2, c0, c0, c2
                            B2 = [c0, c1]
                            Bap = self.pst[:, 4 * d * 512:(4 * d + 2) * 512].rearrange("p (h a k) -> p h a k", h=4, a=2)
                            Xb = X.t.bitcast(BF16)
                            order = list(range(NP_)) if d == 0 else list(range(NP_ - 1, -1, -1))
                            for idx, p in enumerate(order):
                                while idx - done_scan[d] >= 2:
                                    yield
                                O_ = W["out"][idx % 2]
                                Kd, Vb, ITm, Yb, Qg = (O_[k] for k in ("Kd", "Vb", "ITm", "Yb", "Qg"))
                                t0 = 128 * p
                                gcol = GC[:, p, 4 * d:4 * d + 4]
                                for h in range(4):
                                    self.tr(Xb[:, h * 128:(h + 1) * 128], kf[:, h, t0:t0 + 128], identb, r=[kf, self.cstb], w=[X])
                                    self.tr(Xb[:, (4 + h) * 128:(5 + h) * 128], vf[:, h, t0:t0 + 128], identb, r=[vf, self.cstb], w=[X])
                                self.tt(DVE, DG[:], bc_h(ident), bc_i(gcol), ALU.mult, r=[cst, GC], w=[DG])
                                for h in range(4):
                                    self.mm(Y[:, h * 128:(h + 1) * 128], kf[:, h, t0:t0 + 128], kf[:, h, t0:t0 + 128], r=[kf], w=[Y])
                                    self.mm(Z[:, h * 128:(h + 1) * 128], kf[:, h, t0:t0 + 128], qf[:, h, t0:t0 + 128], r=[kf, qf], w=[Z])
                                yield
                                self.tt(DVE, Kd[:], Xb[:, 0:512].rearrange("p (h k) -> p h k", h=4), bc_i(kds[:, p, 4 * d:4 * d + 4]), ALU.mult, r=[X, kds], w=[Kd])
                                self.tt(DVE, Vb[:], Xb[:, 512:1024].rearrange("p (h k) -> p h k", h=4), bc_i(beta[:, p, 4 * d:4 * d + 4]), ALU.mult, r=[X, beta], w=[Vb])
                                yield
                                self.mm(Wk[:, 0:512], ones, DG[:].rearrange("p h k -> p (h k)"), r=[cst, DG], w=[Wk])
                                yield
                                self.cp(ACT, GR[:].rearrange("p h k -> p (h k)"), Wk[:, 0:512], r=[Wk], w=[GR])
                                yield
                                self.tt(DVE, D1[:], GR[:], bc_i(gcol), ALU.subtract, r=[GR, GC], w=[D1])
                                self.tt(DVE, D2[:], bc_i(gcol), GR[:], ALU.subtract, r=[GR, GC], w=[D2])
                                yield
                                self.act(GR[:], GR[:], AF.Exp, r=[GR], w=[GR])
                                self.tt(POOL, D1[:], D1[:], bc_h(cst[:, m1c, :]), ALU.add, r=[D1, cst], w=[D1])
                                self.tt(POOL, D2[:], D2[:], bc_h(cst[:, m2c, :]), ALU.add, r=[D2, cst], w=[D2])
                                yield
                                self.tt(POOL, Qg[:], qf[:, :, t0:t0 + 128], GR[:], ALU.mult, r=[qf, GR], w=[Qg])
                                self.act(D2[:], D2[:], AF.Exp, r=[D2], w=[D2])
                                self.act(D1[:], D1[:], AF.Exp, r=[D1], w=[D1])
                                yield
                                for h in range(4):
                                    self.stt(Pm[:, h, :], Y[:, h * 128:(h + 1) * 128], nbeta[:, p, 4 * d + h:4 * d + h + 1], D2[:, h, :], ALU.mult, ALU.mult,
                                             r=[Y, nbeta, D2], w=[Pm])
                                self.tt(DVE, ITm[:], v3(Z), D1[:], ALU.mult, r=[Z, D1], w=[ITm])
                                yield
                                for h in range(4):
                                    self.tr(V[:, h * 128:(h + 1) * 128], Pm[:, h, :], ident, r=[Pm, cst], w=[V])
                                yield
                                self.cp(ACT, PY[:, :, 0, :], v3(V), r=[V], w=[PY])
                                yield
                                self.tt(POOL, PY[:, :, 1, :], PY[:, :, 0, :], bc_h(ident), ALU.add, r=[PY, cst], w=[PY])
                                for h in range(4):
                                    self.mm(A[:, h * 128:(h + 1) * 128], PY[:, h, 0, :], Pm[:, h, :], r=[PY, Pm], w=[A])
                                    self.mm(Bap[:, h, 0, :], Pm[:, h, :], PY[:, h, 0, :], r=[PY, Pm], w=B2)
                                yield
                                self.cp(ACT, Pm[:], v3(A), r=[A], w=[Pm])
                                self.cp(DVE, PY[:, :, 0, :], Bap[:, :, 0, :], r=B2, w=[PY])
                                yield
                                for k in range(1, 5):
                                    Pi, PYi = (Pm, PY) if k <= 2 else (PmR, PYR)
                                    Po, PYo = (Pm, PY) if k <= 1 else (PmR, PYR)
                                    for h in range(4):
                                        self.mm(A[:, h * 128:(h + 1) * 128], PYi[:, h, 0, :], Pi[:, h, :], r=[PYi, Pi], w=[A])
                                        self.mm(Bap[:, h, :, :].rearrange("p a k -> p (a k)"), Pi[:, h, :], PYi[:, h, :, :].rearrange("p a k -> p (a k)"), r=[PYi, Pi], w=B2)
                                    yield
                                    self.cp(ACT, Po[:], v3(A), r=[A], w=[Po])
                                    self.cp(DVE, PYo[:, :, 0, :], Bap[:, :, 0, :], r=B2, w=[PYo])
                                    self.tt(DVE, PYo[:, :, 1, :], PYi[:, :, 1, :], Bap[:, :, 1, :], ALU.add, r=B2 + [PYi], w=[PYo])
                                    yield
                                for h in range(4):
                                    self.mm(A[:, h * 128:(h + 1) * 128], PmR[:, h, :], PYR[:, h, 1, :], r=[PYR, PmR], w=[A])
                                yield
                                self.tt(DVE, Yb[:], PYR[:, :, 1, :], v3(A), ALU.add, r=[A, PYR], w=[Yb])
                                done_pre[d] = idx + 1
                                yield
                        return gen

                    def scan_chain(d):
                        def gen(slot):
                            W = sets[d]
                            Rb, VNb, Sst, Sb, OT = (W[k] for k in ("Rb", "VNb", "S", "Sb", "OT"))
                            OFd = self.OF if d == 0 else self.OFB
                            C3 = bank[4 * d + 3]
                            self.memset(POOL, Rb[:], 0.0, w=[Rb])
                            self.memset(POOL, VNb[:], 0.0, w=[VNb])
                            if si == 0:
                                self.dma(SP, Sst[:], self.s0[l, :, 4 * d:4 * d + 4, :], w=[Sst])
                            else:
                                self.memset(POOL, Sst[:], 0.0, w=[Sst])
                            self.cp(POOL, Sb[:], Sst[:], r=[Sst], w=[Sb])
                            yield
                            order = list(range(NP_)) if d == 0 else list(range(NP_ - 1, -1, -1))
                            for idx, p in enumerate(order):
                                while done_pre[d] <= idx:
                                    yield
                                O_ = W["out"][idx % 2]
                                Kd, Vb, ITm, Yb, Qg = (O_[k] for k in ("Kd", "Vb", "ITm", "Yb", "Qg"))
                                t0 = 128 * p
                                for hf in ((0, 1) if d == 0 else (1, 0)):
                                    R = slice(64 * hf, 64 * hf + 64)
                                    cs = slice(64 * hf, 64 * hf + 64)
                                    for h in range(4):
                                        self.mm(C3[:, h * 128:(h + 1) * 128], kf[:, h, t0:t0 + 128], Sb[:, h, :], r=[kf, Sb], w=[C3])
                                    yield
                                    for h in range(4):
                                        self.stt(Rb[R, h, :], C3[R, h * 128:(h + 1) * 128], negc[R, p, 4 * d + h:4 * d + h + 1], Vb[R, h, :], ALU.mult, ALU.add,
                                                 r=[C3, negc, Vb], w=[Rb])
                                    yield
                                    for h in range(4):
                                        self.mm(C3[:, h * 128:(h + 1) * 128], Yb[:, h, :], Rb[:, h, :], r=[Yb, Rb], w=[C3])
                                    yield
                                    self.cp(ACT, VNb[R, :, :].rearrange("p h k -> p (h k)"), C3[R, 0:512], r=[C3], w=[VNb])
                                    yield
                                    for h in range(4):
                                        self.mm(C3[:, h * 128:(h + 1) * 128], Kd[R, h, :], VNb[R, h, :], r=[Kd, VNb], w=[C3])
                                    yield
                                    for h in range(4):
                                        self.stt(Sst[:, h, :], Sst[:, h, :], egl[:, p, hf, 4 * d + h:4 * d + h + 1], C3[:, h * 128:(h + 1) * 128], ALU.mult, ALU.add,
                                                 r=[Sst, egl, C3], w=[Sst])
                                    yield
                                    for h in range(4):
                                        self.mm(C3[:, h * 64:(h + 1) * 64], Sb[:, h, :], Qg[:, h, cs], start=True, stop=False, r=[Sb, Qg], w=[C3])
                                        self.mm(C3[:, h * 64:(h + 1) * 64], VNb[R, h, :], ITm[R, h, cs], start=False, stop=True, r=[VNb, ITm], w=[C3])
                                    self.cp(POOL, Sb[:], Sst[:], r=[Sst], w=[Sb])
                                    yield
                                    self.cp(ACT, OT[:, :, cs], C3[:, 0:256].rearrange("p (h k) -> p h k", h=4), r=[C3], w=[OT])
                                    yield
                                ofb[(d, p)] = Buf(f"of{d}_{p}", None)
                                self.dma(SP, OFd[:, tok0 + t0:tok0 + t0 + 128].rearrange("(h p) t -> p h t", p=128), OT[:], r=[OT], w=[ofb[(d, p)]])
                                done_scan[d] = idx + 1
                                yield
                            if si > 0:
                                self.dma(SP, self.stout[si - 1, l, d].rearrange("h k v -> k h v"), Sst[:], r=[Sst])
                        return gen

                    self.run_pool([pre_chain(0), pre_chain(1), scan_chain(0), scan_chain(1)], 4)
                S.barrier()
                with contextlib.ExitStack() as es3:
                    ep = [dict(ofl=self.sb(es3, f"gofl{i}", [128, 4, 128]), ofr=self.sb(es3, f"gofr{i}", [128, 4, 128]), zt=self.sb(es3, f"gzt{i}", [128, 4, 128]),
                               osq=self.sb(es3, f"gosq{i}", [128, 4, 128], BF16), ors=self.sb(es3, f"gors{i}", [128, 4, 128]),
                               ob=self.sb(es3, f"gob{i}", [128, 4, 128], BF16)) for i in range(2)]

                    def epilogue(p):
                        def gen(slot):
                            E = ep[slot]
                            ofl, ofr, zt, osq, ors, ob = (E[k] for k in ("ofl", "ofr", "zt", "osq", "ors", "ob"))
                            t0 = 128 * p
                            cols = slice(tok0 + t0, tok0 + t0 + 128)
                            self.dma(SP, ofl[:], self.OF[:, cols].rearrange("(h p) t -> p h t", p=128), r=[ofb[(0, p)]], w=[ofl])
                            self.dma(SP, ofr[:], self.OFB[:, cols].rearrange("(h p) t -> p h t", p=128), r=[ofb[(1, p)]], w=[ofr])
                            self.dma(SP, zt[:], self.P[16 * 128:20 * 128, cols].rearrange("(h p) t -> p h t", p=128), w=[zt])
                            yield
                            self.tt(POOL, ofl[:], ofl[:], ofr[:], ALU.add, r=[ofl, ofr], w=[ofl])
                            self.act(zt[:], zt[:], AF.Silu, r=[zt], w=[zt])
                            yield
                            self.act(osq[:], ofl[:], AF.Square, r=[ofl], w=[osq])
                            yield
                            pb_ = bank[2 + slot]
                            self.mm(pb_[:, 0:512], self.onesb, osq[:].rearrange("p h k -> p (h k)"), r=[osq, self.cstb], w=[pb_])
                            yield
                            self.act(ors[:].rearrange("p h k -> p (h k)"), pb_[:, 0:512], AF.Sqrt, r=[pb_, self.cst], w=[ors], bias=self.c_eps, scale=1.0 / 128)
                            yield
                            self.recip(ors[:], ors[:], r=[ors], w=[ors])
                            yield
                            self.tt(DVE, ofl[:], ofl[:], ors[:], ALU.mult, r=[ofl, ors], w=[ofl])
                            yield
                            self.stt(ob[:], ofl[:], pvl[:, PV_GNW:PV_GNW + 1], zt[:], ALU.mult, ALU.mult, r=[ofl, self.pv, zt], w=[ob])
                            yield
                            self.dma(SP, self.CAT[256:768, cols].rearrange("(h p) t -> p h t", p=128), ob[:], r=[ob])
                        return gen

                    self.run_pool([epilogue(p) for p in range(NP_)], 2)
                S.barrier()


def _prep_inputs(inp, nl=DEPTH):
    f = lambda a: np.ascontiguousarray(np.asarray(a, dtype=np.float32))
    inp = {k: f(v) for k, v in inp.items()}
    shared = {
        "cst": _consts(),
        "pv": _pack_pv(inp),
        "w_ada": inp["w_ada"][:nl],
        "w_in": np.ascontiguousarray(inp["w_in"][:nl][:, :, _WIN_PERM]),
        "w_out": inp["w_out"][:nl],
        "w_up": inp["w_up"][:nl],
        "w_down": inp["w_down"][:nl],
        "gsc": np.ascontiguousarray(np.concatenate([inp["gdn_a_log"].reshape(DEPTH, 8), inp["gdn_dt_bias"].reshape(DEPTH, 8)], axis=1)[:nl]),
        "rpb": np.ascontiguousarray(inp["na_rpb"].reshape(DEPTH, 60, 31)[:nl]),
    }
    maps = []
    for i in range(8):
        xin = np.concatenate([inp["x_sample"][i].T, inp["x_prompt"][2 * i].T, inp["x_prompt"][2 * i + 1].T], axis=1)
        cv = np.stack([inp["c_ctx"], inp["c"][i]], axis=1)
        cvec = cv.reshape(8, 128, 2).transpose(1, 0, 2).reshape(128, 16)
        kv = inp["cache_attn_kv"][i][:nl]
        kc = kv[:, 0].transpose(0, 1, 3, 2).reshape(nl, 2, 128, TC)
        vc = kv[:, 1].transpose(0, 2, 1, 3).reshape(nl, 2, 128, 256)
        s0 = inp["state_delta"][i][:nl].reshape(nl, 8, 128, 128).transpose(0, 2, 1, 3)
        m = dict(shared)
        m.update(xin=np.ascontiguousarray(xin), cvec=np.ascontiguousarray(cvec), kcache=np.ascontiguousarray(kc),
                 vcache=np.ascontiguousarray(vc), s0=np.ascontiguousarray(s0))
        maps.append(m)
    return maps


def _assemble(results, nl=DEPTH):
    y_prompt = np.zeros((16, TC, D), np.float32)
    y_sample = np.zeros((8, TL, D), np.float32)
    kvn = np.zeros((16, nl, 2, 4, TC, 64), np.float32)
    stn = np.zeros((16, nl, 2, 4, 128, 128), np.float32)
    for i, r in enumerate(results):
        xo = r["xout"]
        y_sample[i] = xo[:, 0:TL].T
        y_prompt[2 * i] = xo[:, TL:TL + TC].T
        y_prompt[2 * i + 1] = xo[:, TL + TC:TT].T
        kvo = r["kvout"]
        for s in range(2):
            kvn[2 * i + s] = kvo[:, s].reshape(nl, 2, 4, 64, TC).transpose(0, 1, 2, 4, 3)
            stn[2 * i + s] = r["stout"][s]
    return y_prompt, y_sample, kvn, stn


_CACHE = {}


def kernel(**inputs):
    if "kb" not in _CACHE:
        kb = KB()
        kb.build()
        _CACHE["kb"] = kb
    kb = _CACHE["kb"]
    maps = _prep_inputs(inputs)
    res = run_bass_kernel_spmd(kb.nc, maps, core_ids=list(range(8)))
    return _assemble(res.results)
```

```python
import contextlib
import numpy as np
import concourse.bass as bass
import concourse.mybir as mybir
from concourse.bass_utils import run_bass_kernel_spmd

F32 = mybir.dt.float32
BF16 = mybir.dt.bfloat16
AF = mybir.ActivationFunctionType
ALU = mybir.AluOpType
AX = mybir.AxisListType

PE, ACT, DVE, POOL, SP = "pe", "act", "dve", "pool", "sp"
COMPUTE = (PE, ACT, DVE, POOL)
ALLENG = (PE, ACT, DVE, POOL, SP)
DMAQ = (SP, ACT, POOL)
ND = 4
SAME_ENGINE_SYNC = True

D = 1024
DEPTH = 4
TL = 4096
TC = 256
TT = TL + 2 * TC
NCH = 27
PROJ = 3344
DFF = 2816
EPS = 1e-6
NEG = -30000.0
NEGBIG = -1.0e5
HW_ = TT + 6
SEQS = ((0, TL, 1, 1), (TL, TC, 4099, 0), (TL + TC, TC, 4357, 0))


class Buf:
    __slots__ = ("name", "t", "lw", "rd")

    def __init__(self, name, t):
        self.name = name
        self.t = t
        self.lw = None
        self.rd = []

    def __getitem__(self, k):
        return self.t[k]


class Op:
    __slots__ = ("eng", "fn", "deps", "marked", "is_dma", "sem", "cnt", "epoch", "kind")

    def __init__(self, eng, fn, is_dma, epoch, kind="op"):
        self.eng = eng
        self.fn = fn
        self.deps = []
        self.marked = False
        self.is_dma = is_dma
        self.sem = None
        self.cnt = 0
        self.epoch = epoch
        self.kind = kind


class Sched:
    def __init__(self, nc):
        self.nc = nc
        self.ops = []
        self.epoch = 0
        self.eng = {PE: nc.tensor, ACT: nc.scalar, DVE: nc.vector, POOL: nc.gpsimd, SP: nc.sync}

    def op(self, eng, fn, reads=(), writes=(), dma=False):
        o = Op(eng, fn, dma, self.epoch)
        deps = {}
        for b in reads:
            if b.lw is not None:
                deps[id(b.lw)] = (b.lw, "raw")
        for b in writes:
            if b.lw is not None:
                deps[id(b.lw)] = (b.lw, "waw")
            for r in b.rd:
                if id(r) not in deps:
                    deps[id(r)] = (r, "war")
        for d, kind in deps.values():
            if d.epoch != o.epoch or d is o:
                continue
            if (not d.is_dma) and d.eng == eng and not dma:
                if eng == PE or kind == "war" or not SAME_ENGINE_SYNC:
                    continue
            o.deps.append(d)
        for b in reads:
            b.rd.append(o)
        for b in writes:
            b.lw = o
            b.rd = []
        self.ops.append(o)
        return o

    def barrier(self, new_epoch=False):
        self.ops.append(Op(None, None, False, self.epoch, kind="barrier_ep" if new_epoch else "barrier"))
        if new_epoch:
            self.epoch += 1

    def emit(self, es):
        nc = self.nc
        nep = self.epoch + 1
        for o in self.ops:
            for d in o.deps:
                d.marked = True
        sems = []
        for ep in range(nep):
            d = {}
            for e in COMPUTE:
                d[e] = es.enter_context(nc.semaphore(f"s{ep}_{e}"))
            for q in DMAQ:
                for k in range(ND):
                    d[(q, k)] = es.enter_context(nc.semaphore(f"d{ep}_{q}{k}"))
            sems.append(d)
        cur = [dict((k, 0) for k in sems[ep]) for ep in range(nep)]
        rr = dict((q, 0) for q in DMAQ)
        seen = {}
        n_wait = 0
        for o in self.ops:
            ep = o.epoch
            if o.kind != "op":
                for e in ALLENG:
                    for key, val in cur[ep].items():
                        if val > 0 and seen.get((e, key), 0) < val:
                            self.eng[e].wait_ge(sems[ep][key], val)
                            seen[(e, key)] = val
                            n_wait += 1
                if o.kind == "barrier_ep":
                    seen = {}
                continue
            e = self.eng[o.eng]
            need = {}
            for d in o.deps:
                if need.get(d.sem, 0) < d.cnt:
                    need[d.sem] = d.cnt
            for key, val in need.items():
                if seen.get((o.eng, key), 0) >= val:
                    continue
                e.wait_ge(sems[ep][key], val)
                n_wait += 1
                seen[(o.eng, key)] = val
            inst = o.fn(e)
            if o.is_dma:
                key = (o.eng, rr[o.eng] % ND)
                rr[o.eng] += 1
                cur[ep][key] += 16
                o.sem, o.cnt = key, cur[ep][key]
                inst.then_inc(sems[ep][key], 16)
            elif o.marked:
                cur[ep][o.eng] += 1
                o.sem, o.cnt = o.eng, cur[ep][o.eng]
                inst.then_inc(sems[ep][o.eng], 1)
        ep = nep - 1
        for e in ALLENG:
            for key, val in cur[ep].items():
                if val > 0:
                    self.eng[e].wait_ge(sems[ep][key], val)
        return dict(n_ops=len(self.ops), n_wait=n_wait, counts=cur)


C_IDENT, C_ONES, C_JREV, C_TRIF, C_TRIB, C_BLK, C_HALFA, C_HALFB = range(8)
C_UINC, C_LSTR, C_LINC, C_USTR = 8, 9, 10, 11
C_CMH = 12
C_EPS = 13
NCST = 14


def _consts():
    p = np.arange(128)[:, None]
    f = np.arange(128)[None, :]
    same = (p // 64) == (f // 64)
    c = np.zeros((128, NCST, 128), np.float32)
    c[:, C_IDENT] = (p == f)
    c[:, C_ONES] = 1.0
    c[:, C_JREV] = same & ((p % 64) == (63 - f % 64))
    c[:, C_TRIF] = same & (p <= f)
    c[:, C_TRIB] = same & (p >= f)
    c[:, C_BLK] = same
    c[:, C_HALFA] = (p < 64) & (f >= 0)
    c[:, C_HALFB] = (p >= 64) & (f >= 0)
    c[:, C_UINC] = np.where(same & (f >= p), 0.0, NEGBIG)
    c[:, C_LSTR] = np.where(same & (p > f), 0.0, NEGBIG)
    c[:, C_LINC] = np.where(same & (f <= p), 0.0, NEGBIG)
    c[:, C_USTR] = np.where(same & (p < f), 0.0, NEGBIG)
    qc = 63 - (np.arange(128) % 64)
    cs = np.clip(qc - 8, 0, 48)
    kc = np.arange(64)[None, :]
    ok = (kc >= cs[:, None]) & (kc < cs[:, None] + 16)
    c[:, C_CMH, 0:64] = np.where(ok, 0.0, NEG)
    c[:, C_EPS, 0] = EPS
    c[:, C_EPS, 1] = 1.0
    return c.reshape(128, NCST * 128)


PV_GPM, PV_GPOM, PV_GPF, PV_GPOF = 0, 8, 16, 24
PV_BADA = 32
PV_CW = 80
PV_CB, PV_CLG, PV_CLB = 142, 144, 146
PV_GCW = 148
PV_GNW = 184
PV_FCW = 185
PV_FCB = 317
PV_N = 361


def _fm(v, nchunk):
    return np.ascontiguousarray(v.reshape(nchunk, 128).T)


def _pack_pv(inp):
    pv = np.zeros((128, DEPTH, PV_N), np.float32)
    for l in range(DEPTH):
        pv[:, l, PV_GPM:PV_GPM + 8] = _fm(inp["g_pre_mix"][l], 8)
        pv[:, l, PV_GPOM:PV_GPOM + 8] = _fm(inp["g_post_mix"][l], 8)
        pv[:, l, PV_GPF:PV_GPF + 8] = _fm(inp["g_pre_ffn"][l], 8)
        pv[:, l, PV_GPOF:PV_GPOF + 8] = _fm(inp["g_post_ffn"][l], 8)
        pv[:, l, PV_BADA:PV_BADA + 48] = _fm(inp["b_ada"][l], 48)
        cw = inp["conv_w"][l]
        pv[:, l, PV_CW:PV_CW + 62] = cw.T.reshape(2, 128, 31).transpose(1, 0, 2).reshape(128, 62)
        pv[:, l, PV_CB:PV_CB + 2] = _fm(inp["conv_b"][l], 2)
        pv[:, l, PV_CLG:PV_CLG + 2] = _fm(inp["conv_ln_g"][l], 2)
        pv[:, l, PV_CLB:PV_CLB + 2] = _fm(inp["conv_ln_b"][l], 2)
        gw = inp["gdn_conv_w"][l]
        pv[:, l, PV_GCW:PV_GCW + 36] = gw.T.reshape(12, 128, 3).transpose(1, 0, 2).reshape(128, 36)
        pv[:, l, PV_GNW] = inp["gdn_norm_w"][l]
        fw = inp["ffn_conv_w"][l]
        pv[:, l, PV_FCW:PV_FCW + 132] = fw.T.reshape(44, 128, 3).transpose(1, 0, 2).reshape(128, 132)
        pv[:, l, PV_FCB:PV_FCB + 44] = _fm(inp["ffn_conv_b"][l], 44)
    return pv.reshape(128, DEPTH * PV_N)


_WIN_PERM = np.concatenate([np.arange(0, 2560), np.arange(2576, 3344), np.arange(2560, 2576)])


class KB:
    def __init__(self, nlayers=DEPTH, dbg=None):
        self.nl = nlayers
        self.dbg = dbg or {}
        nc = self.nc = bass.Bass("TRN2", target_bir_lowering=False)
        self.S = Sched(nc)
        self.es = contextlib.ExitStack()
        self.uid = 0

    def mm(self, out, lhsT, rhs, start=True, stop=True, r=(), w=()):
        return self.S.op(PE, lambda e: e.matmul(out, lhsT, rhs, start=start, stop=stop), r, w)

    def tr(self, out, in_, ident, r=(), w=()):
        return self.S.op(PE, lambda e: e.transpose(out, in_, ident), r, w)

    def act(self, out, in_, func, r=(), w=(), eng=ACT, **kw):
        return self.S.op(eng, lambda e: e.activation(out=out, in_=in_, func=func, **kw), r, w)

    def tt(self, eng, out, in0, in1, op, r=(), w=()):
        return self.S.op(eng, lambda e: e.tensor_tensor(out=out, in0=in0, in1=in1, op=op), r, w)

    def ts(self, eng, out, in0, s1, s2, op0, op1=None, r=(), w=()):
        if op1 is None:
            return self.S.op(eng, lambda e: e.tensor_scalar(out=out, in0=in0, scalar1=s1, scalar2=None, op0=op0), r, w)
        return self.S.op(eng, lambda e: e.tensor_scalar(out=out, in0=in0, scalar1=s1, scalar2=s2, op0=op0, op1=op1), r, w)

    def stt(self, out, in0, scalar, in1, op0, op1, r=(), w=()):
        return self.S.op(DVE, lambda e: e.scalar_tensor_tensor(out=out, in0=in0, scalar=scalar, in1=in1, op0=op0, op1=op1), r, w)

    def cp(self, eng, out, in_, r=(), w=()):
        if eng == ACT:
            return self.S.op(ACT, lambda e: e.copy(out=out, in_=in_), r, w)
        return self.S.op(eng, lambda e: e.tensor_copy(out=out, in_=in_), r, w)

    def memset(self, eng, ap, val, w=()):
        return self.S.op(eng, lambda e: e.memset(ap, val), (), w)

    def recip(self, out, in_, r=(), w=()):
        return self.S.op(DVE, lambda e: e.reciprocal(out=out, in_=in_), r, w)

    def dma(self, q, out, in_, r=(), w=()):
        return self.S.op(q, lambda e: e.dma_start(out=out, in_=in_), r, w, dma=True)

    def sb(self, es, name, shape, dt=F32):
        self.uid += 1
        return Buf(name, es.enter_context(self.nc.sbuf_tensor(f"{name}_{self.uid}", shape, dt)))

    def dram(self, name, shape, dt, kind="Internal"):
        if name in self.dbg.get("out", ()):
            kind = "ExternalOutput"
        if name in self.dbg.get("in", ()):
            kind = "ExternalInput"
        return self.nc.dram_tensor(name, shape, dt, kind=kind).ap()

    def rsqrt_ps(self, out, ps_ap, scale, r, w):
        self.act(out, ps_ap, AF.Ln, r=list(r) + [self.cst], w=w, bias=self.c_eps, scale=scale)
        self.act(out, out, AF.Exp, r=w, w=w, scale=-0.5)

    def build(self):
        nc, S, es = self.nc, self.S, self.es
        nl = self.nl
        I = {}
        def ext(name, shape, dt=F32):
            I[name] = nc.dram_tensor(name, shape, dt, kind="ExternalInput").ap()
            return I[name]
        self.xin = ext("xin", [D, TT])
        self.cvec = ext("cvec", [128, 16])
        self.cst_d = ext("cst", [128, NCST * 128])
        self.pv_d = ext("pv", [128, DEPTH * PV_N])
        self.w_ada = ext("w_ada", [nl, D, 6 * D])
        self.w_in = ext("w_in", [nl, D, PROJ])
        self.w_out = ext("w_out", [nl, D, D])
        self.w_up = ext("w_up", [nl, D, 2 * DFF])
        self.w_down = ext("w_down", [nl, DFF, D])
        self.kcache = ext("kcache", [nl, 2, 128, TC])
        self.vcache = ext("vcache", [nl, 2, 128, 256])
        self.s0 = ext("s0", [nl, 128, 8, 128])
        self.gsc = ext("gsc", [nl, 16])
        self.rpb = ext("rpb", [nl, 60, 31])
        self.xout = nc.dram_tensor("xout", [D, TT], F32, kind="ExternalOutput").ap()
        self.kvout = nc.dram_tensor("kvout", [nl, 2, 2, 256, TC], F32, kind="ExternalOutput").ap()
        self.stout = nc.dram_tensor("stout", [2, nl, 2, 4, 128, 128], F32, kind="ExternalOutput").ap()
        self.P = self.dram("P", [NCH * 128, TT], F32)
        self.PB = self.dram("PB", [6 * 128, TT], BF16)
        self.CAT = self.dram("CAT", [D, TT], BF16)
        self.XA = self.dram("XA", [D, TT], F32)
        self.A = self.dram("A", [DFF, TT], BF16)
        self.OF = self.dram("OF", [512, TT], F32)
        self.OFB = self.dram("OFB", [512, TT], F32)
        self.FP = self.dram("FP", [nl, 60, 127], F32)

        with es:
            pst = es.enter_context(nc.psum_tensor("ps", [128, 4096], F32))
            self.bank = [Buf(f"bank{i}", pst[:, i * 512:(i + 1) * 512]) for i in range(8)]
            self.pst = pst
            self.cst = self.sb(es, "cst", [128, NCST, 128])
            self.cstb = self.sb(es, "cstb", [128, 3, 128], BF16)
            self.pv = self.sb(es, "pv", [128, DEPTH, PV_N])
            self.mod = self.sb(es, "mod", [128, 6, 8, 2])
            self.scv = self.sb(es, "scv", [128, 8, 2])
            self.c_eps = self.cst[:, C_EPS, 0:1]
            self.dma(SP, self.cst[:].rearrange("p a b -> p (a b)"), self.cst_d[:, :], w=[self.cst])
            self.dma(SP, self.pv[:].rearrange("p a b -> p (a b)"), self.pv_d[:, :], w=[self.pv])
            self.dma(SP, self.scv[:].rearrange("p a b -> p (a b)"), self.cvec[:, :], w=[self.scv])
            for i, ci in enumerate((C_IDENT, C_ONES, C_JREV)):
                self.cp(DVE, self.cstb[:, i, :], self.cst[:, ci, :], r=[self.cst], w=[self.cstb])
            self.ident = self.cst[:, C_IDENT, :]
            self.ones = self.cst[:, C_ONES, :]
            self.identb = self.cstb[:, 0, :]
            self.onesb = self.cstb[:, 1, :]
            self.jrevb = self.cstb[:, 2, :]
            self.act(self.scv[:], self.scv[:], AF.Silu, r=[self.scv], w=[self.scv])
            for l in range(nl):
                self.layer(l)
                if l + 1 < nl:
                    S.barrier(new_epoch=True)
            info = S.emit(es)
        return info

    def layer(self, l):
        S = self.S
        stages = self.dbg.get("stages", ("ada", "s1", "conf", "gdn", "attn", "s3", "s4a", "s4b"))
        xsrc = self.xin if l == 0 else self.xout
        if "ada" in stages:
            self.stage_ada(l)
            S.barrier()
        if "s1" in stages:
            self.stage_s1(l, xsrc)
            S.barrier()
        if "conf" in stages:
            self.stage_conf(l)
            S.barrier()
        if "attn" in stages:
            self.stage_attn(l)
            S.barrier()
        if "gdn" in stages:
            self.stage_gdn(l)
            S.barrier()
        if "s3" in stages:
            with contextlib.ExitStack() as es:
                self.Hb = self.sb(es, "H", [128, 8, HW_], BF16)
                self.stage_s3(l, xsrc, es)
                S.barrier()
                if "s4a" in stages:
                    self.stage_s4a(l)
                    S.barrier()
        if "s4b" in stages:
            self.stage_s4b(l)
            S.barrier()

    def stage_ada(self, l):
        with contextlib.ExitStack() as es:
            wb = [self.sb(es, f"adaw{i}", [128, 8, 512]) for i in range(6)]
            raw = self.sb(es, "adaraw", [128, 48, 2])
            ps = self.bank[0]
            for g in range(12):
                w = wb[g % 6]
                self.dma(SP, w[:],
                         self.w_ada[l, :, g * 512:(g + 1) * 512].rearrange("(kc p) n -> p kc n", p=128), w=[w])
                for j in range(4):
                    n = g * 4 + j
                    for kc in range(8):
                        self.mm(ps[:, 2 * n:2 * n + 2], w[:, kc, j * 128:(j + 1) * 128], self.scv[:, kc, :],
                                start=(kc == 0), stop=(kc == 7), r=[w, self.scv], w=[ps])
            pvl = self.pv[:, l, :]
            self.tt(DVE, raw[:], ps[:, 0:96].rearrange("p (n v) -> p n v", v=2),
                    pvl[:, PV_BADA:PV_BADA + 48].unsqueeze(2).to_broadcast([128, 48, 2]), ALU.add,
                    r=[ps, self.pv], w=[raw])
            def gain(off):
                return pvl[:, off:off + 8].unsqueeze(2).to_broadcast([128, 8, 2])
            m = self.mod
            self.ts(DVE, m[:, 0], raw[:, 8:16, :], 1.0, None, ALU.add, r=[raw], w=[m])
            self.tt(DVE, m[:, 0], m[:, 0], gain(PV_GPM), ALU.mult, r=[m, self.pv], w=[m])
            self.cp(DVE, m[:, 1], raw[:, 0:8, :], r=[raw], w=[m])
            self.tt(DVE, m[:, 2], raw[:, 16:24, :], gain(PV_GPOM), ALU.mult, r=[raw, self.pv], w=[m])
            self.ts(DVE, m[:, 3], raw[:, 32:40, :], 1.0, None, ALU.add, r=[raw], w=[m])
            self.tt(DVE, m[:, 3], m[:, 3], gain(PV_GPF), ALU.mult, r=[m, self.pv], w=[m])
            self.cp(DVE, m[:, 4], raw[:, 24:32, :], r=[raw], w=[m])
            self.tt(DVE, m[:, 5], raw[:, 40:48, :], gain(PV_GPOF), ALU.mult, r=[raw, self.pv], w=[m])

    def sumsq_rstd(self, src, sq, ps, rstd, n, nchunk=8, scale=1.0 / D, src_bufs=()):
        for c in range(nchunk):
            self.act(sq[:, c, 0:n], src[:, c, 0:n], AF.Square, r=list(src_bufs), w=[sq], eng=ACT)
        for c in range(nchunk):
            self.mm(ps[:, 0:n], self.onesb, sq[:, c, 0:n], start=(c == 0), stop=(c == nchunk - 1),
                    r=[sq, self.cstb], w=[ps])
        self.rsqrt_ps(rstd[:, 0:n], ps[:, 0:n], scale, r=[ps], w=[rstd])

    def s1_tiles(self):
        tl = [(j * 512, 512, 1, [(1 + j * 512, 512)]) for j in range(8)]
        tl.append((TL, 512, 0, [(4099, 256), (4357, 256)]))
        return tl

    def stage_s1(self, l, xsrc):
        with contextlib.ExitStack() as es:
            H = self.sb(es, "H", [128, 8, HW_], BF16)
            xt = [self.sb(es, f"xt{i}", [128, 8, 512]) for i in range(2)]
            sq = self.sb(es, "sq", [128, 8, 512], BF16)
            rstd = self.sb(es, "rstd", [128, 512])
            wck = [self.sb(es, f"wck{i}", [128, 8, 128], BF16) for i in range(3)]
            ev = [self.sb(es, f"ev{i}", [128, 512]) for i in range(4)]
            evb = [self.sb(es, f"evb{i}", [128, 512], BF16) for i in range(2)]
            tiles = self.s1_tiles()
            m = self.mod
            for ti, (t0, n, vec, hc) in enumerate(tiles):
                x = xt[ti % 2]
                self.dma(SP, x[:], xsrc[:, t0:t0 + n].rearrange("(c p) t -> p c t", p=128), w=[x])
                ps = self.bank[ti % 2]
                self.sumsq_rstd(x, sq, ps, rstd, n, src_bufs=[x])
                self.tt(DVE, x[:], x[:], rstd[:].unsqueeze(1).to_broadcast([128, 8, 512]), ALU.mult, r=[x, rstd], w=[x])
                for c in range(8):
                    off = 0
                    for (h0, nn) in hc:
                        eng = ACT if c % 2 == 0 else POOL
                        if eng == ACT:
                            self.act(H[:, c, h0:h0 + nn], x[:, c, off:off + nn], AF.Identity, r=[x, m], w=[H],
                                     scale=m[:, 0, c, vec:vec + 1], bias=m[:, 1, c, vec:vec + 1])
                        else:
                            self.ts(POOL, H[:, c, h0:h0 + nn], x[:, c, off:off + nn], m[:, 0, c, vec:vec + 1],
                                    m[:, 1, c, vec:vec + 1], ALU.mult, ALU.add, r=[x, m], w=[H])
                        off += nn
            k = 0
            for nci in range(NCH):
                ncols = 128 if nci < 26 else 16
                w = wck[nci % 3]
                self.dma(POOL, w[:, :, 0:ncols],
                         self.w_in[l, :, nci * 128:nci * 128 + ncols].rearrange("(kc p) n -> p kc n", p=128), w=[w])
                for ti, (t0, n, vec, hc) in enumerate(tiles):
                    ps = self.bank[2 + (k % 4)]
                    off = 0
                    for (h0, nn) in hc:
                        for kc in range(8):
                            self.mm(ps[0:ncols, off:off + nn], w[:, kc, 0:ncols], H[:, kc, h0:h0 + nn],
                                    start=(kc == 0), stop=(kc == 7), r=[w, H], w=[ps])
                        off += nn
                    e = ev[k % 4]
                    if k % 2 == 0:
                        self.cp(ACT, e[0:ncols, :], ps[0:ncols, :], r=[ps], w=[e])
                    else:
                        self.cp(DVE, e[0:ncols, :], ps[0:ncols, :], r=[ps], w=[e])
                    self.dma(SP, self.P[nci * 128:nci * 128 + ncols, t0:t0 + n], e[0:ncols, :], r=[e])
                    if 20 <= nci < 26:
                        eb = evb[k % 2]
                        self.cp(POOL, eb[:], e[:], r=[e], w=[eb])
                        self.dma(SP, self.PB[(nci - 20) * 128:(nci - 19) * 128, t0:t0 + n], eb[:], r=[eb])
                    if ti == 8 and 22 <= nci < 26:
                        kv = (nci - 22) // 2
                        f0 = ((nci - 22) % 2) * 128
                        for s in range(2):
                            self.dma(SP, self.kvout[l, s, kv, f0:f0 + 128, :], e[:, s * 256:(s + 1) * 256], r=[e])
                    k += 1

    def stage_s3(self, l, xsrc, es0):
        H = self.Hb
        with contextlib.ExitStack() as es:
            wo = self.sb(es, "wo", [128, 8, D], BF16)
            cat = [self.sb(es, f"cat{i}", [128, 8, 512], BF16) for i in range(2)]
            xt = [self.sb(es, f"xt{i}", [128, 8, 512]) for i in range(2)]
            msb = self.sb(es, "msb", [128, 8, 512])
            sq = self.sb(es, "sq", [128, 8, 512], BF16)
            rstd = self.sb(es, "rstd", [128, 512])
            zc = self.sb(es, "zc", [128, 8, 2], BF16)
            parts = self.dbg.get("s3parts", "wpx")
            if "w" in parts:
                self.dma(POOL, wo[:], self.w_out[l].rearrange("(kc p) n -> p kc n", p=128), w=[wo])
            if "p" in parts:
                self.memset(POOL, zc[:], 0.0, w=[zc])
                for c0, n0 in ((0, 1), (4097, 2), (4355, 2), (4613, 1)):
                    self.cp(POOL, H[:, :, c0:c0 + n0], zc[:, :, 0:n0], r=[zc], w=[H])
            if "x" not in parts:
                return
            m = self.mod
            lvl = int(self.dbg.get("s3n", 9))
            for ti, (t0, n, vec, hc) in enumerate(self.s1_tiles()):
                ct = cat[ti % 2]
                x = xt[ti % 2]
                self.dma(SP, ct[:], self.CAT[:, t0:t0 + n].rearrange("(c p) t -> p c t", p=128), w=[ct])
                self.dma(SP, x[:], xsrc[:, t0:t0 + n].rearrange("(c p) t -> p c t", p=128), w=[x])
                if lvl < 2:
                    continue
                for nn in range(8):
                    ps = self.bank[nn % 4]
                    for kc in range(8):
                        self.mm(ps[:, 0:n], wo[:, kc, nn * 128:(nn + 1) * 128], ct[:, kc, :],
                                start=(kc == 0), stop=(kc == 7), r=[wo, ct], w=[ps])
                    if "c" in self.dbg.get("s3l2", "cs"):
                        self.cp(DVE, msb[:, nn, :], ps[:, 0:n], r=[ps], w=[msb])
                    if "s" in self.dbg.get("s3l2", "cs"):
                        self.act(sq[:, nn, :], msb[:, nn, :], AF.Square, r=[msb], w=[sq])
                if lvl < 3:
                    continue
                pss = self.bank[4 + ti % 2]
                for c in range(8):
                    self.mm(pss[:, 0:n], self.onesb, sq[:, c, :], start=(c == 0), stop=(c == 7), r=[sq, self.cstb], w=[pss])
                self.rsqrt_ps(rstd[:, 0:n], pss[:, 0:n], 1.0 / D, r=[pss], w=[rstd])
                self.tt(DVE, msb[:], msb[:], rstd[:].unsqueeze(1).to_broadcast([128, 8, 512]), ALU.mult, r=[msb, rstd], w=[msb])
                if lvl < 4:
                    continue
                for c in range(8):
                    self.stt(x[:, c, :], msb[:, c, :], m[:, 2, c, vec:vec + 1], x[:, c, :], ALU.mult, ALU.add,
                             r=[msb, m, x], w=[x])
                self.dma(SP, self.XA[:, t0:t0 + n].rearrange("(c p) t -> p c t", p=128), x[:], r=[x])
                if lvl < 5:
                    continue
                pss2 = self.bank[6 + ti % 2]
                self.sumsq_rstd(x, sq, pss2, rstd, n, src_bufs=[x])
                self.tt(DVE, msb[:], x[:], rstd[:].unsqueeze(1).to_broadcast([128, 8, 512]), ALU.mult, r=[x, rstd], w=[msb])
                if lvl < 6:
                    continue
                for c in range(8):
                    off = 0
                    for (h0, nn2) in hc:
                        if c % 2 == 0:
                            self.act(H[:, c, h0:h0 + nn2], msb[:, c, off:off + nn2], AF.Identity, r=[msb, m], w=[H],
                                     scale=m[:, 3, c, vec:vec + 1], bias=m[:, 4, c, vec:vec + 1])
                        else:
                            self.ts(POOL, H[:, c, h0:h0 + nn2], msb[:, c, off:off + nn2], m[:, 3, c, vec:vec + 1],
                                    m[:, 4, c, vec:vec + 1], ALU.mult, ALU.add, r=[msb, m], w=[H])
                        off += nn2

    def ffn_tiles(self):
        tl = []
        t = 0
        while t < TL:
            n = min(456, TL - t)
            tl.append((t, n, t))
            t += n
        tl.append((TL, TC, 4098))
        tl.append((TL + TC, TC, 4356))
        return tl

    def stage_s4a(self, l):
        H = self.Hb
        with contextlib.ExitStack() as es:
            wg = [self.sb(es, f"wg{i}", [128, 8, 128], BF16) for i in range(2)]
            wv = [self.sb(es, f"wv{i}", [128, 8, 128], BF16) for i in range(2)]
            tg = [self.sb(es, f"tg{i}", [128, 512]) for i in range(2)]
            tv = [self.sb(es, f"tv{i}", [128, 512]) for i in range(2)]
            ao = [self.sb(es, f"ao{i}", [128, 512], BF16) for i in range(3)]
            pvl = self.pv[:, l, :]
            k = 0
            for j in range(22):
                g, v = wg[j % 2], wv[j % 2]
                self.dma(POOL, g[:], self.w_up[l, :, j * 128:(j + 1) * 128].rearrange("(kc p) n -> p kc n", p=128), w=[g])
                self.dma(POOL, v[:], self.w_up[l, :, (22 + j) * 128:(23 + j) * 128].rearrange("(kc p) n -> p kc n", p=128), w=[v])
                for (t0, n, h0) in self.ffn_tiles():
                    pg = self.bank[(2 * k) % 8]
                    pv_ = self.bank[(2 * k + 1) % 8]
                    for kc in range(8):
                        self.mm(pg[:, 0:n + 2], g[:, kc, :], H[:, kc, h0:h0 + n + 2], start=(kc == 0), stop=(kc == 7), r=[g, H], w=[pg])
                    for kc in range(8):
                        self.mm(pv_[:, 0:n + 2], v[:, kc, :], H[:, kc, h0:h0 + n + 2], start=(kc == 0), stop=(kc == 7), r=[v, H], w=[pv_])
                    a, b_ = tg[k % 2], tv[k % 2]
                    for (ps, t, ch) in ((pg, a, j), (pv_, b_, 22 + j)):
                        cw = PV_FCW + 3 * ch
                        self.act(t[:, 0:n], ps[:, 0:n], AF.Identity, r=[ps, self.pv], w=[t],
                                 scale=pvl[:, cw:cw + 1], bias=pvl[:, PV_FCB + ch:PV_FCB + ch + 1])
                        self.stt(t[:, 0:n], ps[:, 1:n + 1], pvl[:, cw + 1:cw + 2], t[:, 0:n], ALU.mult, ALU.add, r=[ps, self.pv, t], w=[t])
                        self.stt(t[:, 0:n], ps[:, 2:n + 2], pvl[:, cw + 2:cw + 3], t[:, 0:n], ALU.mult, ALU.add, r=[ps, self.pv, t], w=[t])
                    self.act(a[:, 0:n], a[:, 0:n], AF.Silu, r=[a], w=[a])
                    o = ao[k % 3]
                    self.tt(POOL, o[:, 0:n], a[:, 0:n], b_[:, 0:n], ALU.mult, r=[a, b_], w=[o])
                    self.dma(SP, self.A[j * 128:(j + 1) * 128, t0:t0 + n], o[:, 0:n], r=[o])
                    k += 1

    def stage_s4b(self, l):
        with contextlib.ExitStack() as es:
            wd = self.sb(es, "wd", [128, 22, D], BF16)
            at = [self.sb(es, f"at{i}", [128, 22, 512], BF16) for i in range(2)]
            xa = [self.sb(es, f"xa{i}", [128, 8, 512]) for i in range(2)]
            fsb = self.sb(es, "fsb", [128, 8, 512])
            sq = self.sb(es, "sq", [128, 8, 512], BF16)
            rstd = self.sb(es, "rstd", [128, 512])
            for h in range(2):
                self.dma(POOL, wd[:, h * 11:(h + 1) * 11, :],
                         self.w_down[l, h * 11 * 128:(h + 1) * 11 * 128, :].rearrange("(j p) n -> p j n", p=128), w=[wd])
            m = self.mod
            for ti, (t0, n, vec, hc) in enumerate(self.s1_tiles()):
                a = at[ti % 2]
                x = xa[ti % 2]
                self.dma(SP, a[:], self.A[:, t0:t0 + n].rearrange("(j p) t -> p j t", p=128), w=[a])
                self.dma(SP, x[:], self.XA[:, t0:t0 + n].rearrange("(c p) t -> p c t", p=128), w=[x])
                for nn in range(8):
                    ps = self.bank[nn % 4]
                    for j in range(22):
                        self.mm(ps[:, 0:n], wd[:, j, nn * 128:(nn + 1) * 128], a[:, j, :], start=(j == 0), stop=(j == 21), r=[wd, a], w=[ps])
                    self.cp(DVE, fsb[:, nn, :], ps[:, 0:n], r=[ps], w=[fsb])
                    self.act(sq[:, nn, :], fsb[:, nn, :], AF.Square, r=[fsb], w=[sq])
                pss = self.bank[4 + ti % 2]
                for c in range(8):
                    self.mm(pss[:, 0:n], self.onesb, sq[:, c, :], start=(c == 0), stop=(c == 7), r=[sq, self.cstb], w=[pss])
                self.rsqrt_ps(rstd[:, 0:n], pss[:, 0:n], 1.0 / D, r=[pss], w=[rstd])
                self.tt(DVE, fsb[:], fsb[:], rstd[:].unsqueeze(1).to_broadcast([128, 8, 512]), ALU.mult, r=[fsb, rstd], w=[fsb])
                for c in range(8):
                    self.stt(x[:, c, :], fsb[:, c, :], m[:, 5, c, vec:vec + 1], x[:, c, :], ALU.mult, ALU.add, r=[fsb, m, x], w=[x])
                self.dma(SP, self.xout[:, t0:t0 + n].rearrange("(c p) t -> p c t", p=128), x[:], r=[x])

    def stage_conf(self, l):
        pvl = self.pv[:, l, :]
        with contextlib.ExitStack() as es:
            av = [self.sb(es, f"cav{c}", [128, TL]) for c in range(2)]
            ag = [self.sb(es, f"cag{c}", [128, TL]) for c in range(2)]
            U = [self.sb(es, f"cU{c}", [128, TL + 30], BF16) for c in range(2)]
            DGc = [self.sb(es, f"cDG{c}", [128, 31, 128], BF16) for c in range(2)]
            for c in range(2):
                for kk in range(31):
                    col = PV_CW + 31 * c + kk
                    if c == 0:
                        self.ts(POOL, DGc[c][:, kk, :], self.ident, pvl[:, col:col + 1], None, ALU.mult, r=[self.cst, self.pv], w=[DGc[c]])
                    else:
                        self.act(DGc[c][:, kk, :], self.ident, AF.Copy, r=[self.cst, self.pv], w=[DGc[c]], scale=pvl[:, col:col + 1])
            kconv = 0
            sq = [self.sb(es, f"csq{c}", [128, 512]) for c in range(2)]
            st = self.sb(es, "cst", [128, 3, 512])
            yb = [self.sb(es, f"cy{c}", [128, 512]) for c in range(2)]
            ob = [self.sb(es, f"cob{i}", [128, 512], BF16) for i in range(4)]
            k = 0
            for (tok0, T, hcol0, vec) in SEQS:
                for c in range(2):
                    self.dma(SP, av[c][:, 0:T], self.P[c * 128:(c + 1) * 128, tok0:tok0 + T], w=[av[c]])
                    self.dma(SP, ag[c][:, 0:T], self.P[(2 + c) * 128:(3 + c) * 128, tok0:tok0 + T], w=[ag[c]])
                    self.memset(POOL, U[c][:, 0:15], 0.0, w=[U[c]])
                    self.memset(POOL, U[c][:, 15 + T:30 + T], 0.0, w=[U[c]])
                    self.act(ag[c][:, 0:T], ag[c][:, 0:T], AF.Sigmoid, r=[ag[c]], w=[ag[c]])
                    self.tt(POOL, U[c][:, 15:15 + T], av[c][:, 0:T], ag[c][:, 0:T], ALU.mult, r=[av[c], ag[c]], w=[U[c]])
                    acc = av[c]
                    for t in range(0, T, 512):
                        n = min(512, T - t)
                        psb = self.bank[6 + kconv % 2]
                        kconv += 1
                        for kk in range(31):
                            self.mm(psb[:, 0:n], DGc[c][:, kk, :], U[c][:, t + kk:t + kk + n], start=(kk == 0), stop=(kk == 30),
                                    r=[DGc[c], U[c]], w=[psb])
                        self.act(acc[:, t:t + n], psb[:, 0:n], AF.Identity, r=[psb, self.pv], w=[acc],
                                 bias=pvl[:, PV_CB + c:PV_CB + c + 1], scale=1.0)
                for t in range(0, T, 512):
                    n = min(512, T - t)
                    p1 = self.bank[(2 * k) % 6]
                    p2 = self.bank[(2 * k + 1) % 6]
                    for c in range(2):
                        self.act(sq[c][:, 0:n], av[c][:, t:t + n], AF.Square, r=[av[c]], w=[sq[c]])
                    for c in range(2):
                        self.mm(p1[:, 0:n], self.ones, av[c][:, t:t + n], start=(c == 0), stop=(c == 1), r=[self.cst, av[c]], w=[p1])
                    for c in range(2):
                        self.mm(p2[:, 0:n], self.ones, sq[c][:, 0:n], start=(c == 0), stop=(c == 1), r=[self.cst, sq[c]], w=[p2])
                    self.ts(DVE, st[:, 0, 0:n], p1[:, 0:n], 1.0 / 256, None, ALU.mult, r=[p1], w=[st])
                    self.tt(DVE, st[:, 1, 0:n], st[:, 0, 0:n], st[:, 0, 0:n], ALU.mult, r=[st], w=[st])
                    self.stt(st[:, 1, 0:n], p2[:, 0:n], 1.0 / 256, st[:, 1, 0:n], ALU.mult, ALU.subtract, r=[p2, st], w=[st])
                    self.ts(DVE, st[:, 1, 0:n], st[:, 1, 0:n], 0.0, None, ALU.max, r=[st], w=[st])
                    self.act(st[:, 2, 0:n], st[:, 1, 0:n], AF.Sqrt, r=[st, self.cst], w=[st], bias=self.c_eps, scale=1.0)
                    self.recip(st[:, 2, 0:n], st[:, 2, 0:n], r=[st], w=[st])
                    for c in range(2):
                        self.tt(DVE, yb[c][:, 0:n], av[c][:, t:t + n], st[:, 0, 0:n], ALU.subtract, r=[av[c], st], w=[yb[c]])
                        self.tt(POOL, yb[c][:, 0:n], yb[c][:, 0:n], st[:, 2, 0:n], ALU.mult, r=[yb[c], st], w=[yb[c]])
                        o = ob[(2 * k + c) % 4]
                        self.act(o[:, 0:n], yb[c][:, 0:n], AF.Silu, r=[yb[c], self.pv], w=[o],
                                 scale=pvl[:, PV_CLG + c:PV_CLG + c + 1], bias=pvl[:, PV_CLB + c:PV_CLB + c + 1])
                        self.dma(SP, self.CAT[c * 128:(c + 1) * 128, tok0 + t:tok0 + t + n], o[:, 0:n], r=[o])
                    k += 1

    def attn_unit(self, slot, qf, hp, b0, t0, pieces, vts, otm, h, W):
        nk = sum(p[3] for p in pieces)
        nkt = nk // 128
        base = slot * 1024
        ps = self.pst[:, base:base + nk]
        pb = [self.bank[2 * slot], self.bank[2 * slot + 1]]
        lhsT = qf[b0:b0 + 64, hp, t0:t0 + 128]
        for (col, rhs, bm, n) in pieces:
            self.mm(self.pst[:, base + col:base + col + n], lhsT, rhs, start=True, stop=(bm is None), r=W["qk"], w=pb)
            if bm is not None:
                self.mm(self.pst[:, base + col:base + col + n], self.jrevb, bm, start=False, stop=True, r=W["bm"], w=pb)
        sm = W["sm"][slot * 2 + h % 2]
        pexp = W["pexp"][slot]
        pts = W["pts"][slot]
        yield
        self.S.op(DVE, lambda e: e.reduce_max(out=sm[:, 0:1], in_=ps, axis=AX.X), pb, [sm])
        yield
        self.ts(DVE, sm[:, 1:2], sm[:, 0:1], -0.125, None, ALU.mult, r=[sm], w=[sm])
        self.memset(DVE, sm[:, 2:3], 0.0, w=[sm])
        yield
        self.act(pexp[:, 0:nk], ps, AF.Exp, r=pb + [sm], w=[pexp, sm], scale=0.125, bias=sm[:, 1:2], accum_out=sm[:, 2:3])
        yield
        ptbank = self.bank[4 + slot]
        ptb = ptbank.t.bitcast(BF16)
        for c in range(nkt):
            self.tr(ptb[:, c * 128:(c + 1) * 128], pexp[:, c * 128:(c + 1) * 128], self.identb, r=[pexp, self.cstb], w=[ptbank])
        yield
        self.cp(ACT if slot == 0 else DVE, pts[:, 0:nkt, :].rearrange("p a b -> p (a b)"), ptb[:, 0:nk], r=[ptbank], w=[pts])
        self.recip(sm[:, 3:4], sm[:, 2:3], r=[sm], w=[sm])
        yield
        po = self.bank[6]
        for c in range(nkt):
            self.mm(po[:, 0:64], pts[:, c, :], vts[c], start=(c == 0), stop=(c == nkt - 1), r=[pts] + W["v"], w=[po])
        self.ts(DVE, otm[:, h * 64:(h + 1) * 64], po[:, 0:64], sm[:, 3:4], None, ALU.mult, r=[po, sm], w=[otm])
        yield

    def stage_attn(self, l):
        with contextlib.ExitStack() as es:
            qf = self.sb(es, "aqf", [128, 2, TT], BF16)
            kf = self.sb(es, "akf", [128, 2, TT], BF16)
            vf = self.sb(es, "avf", [128, 2, TT], BF16)
            VT = self.sb(es, "aVT", [128, 36, 256], BF16)
            kctx = self.sb(es, "akctx", [128, 2, TC], BF16)
            vctx = self.sb(es, "avctx", [128, 2, 256], BF16)
            HK = self.sb(es, "aHK", [128, 4, 15, 64])
            BMc = [self.sb(es, f"aBM{c}", [128, 4, 640], BF16) for c in range(5)]
            FT = self.sb(es, "aFT", [64, 127])
            r31 = self.sb(es, "ar31", [64, 31])
            ocr = self.sb(es, "aocr", [128, 2, TT], BF16)
            otms = [self.sb(es, f"aotm{i}", [128, 256], BF16) for i in range(2)]
            W = dict(sm=[self.sb(es, f"asm{i}", [128, 4]) for i in range(4)],
                     pexp=[self.sb(es, f"apexp{i}", [128, 896], BF16) for i in range(2)],
                     pts=[self.sb(es, f"apts{i}", [128, 7, 128], BF16) for i in range(2)])
            fpb = Buf("FPb", None)
            for hp in range(2):
                self.dma(SP, qf[:, hp, :], self.PB[hp * 128:(hp + 1) * 128, :], w=[qf])
                self.dma(SP, kf[:, hp, :], self.PB[(2 + hp) * 128:(3 + hp) * 128, :], w=[kf])
                self.dma(SP, vf[:, hp, :], self.PB[(4 + hp) * 128:(5 + hp) * 128, :], w=[vf])
            self.dma(POOL, kctx[:], self.kcache[l].rearrange("a p t -> p a t"), w=[kctx])
            self.dma(POOL, vctx[:], self.vcache[l].rearrange("a p f -> p a f"), w=[vctx])
            for g in range(9):
                bk = self.bank[g % 2]
                bkb = bk.t.bitcast(BF16)
                for jj in range(4):
                    tile = g * 4 + jj
                    for hp in range(2):
                        self.tr(bkb[:, (jj * 2 + hp) * 128:(jj * 2 + hp + 1) * 128], vf[:, hp, tile * 128:(tile + 1) * 128],
                                self.identb, r=[vf, self.cstb], w=[bk])
                self.cp(DVE if g % 2 == 0 else ACT, VT[:, g * 4:(g + 1) * 4, :].rearrange("p a f -> p (a f)"), bkb[:, 0:1024], r=[bk], w=[VT])
            self.memset(POOL, FT[0:60, :], NEG, w=[FT])
            self.dma(SP, r31[0:60, :], self.rpb[l], w=[r31])
            self.act(FT[0:60, 48:79], r31[0:60, :], AF.Copy, r=[r31, FT], w=[FT], scale=8.0)
            self.dma(SP, self.FP[l], FT[0:60, :], r=[FT], w=[fpb])
            for half in range(2):
                src = bass.AP(tensor=self.FP.tensor, offset=l * 60 * 127, ap=[[1, 64], [127, 60], [1, 64]])
                self.dma(SP, HK[64 * half:64 * half + 64].rearrange("p h d k -> p (h d) k"), src, r=[fpb], w=[HK])
            self.tt(DVE, HK[:].rearrange("p h d k -> p (h d) k"), HK[:].rearrange("p h d k -> p (h d) k"),
                    self.cst[:, C_CMH, 0:64].unsqueeze(1).to_broadcast([128, 60, 64]), ALU.add, r=[HK, self.cst], w=[HK])
            engs = (POOL, DVE, ACT, POOL, DVE)
            for case, i in ((0, 0), (1, 1), (2, 2), (3, 30), (4, 31)):
                bm = BMc[case]
                eng = engs[case]
                self.memset(eng if eng != ACT else POOL, bm[:], NEG, w=[bm])
                start = min(max(2 * i - 4, 0), 54)
                for qr in range(2):
                    r_ = 2 * i + qr
                    rs = min(max(r_ - 4, 0), 56)
                    for kr in range(10):
                        krow = start + kr
                        if rs <= krow < rs + 8:
                            dr = krow - r_ + 7
                            self.cp(eng, bm[64 * qr:64 * qr + 64, :, kr * 64:(kr + 1) * 64], HK[64 * qr:64 * qr + 64, :, dr, :], r=[HK], w=[bm])
            W["qk"] = [qf, kf, kctx]
            W["v"] = [VT, vctx]
            ob7 = self.bank[7].t.bitcast(BF16)

            def qtile(t0, units):
                def gen(slot):
                    otm = otms[slot]
                    for (hp, b0, pieces, vts, h, bmb) in units:
                        W["bm"] = [bmb, self.cstb] if bmb is not None else [self.cstb]
                        yield from self.attn_unit(slot, qf, hp, b0, t0, pieces, vts, otm, h, W)
                    ob = self.bank[7]
                    for hp in range(2):
                        self.tr(ob7[:, hp * 128:(hp + 1) * 128], otm[:, hp * 128:(hp + 1) * 128], self.identb, r=[otm, self.cstb], w=[ob])
                    self.cp(ACT if slot == 0 else DVE, ocr[:, :, t0:t0 + 128], ob7[:, 0:256].rearrange("p (a b) -> p a b", a=2), r=[ob], w=[ocr])
                    yield
                return gen

            jobs = []
            for i in range(32):
                t0 = 128 * i
                start = min(max(2 * i - 4, 0), 54)
                k0 = 64 * start
                case = {0: 0, 1: 1, 30: 3, 31: 4}.get(i, 2)
                units = []
                for h in range(4):
                    hp, b0 = h // 2, 64 * (h % 2)
                    bm = BMc[case]
                    pieces = [(0, kf[b0:b0 + 64, hp, k0:k0 + 512], bm[:, h, 0:512], 512),
                              (512, kf[b0:b0 + 64, hp, k0 + 512:k0 + 640], bm[:, h, 512:640], 128),
                              (640, kctx[b0:b0 + 64, hp, :], None, 256)]
                    vts = [VT[:, start // 2 + c, h * 64:(h + 1) * 64] for c in range(5)] + [vctx[:, c, h * 64:(h + 1) * 64] for c in range(2)]
                    units.append((hp, b0, pieces, vts, h, bm))
                jobs.append(qtile(t0, units))
            for s_ in range(2):
                tok0 = TL + s_ * TC
                for qi in range(2):
                    t0 = tok0 + 128 * qi
                    units = []
                    for h in range(4):
                        hp, b0 = h // 2, 64 * (h % 2)
                        pieces = [(0, kf[b0:b0 + 64, hp, tok0:tok0 + 256], None, 256)]
                        vts = [VT[:, tok0 // 128 + c, h * 64:(h + 1) * 64] for c in range(2)]
                        units.append((hp, b0, pieces, vts, h, None))
                    jobs.append(qtile(t0, units))
            self.run_pool(jobs, 2)
            for hp in range(2):
                self.dma(SP, self.CAT[768 + hp * 128:768 + (hp + 1) * 128, :], ocr[:, hp, :], r=[ocr])

    @staticmethod
    def run_pool(jobs, nslots):
        jobs = iter(jobs)
        active = {}
        free = list(range(nslots))
        while True:
            while free:
                jb = next(jobs, None)
                if jb is None:
                    break
                sl = free.pop(0)
                active[sl] = jb(sl)
            if not active:
                break
            for sl in list(active):
                try:
                    next(active[sl])
                except StopIteration:
                    del active[sl]
                    free.append(sl)

    def stage_gdn(self, l):
        pvl = self.pv[:, l, :]
        bank = self.bank
        S = self.S
        with contextlib.ExitStack() as es:
            sb = lambda name, shape, dt=F32: self.sb(es, "g" + name, shape, dt)
            qf, kf, vf = sb("qf", [128, 4, TL], BF16), sb("kf", [128, 4, TL], BF16), sb("vf", [128, 4, TL], BF16)
            NPM = TL // 128
            GT = sb("GT", [128, NPM, 16])
            beta, nbeta, g, GC, kds, negc = (sb(n, [128, NPM, 8]) for n in ("beta", "nbeta", "g", "GC", "kds", "negc"))
            t1, t2 = sb("t1", [128, NPM, 8]), sb("t2", [128, NPM, 8])
            egl = sb("egl", [128, NPM, 2, 8])
            gsb = sb("gsb", [128, 16])
            nega = sb("nega", [128, 8])
            ident, identb, ones = self.ident, self.identb, self.ones
            cst = self.cst
            one_ap = self.cst[:, C_EPS, 1:2]
            X, Y, Z, Wk, V, A = bank[0], bank[1], bank[2], bank[3], bank[4], bank[5]
            B2 = [bank[6], bank[7]]
            Bap = self.pst[:, 6 * 512:8 * 512].rearrange("p (h a k) -> p h a k", h=4, a=2)
            Xb = X.t.bitcast(BF16)
            def v3(b):
                return b.t.rearrange("p (h k) -> p h k", h=4)
            def bc_h(ap2):
                return ap2.unsqueeze(1).to_broadcast([128, 4, 128])
            def bc_i(ap2):
                return ap2.unsqueeze(2).to_broadcast([128, 4, 128])
            self.dma(SP, gsb[:], self.gsc[l:l + 1, :].partition_broadcast(128), w=[gsb])
            self.act(nega[:], gsb[:, 0:8], AF.Exp, r=[gsb], w=[nega])
            self.ts(DVE, nega[:], nega[:], -1.0, None, ALU.mult, r=[nega], w=[nega])
            for si, (tok0, T, hcol0, vec) in enumerate(SEQS):
                NP_ = T // 128
                with contextlib.ExitStack() as es2:
                    sb2 = lambda name, shape, dt=F32: self.sb(es2, "g" + name, shape, dt)
                    raws = [sb2(f"raw{i}", [128, 2050]) for i in range(2)]
                    cvs = [sb2(f"cv{i}", [128, 2048]) for i in range(2)]
                    sqs = [sb2(f"sq{i}", [128, 512], BF16) for i in range(2)]
                    rss = [sb2(f"rs{i}", [128, 512]) for i in range(2)]
                    bd = sb2("bd", [16, 512])
                    ui = 0
                    tj = 0
                    for ti, (dst, pc0) in enumerate(((qf, 4), (kf, 8), (vf, 12))):
                        for h in range(4):
                            row0 = (pc0 + h) * 128
                            cw = PV_GCW + 3 * (ti * 4 + h)
                            for half in range(0, T, 2048):
                                n = min(2048, T - half)
                                raw, cv = raws[ui % 2], cvs[ui % 2]
                                ui += 1
                                a = max(half - 1, 0)
                                b_ = min(half + n + 1, T)
                                off = a - (half - 1)
                                if off > 0:
                                    self.memset(POOL, raw[:, 0:1], 0.0, w=[raw])
                                if b_ - (half - 1) < n + 2:
                                    self.memset(POOL, raw[:, n + 1:n + 2], 0.0, w=[raw])
                                self.dma(SP, raw[:, off:off + (b_ - a)], self.P[row0:row0 + 128, tok0 + a:tok0 + b_], w=[raw])
                                self.act(cv[:, 0:n], raw[:, 0:n], AF.Copy, r=[raw, self.pv], w=[cv], scale=pvl[:, cw:cw + 1])
                                self.stt(cv[:, 0:n], raw[:, 1:n + 1], pvl[:, cw + 1:cw + 2], cv[:, 0:n], ALU.mult, ALU.add, r=[raw, self.pv, cv], w=[cv])
                                self.stt(cv[:, 0:n], raw[:, 2:n + 2], pvl[:, cw + 2:cw + 3], cv[:, 0:n], ALU.mult, ALU.add, r=[raw, self.pv, cv], w=[cv])
                                self.act(cv[:, 0:n], cv[:, 0:n], AF.Silu, r=[cv], w=[cv])
                                if ti == 2:
                                    self.cp(POOL, dst[:, h, half:half + n], cv[:, 0:n], r=[cv], w=[dst])
                                else:
                                    for t in range(0, n, 512):
                                        m = min(512, n - t)
                                        sq, rs = sqs[tj % 2], rss[tj % 2]
                                        tj += 1
                                        ps = bank[tj % 4]
                                        self.act(sq[:, 0:m], cv[:, t:t + m], AF.Square, r=[cv], w=[sq])
                                        self.mm(ps[:, 0:m], self.onesb, sq[:, 0:m], r=[sq, self.cstb], w=[ps])
                                        self.rsqrt_ps(rs[:, 0:m], ps[:, 0:m], 1.0, r=[ps], w=[rs])
                                        if ti == 0:
                                            self.stt(dst[:, h, half + t:half + t + m], cv[:, t:t + m], 128.0 ** -0.5, rs[:, 0:m], ALU.mult, ALU.mult, r=[cv, rs], w=[dst])
                                        else:
                                            self.tt(DVE, dst[:, h, half + t:half + t + m], cv[:, t:t + m], rs[:, 0:m], ALU.mult, r=[cv, rs], w=[dst])
                    for jb in range(0, T, 512):
                        n = min(512, T - jb)
                        self.dma(SP, bd[:, 0:n], self.P[26 * 128:26 * 128 + 16, tok0 + jb:tok0 + jb + n], w=[bd])
                        for j in range(n // 128):
                            self.tr(X[:, j * 16:(j + 1) * 16], bd[0:16, j * 128:(j + 1) * 128], ident[0:16, 0:16], r=[bd, cst], w=[X])
                        self.cp(DVE, GT[:, jb // 128:jb // 128 + n // 128, :].rearrange("p a b -> p (a b)"), X[:, 0:(n // 128) * 16], r=[X], w=[GT])
                    NB = NP_ * 8
                    def f2(t):
                        return t[:, 0:NP_, :].rearrange("p a b -> p (a b)")
                    dtb = gsb[:, 8:16].unsqueeze(1).to_broadcast([128, NP_, 8])
                    self.act(beta[:, 0:NP_, :], GT[:, 0:NP_, 0:8], AF.Sigmoid, r=[GT], w=[beta])
                    self.ts(DVE, nbeta[:, 0:NP_, :], beta[:, 0:NP_, :], -1.0, None, ALU.mult, r=[beta], w=[nbeta])
                    self.tt(DVE, t1[:, 0:NP_, :], GT[:, 0:NP_, 8:16], dtb, ALU.add, r=[GT, gsb], w=[t1])
                    self.act(t2[:, 0:NP_, :], t1[:, 0:NP_, :], AF.Abs, r=[t1], w=[t2])
                    self.act(t2[:, 0:NP_, :], t2[:, 0:NP_, :], AF.Exp, r=[t2], w=[t2], scale=-1.0)
                    self.act(t2[:, 0:NP_, :], t2[:, 0:NP_, :], AF.Ln, r=[t2, cst], w=[t2], bias=one_ap, scale=1.0)
                    self.ts(DVE, t1[:, 0:NP_, :], t1[:, 0:NP_, :], 0.0, None, ALU.max, r=[t1], w=[t1])
                    self.tt(DVE, t1[:, 0:NP_, :], t1[:, 0:NP_, :], t2[:, 0:NP_, :], ALU.add, r=[t1, t2], w=[t1])
                    self.tt(DVE, g[:, 0:NP_, :], t1[:, 0:NP_, :], nega[:].unsqueeze(1).to_broadcast([128, NP_, 8]), ALU.mult, r=[t1, nega], w=[g])
                    g2 = f2(g)
                    self.mm(X[:, 0:NB], cst[:, C_TRIF, :], g2, r=[cst, g], w=[X])
                    self.mm(Y[:, 0:NB], cst[:, C_TRIB, :], g2, r=[cst, g], w=[Y])
                    self.mm(Z[:, 0:NB], cst[:, C_BLK, :], g2, r=[cst, g], w=[Z])
                    self.mm(Wk[:, 0:NB], cst[:, C_HALFA, :], g2, r=[cst, g], w=[Wk])
                    self.mm(V[:, 0:NB], cst[:, C_HALFB, :], g2, r=[cst, g], w=[V])
                    self.cp(DVE, GC[:, 0:NP_, 0:4], X[:, 0:NB].rearrange("p (a b) -> p a b", b=8)[:, :, 0:4], r=[X], w=[GC])
                    self.cp(DVE, GC[:, 0:NP_, 4:8], Y[:, 0:NB].rearrange("p (a b) -> p a b", b=8)[:, :, 4:8], r=[Y], w=[GC])
                    self.tt(DVE, f2(kds), Z[:, 0:NB], f2(GC), ALU.subtract, r=[Z, GC], w=[kds])
                    self.act(f2(kds), f2(kds), AF.Exp, r=[kds], w=[kds])
                    self.act(f2(t1), f2(GC), AF.Exp, r=[GC], w=[t1])
                    self.tt(DVE, f2(negc), f2(nbeta), f2(t1), ALU.mult, r=[nbeta, t1], w=[negc])
                    self.act(egl[:, 0:NP_, 0, :], Wk[:, 0:NB].rearrange("p (a b) -> p a b", b=8), AF.Exp, r=[Wk], w=[egl])
                    self.act(egl[:, 0:NP_, 1, :], V[:, 0:NB].rearrange("p (a b) -> p a b", b=8), AF.Exp, r=[V], w=[egl])
                S.barrier()
                with contextlib.ExitStack() as es3:
                    def mkset(d):
                        sb3 = lambda name, shape, dt=F32: self.sb(es3, f"g{d}" + name, shape, dt)
                        outs = [dict(Kd=sb3(f"Kd{i}", [128, 4, 128], BF16), Vb=sb3(f"Vb{i}", [128, 4, 128]), ITm=sb3(f"ITm{i}", [128, 4, 128], BF16),
                                     Yb=sb3(f"Yb{i}", [128, 4, 128], BF16), Qg=sb3(f"Qg{i}", [128, 4, 128], BF16)) for i in range(2)]
                        return dict(out=outs, GR=sb3("GR", [128, 4, 128]), D1=sb3("D1", [128, 4, 128]), D2=sb3("D2", [128, 4, 128]),
                                    Pm=sb3("Pm", [128, 4, 128]), PY=sb3("PY", [128, 4, 2, 128]),
                                    Rb=sb3("Rb", [128, 4, 128], BF16), VNb=sb3("VNb", [128, 4, 128], BF16), S=sb3("S", [128, 4, 128]),
                                    Sb=sb3("Sb", [128, 4, 128], BF16), OT=sb3("OT", [128, 4, 128]))
                    sets = [mkset(0), mkset(1)]
                    ofb = {}
                    done_pre = [0, 0]
                    done_scan = [0, 0]

                    def pre_chain(d):
                        def gen(slot):
                            W = sets[d]
                            GR, D1, D2, Pm, PY = (W[k] for k in ("GR", "D1", "D2", "Pm", "PY"))
                            DG = D1
                            m1c, m2c = (C_UINC, C_LSTR) if d == 0 else (C_LINC, C_USTR)
                            c0, c1, c2 = (bank[4 * d + q] for q in range(3))
                            X, Y, Z, Wk, V, A = c0, c1, c2, c0, c0, c2
                            B2 = [c0, c1]
                            Bap = self.pst[:, 4 * d * 512:(4 * d + 2) * 512].rearrange("p (h a k) -> p h a k", h=4, a=2)
                            Xb = X.t.bitcast(BF16)
                            order = list(range(NP_)) if d == 0 else list(range(NP_ - 1, -1, -1))
                            for idx, p in enumerate(order):
                                while idx - done_scan[d] >= 2:
                                    yield
                                O_ = W["out"][idx % 2]
                                Kd, Vb, ITm, Yb, Qg = (O_[k] for k in ("Kd", "Vb", "ITm", "Yb", "Qg"))
                                t0 = 128 * p
                                gcol = GC[:, p, 4 * d:4 * d + 4]
                                for h in range(4):
                                    self.tr(Xb[:, h * 128:(h + 1) * 128], kf[:, h, t0:t0 + 128], identb, r=[kf, self.cstb], w=[X])
                                    self.tr(Xb[:, (4 + h) * 128:(5 + h) * 128], vf[:, h, t0:t0 + 128], identb, r=[vf, self.cstb], w=[X])
                                self.tt(DVE, DG[:], bc_h(ident), bc_i(gcol), ALU.mult, r=[cst, GC], w=[DG])
                                for h in range(4):
                                    self.mm(Y[:, h * 128:(h + 1) * 128], kf[:, h, t0:t0 + 128], kf[:, h, t0:t0 + 128], r=[kf], w=[Y])
                                    self.mm(Z[:, h * 128:(h + 1) * 128], kf[:, h, t0:t0 + 128], qf[:, h, t0:t0 + 128], r=[kf, qf], w=[Z])
                                yield
                                self.tt(DVE, Kd[:], Xb[:, 0:512].rearrange("p (h k) -> p h k", h=4), bc_i(kds[:, p, 4 * d:4 * d + 4]), ALU.mult, r=[X, kds], w=[Kd])
                                self.tt(DVE, Vb[:], Xb[:, 512:1024].rearrange("p (h k) -> p h k", h=4), bc_i(beta[:, p, 4 * d:4 * d + 4]), ALU.mult, r=[X, beta], w=[Vb])
                                yield
                                self.mm(Wk[:, 0:512], ones, DG[:].rearrange("p h k -> p (h k)"), r=[cst, DG], w=[Wk])
                                yield
                                self.cp(ACT, GR[:].rearrange("p h k -> p (h k)"), Wk[:, 0:512], r=[Wk], w=[GR])
                                yield
                                self.tt(DVE, D1[:], GR[:], bc_i(gcol), ALU.subtract, r=[GR, GC], w=[D1])
                                self.tt(DVE, D2[:], bc_i(gcol), GR[:], ALU.subtract, r=[GR, GC], w=[D2])
                                yield
                                self.act(GR[:], GR[:], AF.Exp, r=[GR], w=[GR])
                                self.tt(POOL, D1[:], D1[:], bc_h(cst[:, m1c, :]), ALU.add, r=[D1, cst], w=[D1])
                                self.tt(POOL, D2[:], D2[:], bc_h(cst[:, m2c, :]), ALU.add, r=[D2, cst], w=[D2])
                                yield
                                self.tt(POOL, Qg[:], qf[:, :, t0:t0 + 128], GR[:], ALU.mult, r=[qf, GR], w=[Qg])
                                self.act(D2[:], D2[:], AF.Exp, r=[D2], w=[D2])
                                self.act(D1[:], D1[:], AF.Exp, r=[D1], w=[D1])
                                yield
                                for h in range(4):
                                    self.stt(Pm[:, h, :], Y[:, h * 128:(h + 1) * 128], nbeta[:, p, 4 * d + h:4 * d + h + 1], D2[:, h, :], ALU.mult, ALU.mult,
                                             r=[Y, nbeta, D2], w=[Pm])
                                self.tt(DVE, ITm[:], v3(Z), D1[:], ALU.mult, r=[Z, D1], w=[ITm])
                                yield
                                for h in range(4):
                                    self.tr(V[:, h * 128:(h + 1) * 128], Pm[:, h, :], ident, r=[Pm, cst], w=[V])
                                yield
                                self.cp(ACT, PY[:, :, 0, :], v3(V), r=[V], w=[PY])
                                yield
                                self.tt(POOL, PY[:, :, 1, :], PY[:, :, 0, :], bc_h(ident), ALU.add, r=[PY, cst], w=[PY])
                                for h in range(4):
                                    self.mm(A[:, h * 128:(h + 1) * 128], PY[:, h, 0, :], Pm[:, h, :], r=[PY, Pm], w=[A])
                                    self.mm(Bap[:, h, 0, :], Pm[:, h, :], PY[:, h, 0, :], r=[PY, Pm], w=B2)
                                yield
                                self.cp(ACT, Pm[:], v3(A), r=[A], w=[Pm])
                                self.cp(DVE, PY[:, :, 0, :], Bap[:, :, 0, :], r=B2, w=[PY])
                                yield
                                for k in range(1, 5):
                                    for h in range(4):
                                        self.mm(A[:, h * 128:(h + 1) * 128], PY[:, h, 0, :], Pm[:, h, :], r=[PY, Pm], w=[A])
                                        self.mm(Bap[:, h, :, :].rearrange("p a k -> p (a k)"), Pm[:, h, :], PY[:, h, :, :].rearrange("p a k -> p (a k)"), r=[PY, Pm], w=B2)
                                    yield
                                    self.cp(ACT, Pm[:], v3(A), r=[A], w=[Pm])
                                    self.cp(DVE, PY[:, :, 0, :], Bap[:, :, 0, :], r=B2, w=[PY])
                                    self.tt(DVE, PY[:, :, 1, :], PY[:, :, 1, :], Bap[:, :, 1, :], ALU.add, r=B2 + [PY], w=[PY])
                                    yield
                                for h in range(4):
                                    self.mm(A[:, h * 128:(h + 1) * 128], Pm[:, h, :], PY[:, h, 1, :], r=[PY, Pm], w=[A])
                                yield
                                self.tt(DVE, Yb[:], PY[:, :, 1, :], v3(A), ALU.add, r=[A, PY], w=[Yb])
                                done_pre[d] = idx + 1
                                yield
                        return gen

                    def scan_chain(d):
                        def gen(slot):
                            W = sets[d]
                            Rb, VNb, Sst, Sb, OT = (W[k] for k in ("Rb", "VNb", "S", "Sb", "OT"))
                            OFd = self.OF if d == 0 else self.OFB
                            C3 = bank[4 * d + 3]
                            self.memset(POOL, Rb[:], 0.0, w=[Rb])
                            self.memset(POOL, VNb[:], 0.0, w=[VNb])
                            if si == 0:
                                self.dma(SP, Sst[:], self.s0[l, :, 4 * d:4 * d + 4, :], w=[Sst])
                            else:
                                self.memset(POOL, Sst[:], 0.0, w=[Sst])
                            self.cp(POOL, Sb[:], Sst[:], r=[Sst], w=[Sb])
                            yield
                            order = list(range(NP_)) if d == 0 else list(range(NP_ - 1, -1, -1))
                            for idx, p in enumerate(order):
                                while done_pre[d] <= idx:
                                    yield
                                O_ = W["out"][idx % 2]
                                Kd, Vb, ITm, Yb, Qg = (O_[k] for k in ("Kd", "Vb", "ITm", "Yb", "Qg"))
                                t0 = 128 * p
                                for hf in ((0, 1) if d == 0 else (1, 0)):
                                    R = slice(64 * hf, 64 * hf + 64)
                                    cs = slice(64 * hf, 64 * hf + 64)
                                    for h in range(4):
                                        self.mm(C3[:, h * 128:(h + 1) * 128], kf[:, h, t0:t0 + 128], Sb[:, h, :], r=[kf, Sb], w=[C3])
                                    yield
                                    for h in range(4):
                                        self.stt(Rb[R, h, :], C3[R, h * 128:(h + 1) * 128], negc[R, p, 4 * d + h:4 * d + h + 1], Vb[R, h, :], ALU.mult, ALU.add,
                                                 r=[C3, negc, Vb], w=[Rb])
                                    yield
                                    for h in range(4):
                                        self.mm(C3[:, h * 128:(h + 1) * 128], Yb[:, h, :], Rb[:, h, :], r=[Yb, Rb], w=[C3])
                                    yield
                                    self.cp(ACT, VNb[R, :, :].rearrange("p h k -> p (h k)"), C3[R, 0:512], r=[C3], w=[VNb])
                                    yield
                                    for h in range(4):
                                        self.mm(C3[:, h * 128:(h + 1) * 128], Kd[R, h, :], VNb[R, h, :], r=[Kd, VNb], w=[C3])
                                    yield
                                    for h in range(4):
                                        self.stt(Sst[:, h, :], Sst[:, h, :], egl[:, p, hf, 4 * d + h:4 * d + h + 1], C3[:, h * 128:(h + 1) * 128], ALU.mult, ALU.add,
                                                 r=[Sst, egl, C3], w=[Sst])
                                    yield
                                    for h in range(4):
                                        self.mm(C3[:, h * 64:(h + 1) * 64], Sb[:, h, :], Qg[:, h, cs], start=True, stop=False, r=[Sb, Qg], w=[C3])
                                        self.mm(C3[:, h * 64:(h + 1) * 64], VNb[R, h, :], ITm[R, h, cs], start=False, stop=True, r=[VNb, ITm], w=[C3])
                                    self.cp(POOL, Sb[:], Sst[:], r=[Sst], w=[Sb])
                                    yield
                                    self.cp(ACT, OT[:, :, cs], C3[:, 0:256].rearrange("p (h k) -> p h k", h=4), r=[C3], w=[OT])
                                    yield
                                ofb[(d, p)] = Buf(f"of{d}_{p}", None)
                                self.dma(SP, OFd[:, tok0 + t0:tok0 + t0 + 128].rearrange("(h p) t -> p h t", p=128), OT[:], r=[OT], w=[ofb[(d, p)]])
                                done_scan[d] = idx + 1
                                yield
                            if si > 0:
                                self.dma(SP, self.stout[si - 1, l, d].rearrange("h k v -> k h v"), Sst[:], r=[Sst])
                        return gen

                    self.run_pool([pre_chain(0), pre_chain(1), scan_chain(0), scan_chain(1)], 4)
                    ep = [dict(ofl=self.sb(es3, f"gofl{i}", [128, 4, 128]), ofr=self.sb(es3, f"gofr{i}", [128, 4, 128]), zt=self.sb(es3, f"gzt{i}", [128, 4, 128]),
                               osq=self.sb(es3, f"gosq{i}", [128, 4, 128], BF16), ors=self.sb(es3, f"gors{i}", [128, 4, 128]),
                               ob=self.sb(es3, f"gob{i}", [128, 4, 128], BF16)) for i in range(2)]

                    def epilogue(p):
                        def gen(slot):
                            E = ep[slot]
                            ofl, ofr, zt, osq, ors, ob = (E[k] for k in ("ofl", "ofr", "zt", "osq", "ors", "ob"))
                            t0 = 128 * p
                            cols = slice(tok0 + t0, tok0 + t0 + 128)
                            self.dma(SP, ofl[:], self.OF[:, cols].rearrange("(h p) t -> p h t", p=128), r=[ofb[(0, p)]], w=[ofl])
                            self.dma(SP, ofr[:], self.OFB[:, cols].rearrange("(h p) t -> p h t", p=128), r=[ofb[(1, p)]], w=[ofr])
                            self.dma(SP, zt[:], self.P[16 * 128:20 * 128, cols].rearrange("(h p) t -> p h t", p=128), w=[zt])
                            yield
                            self.tt(POOL, ofl[:], ofl[:], ofr[:], ALU.add, r=[ofl, ofr], w=[ofl])
                            self.act(zt[:], zt[:], AF.Silu, r=[zt], w=[zt])
                            yield
                            self.act(osq[:], ofl[:], AF.Square, r=[ofl], w=[osq])
                            yield
                            pb_ = bank[2 + slot]
                            self.mm(pb_[:, 0:512], self.onesb, osq[:].rearrange("p h k -> p (h k)"), r=[osq, self.cstb], w=[pb_])
                            yield
                            self.act(ors[:].rearrange("p h k -> p (h k)"), pb_[:, 0:512], AF.Sqrt, r=[pb_, self.cst], w=[ors], bias=self.c_eps, scale=1.0 / 128)
                            yield
                            self.recip(ors[:], ors[:], r=[ors], w=[ors])
                            yield
                            self.tt(DVE, ofl[:], ofl[:], ors[:], ALU.mult, r=[ofl, ors], w=[ofl])
                            yield
                            self.stt(ob[:], ofl[:], pvl[:, PV_GNW:PV_GNW + 1], zt[:], ALU.mult, ALU.mult, r=[ofl, self.pv, zt], w=[ob])
                            yield
                            self.dma(SP, self.CAT[256:768, cols].rearrange("(h p) t -> p h t", p=128), ob[:], r=[ob])
                        return gen

                    self.run_pool([epilogue(p) for p in range(NP_)], 2)
                S.barrier()


def _prep_inputs(inp, nl=DEPTH):
    f = lambda a: np.ascontiguousarray(np.asarray(a, dtype=np.float32))
    inp = {k: f(v) for k, v in inp.items()}
    shared = {
        "cst": _consts(),
        "pv": _pack_pv(inp),
        "w_ada": inp["w_ada"][:nl],
        "w_in": np.ascontiguousarray(inp["w_in"][:nl][:, :, _WIN_PERM]),
        "w_out": inp["w_out"][:nl],
        "w_up": inp["w_up"][:nl],
        "w_down": inp["w_down"][:nl],
        "gsc": np.ascontiguousarray(np.concatenate([inp["gdn_a_log"].reshape(DEPTH, 8), inp["gdn_dt_bias"].reshape(DEPTH, 8)], axis=1)[:nl]),
        "rpb": np.ascontiguousarray(inp["na_rpb"].reshape(DEPTH, 60, 31)[:nl]),
    }
    maps = []
    for i in range(8):
        xin = np.concatenate([inp["x_sample"][i].T, inp["x_prompt"][2 * i].T, inp["x_prompt"][2 * i + 1].T], axis=1)
        cv = np.stack([inp["c_ctx"], inp["c"][i]], axis=1)
        cvec = cv.reshape(8, 128, 2).transpose(1, 0, 2).reshape(128, 16)
        kv = inp["cache_attn_kv"][i][:nl]
        kc = kv[:, 0].transpose(0, 1, 3, 2).reshape(nl, 2, 128, TC)
        vc = kv[:, 1].transpose(0, 2, 1, 3).reshape(nl, 2, 128, 256)
        s0 = inp["state_delta"][i][:nl].reshape(nl, 8, 128, 128).transpose(0, 2, 1, 3)
        m = dict(shared)
        m.update(xin=np.ascontiguousarray(xin), cvec=np.ascontiguousarray(cvec), kcache=np.ascontiguousarray(kc),
                 vcache=np.ascontiguousarray(vc), s0=np.ascontiguousarray(s0))
        maps.append(m)
    return maps


def _assemble(results, nl=DEPTH):
    y_prompt = np.zeros((16, TC, D), np.float32)
    y_sample = np.zeros((8, TL, D), np.float32)
    kvn = np.zeros((16, nl, 2, 4, TC, 64), np.float32)
    stn = np.zeros((16, nl, 2, 4, 128, 128), np.float32)
    for i, r in enumerate(results):
        xo = r["xout"]
        y_sample[i] = xo[:, 0:TL].T
        y_prompt[2 * i] = xo[:, TL:TL + TC].T
        y_prompt[2 * i + 1] = xo[:, TL + TC:TT].T
        kvo = r["kvout"]
        for s in range(2):
            kvn[2 * i + s] = kvo[:, s].reshape(nl, 2, 4, 64, TC).transpose(0, 1, 2, 4, 3)
            stn[2 * i + s] = r["stout"][s]
    return y_prompt, y_sample, kvn, stn


_CACHE = {}


def kernel(**inputs):
    if "kb" not in _CACHE:
        kb = KB()
        kb.build()
        _CACHE["kb"] = kb
    kb = _CACHE["kb"]
    maps = _prep_inputs(inputs)
    res = run_bass_kernel_spmd(kb.nc, maps, core_ids=list(range(8)))
    return _assemble(res.results)
```

```python
import contextlib
import numpy as np
import concourse.bass as bass
import concourse.mybir as mybir
from concourse.bass_utils import run_bass_kernel_spmd

F32 = mybir.dt.float32
BF16 = mybir.dt.bfloat16
AF = mybir.ActivationFunctionType
ALU = mybir.AluOpType
AX = mybir.AxisListType

PE, ACT, DVE, POOL, SP = "pe", "act", "dve", "pool", "sp"
COMPUTE = (PE, ACT, DVE, POOL)
ALLENG = (PE, ACT, DVE, POOL, SP)
DMAQ = (SP, ACT, POOL)
ND = 4
SAME_ENGINE_SYNC = True

D = 1024
DEPTH = 4
TL = 4096
TC = 256
TT = TL + 2 * TC
NCH = 27
PROJ = 3344
DFF = 2816
EPS = 1e-6
NEG = -30000.0
NEGBIG = -1.0e5
HW_ = TT + 6
SEQS = ((0, TL, 1, 1), (TL, TC, 4099, 0), (TL + TC, TC, 4357, 0))


class Buf:
    __slots__ = ("name", "t", "lw", "rd")

    def __init__(self, name, t):
        self.name = name
        self.t = t
        self.lw = None
        self.rd = []

    def __getitem__(self, k):
        return self.t[k]


class Op:
    __slots__ = ("eng", "fn", "deps", "marked", "is_dma", "sem", "cnt", "epoch", "kind")

    def __init__(self, eng, fn, is_dma, epoch, kind="op"):
        self.eng = eng
        self.fn = fn
        self.deps = []
        self.marked = False
        self.is_dma = is_dma
        self.sem = None
        self.cnt = 0
        self.epoch = epoch
        self.kind = kind


class Sched:
    def __init__(self, nc):
        self.nc = nc
        self.ops = []
        self.epoch = 0
        self.eng = {PE: nc.tensor, ACT: nc.scalar, DVE: nc.vector, POOL: nc.gpsimd, SP: nc.sync}

    def op(self, eng, fn, reads=(), writes=(), dma=False):
        o = Op(eng, fn, dma, self.epoch)
        deps = {}
        for b in reads:
            if b.lw is not None:
                deps[id(b.lw)] = (b.lw, "raw")
        for b in writes:
            if b.lw is not None:
                deps[id(b.lw)] = (b.lw, "waw")
            for r in b.rd:
                if id(r) not in deps:
                    deps[id(r)] = (r, "war")
        for d, kind in deps.values():
            if d.epoch != o.epoch or d is o:
                continue
            if (not d.is_dma) and d.eng == eng and not dma:
                if eng == PE or kind == "war" or not SAME_ENGINE_SYNC:
                    continue
            o.deps.append(d)
        for b in reads:
            b.rd.append(o)
        for b in writes:
            b.lw = o
            b.rd = []
        self.ops.append(o)
        return o

    def barrier(self, new_epoch=False):
        self.ops.append(Op(None, None, False, self.epoch, kind="barrier_ep" if new_epoch else "barrier"))
        if new_epoch:
            self.epoch += 1

    def emit(self, es):
        nc = self.nc
        nep = self.epoch + 1
        for o in self.ops:
            for d in o.deps:
                d.marked = True
        sems = []
        for ep in range(nep):
            d = {}
            for e in COMPUTE:
                d[e] = es.enter_context(nc.semaphore(f"s{ep}_{e}"))
            for q in DMAQ:
                for k in range(ND):
                    d[(q, k)] = es.enter_context(nc.semaphore(f"d{ep}_{q}{k}"))
            sems.append(d)
        cur = [dict((k, 0) for k in sems[ep]) for ep in range(nep)]
        rr = dict((q, 0) for q in DMAQ)
        seen = {}
        n_wait = 0
        for o in self.ops:
            ep = o.epoch
            if o.kind != "op":
                for e in ALLENG:
                    for key, val in cur[ep].items():
                        if val > 0 and seen.get((e, key), 0) < val:
                            self.eng[e].wait_ge(sems[ep][key], val)
                            seen[(e, key)] = val
                            n_wait += 1
                if o.kind == "barrier_ep":
                    seen = {}
                continue
            e = self.eng[o.eng]
            need = {}
            for d in o.deps:
                if need.get(d.sem, 0) < d.cnt:
                    need[d.sem] = d.cnt
            for key, val in need.items():
                if seen.get((o.eng, key), 0) >= val:
                    continue
                e.wait_ge(sems[ep][key], val)
                n_wait += 1
                seen[(o.eng, key)] = val
            inst = o.fn(e)
            if o.is_dma:
                key = (o.eng, rr[o.eng] % ND)
                rr[o.eng] += 1
                cur[ep][key] += 16
                o.sem, o.cnt = key, cur[ep][key]
                inst.then_inc(sems[ep][key], 16)
            elif o.marked:
                cur[ep][o.eng] += 1
                o.sem, o.cnt = o.eng, cur[ep][o.eng]
                inst.then_inc(sems[ep][o.eng], 1)
        ep = nep - 1
        for e in ALLENG:
            for key, val in cur[ep].items():
                if val > 0:
                    self.eng[e].wait_ge(sems[ep][key], val)
        return dict(n_ops=len(self.ops), n_wait=n_wait, counts=cur)


C_IDENT, C_ONES, C_JREV, C_TRIF, C_TRIB, C_BLK, C_HALFA, C_HALFB = range(8)
C_UINC, C_LSTR, C_LINC, C_USTR = 8, 9, 10, 11
C_CMH = 12
C_EPS = 13
NCST = 14


def _consts():
    p = np.arange(128)[:, None]
    f = np.arange(128)[None, :]
    same = (p // 64) == (f // 64)
    c = np.zeros((128, NCST, 128), np.float32)
    c[:, C_IDENT] = (p == f)
    c[:, C_ONES] = 1.0
    c[:, C_JREV] = same & ((p % 64) == (63 - f % 64))
    c[:, C_TRIF] = same & (p <= f)
    c[:, C_TRIB] = same & (p >= f)
    c[:, C_BLK] = same
    c[:, C_HALFA] = (p < 64) & (f >= 0)
    c[:, C_HALFB] = (p >= 64) & (f >= 0)
    c[:, C_UINC] = np.where(same & (f >= p), 0.0, NEGBIG)
    c[:, C_LSTR] = np.where(same & (p > f), 0.0, NEGBIG)
    c[:, C_LINC] = np.where(same & (f <= p), 0.0, NEGBIG)
    c[:, C_USTR] = np.where(same & (p < f), 0.0, NEGBIG)
    qc = 63 - (np.arange(128) % 64)
    cs = np.clip(qc - 8, 0, 48)
    kc = np.arange(64)[None, :]
    ok = (kc >= cs[:, None]) & (kc < cs[:, None] + 16)
    c[:, C_CMH, 0:64] = np.where(ok, 0.0, NEG)
    c[:, C_EPS, 0] = EPS
    c[:, C_EPS, 1] = 1.0
    return c.reshape(128, NCST * 128)


PV_GPM, PV_GPOM, PV_GPF, PV_GPOF = 0, 8, 16, 24
PV_BADA = 32
PV_CW = 80
PV_CB, PV_CLG, PV_CLB = 142, 144, 146
PV_GCW = 148
PV_GNW = 184
PV_FCW = 185
PV_FCB = 317
PV_N = 361


def _fm(v, nchunk):
    return np.ascontiguousarray(v.reshape(nchunk, 128).T)


def _pack_pv(inp):
    pv = np.zeros((128, DEPTH, PV_N), np.float32)
    for l in range(DEPTH):
        pv[:, l, PV_GPM:PV_GPM + 8] = _fm(inp["g_pre_mix"][l], 8)
        pv[:, l, PV_GPOM:PV_GPOM + 8] = _fm(inp["g_post_mix"][l], 8)
        pv[:, l, PV_GPF:PV_GPF + 8] = _fm(inp["g_pre_ffn"][l], 8)
        pv[:, l, PV_GPOF:PV_GPOF + 8] = _fm(inp["g_post_ffn"][l], 8)
        pv[:, l, PV_BADA:PV_BADA + 48] = _fm(inp["b_ada"][l], 48)
        cw = inp["conv_w"][l]
        pv[:, l, PV_CW:PV_CW + 62] = cw.T.reshape(2, 128, 31).transpose(1, 0, 2).reshape(128, 62)
        pv[:, l, PV_CB:PV_CB + 2] = _fm(inp["conv_b"][l], 2)
        pv[:, l, PV_CLG:PV_CLG + 2] = _fm(inp["conv_ln_g"][l], 2)
        pv[:, l, PV_CLB:PV_CLB + 2] = _fm(inp["conv_ln_b"][l], 2)
        gw = inp["gdn_conv_w"][l]
        pv[:, l, PV_GCW:PV_GCW + 36] = gw.T.reshape(12, 128, 3).transpose(1, 0, 2).reshape(128, 36)
        pv[:, l, PV_GNW] = inp["gdn_norm_w"][l]
        fw = inp["ffn_conv_w"][l]
        pv[:, l, PV_FCW:PV_FCW + 132] = fw.T.reshape(44, 128, 3).transpose(1, 0, 2).reshape(128, 132)
        pv[:, l, PV_FCB:PV_FCB + 44] = _fm(inp["ffn_conv_b"][l], 44)
    return pv.reshape(128, DEPTH * PV_N)


_WIN_PERM = np.concatenate([np.arange(0, 2560), np.arange(2576, 3344), np.arange(2560, 2576)])


class KB:
    def __init__(self, nlayers=DEPTH, dbg=None):
        self.nl = nlayers
        self.dbg = dbg or {}
        nc = self.nc = bass.Bass("TRN2", target_bir_lowering=False)
        self.S = Sched(nc)
        self.es = contextlib.ExitStack()
        self.uid = 0

    def mm(self, out, lhsT, rhs, start=True, stop=True, r=(), w=()):
        return self.S.op(PE, lambda e: e.matmul(out, lhsT, rhs, start=start, stop=stop), r, w)

    def tr(self, out, in_, ident, r=(), w=()):
        return self.S.op(PE, lambda e: e.transpose(out, in_, ident), r, w)

    def act(self, out, in_, func, r=(), w=(), eng=ACT, **kw):
        return self.S.op(eng, lambda e: e.activation(out=out, in_=in_, func=func, **kw), r, w)

    def tt(self, eng, out, in0, in1, op, r=(), w=()):
        return self.S.op(eng, lambda e: e.tensor_tensor(out=out, in0=in0, in1=in1, op=op), r, w)

    def ts(self, eng, out, in0, s1, s2, op0, op1=None, r=(), w=()):
        if op1 is None:
            return self.S.op(eng, lambda e: e.tensor_scalar(out=out, in0=in0, scalar1=s1, scalar2=None, op0=op0), r, w)
        return self.S.op(eng, lambda e: e.tensor_scalar(out=out, in0=in0, scalar1=s1, scalar2=s2, op0=op0, op1=op1), r, w)

    def stt(self, out, in0, scalar, in1, op0, op1, r=(), w=()):
        return self.S.op(DVE, lambda e: e.scalar_tensor_tensor(out=out, in0=in0, scalar=scalar, in1=in1, op0=op0, op1=op1), r, w)

    def cp(self, eng, out, in_, r=(), w=()):
        if eng == ACT:
            return self.S.op(ACT, lambda e: e.copy(out=out, in_=in_), r, w)
        return self.S.op(eng, lambda e: e.tensor_copy(out=out, in_=in_), r, w)

    def memset(self, eng, ap, val, w=()):
        return self.S.op(eng, lambda e: e.memset(ap, val), (), w)

    def recip(self, out, in_, r=(), w=()):
        return self.S.op(DVE, lambda e: e.reciprocal(out=out, in_=in_), r, w)

    def dma(self, q, out, in_, r=(), w=()):
        return self.S.op(q, lambda e: e.dma_start(out=out, in_=in_), r, w, dma=True)

    def sb(self, es, name, shape, dt=F32):
        self.uid += 1
        return Buf(name, es.enter_context(self.nc.sbuf_tensor(f"{name}_{self.uid}", shape, dt)))

    def dram(self, name, shape, dt, kind="Internal"):
        if name in self.dbg.get("out", ()):
            kind = "ExternalOutput"
        if name in self.dbg.get("in", ()):
            kind = "ExternalInput"
        return self.nc.dram_tensor(name, shape, dt, kind=kind).ap()

    def rsqrt_ps(self, out, ps_ap, scale, r, w):
        self.act(out, ps_ap, AF.Ln, r=list(r) + [self.cst], w=w, bias=self.c_eps, scale=scale)
        self.act(out, out, AF.Exp, r=w, w=w, scale=-0.5)

    def build(self):
        nc, S, es = self.nc, self.S, self.es
        nl = self.nl
        I = {}
        def ext(name, shape, dt=F32):
            I[name] = nc.dram_tensor(name, shape, dt, kind="ExternalInput").ap()
            return I[name]
        self.xin = ext("xin", [D, TT])
        self.cvec = ext("cvec", [128, 16])
        self.cst_d = ext("cst", [128, NCST * 128])
        self.pv_d = ext("pv", [128, DEPTH * PV_N])
        self.w_ada = ext("w_ada", [nl, D, 6 * D])
        self.w_in = ext("w_in", [nl, D, PROJ])
        self.w_out = ext("w_out", [nl, D, D])
        self.w_up = ext("w_up", [nl, D, 2 * DFF])
        self.w_down = ext("w_down", [nl, DFF, D])
        self.kcache = ext("kcache", [nl, 2, 128, TC])
        self.vcache = ext("vcache", [nl, 2, 128, 256])
        self.s0 = ext("s0", [nl, 128, 8, 128])
        self.gsc = ext("gsc", [nl, 16])
        self.rpb = ext("rpb", [nl, 60, 31])
        self.xout = nc.dram_tensor("xout", [D, TT], F32, kind="ExternalOutput").ap()
        self.kvout = nc.dram_tensor("kvout", [nl, 2, 2, 256, TC], F32, kind="ExternalOutput").ap()
        self.stout = nc.dram_tensor("stout", [2, nl, 2, 4, 128, 128], F32, kind="ExternalOutput").ap()
        self.P = self.dram("P", [NCH * 128, TT], F32)
        self.PB = self.dram("PB", [6 * 128, TT], BF16)
        self.CAT = self.dram("CAT", [D, TT], BF16)
        self.XA = self.dram("XA", [D, TT], F32)
        self.A = self.dram("A", [DFF, TT], BF16)
        self.OF = self.dram("OF", [512, TT], F32)
        self.OFB = self.dram("OFB", [512, TT], F32)
        self.FP = self.dram("FP", [nl, 60, 127], F32)

        with es:
            pst = es.enter_context(nc.psum_tensor("ps", [128, 4096], F32))
            self.bank = [Buf(f"bank{i}", pst[:, i * 512:(i + 1) * 512]) for i in range(8)]
            self.pst = pst
            self.cst = self.sb(es, "cst", [128, NCST, 128])
            self.cstb = self.sb(es, "cstb", [128, 3, 128], BF16)
            self.pv = self.sb(es, "pv", [128, DEPTH, PV_N])
            self.mod = self.sb(es, "mod", [128, 6, 8, 2])
            self.scv = self.sb(es, "scv", [128, 8, 2])
            self.c_eps = self.cst[:, C_EPS, 0:1]
            self.dma(SP, self.cst[:].rearrange("p a b -> p (a b)"), self.cst_d[:, :], w=[self.cst])
            self.dma(SP, self.pv[:].rearrange("p a b -> p (a b)"), self.pv_d[:, :], w=[self.pv])
            self.dma(SP, self.scv[:].rearrange("p a b -> p (a b)"), self.cvec[:, :], w=[self.scv])
            for i, ci in enumerate((C_IDENT, C_ONES, C_JREV)):
                self.cp(DVE, self.cstb[:, i, :], self.cst[:, ci, :], r=[self.cst], w=[self.cstb])
            self.ident = self.cst[:, C_IDENT, :]
            self.ones = self.cst[:, C_ONES, :]
            self.identb = self.cstb[:, 0, :]
            self.onesb = self.cstb[:, 1, :]
            self.jrevb = self.cstb[:, 2, :]
            self.act(self.scv[:], self.scv[:], AF.Silu, r=[self.scv], w=[self.scv])
            for l in range(nl):
                self.layer(l)
                if l + 1 < nl:
                    S.barrier(new_epoch=True)
            info = S.emit(es)
        return info

    def layer(self, l):
        S = self.S
        stages = self.dbg.get("stages", ("ada", "s1", "conf", "gdn", "attn", "s3", "s4a", "s4b"))
        xsrc = self.xin if l == 0 else self.xout
        if "ada" in stages:
            self.stage_ada(l)
            S.barrier()
        if "s1" in stages:
            self.stage_s1(l, xsrc)
            S.barrier()
        if "conf" in stages:
            self.stage_conf(l)
            S.barrier()
        if "attn" in stages:
            self.stage_attn(l)
            S.barrier()
        if "gdn" in stages:
            self.stage_gdn(l)
            S.barrier()
        if "s3" in stages:
            with contextlib.ExitStack() as es:
                self.Hb = self.sb(es, "H", [128, 8, HW_], BF16)
                self.stage_s3(l, xsrc, es)
                S.barrier()
                if "s4a" in stages:
                    self.stage_s4a(l)
                    S.barrier()
        if "s4b" in stages:
            self.stage_s4b(l)
            S.barrier()

    def stage_ada(self, l):
        with contextlib.ExitStack() as es:
            wb = [self.sb(es, f"adaw{i}", [128, 8, 512]) for i in range(6)]
            raw = self.sb(es, "adaraw", [128, 48, 2])
            ps = self.bank[0]
            for g in range(12):
                w = wb[g % 6]
                self.dma(SP, w[:],
                         self.w_ada[l, :, g * 512:(g + 1) * 512].rearrange("(kc p) n -> p kc n", p=128), w=[w])
                for j in range(4):
                    n = g * 4 + j
                    for kc in range(8):
                        self.mm(ps[:, 2 * n:2 * n + 2], w[:, kc, j * 128:(j + 1) * 128], self.scv[:, kc, :],
                                start=(kc == 0), stop=(kc == 7), r=[w, self.scv], w=[ps])
            pvl = self.pv[:, l, :]
            self.tt(DVE, raw[:], ps[:, 0:96].rearrange("p (n v) -> p n v", v=2),
                    pvl[:, PV_BADA:PV_BADA + 48].unsqueeze(2).to_broadcast([128, 48, 2]), ALU.add,
                    r=[ps, self.pv], w=[raw])
            def gain(off):
                return pvl[:, off:off + 8].unsqueeze(2).to_broadcast([128, 8, 2])
            m = self.mod
            self.ts(DVE, m[:, 0], raw[:, 8:16, :], 1.0, None, ALU.add, r=[raw], w=[m])
            self.tt(DVE, m[:, 0], m[:, 0], gain(PV_GPM), ALU.mult, r=[m, self.pv], w=[m])
            self.cp(DVE, m[:, 1], raw[:, 0:8, :], r=[raw], w=[m])
            self.tt(DVE, m[:, 2], raw[:, 16:24, :], gain(PV_GPOM), ALU.mult, r=[raw, self.pv], w=[m])
            self.ts(DVE, m[:, 3], raw[:, 32:40, :], 1.0, None, ALU.add, r=[raw], w=[m])
            self.tt(DVE, m[:, 3], m[:, 3], gain(PV_GPF), ALU.mult, r=[m, self.pv], w=[m])
            self.cp(DVE, m[:, 4], raw[:, 24:32, :], r=[raw], w=[m])
            self.tt(DVE, m[:, 5], raw[:, 40:48, :], gain(PV_GPOF), ALU.mult, r=[raw, self.pv], w=[m])

    def sumsq_rstd(self, src, sq, ps, rstd, n, nchunk=8, scale=1.0 / D, src_bufs=()):
        for c in range(nchunk):
            self.act(sq[:, c, 0:n], src[:, c, 0:n], AF.Square, r=list(src_bufs), w=[sq], eng=ACT)
        for c in range(nchunk):
            self.mm(ps[:, 0:n], self.onesb, sq[:, c, 0:n], start=(c == 0), stop=(c == nchunk - 1),
                    r=[sq, self.cstb], w=[ps])
        self.rsqrt_ps(rstd[:, 0:n], ps[:, 0:n], scale, r=[ps], w=[rstd])

    def s1_tiles(self):
        tl = [(j * 512, 512, 1, [(1 + j * 512, 512)]) for j in range(8)]
        tl.append((TL, 512, 0, [(4099, 256), (4357, 256)]))
        return tl

    def stage_s1(self, l, xsrc):
        with contextlib.ExitStack() as es:
            H = self.sb(es, "H", [128, 8, HW_], BF16)
            xt = [self.sb(es, f"xt{i}", [128, 8, 512]) for i in range(2)]
            sq = self.sb(es, "sq", [128, 8, 512], BF16)
            rstd = self.sb(es, "rstd", [128, 512])
            wck = [self.sb(es, f"wck{i}", [128, 8, 128], BF16) for i in range(3)]
            ev = [self.sb(es, f"ev{i}", [128, 512]) for i in range(4)]
            evb = [self.sb(es, f"evb{i}", [128, 512], BF16) for i in range(2)]
            tiles = self.s1_tiles()
            m = self.mod
            for ti, (t0, n, vec, hc) in enumerate(tiles):
                x = xt[ti % 2]
                self.dma(SP, x[:], xsrc[:, t0:t0 + n].rearrange("(c p) t -> p c t", p=128), w=[x])
                ps = self.bank[ti % 2]
                self.sumsq_rstd(x, sq, ps, rstd, n, src_bufs=[x])
                self.tt(DVE, x[:], x[:], rstd[:].unsqueeze(1).to_broadcast([128, 8, 512]), ALU.mult, r=[x, rstd], w=[x])
                for c in range(8):
                    off = 0
                    for (h0, nn) in hc:
                        eng = ACT if c % 2 == 0 else POOL
                        if eng == ACT:
                            self.act(H[:, c, h0:h0 + nn], x[:, c, off:off + nn], AF.Identity, r=[x, m], w=[H],
                                     scale=m[:, 0, c, vec:vec + 1], bias=m[:, 1, c, vec:vec + 1])
                        else:
                            self.ts(POOL, H[:, c, h0:h0 + nn], x[:, c, off:off + nn], m[:, 0, c, vec:vec + 1],
                                    m[:, 1, c, vec:vec + 1], ALU.mult, ALU.add, r=[x, m], w=[H])
                        off += nn
            k = 0
            for nci in range(NCH):
                ncols = 128 if nci < 26 else 16
                w = wck[nci % 3]
                self.dma(POOL, w[:, :, 0:ncols],
                         self.w_in[l, :, nci * 128:nci * 128 + ncols].rearrange("(kc p) n -> p kc n", p=128), w=[w])
                for ti, (t0, n, vec, hc) in enumerate(tiles):
                    ps = self.bank[2 + (k % 4)]
                    off = 0
                    for (h0, nn) in hc:
                        for kc in range(8):
                            self.mm(ps[0:ncols, off:off + nn], w[:, kc, 0:ncols], H[:, kc, h0:h0 + nn],
                                    start=(kc == 0), stop=(kc == 7), r=[w, H], w=[ps])
                        off += nn
                    e = ev[k % 4]
                    if k % 2 == 0:
                        self.cp(ACT, e[0:ncols, :], ps[0:ncols, :], r=[ps], w=[e])
                    else:
                        self.cp(DVE, e[0:ncols, :], ps[0:ncols, :], r=[ps], w=[e])
                    self.dma(SP, self.P[nci * 128:nci * 128 + ncols, t0:t0 + n], e[0:ncols, :], r=[e])
                    if 20 <= nci < 26:
                        eb = evb[k % 2]
                        self.cp(POOL, eb[:], e[:], r=[e], w=[eb])
                        self.dma(SP, self.PB[(nci - 20) * 128:(nci - 19) * 128, t0:t0 + n], eb[:], r=[eb])
                    if ti == 8 and 22 <= nci < 26:
                        kv = (nci - 22) // 2
                        f0 = ((nci - 22) % 2) * 128
                        for s in range(2):
                            self.dma(SP, self.kvout[l, s, kv, f0:f0 + 128, :], e[:, s * 256:(s + 1) * 256], r=[e])
                    k += 1

    def stage_s3(self, l, xsrc, es0):
        H = self.Hb
        with contextlib.ExitStack() as es:
            wo = self.sb(es, "wo", [128, 8, D], BF16)
            cat = [self.sb(es, f"cat{i}", [128, 8, 512], BF16) for i in range(2)]
            xt = [self.sb(es, f"xt{i}", [128, 8, 512]) for i in range(2)]
            msb = self.sb(es, "msb", [128, 8, 512])
            sq = self.sb(es, "sq", [128, 8, 512], BF16)
            rstd = self.sb(es, "rstd", [128, 512])
            zc = self.sb(es, "zc", [128, 8, 2], BF16)
            parts = self.dbg.get("s3parts", "wpx")
            if "w" in parts:
                self.dma(POOL, wo[:], self.w_out[l].rearrange("(kc p) n -> p kc n", p=128), w=[wo])
            if "p" in parts:
                self.memset(POOL, zc[:], 0.0, w=[zc])
                for c0, n0 in ((0, 1), (4097, 2), (4355, 2), (4613, 1)):
                    self.cp(POOL, H[:, :, c0:c0 + n0], zc[:, :, 0:n0], r=[zc], w=[H])
            if "x" not in parts:
                return
            m = self.mod
            lvl = int(self.dbg.get("s3n", 9))
            for ti, (t0, n, vec, hc) in enumerate(self.s1_tiles()):
                ct = cat[ti % 2]
                x = xt[ti % 2]
                self.dma(SP, ct[:], self.CAT[:, t0:t0 + n].rearrange("(c p) t -> p c t", p=128), w=[ct])
                self.dma(SP, x[:], xsrc[:, t0:t0 + n].rearrange("(c p) t -> p c t", p=128), w=[x])
                if lvl < 2:
                    continue
                for nn in range(8):
                    ps = self.bank[nn % 4]
                    for kc in range(8):
                        self.mm(ps[:, 0:n], wo[:, kc, nn * 128:(nn + 1) * 128], ct[:, kc, :],
                                start=(kc == 0), stop=(kc == 7), r=[wo, ct], w=[ps])
                    if "c" in self.dbg.get("s3l2", "cs"):
                        self.cp(DVE, msb[:, nn, :], ps[:, 0:n], r=[ps], w=[msb])
                    if "s" in self.dbg.get("s3l2", "cs"):
                        self.act(sq[:, nn, :], msb[:, nn, :], AF.Square, r=[msb], w=[sq])
                if lvl < 3:
                    continue
                pss = self.bank[4 + ti % 2]
                for c in range(8):
                    self.mm(pss[:, 0:n], self.onesb, sq[:, c, :], start=(c == 0), stop=(c == 7), r=[sq, self.cstb], w=[pss])
                self.rsqrt_ps(rstd[:, 0:n], pss[:, 0:n], 1.0 / D, r=[pss], w=[rstd])
                self.tt(DVE, msb[:], msb[:], rstd[:].unsqueeze(1).to_broadcast([128, 8, 512]), ALU.mult, r=[msb, rstd], w=[msb])
                if lvl < 4:
                    continue
                for c in range(8):
                    self.stt(x[:, c, :], msb[:, c, :], m[:, 2, c, vec:vec + 1], x[:, c, :], ALU.mult, ALU.add,
                             r=[msb, m, x], w=[x])
                self.dma(SP, self.XA[:, t0:t0 + n].rearrange("(c p) t -> p c t", p=128), x[:], r=[x])
                if lvl < 5:
                    continue
                pss2 = self.bank[6 + ti % 2]
                self.sumsq_rstd(x, sq, pss2, rstd, n, src_bufs=[x])
                self.tt(DVE, msb[:], x[:], rstd[:].unsqueeze(1).to_broadcast([128, 8, 512]), ALU.mult, r=[x, rstd], w=[msb])
                if lvl < 6:
                    continue
                for c in range(8):
                    off = 0
                    for (h0, nn2) in hc:
                        if c % 2 == 0:
                            self.act(H[:, c, h0:h0 + nn2], msb[:, c, off:off + nn2], AF.Identity, r=[msb, m], w=[H],
                                     scale=m[:, 3, c, vec:vec + 1], bias=m[:, 4, c, vec:vec + 1])
                        else:
                            self.ts(POOL, H[:, c, h0:h0 + nn2], msb[:, c, off:off + nn2], m[:, 3, c, vec:vec + 1],
                                    m[:, 4, c, vec:vec + 1], ALU.mult, ALU.add, r=[msb, m], w=[H])
                        off += nn2

    def ffn_tiles(self):
        tl = []
        t = 0
        while t < TL:
            n = min(456, TL - t)
            tl.append((t, n, t))
            t += n
        tl.append((TL, TC, 4098))
        tl.append((TL + TC, TC, 4356))
        return tl

    def stage_s4a(self, l):
        H = self.Hb
        with contextlib.ExitStack() as es:
            wg = [self.sb(es, f"wg{i}", [128, 8, 128], BF16) for i in range(2)]
            wv = [self.sb(es, f"wv{i}", [128, 8, 128], BF16) for i in range(2)]
            tg = [self.sb(es, f"tg{i}", [128, 512]) for i in range(2)]
            tv = [self.sb(es, f"tv{i}", [128, 512]) for i in range(2)]
            ao = [self.sb(es, f"ao{i}", [128, 512], BF16) for i in range(3)]
            pvl = self.pv[:, l, :]
            k = 0
            for j in range(22):
                g, v = wg[j % 2], wv[j % 2]
                self.dma(POOL, g[:], self.w_up[l, :, j * 128:(j + 1) * 128].rearrange("(kc p) n -> p kc n", p=128), w=[g])
                self.dma(POOL, v[:], self.w_up[l, :, (22 + j) * 128:(23 + j) * 128].rearrange("(kc p) n -> p kc n", p=128), w=[v])
                for (t0, n, h0) in self.ffn_tiles():
                    pg = self.bank[(2 * k) % 8]
                    pv_ = self.bank[(2 * k + 1) % 8]
                    for kc in range(8):
                        self.mm(pg[:, 0:n + 2], g[:, kc, :], H[:, kc, h0:h0 + n + 2], start=(kc == 0), stop=(kc == 7), r=[g, H], w=[pg])
                    for kc in range(8):
                        self.mm(pv_[:, 0:n + 2], v[:, kc, :], H[:, kc, h0:h0 + n + 2], start=(kc == 0), stop=(kc == 7), r=[v, H], w=[pv_])
                    a, b_ = tg[k % 2], tv[k % 2]
                    for (ps, t, ch) in ((pg, a, j), (pv_, b_, 22 + j)):
                        cw = PV_FCW + 3 * ch
                        self.act(t[:, 0:n], ps[:, 0:n], AF.Identity, r=[ps, self.pv], w=[t],
                                 scale=pvl[:, cw:cw + 1], bias=pvl[:, PV_FCB + ch:PV_FCB + ch + 1])
                        self.stt(t[:, 0:n], ps[:, 1:n + 1], pvl[:, cw + 1:cw + 2], t[:, 0:n], ALU.mult, ALU.add, r=[ps, self.pv, t], w=[t])
                        self.stt(t[:, 0:n], ps[:, 2:n + 2], pvl[:, cw + 2:cw + 3], t[:, 0:n], ALU.mult, ALU.add, r=[ps, self.pv, t], w=[t])
                    self.act(a[:, 0:n], a[:, 0:n], AF.Silu, r=[a], w=[a])
                    o = ao[k % 3]
                    self.tt(POOL, o[:, 0:n], a[:, 0:n], b_[:, 0:n], ALU.mult, r=[a, b_], w=[o])
                    self.dma(SP, self.A[j * 128:(j + 1) * 128, t0:t0 + n], o[:, 0:n], r=[o])
                    k += 1

    def stage_s4b(self, l):
        with contextlib.ExitStack() as es:
            wd = self.sb(es, "wd", [128, 22, D], BF16)
            at = [self.sb(es, f"at{i}", [128, 22, 512], BF16) for i in range(2)]
            xa = [self.sb(es, f"xa{i}", [128, 8, 512]) for i in range(2)]
            fsb = self.sb(es, "fsb", [128, 8, 512])
            sq = self.sb(es, "sq", [128, 8, 512], BF16)
            rstd = self.sb(es, "rstd", [128, 512])
            for h in range(2):
                self.dma(POOL, wd[:, h * 11:(h + 1) * 11, :],
                         self.w_down[l, h * 11 * 128:(h + 1) * 11 * 128, :].rearrange("(j p) n -> p j n", p=128), w=[wd])
            m = self.mod
            for ti, (t0, n, vec, hc) in enumerate(self.s1_tiles()):
                a = at[ti % 2]
                x = xa[ti % 2]
                self.dma(SP, a[:], self.A[:, t0:t0 + n].rearrange("(j p) t -> p j t", p=128), w=[a])
                self.dma(SP, x[:], self.XA[:, t0:t0 + n].rearrange("(c p) t -> p c t", p=128), w=[x])
                for nn in range(8):
                    ps = self.bank[nn % 4]
                    for j in range(22):
                        self.mm(ps[:, 0:n], wd[:, j, nn * 128:(nn + 1) * 128], a[:, j, :], start=(j == 0), stop=(j == 21), r=[wd, a], w=[ps])
                    self.cp(DVE, fsb[:, nn, :], ps[:, 0:n], r=[ps], w=[fsb])
                    self.act(sq[:, nn, :], fsb[:, nn, :], AF.Square, r=[fsb], w=[sq])
                pss = self.bank[4 + ti % 2]
                for c in range(8):
                    self.mm(pss[:, 0:n], self.onesb, sq[:, c, :], start=(c == 0), stop=(c == 7), r=[sq, self.cstb], w=[pss])
                self.rsqrt_ps(rstd[:, 0:n], pss[:, 0:n], 1.0 / D, r=[pss], w=[rstd])
                self.tt(DVE, fsb[:], fsb[:], rstd[:].unsqueeze(1).to_broadcast([128, 8, 512]), ALU.mult, r=[fsb, rstd], w=[fsb])
                for c in range(8):
                    self.stt(x[:, c, :], fsb[:, c, :], m[:, 5, c, vec:vec + 1], x[:, c, :], ALU.mult, ALU.add, r=[fsb, m, x], w=[x])
                self.dma(SP, self.xout[:, t0:t0 + n].rearrange("(c p) t -> p c t", p=128), x[:], r=[x])

    def stage_conf(self, l):
        pvl = self.pv[:, l, :]
        with contextlib.ExitStack() as es:
            av = [self.sb(es, f"cav{c}", [128, TL]) for c in range(2)]
            ag = [self.sb(es, f"cag{c}", [128, TL]) for c in range(2)]
            U = [self.sb(es, f"cU{c}", [128, TL + 30], BF16) for c in range(2)]
            DGc = [self.sb(es, f"cDG{c}", [128, 31, 128], BF16) for c in range(2)]
            for c in range(2):
                for kk in range(31):
                    col = PV_CW + 31 * c + kk
                    if c == 0:
                        self.ts(POOL, DGc[c][:, kk, :], self.ident, pvl[:, col:col + 1], None, ALU.mult, r=[self.cst, self.pv], w=[DGc[c]])
                    else:
                        self.act(DGc[c][:, kk, :], self.ident, AF.Copy, r=[self.cst, self.pv], w=[DGc[c]], scale=pvl[:, col:col + 1])
            kconv = 0
            sq = [self.sb(es, f"csq{c}", [128, 512]) for c in range(2)]
            st = self.sb(es, "cst", [128, 3, 512])
            yb = [self.sb(es, f"cy{c}", [128, 512]) for c in range(2)]
            ob = [self.sb(es, f"cob{i}", [128, 512], BF16) for i in range(4)]
            k = 0
            for (tok0, T, hcol0, vec) in SEQS:
                for c in range(2):
                    self.dma(SP, av[c][:, 0:T], self.P[c * 128:(c + 1) * 128, tok0:tok0 + T], w=[av[c]])
                    self.dma(SP, ag[c][:, 0:T], self.P[(2 + c) * 128:(3 + c) * 128, tok0:tok0 + T], w=[ag[c]])
                    self.memset(POOL, U[c][:, 0:15], 0.0, w=[U[c]])
                    self.memset(POOL, U[c][:, 15 + T:30 + T], 0.0, w=[U[c]])
                    self.act(ag[c][:, 0:T], ag[c][:, 0:T], AF.Sigmoid, r=[ag[c]], w=[ag[c]])
                    self.tt(POOL, U[c][:, 15:15 + T], av[c][:, 0:T], ag[c][:, 0:T], ALU.mult, r=[av[c], ag[c]], w=[U[c]])
                    acc = av[c]
                    for t in range(0, T, 512):
                        n = min(512, T - t)
                        psb = self.bank[6 + kconv % 2]
                        kconv += 1
                        for kk in range(31):
                            self.mm(psb[:, 0:n], DGc[c][:, kk, :], U[c][:, t + kk:t + kk + n], start=(kk == 0), stop=(kk == 30),
                                    r=[DGc[c], U[c]], w=[psb])
                        self.act(acc[:, t:t + n], psb[:, 0:n], AF.Identity, r=[psb, self.pv], w=[acc],
                                 bias=pvl[:, PV_CB + c:PV_CB + c + 1], scale=1.0)
                for t in range(0, T, 512):
                    n = min(512, T - t)
                    p1 = self.bank[(2 * k) % 6]
                    p2 = self.bank[(2 * k + 1) % 6]
                    for c in range(2):
                        self.act(sq[c][:, 0:n], av[c][:, t:t + n], AF.Square, r=[av[c]], w=[sq[c]])
                    for c in range(2):
                        self.mm(p1[:, 0:n], self.ones, av[c][:, t:t + n], start=(c == 0), stop=(c == 1), r=[self.cst, av[c]], w=[p1])
                    for c in range(2):
                        self.mm(p2[:, 0:n], self.ones, sq[c][:, 0:n], start=(c == 0), stop=(c == 1), r=[self.cst, sq[c]], w=[p2])
                    self.ts(DVE, st[:, 0, 0:n], p1[:, 0:n], 1.0 / 256, None, ALU.mult, r=[p1], w=[st])
                    self.tt(DVE, st[:, 1, 0:n], st[:, 0, 0:n], st[:, 0, 0:n], ALU.mult, r=[st], w=[st])
                    self.stt(st[:, 1, 0:n], p2[:, 0:n], 1.0 / 256, st[:, 1, 0:n], ALU.mult, ALU.subtract, r=[p2, st], w=[st])
                    self.ts(DVE, st[:, 1, 0:n], st[:, 1, 0:n], 0.0, None, ALU.max, r=[st], w=[st])
                    self.act(st[:, 2, 0:n], st[:, 1, 0:n], AF.Ln, r=[st, self.cst], w=[st], bias=self.c_eps, scale=1.0)
                    self.act(st[:, 2, 0:n], st[:, 2, 0:n], AF.Exp, r=[st], w=[st], scale=-0.5)
                    for c in range(2):
                        self.tt(DVE, yb[c][:, 0:n], av[c][:, t:t + n], st[:, 0, 0:n], ALU.subtract, r=[av[c], st], w=[yb[c]])
                        self.tt(POOL, yb[c][:, 0:n], yb[c][:, 0:n], st[:, 2, 0:n], ALU.mult, r=[yb[c], st], w=[yb[c]])
                        o = ob[(2 * k + c) % 4]
                        self.act(o[:, 0:n], yb[c][:, 0:n], AF.Silu, r=[yb[c], self.pv], w=[o],
                                 scale=pvl[:, PV_CLG + c:PV_CLG + c + 1], bias=pvl[:, PV_CLB + c:PV_CLB + c + 1])
                        self.dma(SP, self.CAT[c * 128:(c + 1) * 128, tok0 + t:tok0 + t + n], o[:, 0:n], r=[o])
                    k += 1

    def attn_unit(self, slot, qf, hp, b0, t0, pieces, vts, otm, h, W):
        nk = sum(p[3] for p in pieces)
        nkt = nk // 128
        base = slot * 1024
        ps = self.pst[:, base:base + nk]
        pb = [self.bank[2 * slot], self.bank[2 * slot + 1]]
        lhsT = qf[b0:b0 + 64, hp, t0:t0 + 128]
        for (col, rhs, bm, n) in pieces:
            self.mm(self.pst[:, base + col:base + col + n], lhsT, rhs, start=True, stop=(bm is None), r=W["qk"], w=pb)
            if bm is not None:
                self.mm(self.pst[:, base + col:base + col + n], self.jrevb, bm, start=False, stop=True, r=W["bm"], w=pb)
        sm = W["sm"][slot * 2 + h % 2]
        pexp = W["pexp"][slot]
        pts = W["pts"][slot]
        yield
        self.S.op(DVE, lambda e: e.reduce_max(out=sm[:, 0:1], in_=ps, axis=AX.X), pb, [sm])
        yield
        self.ts(DVE, sm[:, 1:2], sm[:, 0:1], -0.125, None, ALU.mult, r=[sm], w=[sm])
        self.memset(DVE, sm[:, 2:3], 0.0, w=[sm])
        yield
        self.act(pexp[:, 0:nk], ps, AF.Exp, r=pb + [sm], w=[pexp, sm], scale=0.125, bias=sm[:, 1:2], accum_out=sm[:, 2:3])
        yield
        ptbank = self.bank[4 + slot]
        ptb = ptbank.t.bitcast(BF16)
        for c in range(nkt):
            self.tr(ptb[:, c * 128:(c + 1) * 128], pexp[:, c * 128:(c + 1) * 128], self.identb, r=[pexp, self.cstb], w=[ptbank])
        yield
        self.cp(ACT if slot == 0 else DVE, pts[:, 0:nkt, :].rearrange("p a b -> p (a b)"), ptb[:, 0:nk], r=[ptbank], w=[pts])
        self.recip(sm[:, 3:4], sm[:, 2:3], r=[sm], w=[sm])
        yield
        po = self.bank[6]
        for c in range(nkt):
            self.mm(po[:, 0:64], pts[:, c, :], vts[c], start=(c == 0), stop=(c == nkt - 1), r=[pts] + W["v"], w=[po])
        self.ts(DVE, otm[:, h * 64:(h + 1) * 64], po[:, 0:64], sm[:, 3:4], None, ALU.mult, r=[po, sm], w=[otm])
        yield

    def stage_attn(self, l):
        with contextlib.ExitStack() as es:
            qf = self.sb(es, "aqf", [128, 2, TT], BF16)
            kf = self.sb(es, "akf", [128, 2, TT], BF16)
            vf = self.sb(es, "avf", [128, 2, TT], BF16)
            VT = self.sb(es, "aVT", [128, 36, 256], BF16)
            kctx = self.sb(es, "akctx", [128, 2, TC], BF16)
            vctx = self.sb(es, "avctx", [128, 2, 256], BF16)
            HK = self.sb(es, "aHK", [128, 4, 15, 64])
            BMc = [self.sb(es, f"aBM{c}", [128, 4, 640], BF16) for c in range(5)]
            FT = self.sb(es, "aFT", [64, 127])
            r31 = self.sb(es, "ar31", [64, 31])
            ocr = self.sb(es, "aocr", [128, 2, TT], BF16)
            otms = [self.sb(es, f"aotm{i}", [128, 256], BF16) for i in range(2)]
            W = dict(sm=[self.sb(es, f"asm{i}", [128, 4]) for i in range(4)],
                     pexp=[self.sb(es, f"apexp{i}", [128, 896], BF16) for i in range(2)],
                     pts=[self.sb(es, f"apts{i}", [128, 7, 128], BF16) for i in range(2)])
            fpb = Buf("FPb", None)
            for hp in range(2):
                self.dma(SP, qf[:, hp, :], self.PB[hp * 128:(hp + 1) * 128, :], w=[qf])
                self.dma(SP, kf[:, hp, :], self.PB[(2 + hp) * 128:(3 + hp) * 128, :], w=[kf])
                self.dma(SP, vf[:, hp, :], self.PB[(4 + hp) * 128:(5 + hp) * 128, :], w=[vf])
            self.dma(POOL, kctx[:], self.kcache[l].rearrange("a p t -> p a t"), w=[kctx])
            self.dma(POOL, vctx[:], self.vcache[l].rearrange("a p f -> p a f"), w=[vctx])
            for g in range(9):
                bk = self.bank[g % 2]
                bkb = bk.t.bitcast(BF16)
                for jj in range(4):
                    tile = g * 4 + jj
                    for hp in range(2):
                        self.tr(bkb[:, (jj * 2 + hp) * 128:(jj * 2 + hp + 1) * 128], vf[:, hp, tile * 128:(tile + 1) * 128],
                                self.identb, r=[vf, self.cstb], w=[bk])
                self.cp(DVE if g % 2 == 0 else ACT, VT[:, g * 4:(g + 1) * 4, :].rearrange("p a f -> p (a f)"), bkb[:, 0:1024], r=[bk], w=[VT])
            self.memset(POOL, FT[0:60, :], NEG, w=[FT])
            self.dma(SP, r31[0:60, :], self.rpb[l], w=[r31])
            self.act(FT[0:60, 48:79], r31[0:60, :], AF.Copy, r=[r31, FT], w=[FT], scale=8.0)
            self.dma(SP, self.FP[l], FT[0:60, :], r=[FT], w=[fpb])
            for half in range(2):
                src = bass.AP(tensor=self.FP.tensor, offset=l * 60 * 127, ap=[[1, 64], [127, 60], [1, 64]])
                self.dma(SP, HK[64 * half:64 * half + 64].rearrange("p h d k -> p (h d) k"), src, r=[fpb], w=[HK])
            self.tt(DVE, HK[:].rearrange("p h d k -> p (h d) k"), HK[:].rearrange("p h d k -> p (h d) k"),
                    self.cst[:, C_CMH, 0:64].unsqueeze(1).to_broadcast([128, 60, 64]), ALU.add, r=[HK, self.cst], w=[HK])
            engs = (POOL, DVE, ACT, POOL, DVE)
            for case, i in ((0, 0), (1, 1), (2, 2), (3, 30), (4, 31)):
                bm = BMc[case]
                eng = engs[case]
                self.memset(eng if eng != ACT else POOL, bm[:], NEG, w=[bm])
                start = min(max(2 * i - 4, 0), 54)
                for qr in range(2):
                    r_ = 2 * i + qr
                    rs = min(max(r_ - 4, 0), 56)
                    for kr in range(10):
                        krow = start + kr
                        if rs <= krow < rs + 8:
                            dr = krow - r_ + 7
                            self.cp(eng, bm[64 * qr:64 * qr + 64, :, kr * 64:(kr + 1) * 64], HK[64 * qr:64 * qr + 64, :, dr, :], r=[HK], w=[bm])
            W["qk"] = [qf, kf, kctx]
            W["v"] = [VT, vctx]
            ob7 = self.bank[7].t.bitcast(BF16)

            def qtile(t0, units):
                def gen(slot):
                    otm = otms[slot]
                    for (hp, b0, pieces, vts, h, bmb) in units:
                        W["bm"] = [bmb, self.cstb] if bmb is not None else [self.cstb]
                        yield from self.attn_unit(slot, qf, hp, b0, t0, pieces, vts, otm, h, W)
                    ob = self.bank[7]
                    for hp in range(2):
                        self.tr(ob7[:, hp * 128:(hp + 1) * 128], otm[:, hp * 128:(hp + 1) * 128], self.identb, r=[otm, self.cstb], w=[ob])
                    self.cp(ACT if slot == 0 else DVE, ocr[:, :, t0:t0 + 128], ob7[:, 0:256].rearrange("p (a b) -> p a b", a=2), r=[ob], w=[ocr])
                    yield
                return gen

            jobs = []
            for i in range(32):
                t0 = 128 * i
                start = min(max(2 * i - 4, 0), 54)
                k0 = 64 * start
                case = {0: 0, 1: 1, 30: 3, 31: 4}.get(i, 2)
                units = []
                for h in range(4):
                    hp, b0 = h // 2, 64 * (h % 2)
                    bm = BMc[case]
                    pieces = [(0, kf[b0:b0 + 64, hp, k0:k0 + 512], bm[:, h, 0:512], 512),
                              (512, kf[b0:b0 + 64, hp, k0 + 512:k0 + 640], bm[:, h, 512:640], 128),
                              (640, kctx[b0:b0 + 64, hp, :], None, 256)]
                    vts = [VT[:, start // 2 + c, h * 64:(h + 1) * 64] for c in range(5)] + [vctx[:, c, h * 64:(h + 1) * 64] for c in range(2)]
                    units.append((hp, b0, pieces, vts, h, bm))
                jobs.append(qtile(t0, units))
            for s_ in range(2):
                tok0 = TL + s_ * TC
                for qi in range(2):
                    t0 = tok0 + 128 * qi
                    units = []
                    for h in range(4):
                        hp, b0 = h // 2, 64 * (h % 2)
                        pieces = [(0, kf[b0:b0 + 64, hp, tok0:tok0 + 256], None, 256)]
                        vts = [VT[:, tok0 // 128 + c, h * 64:(h + 1) * 64] for c in range(2)]
                        units.append((hp, b0, pieces, vts, h, None))
                    jobs.append(qtile(t0, units))
            self.run_pool(jobs, 2)
            for hp in range(2):
                self.dma(SP, self.CAT[768 + hp * 128:768 + (hp + 1) * 128, :], ocr[:, hp, :], r=[ocr])

    @staticmethod
    def run_pool(jobs, nslots):
        jobs = iter(jobs)
        active = {}
        free = list(range(nslots))
        while True:
            while free:
                jb = next(jobs, None)
                if jb is None:
                    break
                sl = free.pop(0)
                active[sl] = jb(sl)
            if not active:
                break
            for sl in list(active):
                try:
                    next(active[sl])
                except StopIteration:
                    del active[sl]
                    free.append(sl)

    def stage_gdn(self, l):
        pvl = self.pv[:, l, :]
        bank = self.bank
        S = self.S
        with contextlib.ExitStack() as es:
            sb = lambda name, shape, dt=F32: self.sb(es, "g" + name, shape, dt)
            qf, kf, vf = sb("qf", [128, 4, TL], BF16), sb("kf", [128, 4, TL], BF16), sb("vf", [128, 4, TL], BF16)
            NPM = TL // 128
            GT = sb("GT", [128, NPM, 16])
            beta, nbeta, g, GC, kds, negc = (sb(n, [128, NPM, 8]) for n in ("beta", "nbeta", "g", "GC", "kds", "negc"))
            t1, t2 = sb("t1", [128, NPM, 8]), sb("t2", [128, NPM, 8])
            egl = sb("egl", [128, NPM, 2, 8])
            gsb = sb("gsb", [128, 16])
            nega = sb("nega", [128, 8])
            ident, identb, ones = self.ident, self.identb, self.ones
            cst = self.cst
            one_ap = self.cst[:, C_EPS, 1:2]
            X, Y, Z, Wk, V, A = bank[0], bank[1], bank[2], bank[3], bank[4], bank[5]
            B2 = [bank[6], bank[7]]
            Bap = self.pst[:, 6 * 512:8 * 512].rearrange("p (h a k) -> p h a k", h=4, a=2)
            Xb = X.t.bitcast(BF16)
            def v3(b):
                return b.t.rearrange("p (h k) -> p h k", h=4)
            def bc_h(ap2):
                return ap2.unsqueeze(1).to_broadcast([128, 4, 128])
            def bc_i(ap2):
                return ap2.unsqueeze(2).to_broadcast([128, 4, 128])
            self.dma(SP, gsb[:], self.gsc[l:l + 1, :].partition_broadcast(128), w=[gsb])
            self.act(nega[:], gsb[:, 0:8], AF.Exp, r=[gsb], w=[nega])
            self.ts(DVE, nega[:], nega[:], -1.0, None, ALU.mult, r=[nega], w=[nega])
            for si, (tok0, T, hcol0, vec) in enumerate(SEQS):
                NP_ = T // 128
                with contextlib.ExitStack() as es2:
                    sb2 = lambda name, shape, dt=F32: self.sb(es2, "g" + name, shape, dt)
                    raws = [sb2(f"raw{i}", [128, 2050]) for i in range(2)]
                    cvs = [sb2(f"cv{i}", [128, 2048]) for i in range(2)]
                    sqs = [sb2(f"sq{i}", [128, 512], BF16) for i in range(2)]
                    rss = [sb2(f"rs{i}", [128, 512]) for i in range(2)]
                    bd = sb2("bd", [16, 512])
                    ui = 0
                    tj = 0
                    for ti, (dst, pc0) in enumerate(((qf, 4), (kf, 8), (vf, 12))):
                        for h in range(4):
                            row0 = (pc0 + h) * 128
                            cw = PV_GCW + 3 * (ti * 4 + h)
                            for half in range(0, T, 2048):
                                n = min(2048, T - half)
                                raw, cv = raws[ui % 2], cvs[ui % 2]
                                ui += 1
                                a = max(half - 1, 0)
                                b_ = min(half + n + 1, T)
                                off = a - (half - 1)
                                if off > 0:
                                    self.memset(POOL, raw[:, 0:1], 0.0, w=[raw])
                                if b_ - (half - 1) < n + 2:
                                    self.memset(POOL, raw[:, n + 1:n + 2], 0.0, w=[raw])
                                self.dma(SP, raw[:, off:off + (b_ - a)], self.P[row0:row0 + 128, tok0 + a:tok0 + b_], w=[raw])
                                self.act(cv[:, 0:n], raw[:, 0:n], AF.Copy, r=[raw, self.pv], w=[cv], scale=pvl[:, cw:cw + 1])
                                self.stt(cv[:, 0:n], raw[:, 1:n + 1], pvl[:, cw + 1:cw + 2], cv[:, 0:n], ALU.mult, ALU.add, r=[raw, self.pv, cv], w=[cv])
                                self.stt(cv[:, 0:n], raw[:, 2:n + 2], pvl[:, cw + 2:cw + 3], cv[:, 0:n], ALU.mult, ALU.add, r=[raw, self.pv, cv], w=[cv])
                                self.act(cv[:, 0:n], cv[:, 0:n], AF.Silu, r=[cv], w=[cv])
                                if ti == 2:
                                    self.cp(POOL, dst[:, h, half:half + n], cv[:, 0:n], r=[cv], w=[dst])
                                else:
                                    for t in range(0, n, 512):
                                        m = min(512, n - t)
                                        sq, rs = sqs[tj % 2], rss[tj % 2]
                                        tj += 1
                                        ps = bank[tj % 4]
                                        self.act(sq[:, 0:m], cv[:, t:t + m], AF.Square, r=[cv], w=[sq])
                                        self.mm(ps[:, 0:m], self.onesb, sq[:, 0:m], r=[sq, self.cstb], w=[ps])
                                        self.rsqrt_ps(rs[:, 0:m], ps[:, 0:m], 1.0, r=[ps], w=[rs])
                                        if ti == 0:
                                            self.stt(dst[:, h, half + t:half + t + m], cv[:, t:t + m], 128.0 ** -0.5, rs[:, 0:m], ALU.mult, ALU.mult, r=[cv, rs], w=[dst])
                                        else:
                                            self.tt(DVE, dst[:, h, half + t:half + t + m], cv[:, t:t + m], rs[:, 0:m], ALU.mult, r=[cv, rs], w=[dst])
                    for jb in range(0, T, 512):
                        n = min(512, T - jb)
                        self.dma(SP, bd[:, 0:n], self.P[26 * 128:26 * 128 + 16, tok0 + jb:tok0 + jb + n], w=[bd])
                        for j in range(n // 128):
                            self.tr(X[:, j * 16:(j + 1) * 16], bd[0:16, j * 128:(j + 1) * 128], ident[0:16, 0:16], r=[bd, cst], w=[X])
                        self.cp(DVE, GT[:, jb // 128:jb // 128 + n // 128, :].rearrange("p a b -> p (a b)"), X[:, 0:(n // 128) * 16], r=[X], w=[GT])
                    NB = NP_ * 8
                    def f2(t):
                        return t[:, 0:NP_, :].rearrange("p a b -> p (a b)")
                    dtb = gsb[:, 8:16].unsqueeze(1).to_broadcast([128, NP_, 8])
                    self.act(beta[:, 0:NP_, :], GT[:, 0:NP_, 0:8], AF.Sigmoid, r=[GT], w=[beta])
                    self.ts(DVE, nbeta[:, 0:NP_, :], beta[:, 0:NP_, :], -1.0, None, ALU.mult, r=[beta], w=[nbeta])
                    self.tt(DVE, t1[:, 0:NP_, :], GT[:, 0:NP_, 8:16], dtb, ALU.add, r=[GT, gsb], w=[t1])
                    self.act(t2[:, 0:NP_, :], t1[:, 0:NP_, :], AF.Abs, r=[t1], w=[t2])
                    self.act(t2[:, 0:NP_, :], t2[:, 0:NP_, :], AF.Exp, r=[t2], w=[t2], scale=-1.0)
                    self.act(t2[:, 0:NP_, :], t2[:, 0:NP_, :], AF.Ln, r=[t2, cst], w=[t2], bias=one_ap, scale=1.0)
                    self.ts(DVE, t1[:, 0:NP_, :], t1[:, 0:NP_, :], 0.0, None, ALU.max, r=[t1], w=[t1])
                    self.tt(DVE, t1[:, 0:NP_, :], t1[:, 0:NP_, :], t2[:, 0:NP_, :], ALU.add, r=[t1, t2], w=[t1])
                    self.tt(DVE, g[:, 0:NP_, :], t1[:, 0:NP_, :], nega[:].unsqueeze(1).to_broadcast([128, NP_, 8]), ALU.mult, r=[t1, nega], w=[g])
                    g2 = f2(g)
                    self.mm(X[:, 0:NB], cst[:, C_TRIF, :], g2, r=[cst, g], w=[X])
                    self.mm(Y[:, 0:NB], cst[:, C_TRIB, :], g2, r=[cst, g], w=[Y])
                    self.mm(Z[:, 0:NB], cst[:, C_BLK, :], g2, r=[cst, g], w=[Z])
                    self.mm(Wk[:, 0:NB], cst[:, C_HALFA, :], g2, r=[cst, g], w=[Wk])
                    self.mm(V[:, 0:NB], cst[:, C_HALFB, :], g2, r=[cst, g], w=[V])
                    self.cp(DVE, GC[:, 0:NP_, 0:4], X[:, 0:NB].rearrange("p (a b) -> p a b", b=8)[:, :, 0:4], r=[X], w=[GC])
                    self.cp(DVE, GC[:, 0:NP_, 4:8], Y[:, 0:NB].rearrange("p (a b) -> p a b", b=8)[:, :, 4:8], r=[Y], w=[GC])
                    self.tt(DVE, f2(kds), Z[:, 0:NB], f2(GC), ALU.subtract, r=[Z, GC], w=[kds])
                    self.act(f2(kds), f2(kds), AF.Exp, r=[kds], w=[kds])
                    self.act(f2(t1), f2(GC), AF.Exp, r=[GC], w=[t1])
                    self.tt(DVE, f2(negc), f2(nbeta), f2(t1), ALU.mult, r=[nbeta, t1], w=[negc])
                    self.act(egl[:, 0:NP_, 0, :], Wk[:, 0:NB].rearrange("p (a b) -> p a b", b=8), AF.Exp, r=[Wk], w=[egl])
                    self.act(egl[:, 0:NP_, 1, :], V[:, 0:NB].rearrange("p (a b) -> p a b", b=8), AF.Exp, r=[V], w=[egl])
                S.barrier()
                with contextlib.ExitStack() as es3:
                    def mkset(d):
                        sb3 = lambda name, shape, dt=F32: self.sb(es3, f"g{d}" + name, shape, dt)
                        outs = [dict(Kd=sb3(f"Kd{i}", [128, 4, 128], BF16), Vb=sb3(f"Vb{i}", [128, 4, 128]), ITm=sb3(f"ITm{i}", [128, 4, 128], BF16),
                                     Yb=sb3(f"Yb{i}", [128, 4, 128], BF16), Qg=sb3(f"Qg{i}", [128, 4, 128], BF16)) for i in range(2)]
                        return dict(out=outs, GR=sb3("GR", [128, 4, 128]), D1=sb3("D1", [128, 4, 128]), D2=sb3("D2", [128, 4, 128]),
                                    Pm=sb3("Pm", [128, 4, 128]), PY=sb3("PY", [128, 4, 2, 128]),
                                    Rb=sb3("Rb", [128, 4, 128], BF16), VNb=sb3("VNb", [128, 4, 128], BF16), S=sb3("S", [128, 4, 128]),
                                    Sb=sb3("Sb", [128, 4, 128], BF16), OT=sb3("OT", [128, 4, 128]))
                    sets = [mkset(0), mkset(1)]
                    ofb = {}
                    done_pre = [0, 0]
                    done_scan = [0, 0]

                    def pre_chain(d):
                        def gen(slot):
                            W = sets[d]
                            GR, D1, D2, Pm, PY = (W[k] for k in ("GR", "D1", "D2", "Pm", "PY"))
                            DG = D1
                            m1c, m2c = (C_UINC, C_LSTR) if d == 0 else (C_LINC, C_USTR)
                            c0, c1, c2 = (bank[4 * d + q] for q in range(3))
                            X, Y, Z, Wk, V, A = c0, c1, c2, c0, c0, c2
                            B2 = [c0, c1]
                            Bap = self.pst[:, 4 * d * 512:(4 * d + 2) * 512].rearrange("p (h a k) -> p h a k", h=4, a=2)
                            Xb = X.t.bitcast(BF16)
                            order = list(range(NP_)) if d == 0 else list(range(NP_ - 1, -1, -1))
                            for idx, p in enumerate(order):
                                while idx - done_scan[d] >= 2:
                                    yield
                                O_ = W["out"][idx % 2]
                                Kd, Vb, ITm, Yb, Qg = (O_[k] for k in ("Kd", "Vb", "ITm", "Yb", "Qg"))
                                t0 = 128 * p
                                gcol = GC[:, p, 4 * d:4 * d + 4]
                                for h in range(4):
                                    self.tr(Xb[:, h * 128:(h + 1) * 128], kf[:, h, t0:t0 + 128], identb, r=[kf, self.cstb], w=[X])
                                    self.tr(Xb[:, (4 + h) * 128:(5 + h) * 128], vf[:, h, t0:t0 + 128], identb, r=[vf, self.cstb], w=[X])
                                self.tt(DVE, DG[:], bc_h(ident), bc_i(gcol), ALU.mult, r=[cst, GC], w=[DG])
                                for h in range(4):
                                    self.mm(Y[:, h * 128:(h + 1) * 128], kf[:, h, t0:t0 + 128], kf[:, h, t0:t0 + 128], r=[kf], w=[Y])
                                    self.mm(Z[:, h * 128:(h + 1) * 128], kf[:, h, t0:t0 + 128], qf[:, h, t0:t0 + 128], r=[kf, qf], w=[Z])
                                yield
                                self.tt(DVE, Kd[:], Xb[:, 0:512].rearrange("p (h k) -> p h k", h=4), bc_i(kds[:, p, 4 * d:4 * d + 4]), ALU.mult, r=[X, kds], w=[Kd])
                                self.tt(DVE, Vb[:], Xb[:, 512:1024].rearrange("p (h k) -> p h k", h=4), bc_i(beta[:, p, 4 * d:4 * d + 4]), ALU.mult, r=[X, beta], w=[Vb])
                                yield
                                self.mm(Wk[:, 0:512], ones, DG[:].rearrange("p h k -> p (h k)"), r=[cst, DG], w=[Wk])
                                yield
                                self.cp(ACT, GR[:].rearrange("p h k -> p (h k)"), Wk[:, 0:512], r=[Wk], w=[GR])
                                yield
                                self.tt(DVE, D1[:], GR[:], bc_i(gcol), ALU.subtract, r=[GR, GC], w=[D1])
                                self.tt(DVE, D2[:], bc_i(gcol), GR[:], ALU.subtract, r=[GR, GC], w=[D2])
                                yield
                                self.act(GR[:], GR[:], AF.Exp, r=[GR], w=[GR])
                                self.tt(POOL, D1[:], D1[:], bc_h(cst[:, m1c, :]), ALU.add, r=[D1, cst], w=[D1])
                                self.tt(POOL, D2[:], D2[:], bc_h(cst[:, m2c, :]), ALU.add, r=[D2, cst], w=[D2])
                                yield
                                self.tt(POOL, Qg[:], qf[:, :, t0:t0 + 128], GR[:], ALU.mult, r=[qf, GR], w=[Qg])
                                self.act(D2[:], D2[:], AF.Exp, r=[D2], w=[D2])
                                self.act(D1[:], D1[:], AF.Exp, r=[D1], w=[D1])
                                yield
                                for h in range(4):
                                    self.stt(Pm[:, h, :], Y[:, h * 128:(h + 1) * 128], nbeta[:, p, 4 * d + h:4 * d + h + 1], D2[:, h, :], ALU.mult, ALU.mult,
                                             r=[Y, nbeta, D2], w=[Pm])
                                self.tt(DVE, ITm[:], v3(Z), D1[:], ALU.mult, r=[Z, D1], w=[ITm])
                                yield
                                for h in range(4):
                                    self.tr(V[:, h * 128:(h + 1) * 128], Pm[:, h, :], ident, r=[Pm, cst], w=[V])
                                yield
                                self.cp(ACT, PY[:, :, 0, :], v3(V), r=[V], w=[PY])
                                yield
                                self.tt(POOL, PY[:, :, 1, :], PY[:, :, 0, :], bc_h(ident), ALU.add, r=[PY, cst], w=[PY])
                                for h in range(4):
                                    self.mm(A[:, h * 128:(h + 1) * 128], PY[:, h, 0, :], Pm[:, h, :], r=[PY, Pm], w=[A])
                                    self.mm(Bap[:, h, 0, :], Pm[:, h, :], PY[:, h, 0, :], r=[PY, Pm], w=B2)
                                yield
                                self.cp(ACT, Pm[:], v3(A), r=[A], w=[Pm])
                                self.cp(DVE, PY[:, :, 0, :], Bap[:, :, 0, :], r=B2, w=[PY])
                                yield
                                for k in range(1, 5):
                                    for h in range(4):
                                        self.mm(A[:, h * 128:(h + 1) * 128], PY[:, h, 0, :], Pm[:, h, :], r=[PY, Pm], w=[A])
                                        self.mm(Bap[:, h, :, :].rearrange("p a k -> p (a k)"), Pm[:, h, :], PY[:, h, :, :].rearrange("p a k -> p (a k)"), r=[PY, Pm], w=B2)
                                    yield
                                    self.cp(ACT, Pm[:], v3(A), r=[A], w=[Pm])
                                    self.cp(DVE, PY[:, :, 0, :], Bap[:, :, 0, :], r=B2, w=[PY])
                                    self.tt(DVE, PY[:, :, 1, :], PY[:, :, 1, :], Bap[:, :, 1, :], ALU.add, r=B2 + [PY], w=[PY])
                                    yield
                                for h in range(4):
                                    self.mm(A[:, h * 128:(h + 1) * 128], Pm[:, h, :], PY[:, h, 1, :], r=[PY, Pm], w=[A])
                                yield
                                self.tt(DVE, Yb[:], PY[:, :, 1, :], v3(A), ALU.add, r=[A, PY], w=[Yb])
                                done_pre[d] = idx + 1
                                yield
                        return gen

                    def scan_chain(d):
                        def gen(slot):
                            W = sets[d]
                            Rb, VNb, Sst, Sb, OT = (W[k] for k in ("Rb", "VNb", "S", "Sb", "OT"))
                            OFd = self.OF if d == 0 else self.OFB
                            C3 = bank[4 * d + 3]
                            self.memset(POOL, Rb[:], 0.0, w=[Rb])
                            self.memset(POOL, VNb[:], 0.0, w=[VNb])
                            if si == 0:
                                self.dma(SP, Sst[:], self.s0[l, :, 4 * d:4 * d + 4, :], w=[Sst])
                            else:
                                self.memset(POOL, Sst[:], 0.0, w=[Sst])
                            self.cp(POOL, Sb[:], Sst[:], r=[Sst], w=[Sb])
                            yield
                            order = list(range(NP_)) if d == 0 else list(range(NP_ - 1, -1, -1))
                            for idx, p in enumerate(order):
                                while done_pre[d] <= idx:
                                    yield
                                O_ = W["out"][idx % 2]
                                Kd, Vb, ITm, Yb, Qg = (O_[k] for k in ("Kd", "Vb", "ITm", "Yb", "Qg"))
                                t0 = 128 * p
                                for hf in ((0, 1) if d == 0 else (1, 0)):
                                    R = slice(64 * hf, 64 * hf + 64)
                                    cs = slice(64 * hf, 64 * hf + 64)
                                    for h in range(4):
                                        self.mm(C3[:, h * 128:(h + 1) * 128], kf[:, h, t0:t0 + 128], Sb[:, h, :], r=[kf, Sb], w=[C3])
                                    yield
                                    for h in range(4):
                                        self.stt(Rb[R, h, :], C3[R, h * 128:(h + 1) * 128], negc[R, p, 4 * d + h:4 * d + h + 1], Vb[R, h, :], ALU.mult, ALU.add,
                                                 r=[C3, negc, Vb], w=[Rb])
                                    yield
                                    for h in range(4):
                                        self.mm(C3[:, h * 128:(h + 1) * 128], Yb[:, h, :], Rb[:, h, :], r=[Yb, Rb], w=[C3])
                                    yield
                                    self.cp(ACT, VNb[R, :, :].rearrange("p h k -> p (h k)"), C3[R, 0:512], r=[C3], w=[VNb])
                                    yield
                                    for h in range(4):
                                        self.mm(C3[:, h * 128:(h + 1) * 128], Kd[R, h, :], VNb[R, h, :], r=[Kd, VNb], w=[C3])
                                    yield
                                    for h in range(4):
                                        self.stt(Sst[:, h, :], Sst[:, h, :], egl[:, p, hf, 4 * d + h:4 * d + h + 1], C3[:, h * 128:(h + 1) * 128], ALU.mult, ALU.add,
                                                 r=[Sst, egl, C3], w=[Sst])
                                    yield
                                    for h in range(4):
                                        self.mm(C3[:, h * 64:(h + 1) * 64], Sb[:, h, :], Qg[:, h, cs], start=True, stop=False, r=[Sb, Qg], w=[C3])
                                        self.mm(C3[:, h * 64:(h + 1) * 64], VNb[R, h, :], ITm[R, h, cs], start=False, stop=True, r=[VNb, ITm], w=[C3])
                                    self.cp(POOL, Sb[:], Sst[:], r=[Sst], w=[Sb])
                                    yield
                                    self.cp(ACT, OT[:, :, cs], C3[:, 0:256].rearrange("p (h k) -> p h k", h=4), r=[C3], w=[OT])
                                    yield
                                ofb[(d, p)] = Buf(f"of{d}_{p}", None)
                                self.dma(SP, OFd[:, tok0 + t0:tok0 + t0 + 128].rearrange("(h p) t -> p h t", p=128), OT[:], r=[OT], w=[ofb[(d, p)]])
                                done_scan[d] = idx + 1
                                yield
                            if si > 0:
                                self.dma(SP, self.stout[si - 1, l, d].rearrange("h k v -> k h v"), Sst[:], r=[Sst])
                        return gen

                    self.run_pool([pre_chain(0), pre_chain(1), scan_chain(0), scan_chain(1)], 4)
                    ep = [dict(ofl=self.sb(es3, f"gofl{i}", [128, 4, 128]), ofr=self.sb(es3, f"gofr{i}", [128, 4, 128]), zt=self.sb(es3, f"gzt{i}", [128, 4, 128]),
                               osq=self.sb(es3, f"gosq{i}", [128, 4, 128], BF16), ors=self.sb(es3, f"gors{i}", [128, 4, 128]),
                               ob=self.sb(es3, f"gob{i}", [128, 4, 128], BF16)) for i in range(2)]

                    def epilogue(p):
                        def gen(slot):
                            E = ep[slot]
                            ofl, ofr, zt, osq, ors, ob = (E[k] for k in ("ofl", "ofr", "zt", "osq", "ors", "ob"))
                            t0 = 128 * p
                            cols = slice(tok0 + t0, tok0 + t0 + 128)
                            self.dma(SP, ofl[:], self.OF[:, cols].rearrange("(h p) t -> p h t", p=128), r=[ofb[(0, p)]], w=[ofl])
                            self.dma(SP, ofr[:], self.OFB[:, cols].rearrange("(h p) t -> p h t", p=128), r=[ofb[(1, p)]], w=[ofr])
                            self.dma(SP, zt[:], self.P[16 * 128:20 * 128, cols].rearrange("(h p) t -> p h t", p=128), w=[zt])
                            yield
                            self.tt(POOL, ofl[:], ofl[:], ofr[:], ALU.add, r=[ofl, ofr], w=[ofl])
                            self.act(zt[:], zt[:], AF.Silu, r=[zt], w=[zt])
                            yield
                            self.act(osq[:], ofl[:], AF.Square, r=[ofl], w=[osq])
                            yield
                            pb_ = bank[2 + slot]
                            self.mm(pb_[:, 0:512], self.onesb, osq[:].rearrange("p h k -> p (h k)"), r=[osq, self.cstb], w=[pb_])
                            yield
                            self.act(ors[:].rearrange("p h k -> p (h k)"), pb_[:, 0:512], AF.Ln, r=[pb_, self.cst], w=[ors], bias=self.c_eps, scale=1.0 / 128)
                            yield
                            self.act(ors[:], ors[:], AF.Exp, r=[ors], w=[ors], scale=-0.5)
                            yield
                            self.tt(DVE, ofl[:], ofl[:], ors[:], ALU.mult, r=[ofl, ors], w=[ofl])
                            yield
                            self.stt(ob[:], ofl[:], pvl[:, PV_GNW:PV_GNW + 1], zt[:], ALU.mult, ALU.mult, r=[ofl, self.pv, zt], w=[ob])
                            yield
                            self.dma(SP, self.CAT[256:768, cols].rearrange("(h p) t -> p h t", p=128), ob[:], r=[ob])
                        return gen

                    self.run_pool([epilogue(p) for p in range(NP_)], 2)
                S.barrier()


def _prep_inputs(inp, nl=DEPTH):
    f = lambda a: np.ascontiguousarray(np.asarray(a, dtype=np.float32))
    inp = {k: f(v) for k, v in inp.items()}
    shared = {
        "cst": _consts(),
        "pv": _pack_pv(inp),
        "w_ada": inp["w_ada"][:nl],
        "w_in": np.ascontiguousarray(inp["w_in"][:nl][:, :, _WIN_PERM]),
        "w_out": inp["w_out"][:nl],
        "w_up": inp["w_up"][:nl],
        "w_down": inp["w_down"][:nl],
        "gsc": np.ascontiguousarray(np.concatenate([inp["gdn_a_log"].reshape(DEPTH, 8), inp["gdn_dt_bias"].reshape(DEPTH, 8)], axis=1)[:nl]),
        "rpb": np.ascontiguousarray(inp["na_rpb"].reshape(DEPTH, 60, 31)[:nl]),
    }
    maps = []
    for i in range(8):
        xin = np.concatenate([inp["x_sample"][i].T, inp["x_prompt"][2 * i].T, inp["x_prompt"][2 * i + 1].T], axis=1)
        cv = np.stack([inp["c_ctx"], inp["c"][i]], axis=1)
        cvec = cv.reshape(8, 128, 2).transpose(1, 0, 2).reshape(128, 16)
        kv = inp["cache_attn_kv"][i][:nl]
        kc = kv[:, 0].transpose(0, 1, 3, 2).reshape(nl, 2, 128, TC)
        vc = kv[:, 1].transpose(0, 2, 1, 3).reshape(nl, 2, 128, 256)
        s0 = inp["state_delta"][i][:nl].reshape(nl, 8, 128, 128).transpose(0, 2, 1, 3)
        m = dict(shared)
        m.update(xin=np.ascontiguousarray(xin), cvec=np.ascontiguousarray(cvec), kcache=np.ascontiguousarray(kc),
                 vcache=np.ascontiguousarray(vc), s0=np.ascontiguousarray(s0))
        maps.append(m)
    return maps


def _assemble(results, nl=DEPTH):
    y_prompt = np.zeros((16, TC, D), np.float32)
    y_sample = np.zeros((8, TL, D), np.float32)
    kvn = np.zeros((16, nl, 2, 4, TC, 64), np.float32)
    stn = np.zeros((16, nl, 2, 4, 128, 128), np.float32)
    for i, r in enumerate(results):
        xo = r["xout"]
        y_sample[i] = xo[:, 0:TL].T
        y_prompt[2 * i] = xo[:, TL:TL + TC].T
        y_prompt[2 * i + 1] = xo[:, TL + TC:TT].T
        kvo = r["kvout"]
        for s in range(2):
            kvn[2 * i + s] = kvo[:, s].reshape(nl, 2, 4, 64, TC).transpose(0, 1, 2, 4, 3)
            stn[2 * i + s] = r["stout"][s]
    return y_prompt, y_sample, kvn, stn


_CACHE = {}


def kernel(**inputs):
    if "kb" not in _CACHE:
        kb = KB()
        kb.build()
        _CACHE["kb"] = kb
    kb = _CACHE["kb"]
    maps = _prep_inputs(inputs)
    res = run_bass_kernel_spmd(kb.nc, maps, core_ids=list(range(8)))
    return _assemble(res.results)
```
